# Optimizing a Trainium2 kernel written in Bass

```python
import math
import jax, jax.numpy as jnp
from jax import lax
import numpy as np

D_MODEL = 4096
BATCH = 1
SEQ = 8192
DEPTH = 2

HEAD_DIM = 128
N_HEADS = D_MODEL // HEAD_DIM
ROPE_DIMS = HEAD_DIM // 4
ROPE_THETA = 500000.0
AXIAL_THETA = 10000.0
GRID_W = 64
Q_BLOCK = 128
A_HEADS = N_HEADS // 2
A_KV_HEADS = A_HEADS // 4
A_GROUP = A_HEADS // A_KV_HEADS
A_WINDOW = 128
B_PAIRS = ((128, 1), (512, 4), (2048, 16))
B_GROUP_HEADS = N_HEADS // 4
B_HEADS = len(B_PAIRS) * B_GROUP_HEADS
C_HEADS = N_HEADS // 4
D_HEADS = N_HEADS // 2
D_KV_HEADS = D_HEADS // 4
D_GROUP = D_HEADS // D_KV_HEADS
PEER_HEADS = 8
PEER_NKEYS = 128
PEER_EXPERTS = PEER_NKEYS * PEER_NKEYS
PEER_QDIM = 256
PEER_HALF = PEER_QDIM // 2
PEER_TOPK = 16
PEER_CHUNK = 128
DEEPNORM_ALPHA = (2 * DEPTH) ** 0.25
DEEPNORM_BETA = (8 * DEPTH) ** -0.25
N_EVEN = (DEPTH + 1) // 2
N_ODD = DEPTH // 2
LN_EPS = 1e-5
RMS_EPS = 1e-6
NEG_INF = -1e30

AB_SIZES = (A_HEADS * HEAD_DIM, A_KV_HEADS * HEAD_DIM, A_KV_HEADS * HEAD_DIM,
            B_HEADS * HEAD_DIM, B_HEADS * HEAD_DIM, B_HEADS * HEAD_DIM)
AB_OUT = (A_HEADS + B_GROUP_HEADS) * HEAD_DIM
CD_SIZES = (C_HEADS * 2 * HEAD_DIM, C_HEADS * 2 * HEAD_DIM, C_HEADS * 2 * HEAD_DIM,
            D_HEADS * HEAD_DIM, D_KV_HEADS * HEAD_DIM, D_KV_HEADS * HEAD_DIM)
CD_OUT = (2 * C_HEADS + D_HEADS) * HEAD_DIM

kernel_name = "hybrid_window_dilated_diff_axial_peer_encoder"


def layer_norm(x, g, b):
    xf = x.astype(jnp.float32)
    mu = xf.mean(-1, keepdims=True)
    var = jnp.square(xf - mu).mean(-1, keepdims=True)
    return ((xf - mu) * lax.rsqrt(var + LN_EPS) * g.astype(jnp.float32) + b.astype(jnp.float32)).astype(x.dtype)


def rms_norm(x, g):
    xf = x.astype(jnp.float32)
    return (xf * lax.rsqrt(jnp.square(xf).mean(-1, keepdims=True) + RMS_EPS) * g.astype(jnp.float32)).astype(x.dtype)


def rope(x, pos, start, n, theta):
    half = n // 2
    inv = theta ** (-jnp.arange(half, dtype=jnp.float32) / half)
    ang = pos.astype(jnp.float32)[:, None] * inv[None, :]
    cos, sin = jnp.cos(ang), jnp.sin(ang)
    xs = x[..., start:start + n].astype(jnp.float32)
    x1, x2 = xs[..., :half], xs[..., half:]
    rot = jnp.concatenate([x1 * cos - x2 * sin, x2 * cos + x1 * sin], -1).astype(x.dtype)
    return jnp.concatenate([x[..., :start], rot, x[..., start + n:]], -1)


def partial_rope(x, pos):
    return rope(x, pos, 0, ROPE_DIMS, ROPE_THETA)


def axial_rope(x, rows, cols):
    half = HEAD_DIM // 2
    return rope(rope(x, rows, 0, half, AXIAL_THETA), cols, half, half, AXIAL_THETA)


def split_cols(h, sizes):
    return jnp.split(h, np.cumsum(sizes)[:-1].tolist(), axis=-1)


def to_heads(t, n, dh):
    b, s, _ = t.shape
    return t.reshape(b, s, n, dh).transpose(0, 2, 1, 3)


def from_heads(o):
    b, h, s, dh = o.shape
    return o.transpose(0, 2, 1, 3).reshape(b, s, h * dh)


def banded_attention(q, k, v, window, block, sink=None):
    L, hd = q.shape[-2], q.shape[-1]
    nb = -(-L // block)
    pad = nb * block - L
    qb = jnp.pad(q, [(0, 0)] * (q.ndim - 2) + [(0, pad), (0, 0)])
    qb = qb.reshape(q.shape[:-2] + (nb, block, hd))

    def windows(a):
        ab = jnp.pad(a, [(0, 0)] * (a.ndim - 2) + [(block, pad + block), (0, 0)])
        ab = ab.reshape(a.shape[:-2] + (nb + 2, block, a.shape[-1]))
        return jnp.concatenate([ab[..., :-2, :, :], ab[..., 1:-1, :, :], ab[..., 2:, :, :]], axis=-2)

    kw, vw = windows(k), windows(v)
    s = jnp.einsum('...gnqd,...nkd->...gnqk', qb, kw,
                   preferred_element_type=jnp.float32) * (hd ** -0.5)
    qpos = jnp.arange(nb)[:, None] * block + jnp.arange(block)[None, :]
    kpos = (jnp.arange(nb)[:, None] - 1) * block + jnp.arange(3 * block)[None, :]
    mask = ((jnp.abs(qpos[:, :, None] - kpos[:, None, :]) <= window)
            & (kpos[:, None, :] >= 0) & (kpos[:, None, :] < L))
    s = jnp.where(mask, s, NEG_INF)
    m = s.max(-1)
    if sink is not None:
        sink_b = sink.astype(jnp.float32).reshape(sink.shape + (1, 1))
        m = jnp.maximum(m, sink_b)
    p = jnp.exp(s - m[..., None])
    denom = p.sum(-1)
    if sink is not None:
        denom = denom + jnp.exp(sink_b - m)
    o = jnp.einsum('...gnqk,...nkd->...gnqd', (p / denom[..., None]).astype(v.dtype), vw)
    o = o.reshape(q.shape[:-2] + (nb * block, hd))[..., :L, :]
    lse = (m + jnp.log(denom)).reshape(q.shape[:-2] + (nb * block,))[..., :L]
    return o, lse


def dilated_attention(q, k, v, dilation, n_side):
    b, h, s, hd = q.shape
    L = s // dilation
    fold = lambda t: t.reshape(b, h, L, dilation, hd).transpose(0, 1, 3, 2, 4)
    o, lse = banded_attention(fold(q)[:, :, :, None], fold(k), fold(v), n_side, n_side)
    o = o[:, :, :, 0].transpose(0, 1, 3, 2, 4).reshape(b, h, s, hd)
    lse = lse[:, :, :, 0].transpose(0, 1, 3, 2).reshape(b, h, s)
    return o, lse


def sweep_query_blocks(fn, qs):
    s = qs[0].shape[-2]
    nb = s // Q_BLOCK

    def split(a):
        return jnp.moveaxis(a.reshape(a.shape[:-2] + (nb, Q_BLOCK, a.shape[-1])), -3, 0)

    out = lax.map(lambda blks: fn(*blks), tuple(split(a) for a in qs))
    out = jnp.moveaxis(out, 0, -3)
    return out.reshape(out.shape[:-3] + (s, out.shape[-1]))


def even_mixer(x, w_in, sink, w_out, pos):
    b, s, _ = x.shape
    qa, ka, va, qb, kb, vb = split_cols(x @ w_in, AB_SIZES)
    qa = partial_rope(to_heads(qa, A_HEADS, HEAD_DIM), pos).reshape(b, A_KV_HEADS, A_GROUP, s, HEAD_DIM)
    ka = partial_rope(to_heads(ka, A_KV_HEADS, HEAD_DIM), pos)
    va = to_heads(va, A_KV_HEADS, HEAD_DIM)
    oa, _ = banded_attention(qa, ka, va, A_WINDOW, A_WINDOW, sink)
    oa = from_heads(oa.reshape(b, A_HEADS, s, HEAD_DIM))
    qb = partial_rope(to_heads(qb, B_HEADS, HEAD_DIM), pos)
    kb = partial_rope(to_heads(kb, B_HEADS, HEAD_DIM), pos)
    vb = to_heads(vb, B_HEADS, HEAD_DIM)
    outs, lses = [], []
    for g, (win, dil) in enumerate(B_PAIRS):
        sl = slice(g * B_GROUP_HEADS, (g + 1) * B_GROUP_HEADS)
        o, l = dilated_attention(qb[:, sl], kb[:, sl], vb[:, sl], dil, win // (2 * dil))
        outs.append(o)
        lses.append(l)
    wts = jax.nn.softmax(jnp.stack(lses), axis=0)
    ob = jnp.sum(wts[..., None] * jnp.stack(outs).astype(jnp.float32), axis=0).astype(x.dtype)
    ob = from_heads(ob)
    return jnp.concatenate([oa, ob], -1) @ w_out


def odd_mixer(x, w_in, lam_q1, lam_k1, lam_q2, lam_k2, subln_g, q_norm_g, k_norm_g, w_out,
              pos, rows, cols, lambda_init):
    b, s, _ = x.shape
    qc, kc, vc, qd, kd, vd = split_cols(x @ w_in, CD_SIZES)
    qc = to_heads(qc, C_HEADS, 2 * HEAD_DIM)
    kc = to_heads(kc, C_HEADS, 2 * HEAD_DIM)
    vc = to_heads(vc, C_HEADS, 2 * HEAD_DIM)
    q1, q2 = partial_rope(qc[..., :HEAD_DIM], pos), partial_rope(qc[..., HEAD_DIM:], pos)
    k1, k2 = partial_rope(kc[..., :HEAD_DIM], pos), partial_rope(kc[..., HEAD_DIM:], pos)
    f32 = jnp.float32
    lam = (jnp.exp(jnp.sum(lam_q1.astype(f32) * lam_k1.astype(f32)))
           - jnp.exp(jnp.sum(lam_q2.astype(f32) * lam_k2.astype(f32))) + lambda_init)
    scale = HEAD_DIM ** -0.5

    def c_block(q1b, q2b):
        s1 = jnp.einsum('bhqd,bhsd->bhqs', q1b, k1, preferred_element_type=f32) * scale
        s2 = jnp.einsum('bhqd,bhsd->bhqs', q2b, k2, preferred_element_type=f32) * scale
        a = jax.nn.softmax(s1, -1) - lam * jax.nn.softmax(s2, -1)
        return jnp.einsum('bhqs,bhsd->bhqd', a.astype(vc.dtype), vc)

    oc = sweep_query_blocks(c_block, (q1, q2))
    oc = from_heads(rms_norm(oc, subln_g) * (1.0 - lambda_init))
    qd = axial_rope(rms_norm(to_heads(qd, D_HEADS, HEAD_DIM), q_norm_g), rows, cols)
    qd = qd.reshape(b, D_KV_HEADS, D_GROUP, s, HEAD_DIM)
    kd = axial_rope(rms_norm(to_heads(kd, D_KV_HEADS, HEAD_DIM), k_norm_g), rows, cols)
    vd = to_heads(vd, D_KV_HEADS, HEAD_DIM)

    def d_block(qblk):
        sc = jnp.einsum('bkgqd,bksd->bkgqs', qblk, kd, preferred_element_type=f32) * scale
        return jnp.einsum('bkgqs,bksd->bkgqd', jax.nn.softmax(sc, -1).astype(vd.dtype), vd)

    od = sweep_query_blocks(d_block, (qd,))
    od = from_heads(od.reshape(b, D_HEADS, s, HEAD_DIM))
    return jnp.concatenate([oc, od], -1) @ w_out


def peer(x, w_q, sub_keys, u, v):
    b, s, d = x.shape
    t = b * s
    xt = x.reshape(t, d)
    q = (xt @ w_q).reshape(t, PEER_HEADS, 2, PEER_HALF)
    sc = jnp.einsum('thcd,hcnd->thcn', q, sub_keys, preferred_element_type=jnp.float32)
    sv, si = lax.top_k(sc, PEER_TOPK)
    cand = (sv[:, :, 0, :, None] + sv[:, :, 1, None, :]).reshape(t, PEER_HEADS, PEER_TOPK * PEER_TOPK)
    cidx = (si[:, :, 0, :, None] * PEER_NKEYS + si[:, :, 1, None, :]).reshape(t, PEER_HEADS, PEER_TOPK * PEER_TOPK)
    top_s, top_i = lax.top_k(cand, PEER_TOPK)
    eidx = jnp.take_along_axis(cidx, top_i, axis=-1)
    gate = jax.nn.softmax(top_s, axis=-1)
    n_chunks = t // PEER_CHUNK
    xc = xt.reshape(n_chunks, PEER_CHUNK, d)
    ec = eidx.reshape(n_chunks, PEER_CHUNK, PEER_HEADS * PEER_TOPK)
    gc = gate.reshape(n_chunks, PEER_CHUNK, PEER_HEADS * PEER_TOPK)

    def chunk(args):
        xk, ek, gk = args
        hid = jax.nn.gelu(jnp.einsum('cd,ced->ce', xk, u[ek], preferred_element_type=jnp.float32))
        return jnp.einsum('ce,ced->cd', (gk * hid).astype(v.dtype), v[ek])

    return lax.map(chunk, (xc, ec, gc)).reshape(b, s, d)


def setup_inputs(seed: int = 0) -> dict:
    key = jax.random.key(seed)
    ks = jax.random.split(key, 24)
    nrm = lambda k, shape, scale: jax.random.normal(k, shape, jnp.float32) * scale
    gain = lambda k, shape: 1.0 + nrm(k, shape, 0.02)
    return {
        "x": nrm(ks[0], (BATCH, SEQ, D_MODEL), 1.0),
        "w_in_ab": nrm(ks[1], (N_EVEN, D_MODEL, sum(AB_SIZES)), D_MODEL ** -0.5),
        "sink_a": nrm(ks[2], (N_EVEN, A_KV_HEADS, A_GROUP), 0.5),
        "w_out_ab": nrm(ks[3], (N_EVEN, AB_OUT, D_MODEL), DEEPNORM_BETA * AB_OUT ** -0.5),
        "w_in_cd": nrm(ks[4], (N_ODD, D_MODEL, sum(CD_SIZES)), D_MODEL ** -0.5),
        "lam_q1": nrm(ks[5], (N_ODD, HEAD_DIM), 0.1),
        "lam_k1": nrm(ks[6], (N_ODD, HEAD_DIM), 0.1),
        "lam_q2": nrm(ks[7], (N_ODD, HEAD_DIM), 0.1),
        "lam_k2": nrm(ks[8], (N_ODD, HEAD_DIM), 0.1),
        "subln_g": gain(ks[9], (N_ODD, 2 * HEAD_DIM)),
        "q_norm_g": gain(ks[10], (N_ODD, HEAD_DIM)),
        "k_norm_g": gain(ks[11], (N_ODD, HEAD_DIM)),
        "w_out_cd": nrm(ks[12], (N_ODD, CD_OUT, D_MODEL), DEEPNORM_BETA * CD_OUT ** -0.5),
        "ln_mix_g": gain(ks[13], (DEPTH, D_MODEL)),
        "ln_mix_b": nrm(ks[14], (DEPTH, D_MODEL), 0.02),
        "peer_wq": nrm(ks[15], (DEPTH, D_MODEL, PEER_HEADS * PEER_QDIM), D_MODEL ** -0.5),
        "peer_keys": nrm(ks[16], (DEPTH, PEER_HEADS, 2, PEER_NKEYS, PEER_HALF), PEER_HALF ** -0.5),
        "peer_u": nrm(ks[17], (DEPTH, PEER_EXPERTS, D_MODEL), D_MODEL ** -0.5),
        "peer_v": nrm(ks[18], (DEPTH, PEER_EXPERTS, D_MODEL), DEEPNORM_BETA * PEER_HEADS ** -0.5),
        "ln_ffn_g": gain(ks[19], (DEPTH, D_MODEL)),
        "ln_ffn_b": nrm(ks[20], (DEPTH, D_MODEL), 0.02),
    }


def reference(x, w_in_ab, sink_a, w_out_ab, w_in_cd, lam_q1, lam_k1, lam_q2, lam_k2, subln_g,
              q_norm_g, k_norm_g, w_out_cd, ln_mix_g, ln_mix_b, peer_wq, peer_keys, peer_u, peer_v,
              ln_ffn_g, ln_ffn_b):
    b, s, _ = x.shape
    pos = jnp.arange(s)
    ROWS = s // GRID_W
    rows = jnp.repeat(jnp.arange(ROWS), GRID_W)
    cols = jnp.tile(jnp.arange(GRID_W), ROWS)
    for layer in range(DEPTH):
        i = layer // 2
        if layer % 2 == 0:
            mix = even_mixer(x, w_in_ab[i], sink_a[i], w_out_ab[i], pos)
        else:
            lambda_init = 0.8 - 0.6 * math.exp(-0.3 * layer)
            mix = odd_mixer(x, w_in_cd[i], lam_q1[i], lam_k1[i], lam_q2[i], lam_k2[i], subln_g[i],
                            q_norm_g[i], k_norm_g[i], w_out_cd[i], pos, rows, cols, lambda_init)
        x = layer_norm(DEEPNORM_ALPHA * x + mix, ln_mix_g[layer], ln_mix_b[layer])
        ffn = peer(x, peer_wq[layer], peer_keys[layer], peer_u[layer], peer_v[layer])
        x = layer_norm(DEEPNORM_ALPHA * x + ffn, ln_ffn_g[layer], ln_ffn_b[layer])
    return x
```

```python
import math
from contextlib import ExitStack

import numpy as np
import ml_dtypes

import concourse.bass as bass
import concourse.mybir as mybir
from concourse.bass_utils import run_bass_kernel_spmd

F32 = mybir.dt.float32
BF16 = mybir.dt.bfloat16
I32 = mybir.dt.int32
U32 = mybir.dt.uint32
AF = mybir.ActivationFunctionType
ALU = mybir.AluOpType
AX = mybir.AxisListType

NCORES = 8
SEQ = 8192
D = 4096
T = SEQ // NCORES
MT = T // 128
KT = D // 128
HD = 128
SEM_LIMIT = 2000
NP_BF16 = ml_dtypes.bfloat16


class Buf:
    __slots__ = ("t", "lw", "rd", "name")

    def __init__(self, t, name=""):
        self.t = t
        self.lw = None
        self.rd = []
        self.name = name

    def __getitem__(self, idx):
        return self.t[idx]


class K:
    def __init__(self, nc, stack):
        self.nc = nc
        self.stack = stack
        self.eng = {"pe": nc.tensor, "dve": nc.vector, "act": nc.scalar, "pool": nc.gpsimd, "sp": nc.sync}
        self.cur = {}
        self.seen = {}
        self.nsem = 0
        self.dma_rr = {}
        self.same_engine_wait = {"pe": False, "dve": True, "act": True, "pool": True, "sp": False}
        self.n_inst = 0

    def sbuf(self, name, shape, dtype):
        return Buf(self.stack.enter_context(self.nc.sbuf_tensor(name, list(shape), dtype)), name)

    def psum(self, name, shape, dtype=F32):
        return Buf(self.stack.enter_context(self.nc.psum_tensor(name, list(shape), dtype)), name)

    def dram(self, name, shape, dtype, kind="Internal"):
        return Buf(self.nc.dram_tensor(name, list(shape), dtype, kind=kind).ap(), name)

    def new_sem(self, name):
        self.nsem += 1
        return self.stack.enter_context(self.nc.semaphore(f"{name}_{self.nsem}"))

    def _wait(self, e, tok):
        if tok is None:
            return
        sem, val, src = tok
        if src == e and not self.same_engine_wait[e]:
            return
        seen = self.seen.setdefault(e, {})
        kk = id(sem)
        if seen.get(kk, 0) >= val:
            return
        self.eng[e].wait_ge(sem, val)
        seen[kk] = val

    def _deps(self, e, reads, writes):
        for b in reads:
            self._wait(e, b.lw)
        for b in writes:
            self._wait(e, b.lw)
            for r in b.rd:
                self._wait(e, r)

    def _commit(self, tok, reads, writes):
        for b in reads:
            b.rd.append(tok)
            if len(b.rd) > 64:
                b.rd = b.rd[-48:]
        for b in writes:
            b.lw = tok
            b.rd = []

    def op(self, e, fn, reads=(), writes=()):
        self._deps(e, reads, writes)
        c = self.cur.get(e)
        if c is None or c[1] >= SEM_LIMIT:
            c = [self.new_sem("s" + e), 0]
            self.cur[e] = c
        ins = fn(self.eng[e])
        c[1] += 1
        ins.then_inc(c[0], 1)
        tok = (c[0], c[1], e)
        self._commit(tok, reads, writes)
        self.n_inst += 1
        return tok

    def _dma_slot(self, q):
        R = 8
        ring = self.dma_rr.setdefault(q, {"sems": [], "i": 0})
        i = ring["i"]
        ring["i"] += 1
        slot = i % R
        if len(ring["sems"]) <= slot:
            ring["sems"].append([self.new_sem("d" + q), 0, None])
        s = ring["sems"][slot]
        self._wait(q, s[2])
        if s[1] >= SEM_LIMIT:
            s[0] = self.new_sem("d" + q)
            s[1] = 0
            s[2] = None
        return s

    def dma(self, q, out, in_, reads=(), writes=(), **kw):
        s = self._dma_slot(q)
        self._deps(q, reads, writes)
        ins = self.eng[q].dma_start(out=out, in_=in_, **kw)
        s[1] += 16
        ins.then_inc(s[0], 16)
        tok = (s[0], s[1], "dma")
        s[2] = tok
        self._commit(tok, reads, writes)
        self.n_inst += 1
        return tok

    def gather(self, out, in_, idx_ap, reads=(), writes=(), **kw):
        q = "pool"
        s = self._dma_slot(q)
        self._deps(q, reads, writes)
        ins = self.eng[q].indirect_dma_start(out=out, out_offset=None, in_=in_,
                                             in_offset=bass.IndirectOffsetOnAxis(ap=idx_ap, axis=0), **kw)
        s[1] += 16
        ins.then_inc(s[0], 16)
        tok = (s[0], s[1], "dma")
        s[2] = tok
        self._commit(tok, reads, writes)
        self.n_inst += 1
        return tok

    def finish(self, e="sp"):
        for q, ring in self.dma_rr.items():
            for s in ring["sems"]:
                self._wait(e, s[2])

    def copy(self, e, out_b, out_ap, in_b, in_ap):
        if e == "act":
            return self.op("act", lambda g: g.copy(out=out_ap, in_=in_ap), reads=[in_b], writes=[out_b])
        return self.op(e, lambda g: g.tensor_copy(out=out_ap, in_=in_ap), reads=[in_b], writes=[out_b])


def bcast_free(ap, shape):
    return ap.to_broadcast(list(shape))


def load_xT(k, xT_dram, xb, stage):
    for kt in range(KT):
        s = stage[kt % len(stage)]
        k.dma("sp", s[:], xT_dram[kt * 128:(kt + 1) * 128, :], reads=[xT_dram], writes=[s])
        k.copy("act" if kt % 2 == 0 else "dve", xb[kt], xb[kt][:], s, s[:])


def stream_linear(k, xb, w_dram, N, wst, wb, ps_banks, epilogue, NCH=512, KQ=4, cast_engs=("act", "dve", "pool"),
                  state=None):
    nkt = len(xb)
    NKQ = nkt // KQ
    st = state if state is not None else {"cnt": 0, "oc": 0}
    for c in range(N // NCH):
        n0 = c * NCH
        b = c % 2
        for q in range(NKQ):
            s = wst[st["cnt"] % len(wst)]
            src = w_dram[q * KQ * 128:(q + 1) * KQ * 128, n0:n0 + NCH].rearrange("(j p) c -> p j c", p=128)
            k.dma("sp", s[:], src, reads=[w_dram], writes=[s])
            k.copy(cast_engs[st["cnt"] % len(cast_engs)], wb[b][q], wb[b][q][:], s, s[:])
            st["cnt"] += 1
        for m in range(MT):
            p = ps_banks[st["oc"] % len(ps_banks)]
            for kt in range(nkt):
                q, j = divmod(kt, KQ)
                k.op("pe", lambda g: g.matmul(p[:], lhsT=xb[kt][:, m * 128:(m + 1) * 128], rhs=wb[b][q][:, j, :],
                                              start=(kt == 0), stop=(kt == nkt - 1)),
                     reads=[xb[kt], wb[b][q]], writes=[p])
            epilogue(c, m, p)
            st["oc"] += 1
    return st


def rope_tile(k, src_b, src3, cc, ss, m, groups, rot_w, rb, tmp_t, tmp_u):
    H4 = 4
    w = rot_w
    ccb = cc[:, m, 0:w].unsqueeze(1).to_broadcast([128, H4, w])
    k.op("dve", lambda g: g.tensor_tensor(out=tmp_t[:, :, 0:w], in0=src3[:, :, 0:w], in1=ccb, op=ALU.mult),
         reads=[src_b, cc], writes=[tmp_t])
    first = True
    for (lo, half) in groups:
        hi = lo + half
        s_lo = ss[:, m, lo:lo + half].unsqueeze(1).to_broadcast([128, H4, half])
        s_hi = ss[:, m, hi:hi + half].unsqueeze(1).to_broadcast([128, H4, half])
        k.op("dve", lambda g: g.tensor_tensor(out=tmp_u[:, :, lo:lo + half], in0=src3[:, :, hi:hi + half], in1=s_lo,
                                              op=ALU.mult), reads=[src_b, ss], writes=[tmp_u])
        k.op("dve", lambda g: g.tensor_tensor(out=tmp_u[:, :, hi:hi + half], in0=src3[:, :, lo:lo + half], in1=s_hi,
                                              op=ALU.mult), reads=[src_b, ss], writes=[tmp_u])
    k.op("dve", lambda g: g.tensor_tensor(out=rb[:, :, 0:w], in0=tmp_t[:, :, 0:w], in1=tmp_u[:, :, 0:w], op=ALU.add),
         reads=[tmp_t, tmp_u], writes=[rb])
    if w < 128:
        k.op("act", lambda g: g.copy(out=rb[:, :, w:128], in_=src3[:, :, w:128]), reads=[src_b], writes=[rb])


def phase_inproj(k, pre, xT_dram, w_dram, N, chunk_types, tabs, gains, ident_b, qkT_dram, v_dram, res):
    xb = res["xb"]
    wst = res["wst"]
    wb = res["wb"]
    ps = res["ps_mm"]
    psT = res["ps_tr"]
    rbs = [k.sbuf(f"{pre}rb{i}", [128, 4, 128], BF16) for i in range(2)]
    tmp_t = k.sbuf(f"{pre}tt", [128, 4, 128], F32)
    tmp_u = k.sbuf(f"{pre}tu", [128, 4, 128], F32)
    qn = k.sbuf(f"{pre}qn", [128, 4, 128], F32)
    junk = k.sbuf(f"{pre}junk", [128, 128], F32)
    ssq = k.sbuf(f"{pre}ssq", [128, 4], F32)
    rstd = k.sbuf(f"{pre}rstd", [128, 4], F32)
    fmst = [k.sbuf(f"{pre}fm{i}", [128, 4, T], BF16) for i in range(2)]
    vst = [k.sbuf(f"{pre}vst{i}", [128, 512], BF16) for i in range(2)]
    cnt = {"e": 0}

    def epilogue(c, m, p):
        ty = chunk_types[c]
        i = cnt["e"]
        cnt["e"] += 1
        if ty[0] == "v":
            vs = vst[i % 2]
            k.copy("act", vs, vs[:], p, p[:])
            k.dma("sp", v_dram[m * 128:(m + 1) * 128, ty[1]:ty[1] + 512], vs[:], reads=[vs], writes=[v_dram])
            return
        rb = rbs[i % 2]
        p3 = p[:].rearrange("p (h d) -> p h d", h=4)
        if ty[0] == "rope":
            rope_tile(k, p, p3, tabs["ccp"], tabs["ssp"], m, [(0, 16)], 32, rb, tmp_t, tmp_u)
        else:
            gb = gains[ty[2]]
            for j in range(4):
                k.op("act", lambda g: g.activation(out=junk[:], in_=p[:, j * 128:(j + 1) * 128], func=AF.Square,
                                                   accum_out=ssq[:, j:j + 1]), reads=[p], writes=[junk, ssq])
            k.op("dve", lambda g: g.tensor_scalar(out=rstd[:], in0=ssq[:], scalar1=1.0 / 128, scalar2=1e-6,
                                                  op0=ALU.mult, op1=ALU.add), reads=[ssq], writes=[rstd])
            k.op("act", lambda g: g.activation(out=rstd[:], in_=rstd[:], func=AF.Sqrt), reads=[rstd], writes=[rstd])
            k.op("dve", lambda g: g.reciprocal(out=rstd[:], in_=rstd[:]), reads=[rstd], writes=[rstd])
            k.op("dve", lambda g: g.tensor_tensor(out=qn[:], in0=p3, in1=rstd[:].unsqueeze(2).to_broadcast([128, 4, 128]),
                                                  op=ALU.mult), reads=[p, rstd], writes=[qn])
            k.op("dve", lambda g: g.tensor_tensor(out=qn[:], in0=qn[:], in1=gb[:].unsqueeze(1).to_broadcast([128, 4, 128]),
                                                  op=ALU.mult), reads=[qn, gb], writes=[qn])
            rope_tile(k, qn, qn[:], tabs["cca"], tabs["ssa"], m, [(0, 32), (64, 32)], 128, rb, tmp_t, tmp_u)
        pt = psT[i % len(psT)]
        for j in range(4):
            k.op("pe", lambda g: g.transpose(out=pt[:, j, 0:128], in_=rb[:, j, :], identity=ident_b[:]),
                 reads=[rb, ident_b], writes=[pt])
        fm = fmst[c % 2]
        k.copy("act" if i % 2 else "dve", fm, fm[:, :, m * 128:(m + 1) * 128], pt, pt[:, :, 0:128])
        if m == MT - 1:
            fm0 = ty[1]
            k.dma("sp", qkT_dram[fm0:fm0 + 4].rearrange("j p t -> p j t"), fm[:], reads=[fm], writes=[qkT_dram])

    stream_linear(k, xb, w_dram, N, wst, wb, ps, epilogue, state=res["lin_state"])


def rope_tables():
    pos = np.arange(SEQ, dtype=np.float32)
    half = 16
    inv = (np.float32(500000.0) ** (-np.arange(half, dtype=np.float32) / half)).astype(np.float32)
    ang = pos[:, None] * inv[None, :]
    c, s = np.cos(ang).astype(np.float32), np.sin(ang).astype(np.float32)
    ccp = np.ones((SEQ, 128), np.float32)
    ssp = np.zeros((SEQ, 128), np.float32)
    ccp[:, 0:16] = c
    ccp[:, 16:32] = c
    ssp[:, 0:16] = -s
    ssp[:, 16:32] = s
    rows = (np.arange(SEQ) // 64).astype(np.float32)
    cols = (np.arange(SEQ) % 64).astype(np.float32)
    inv2 = (np.float32(10000.0) ** (-np.arange(32, dtype=np.float32) / 32)).astype(np.float32)
    ar = rows[:, None] * inv2[None, :]
    ac = cols[:, None] * inv2[None, :]
    cr, sr, cc_, sc = (np.cos(ar).astype(np.float32), np.sin(ar).astype(np.float32),
                       np.cos(ac).astype(np.float32), np.sin(ac).astype(np.float32))
    cca = np.concatenate([cr, cr, cc_, cc_], 1)
    ssa = np.concatenate([-sr, sr, -sc, sc], 1)
    return ccp, ssp, cca, ssa


def alloc_linear_res(k, pre):
    res = {}
    res["xb"] = [k.sbuf(f"{pre}xb{i}", [128, T], BF16) for i in range(KT)]
    res["xst"] = [k.sbuf(f"{pre}xst{i}", [128, T], F32) for i in range(2)]
    res["wst"] = [k.sbuf(f"{pre}wst{i}", [128, 4, 512], F32) for i in range(2)]
    res["wb"] = [[k.sbuf(f"{pre}wb{b}_{q}", [128, 4, 512], BF16) for q in range(KT // 4)] for b in range(2)]
    res["ps_mm"] = [k.psum(f"{pre}psmm{i}", [128, 512], F32) for i in range(4)]
    res["ps_tr"] = [k.psum(f"{pre}pstr{i}", [128, 4, 256], BF16) for i in range(2)]
    res["lin_state"] = {"cnt": 0, "oc": 0}
    return res


def load_tabs(k, pre, names, drams):
    tabs = {}
    for nm in names:
        t = k.sbuf(f"{pre}tab_{nm}", [128, MT, 128], F32)
        k.dma("sp", t[:], drams[nm][:, :].rearrange("(m p) d -> p m d", p=128), reads=[drams[nm]], writes=[t])
        tabs[nm] = t
    return tabs


AB_TYPES = ([("rope", c * 4) for c in range(4)] + [("rope", 16)] + [("v", 0)]
            + [("rope", 20 + c * 4) for c in range(6)] + [("rope", 44 + c * 4) for c in range(6)]
            + [("v", 512 + c * 512) for c in range(6)])
AB_NFM, AB_NV = 68, 3584
CD_TYPES = ([("rope", c * 4) for c in range(4)] + [("rope", 16 + c * 4) for c in range(4)]
            + [("v", c * 512) for c in range(4)]
            + [("axial", 32 + c * 4, "qg") for c in range(4)] + [("axial", 48, "kg")] + [("v", 2048)])
CD_NFM, CD_NV = 52, 2560


def build_inproj(layer, types=None, x_bf16=False):
    nc = bass.Bass("TRN2", target_bir_lowering=False)
    if types is None:
        types = AB_TYPES if layer == 0 else CD_TYPES
    N = 512 * len(types)
    nfm, nv = (AB_NFM, AB_NV) if layer == 0 else (CD_NFM, CD_NV)
    with ExitStack() as st:
        k = K(nc, st)
        xT = k.dram("xT", [D, T], BF16 if x_bf16 else F32, "ExternalInput")
        w = k.dram("w", [D, N], F32, "ExternalInput")
        identb = k.dram("identb", [128, 128], BF16, "ExternalInput")
        tabd = {nm: k.dram(nm, [T, 128], F32, "ExternalInput") for nm in (("ccp", "ssp") if layer == 0 else ("ccp", "ssp", "cca", "ssa"))}
        qkT = k.dram("qkT", [nfm, 128, T], BF16, "ExternalOutput")
        vtm = k.dram("vtm", [T, nv], BF16, "ExternalOutput")
        res = alloc_linear_res(k, "a")
        ident_b = k.sbuf("identb_s", [128, 128], BF16)
        k.dma("sp", ident_b[:], identb[:, :], reads=[identb], writes=[ident_b])
        tabs = load_tabs(k, "a", list(tabd.keys()), tabd)
        gains = {}
        if layer == 1:
            for nm in ("qg", "kg"):
                gd = k.dram(nm, [1, 128], F32, "ExternalInput")
                gs = k.sbuf("g_" + nm, [128, 128], F32)
                k.dma("sp", gs[:], gd[0:1, :].to_broadcast([128, 128]), reads=[gd], writes=[gs])
                gains[nm] = gs
        if x_bf16:
            for kt in range(KT):
                k.dma("sp", res["xb"][kt][:], xT[kt * 128:(kt + 1) * 128, :], reads=[xT], writes=[res["xb"][kt]])
        else:
            load_xT(k, xT, res["xb"], res["xst"])
        phase_inproj(k, "a", xT, w, N, types, tabs, gains, ident_b, qkT, vtm, res)
        k.finish("sp")
    return nc


HALO = 1024
TE = T + 2 * HALO
NTE = TE // 128
B_DIL = (1, 4, 16)
B_DT = (1, 2, 8)
MASK_OFF = {"A": 0, 0: 3, 1: 6, 2: 11}
N_MASKS = 28


def band_masks():
    m = np.zeros((128, N_MASKS, 128), np.float32)
    kl = np.arange(128)[:, None]
    ql = np.arange(128)[None, :]
    for j, dt in enumerate((-1, 0, 1)):
        diff = (ql - kl) - dt * 128
        m[:, MASK_OFF["A"] + j, :] = (np.abs(diff) <= 128)
    for g in range(3):
        d, r = B_DIL[g], B_DT[g]
        for j, dt in enumerate(range(-r, r + 1)):
            diff = (ql - kl) - dt * 128
            m[:, MASK_OFF[g] + j, :] = (np.abs(diff) <= 64 * d) & (diff % d == 0)
    return m.astype(NP_BF16)


def attn_unit(k, qT_ap, qT_b, kT_b, v_b, ktiles, mask_b, mask0, acc, first, last, ps_s, pts, cnt, scale):
    n = len(ktiles)
    i0 = 0
    while i0 < n:
        nt = min(4, n - i0)
        ps = ps_s[cnt["s"] % len(ps_s)]
        pt = pts[cnt["s"] % len(pts)]
        cnt["s"] += 1
        for j in range(nt):
            lt = ktiles[i0 + j]
            k.op("pe", lambda g: g.matmul(ps[:, j, :], lhsT=kT_b[:, lt * 128:(lt + 1) * 128], rhs=qT_ap,
                                          start=True, stop=True), reads=[kT_b, qT_b], writes=[ps])
        k.op("act", lambda g: g.activation(out=pt[:, 0:nt, :], in_=ps[:, 0:nt, :], func=AF.Exp, scale=scale),
             reads=[ps], writes=[pt])
        k.op("dve", lambda g: g.tensor_tensor(out=pt[:, 0:nt, :], in0=pt[:, 0:nt, :],
                                              in1=mask_b[:, mask0 + i0:mask0 + i0 + nt, :], op=ALU.mult),
             reads=[pt, mask_b], writes=[pt])
        for j in range(nt):
            lt = ktiles[i0 + j]
            st_ = first and (i0 + j == 0)
            sp_ = last and (i0 + j == n - 1)
            k.op("pe", lambda g: g.matmul(acc[:, 0:129], lhsT=pt[:, j, :], rhs=v_b[:, lt, :], start=st_, stop=sp_),
                 reads=[pt, v_b], writes=[acc])
        i0 += nt


def phase_attn_ab(k, pre, qT_d, kTe_d, vaug_d, sink_d, mask_d, identb_d, oT_d):
    scale = 1.0 / math.sqrt(128.0)
    mask_b = k.sbuf(f"{pre}mask", [128, N_MASKS, 128], BF16)
    k.dma("sp", mask_b[:], mask_d[:, :, :], reads=[mask_d], writes=[mask_b])
    ident_b = k.sbuf(f"{pre}ident", [128, 128], BF16)
    k.dma("sp", ident_b[:], identb_d[:, :], reads=[identb_d], writes=[ident_b])
    esink = k.sbuf(f"{pre}esink", [128, 16], F32)
    k.dma("sp", esink[:], sink_d[0:1, :].to_broadcast([128, 16]), reads=[sink_d], writes=[esink])
    k.op("act", lambda g: g.activation(out=esink[:], in_=esink[:], func=AF.Exp), reads=[esink], writes=[esink])
    NKT = 46
    kbuf = [k.sbuf(f"{pre}kb{i}", [128, NKT * 128], BF16) for i in range(2)]
    vbuf = [k.sbuf(f"{pre}vb{i}", [128, NKT, 129], BF16) for i in range(2)]
    qbuf = [k.sbuf(f"{pre}qb{i}", [128, 4, T], BF16) for i in range(2)]
    ps_s = [k.psum(f"{pre}pss{i}", [128, 4, 128], F32) for i in range(3)]
    pts = [k.sbuf(f"{pre}pt{i}", [128, 4, 128], BF16) for i in range(3)]
    accs = [k.psum(f"{pre}acc{i}", [128, 512], F32) for i in range(2)]
    ps_t = [k.psum(f"{pre}pst{i}", [128, 1024], BF16) for i in range(2)]
    den = k.sbuf(f"{pre}den", [128, 1], F32)
    ob = [k.sbuf(f"{pre}ob{i}", [128, 128], BF16) for i in range(2)]
    oTs = [k.sbuf(f"{pre}oTs{i}", [128, T], BF16) for i in range(2)]
    cnt = {"s": 0, "a": 0, "o": 0}

    def finalize(acc, sink_col, ohead, m):
        if sink_col is not None:
            k.op("dve", lambda g: g.tensor_tensor(out=den[:], in0=acc[:, 128:129], in1=esink[:, sink_col:sink_col + 1],
                                                  op=ALU.add), reads=[acc, esink], writes=[den])
        else:
            k.op("dve", lambda g: g.tensor_copy(out=den[:], in_=acc[:, 128:129]), reads=[acc], writes=[den])
        k.op("dve", lambda g: g.reciprocal(out=den[:], in_=den[:]), reads=[den], writes=[den])
        o = ob[cnt["o"] % 2]
        k.op("dve", lambda g: g.tensor_scalar(out=o[:], in0=acc[:, 0:128], scalar1=den[:, 0:1], scalar2=None,
                                              op0=ALU.mult), reads=[acc, den], writes=[o])
        pt_ = ps_t[cnt["o"] % 2]
        k.op("pe", lambda g: g.transpose(out=pt_[:, 0:128], in_=o[:], identity=ident_b[:]),
             reads=[o, ident_b], writes=[pt_])
        oT = oTs[ohead % 2]
        k.copy("act", oT, oT[:, m * 128:(m + 1) * 128], pt_, pt_[:, 0:128])
        cnt["o"] += 1
        if m == MT - 1:
            k.dma("sp", oT_d[ohead], oT[:], reads=[oT], writes=[oT_d])

    job = 0
    for kvh in range(4):
        kb_, vb_, qb_ = kbuf[job % 2], vbuf[job % 2], qbuf[job % 2]
        job += 1
        k.dma("sp", kb_[:, 0:10 * 128], kTe_d[kvh, :, 7 * 128:17 * 128], reads=[kTe_d], writes=[kb_])
        k.dma("sp", vb_[:, 0:10, :], vaug_d[7 * 128:17 * 128, kvh, :].rearrange("(j p) c -> p j c", p=128),
              reads=[vaug_d], writes=[vb_])
        k.dma("sp", qb_[:], qT_d[kvh * 4:kvh * 4 + 4].rearrange("j p t -> p j t"), reads=[qT_d], writes=[qb_])
        for g4 in range(4):
            h = kvh * 4 + g4
            for m in range(MT):
                acc = accs[cnt["a"] % 2]
                cnt["a"] += 1
                attn_unit(k, qb_[:, g4, m * 128:(m + 1) * 128], qb_, kb_, vb_, [m, m + 1, m + 2], mask_b,
                          MASK_OFF["A"], acc, True, True, ps_s, pts, cnt, scale)
                finalize(acc, h, h, m)
    base = [0, 10, 22]
    lo_ext = [7, 6, 0]
    nload = [10, 12, 24]
    for h in range(8):
        kb_, vb_, qb_ = kbuf[job % 2], vbuf[job % 2], qbuf[job % 2]
        job += 1
        for g in range(3):
            kblk = 4 + g * 8 + h
            k.dma("sp", kb_[:, base[g] * 128:(base[g] + nload[g]) * 128],
                  kTe_d[kblk, :, lo_ext[g] * 128:(lo_ext[g] + nload[g]) * 128], reads=[kTe_d], writes=[kb_])
            k.dma("sp", vb_[:, base[g]:base[g] + nload[g], :],
                  vaug_d[lo_ext[g] * 128:(lo_ext[g] + nload[g]) * 128, kblk, :].rearrange("(j p) c -> p j c", p=128),
                  reads=[vaug_d], writes=[vb_])
            k.dma("sp", qb_[:, g, :], qT_d[20 + g * 8 + h], reads=[qT_d], writes=[qb_])
        for m in range(MT):
            acc = accs[cnt["a"] % 2]
            cnt["a"] += 1
            for g in range(3):
                r = B_DT[g]
                tiles = [base[g] + (8 + m + dt) - lo_ext[g] for dt in range(-r, r + 1)]
                attn_unit(k, qb_[:, g, m * 128:(m + 1) * 128], qb_, kb_, vb_, tiles, mask_b, MASK_OFF[g], acc,
                          g == 0, g == 2, ps_s, pts, cnt, scale)
            finalize(acc, None, 16 + h, m)


def build_attn_ab():
    nc = bass.Bass("TRN2", target_bir_lowering=False)
    with ExitStack() as st:
        k = K(nc, st)
        qT = k.dram("qT", [AB_NFM, 128, T], BF16, "ExternalInput")
        kTe = k.dram("kTe", [28, 128, TE], BF16, "ExternalInput")
        vaug = k.dram("vaug", [TE, 28, 129], BF16, "ExternalInput")
        sink = k.dram("sink", [1, 16], F32, "ExternalInput")
        masks = k.dram("masks", [128, N_MASKS, 128], BF16, "ExternalInput")
        identb = k.dram("identb", [128, 128], BF16, "ExternalInput")
        oT = k.dram("oT", [24, 128, T], BF16, "ExternalOutput")
        phase_attn_ab(k, "b", qT, kTe, vaug, sink, masks, identb, oT)
        k.finish("sp")
    return nc


def host_ext_kv(qkT_all, vtm_all, kblocks, vheads_cols):
    kfull = np.concatenate([q[kblocks] for q in qkT_all], axis=2)
    vfull = np.concatenate(vtm_all, axis=0)
    nk = len(kblocks)
    kpad = np.zeros((nk, 128, SEQ + 2 * HALO), dtype=kfull.dtype)
    kpad[:, :, HALO:HALO + SEQ] = kfull
    nvh = vfull.shape[1] // 128
    vpad = np.zeros((SEQ + 2 * HALO, nvh, 129), dtype=vfull.dtype)
    vpad[HALO:HALO + SEQ, :, 0:128] = vfull.reshape(SEQ, nvh, 128)
    vpad[HALO:HALO + SEQ, :, 128] = 1.0
    outs = []
    for c in range(NCORES):
        outs.append((np.ascontiguousarray(kpad[:, :, c * T:c * T + TE]), np.ascontiguousarray(vpad[c * T:c * T + TE])))
    return outs


ALPHA = float((2 * 2) ** 0.25)


def barrier(k):
    toks = []
    for e, c in k.cur.items():
        if c[1] > 0:
            toks.append((c[0], c[1], e))
    for q, ring in k.dma_rr.items():
        for s in ring["sems"]:
            if s[2] is not None:
                toks.append(s[2])
    for e in ("pe", "dve", "act", "pool", "sp"):
        for tk in toks:
            if tk[2] == e:
                continue
            k._wait(e, tk)


def phase_outproj(k, pre, oT_d, nkt, w_d, x_d, z_d, res):
    xb = res["xb"][:nkt]
    for kt in range(nkt):
        k.dma("sp", xb[kt][:], oT_d[kt], reads=[oT_d], writes=[xb[kt]])
    xin = [k.sbuf(f"{pre}xin{i}", [128, 512], F32) for i in range(3)]
    zt = [k.sbuf(f"{pre}zt{i}", [128, 512], F32) for i in range(3)]
    cnt = {"e": 0}

    def epilogue(c, m, p):
        i = cnt["e"]
        cnt["e"] += 1
        xi, zo = xin[i % 3], zt[i % 3]
        k.dma("sp", xi[:], x_d[m * 128:(m + 1) * 128, c * 512:(c + 1) * 512], reads=[x_d], writes=[xi])
        k.op("dve", lambda g: g.scalar_tensor_tensor(out=zo[:], in0=xi[:], scalar=ALPHA, in1=p[:], op0=ALU.mult,
                                                     op1=ALU.add), reads=[xi, p], writes=[zo])
        k.dma("sp", z_d[m * 128:(m + 1) * 128, c * 512:(c + 1) * 512], zo[:], reads=[zo], writes=[z_d])

    stream_linear(k, xb, w_d, D, res["wst"], res["wb"], res["ps_mm"], epilogue, state=res["lin_state"])


def phase_ln(k, pre, z_d, g_d, b_d, out_d, outT_d, identb_d, ps_tr, mt=MT):
    gt = k.sbuf(f"{pre}g", [128, D], F32)
    bt = k.sbuf(f"{pre}b", [128, D], F32)
    k.dma("sp", gt[:], g_d[0:1, :].to_broadcast([128, D]), reads=[g_d], writes=[gt])
    k.dma("sp", bt[:], b_d[0:1, :].to_broadcast([128, D]), reads=[b_d], writes=[bt])
    ident_b = k.sbuf(f"{pre}ident", [128, 128], BF16)
    k.dma("sp", ident_b[:], identb_d[:, :], reads=[identb_d], writes=[ident_b])
    zs = [k.sbuf(f"{pre}z{i}", [128, D], F32) for i in range(2)]
    os_ = [k.sbuf(f"{pre}o{i}", [128, D], F32) for i in range(2)]
    ob16 = k.sbuf(f"{pre}o16", [128, D], BF16)
    stg = [k.sbuf(f"{pre}stg{i}", [128, KT, 128], BF16) for i in range(2)]
    stats = k.sbuf(f"{pre}stats", [128, 8, 6], F32)
    mv = k.sbuf(f"{pre}mv", [128, 2], F32)
    rstd = k.sbuf(f"{pre}rstd", [128, 1], F32)
    for m in range(mt):
        z, o = zs[m % 2], os_[m % 2]
        k.dma("sp", z[:], z_d[m * 128:(m + 1) * 128, :], reads=[z_d], writes=[z])
        for c in range(8):
            k.op("dve", lambda g: g.bn_stats(out=stats[:, c, :], in_=z[:, c * 512:(c + 1) * 512]), reads=[z], writes=[stats])
        k.op("dve", lambda g: g.bn_aggr(out=mv[:], in_=stats[:].rearrange("p a b -> p (a b)")), reads=[stats], writes=[mv])
        k.op("dve", lambda g: g.tensor_scalar(out=rstd[:], in0=mv[:, 1:2], scalar1=1e-5, scalar2=None, op0=ALU.add),
             reads=[mv], writes=[rstd])
        k.op("act", lambda g: g.activation(out=rstd[:], in_=rstd[:], func=AF.Sqrt), reads=[rstd], writes=[rstd])
        k.op("dve", lambda g: g.reciprocal(out=rstd[:], in_=rstd[:]), reads=[rstd], writes=[rstd])
        k.op("dve", lambda g: g.tensor_scalar(out=o[:], in0=z[:], scalar1=mv[:, 0:1], scalar2=rstd[:, 0:1],
                                              op0=ALU.subtract, op1=ALU.mult), reads=[z, mv, rstd], writes=[o])
        k.op("pool", lambda g: g.tensor_tensor(out=o[:], in0=o[:], in1=gt[:], op=ALU.mult), reads=[o, gt], writes=[o])
        k.op("pool", lambda g: g.tensor_tensor(out=o[:], in0=o[:], in1=bt[:], op=ALU.add), reads=[o, bt], writes=[o])
        k.dma("sp", out_d[m * 128:(m + 1) * 128, :], o[:], reads=[o], writes=[out_d])
        if outT_d is not None:
            k.op("act", lambda g: g.copy(out=ob16[:], in_=o[:]), reads=[o], writes=[ob16])
            sg = stg[m % 2]
            for q in range(4):
                pt = ps_tr[q % len(ps_tr)]
                for j in range(8):
                    kt = q * 8 + j
                    k.op("pe", lambda g: g.transpose(out=pt[:, j, :], in_=ob16[:, kt * 128:(kt + 1) * 128],
                                                     identity=ident_b[:]), reads=[ob16, ident_b], writes=[pt])
                k.copy("act" if q % 2 else "dve", sg, sg[:, q * 8:(q + 1) * 8, :], pt, pt[:])
            k.dma("sp", outT_d[:, m * 128:(m + 1) * 128].rearrange("(kt p) t -> p kt t", p=128), sg[:],
                  reads=[sg], writes=[outT_d])


def build_outproj_ln(nkt, with_T):
    nc = bass.Bass("TRN2", target_bir_lowering=False)
    with ExitStack() as st:
        k = K(nc, st)
        oT = k.dram("oT", [nkt, 128, T], BF16, "ExternalInput")
        w = k.dram("w", [nkt * 128, D], F32, "ExternalInput")
        x = k.dram("x", [T, D], F32, "ExternalInput")
        g = k.dram("g", [1, D], F32, "ExternalInput")
        b = k.dram("b", [1, D], F32, "ExternalInput")
        identb = k.dram("identb", [128, 128], BF16, "ExternalInput")
        z = k.dram("z", [T, D], F32, "Internal")
        x1 = k.dram("x1", [T, D], F32, "ExternalOutput")
        x1T = k.dram("x1T", [D, T], BF16, "ExternalOutput") if with_T else None
        with ExitStack() as ph:
            k.stack = ph
            res = alloc_linear_res(k, "c")
            phase_outproj(k, "c", oT, nkt, w, x, z, res)
            barrier(k)
        k.stack = st
        with ExitStack() as ph:
            k.stack = ph
            ps_tr = [k.psum(f"lnpt{i}", [128, 8, 128], BF16) for i in range(2)]
            phase_ln(k, "l", z, g, b, x1, x1T, identb, ps_tr)
            barrier(k)
        k.stack = st
        k.finish("sp")
    return nc


NEG = -1.0e30


def phase_peer_scores(k, pre, x1T_d, wq_d, keysT_d, sc_d, xb):
    for kt in range(KT):
        k.dma("sp", xb[kt][:], x1T_d[kt * 128:(kt + 1) * 128, :], reads=[x1T_d], writes=[xb[kt]])
    keysT = k.sbuf(f"{pre}keysT", [128, 2048], F32)
    k.dma("sp", keysT[:], keysT_d[:, :], reads=[keysT_d], writes=[keysT])
    wst = [k.sbuf(f"{pre}wst{i}", [128, KT, 128], F32) for i in range(2)]
    wqb = [k.sbuf(f"{pre}wqb{i}", [128, KT, 128], BF16) for i in range(2)]
    qg = [k.sbuf(f"{pre}qg{i}", [128, T], F32) for i in range(2)]
    scst = [k.sbuf(f"{pre}scst{i}", [128, MT, 128], F32) for i in range(2)]
    ps = [k.psum(f"{pre}ps{i}", [128, 512], F32) for i in range(4)]
    ps2 = [k.psum(f"{pre}ps2{i}", [128, 512], F32) for i in range(2)]
    for g in range(16):
        ws, wb_, q_, sc_ = wst[g % 2], wqb[g % 2], qg[g % 2], scst[g % 2]
        k.dma("sp", ws[:], wq_d[:, g * 128:(g + 1) * 128].rearrange("(kt p) c -> p kt c", p=128), reads=[wq_d], writes=[ws])
        k.copy(("act", "dve")[g % 2], wb_, wb_[:], ws, ws[:])
        for tc in range(T // 512):
            p = ps[(g * 2 + tc) % 4]
            for kt in range(KT):
                k.op("pe", lambda e: e.matmul(p[:], lhsT=wb_[:, kt, :], rhs=xb[kt][:, tc * 512:(tc + 1) * 512],
                                              start=(kt == 0), stop=(kt == KT - 1)), reads=[wb_, xb[kt]], writes=[p])
            k.copy(("dve", "act")[tc % 2], q_, q_[:, tc * 512:(tc + 1) * 512], p, p[:])
        for m in range(MT):
            p2 = ps2[m % 2]
            k.op("pe", lambda e: e.matmul(p2[:, 0:128], lhsT=q_[:, m * 128:(m + 1) * 128], rhs=keysT[:, g * 128:(g + 1) * 128],
                                          start=True, stop=True), reads=[q_, keysT], writes=[p2])
            k.copy(("act", "dve")[m % 2], sc_, sc_[:, m, :], p2, p2[:, 0:128])
        k.dma("sp", sc_d[:, g * 128:(g + 1) * 128].rearrange("(m p) n -> p m n", p=128), sc_[:], reads=[sc_], writes=[sc_d])


def phase_peer_main(k, pre, x1_d, sc_d, u_d, v_d, pconst_d, identb_d, z_d, NG=3, mt=MT):
    ident_b = k.sbuf(f"{pre}ident", [128, 128], BF16)
    k.dma("sp", ident_b[:], identb_d[:, :], reads=[identb_d], writes=[ident_b])
    pc = k.sbuf(f"{pre}pc", [128, 48], F32)
    k.dma("sp", pc[:], pconst_d[:, :], reads=[pconst_d], writes=[pc])
    iota16, lo16, hi16 = pc[:, 0:16], pc[:, 16:32], pc[:, 32:48]
    sc = k.sbuf(f"{pre}sc", [128, 16, 128], F32)
    scr = k.sbuf(f"{pre}scr", [128, 16, 128], F32)
    sv = k.sbuf(f"{pre}sv", [128, 16, 16], F32)
    si = k.sbuf(f"{pre}si", [128, 16, 16], U32)
    sif = k.sbuf(f"{pre}sif", [128, 16, 16], F32)
    cand = k.sbuf(f"{pre}cand", [128, 8, 256], F32)
    cscr = k.sbuf(f"{pre}cscr", [128, 8, 256], F32)
    ts = k.sbuf(f"{pre}ts", [128, 8, 16], F32)
    tp = k.sbuf(f"{pre}tp", [128, 8, 16], U32)
    tpf = k.sbuf(f"{pre}tpf", [128, 8, 16], F32)
    w4a = k.sbuf(f"{pre}w4a", [128, 8, 16, 16], F32)
    w4b = k.sbuf(f"{pre}w4b", [128, 8, 16, 16], F32)
    r3a = k.sbuf(f"{pre}r3a", [128, 8, 16], F32)
    r3b = k.sbuf(f"{pre}r3b", [128, 8, 16], F32)
    r3c = k.sbuf(f"{pre}r3c", [128, 8, 16], F32)
    eidx_f = k.sbuf(f"{pre}eidxf", [128, 128], F32)
    eidx = k.sbuf(f"{pre}eidx", [128, 128], U32)
    gate = k.sbuf(f"{pre}gate", [128, 8, 16], F32)
    zsum = k.sbuf(f"{pre}zsum", [128, 8], F32)
    hid = k.sbuf(f"{pre}hid", [128, 128], F32)
    g1 = k.sbuf(f"{pre}g1", [128, 128], F32)
    wgt = k.sbuf(f"{pre}wgt", [128, 128], F32)
    xt = k.sbuf(f"{pre}xt", [128, D], F32)
    junk = k.sbuf(f"{pre}junk", [128, D], BF16)
    ug = [k.sbuf(f"{pre}ug{i}", [128, D], F32) for i in range(NG)]
    vg = [k.sbuf(f"{pre}vg{i}", [128, D], F32) for i in range(NG)]
    vs = [k.sbuf(f"{pre}vs{i}", [128, D], BF16) for i in range(2)]
    zo = [k.sbuf(f"{pre}zo{i}", [128, 512], F32) for i in range(2)]
    acc = [k.psum(f"{pre}acc{i}", [128, 512], F32) for i in range(8)]

    def dve(fn, reads, writes):
        return k.op("dve", fn, reads=reads, writes=writes)

    for m in range(mt):
        k.dma("sp", xt[:], x1_d[m * 128:(m + 1) * 128, :], reads=[x1_d], writes=[xt])
        k.dma("sp", sc[:], sc_d[m * 128:(m + 1) * 128, :].rearrange("p (g n) -> p g n", g=16), reads=[sc_d], writes=[sc])
        for g in range(16):
            dve(lambda e: e.max(out=sv[:, g, 0:8], in_=sc[:, g, :]), [sc], [sv])
            dve(lambda e: e.max_index(out=si[:, g, 0:8], in_max=sv[:, g, 0:8], in_values=sc[:, g, :]), [sc, sv], [si])
            dve(lambda e: e.match_replace(out=scr[:, g, :], in_to_replace=sv[:, g, 0:8], in_values=sc[:, g, :],
                                          imm_value=NEG), [sc, sv], [scr])
            dve(lambda e: e.max(out=sv[:, g, 8:16], in_=scr[:, g, :]), [scr], [sv])
            dve(lambda e: e.max_index(out=si[:, g, 8:16], in_max=sv[:, g, 8:16], in_values=scr[:, g, :]), [scr, sv], [si])
        dve(lambda e: e.tensor_copy(out=sif[:], in_=si[:]), [si], [sif])
        sv4 = sv[:].rearrange("p (h c) r -> p h c r", c=2)
        sif4 = sif[:].rearrange("p (h c) r -> p h c r", c=2)
        c4 = cand[:].rearrange("p h (a b) -> p h a b", a=16)
        dve(lambda e: e.tensor_tensor(out=c4, in0=sv4[:, :, 0, :].unsqueeze(3).to_broadcast([128, 8, 16, 16]),
                                      in1=sv4[:, :, 1, :].unsqueeze(2).to_broadcast([128, 8, 16, 16]), op=ALU.add),
            [sv], [cand])
        for h in range(8):
            dve(lambda e: e.max(out=ts[:, h, 0:8], in_=cand[:, h, :]), [cand], [ts])
            dve(lambda e: e.max_index(out=tp[:, h, 0:8], in_max=ts[:, h, 0:8], in_values=cand[:, h, :]), [cand, ts], [tp])
            dve(lambda e: e.match_replace(out=cscr[:, h, :], in_to_replace=ts[:, h, 0:8], in_values=cand[:, h, :],
                                          imm_value=NEG), [cand, ts], [cscr])
            dve(lambda e: e.max(out=ts[:, h, 8:16], in_=cscr[:, h, :]), [cscr], [ts])
            dve(lambda e: e.max_index(out=tp[:, h, 8:16], in_max=ts[:, h, 8:16], in_values=cscr[:, h, :]), [cscr, ts], [tp])
        dve(lambda e: e.tensor_copy(out=tpf[:], in_=tp[:]), [tp], [tpf])
        tpf4 = tpf[:].unsqueeze(3).to_broadcast([128, 8, 16, 16])
        lo4 = lo16.unsqueeze(1).unsqueeze(1).to_broadcast([128, 8, 16, 16])
        hi4 = hi16.unsqueeze(1).unsqueeze(1).to_broadcast([128, 8, 16, 16])
        io4 = iota16.unsqueeze(1).unsqueeze(1).to_broadcast([128, 8, 16, 16])
        dve(lambda e: e.tensor_tensor(out=w4a[:], in0=tpf4, in1=lo4, op=ALU.is_ge), [tpf, pc], [w4a])
        dve(lambda e: e.tensor_tensor(out=w4b[:], in0=tpf4, in1=hi4, op=ALU.is_ge), [tpf, pc], [w4b])
        dve(lambda e: e.tensor_tensor(out=w4a[:], in0=w4a[:], in1=w4b[:], op=ALU.subtract), [w4a, w4b], [w4a])
        dve(lambda e: e.tensor_tensor(out=w4b[:], in0=w4a[:], in1=sif4[:, :, 0, :].unsqueeze(2).to_broadcast([128, 8, 16, 16]),
                                      op=ALU.mult), [w4a, sif], [w4b])
        dve(lambda e: e.tensor_reduce(out=r3a[:], in_=w4b[:], axis=AX.X, op=ALU.add), [w4b], [r3a])
        dve(lambda e: e.tensor_tensor(out=w4b[:], in0=w4a[:], in1=lo4, op=ALU.mult), [w4a, pc], [w4b])
        dve(lambda e: e.tensor_reduce(out=r3b[:], in_=w4b[:], axis=AX.X, op=ALU.add), [w4b], [r3b])
        dve(lambda e: e.tensor_tensor(out=r3b[:], in0=tpf[:], in1=r3b[:], op=ALU.subtract), [tpf, r3b], [r3b])
        dve(lambda e: e.tensor_tensor(out=w4a[:], in0=r3b[:].unsqueeze(3).to_broadcast([128, 8, 16, 16]), in1=io4,
                                      op=ALU.is_equal), [r3b, pc], [w4a])
        dve(lambda e: e.tensor_tensor(out=w4b[:], in0=w4a[:], in1=sif4[:, :, 1, :].unsqueeze(2).to_broadcast([128, 8, 16, 16]),
                                      op=ALU.mult), [w4a, sif], [w4b])
        dve(lambda e: e.tensor_reduce(out=r3c[:], in_=w4b[:], axis=AX.X, op=ALU.add), [w4b], [r3c])
        ef3 = eidx_f[:].rearrange("p (h r) -> p h r", h=8)
        dve(lambda e: e.scalar_tensor_tensor(out=ef3, in0=r3a[:], scalar=128.0, in1=r3c[:], op0=ALU.mult, op1=ALU.add),
            [r3a, r3c], [eidx_f])
        dve(lambda e: e.tensor_copy(out=eidx[:], in_=eidx_f[:]), [eidx_f], [eidx])
        dve(lambda e: e.tensor_tensor(out=gate[:], in0=ts[:], in1=ts[:, :, 0:1].to_broadcast([128, 8, 16]), op=ALU.subtract),
            [ts], [gate])
        k.op("act", lambda e: e.activation(out=gate[:], in_=gate[:], func=AF.Exp), reads=[gate], writes=[gate])
        dve(lambda e: e.tensor_reduce(out=zsum[:], in_=gate[:], axis=AX.X, op=ALU.add), [gate], [zsum])
        dve(lambda e: e.reciprocal(out=zsum[:], in_=zsum[:]), [zsum], [zsum])
        dve(lambda e: e.tensor_tensor(out=gate[:], in0=gate[:], in1=zsum[:].unsqueeze(2).to_broadcast([128, 8, 16]), op=ALU.mult),
            [gate, zsum], [gate])
        for s in range(128):
            ub = ug[s % NG]
            k.gather(ub[:], u_d[:, :], eidx[:, s:s + 1], reads=[eidx, u_d], writes=[ub])
            dve(lambda e: e.scalar_tensor_tensor(out=junk[:], in0=ub[:], scalar=1.0, in1=xt[:], op0=ALU.mult, op1=ALU.mult,
                                                 accum_out=hid[:, s:s + 1]), [ub, xt], [junk, hid])
        dve(lambda e: e.tensor_tensor(out=g1[:], in0=hid[:], in1=hid[:], op=ALU.mult), [hid], [g1])
        dve(lambda e: e.tensor_scalar(out=g1[:], in0=g1[:], scalar1=0.044715 * 1.5957691216057308,
                                      scalar2=1.5957691216057308, op0=ALU.mult, op1=ALU.add), [g1], [g1])
        dve(lambda e: e.tensor_tensor(out=g1[:], in0=g1[:], in1=hid[:], op=ALU.mult), [g1, hid], [g1])
        k.op("act", lambda e: e.activation(out=g1[:], in_=g1[:], func=AF.Sigmoid), reads=[g1], writes=[g1])
        dve(lambda e: e.tensor_tensor(out=g1[:], in0=g1[:], in1=hid[:], op=ALU.mult), [g1, hid], [g1])
        dve(lambda e: e.tensor_tensor(out=wgt[:], in0=g1[:], in1=gate[:].rearrange("p h r -> p (h r)"), op=ALU.mult),
            [g1, gate], [wgt])
        for s in range(128):
            vb_ = vg[s % NG]
            k.gather(vb_[:], v_d[:, :], eidx[:, s:s + 1], reads=[eidx, v_d], writes=[vb_])
            vs_ = vs[s % 2]
            k.op("act", lambda e: e.activation(out=vs_[:], in_=vb_[:], func=AF.Identity, scale=wgt[:, s:s + 1]),
                 reads=[vb_, wgt], writes=[vs_])
            for c in range(8):
                k.op("pe", lambda e: e.matmul(acc[c][:], lhsT=ident_b[:], rhs=vs_[:, c * 512:(c + 1) * 512],
                                              start=(s == 0), stop=(s == 127)), reads=[ident_b, vs_], writes=[acc[c]])
        for c in range(8):
            z_ = zo[c % 2]
            dve(lambda e: e.scalar_tensor_tensor(out=z_[:], in0=xt[:, c * 512:(c + 1) * 512], scalar=ALPHA, in1=acc[c][:],
                                                 op0=ALU.mult, op1=ALU.add), [xt, acc[c]], [z_])
            k.dma("sp", z_d[m * 128:(m + 1) * 128, c * 512:(c + 1) * 512], z_[:], reads=[z_], writes=[z_d])


def peer_consts():
    pc = np.zeros((128, 48), np.float32)
    pc[:, 0:16] = np.arange(16)
    pc[:, 16:32] = 16 * np.arange(16)
    pc[:, 32:48] = 16 * np.arange(16) + 16
    return pc


def build_peer(final, mt=MT):
    nc = bass.Bass("TRN2", target_bir_lowering=False)
    with ExitStack() as st:
        k = K(nc, st)
        x1 = k.dram("x1", [T, D], F32, "ExternalInput")
        x1T = k.dram("x1T", [D, T], BF16, "ExternalInput")
        wq = k.dram("wq", [D, 2048], F32, "ExternalInput")
        keysT = k.dram("keysT", [128, 2048], F32, "ExternalInput")
        u = k.dram("u", [16384, D], F32, "ExternalInput")
        v = k.dram("v", [16384, D], F32, "ExternalInput")
        g = k.dram("g", [1, D], F32, "ExternalInput")
        b = k.dram("b", [1, D], F32, "ExternalInput")
        pconst = k.dram("pconst", [128, 48], F32, "ExternalInput")
        identb = k.dram("identb", [128, 128], BF16, "ExternalInput")
        sc = k.dram("sc", [T, 2048], F32, "Internal")
        z = k.dram("z", [T, D], F32, "Internal")
        x2 = k.dram("x2", [T, D], F32, "ExternalOutput")
        x2T = None if final else k.dram("x2T", [D, T], BF16, "ExternalOutput")
        with ExitStack() as ph:
            k.stack = ph
            xb = [k.sbuf(f"pxb{i}", [128, T], BF16) for i in range(KT)]
            phase_peer_scores(k, "ps", x1T, wq, keysT, sc, xb)
            barrier(k)
        with ExitStack() as ph:
            k.stack = ph
            phase_peer_main(k, "pm", x1, sc, u, v, pconst, identb, z, mt=mt)
            barrier(k)
        with ExitStack() as ph:
            k.stack = ph
            ps_tr = [k.psum(f"lnpt{i}", [128, 8, 128], BF16) for i in range(2)]
            phase_ln(k, "pl", z, g, b, x2, x2T, identb, ps_tr, mt=mt)
            barrier(k)
        k.stack = st
        k.finish("sp")
    return nc


LAMBDA_INIT = 0.8 - 0.6 * math.exp(-0.3 * 1)
NKT_FULL = SEQ // 128


def dense_unit(k, kT, qT, q0, vaug, W, accs, ps_s, pts, cnt, scale):
    for kt in range(NKT_FULL):
        ps = ps_s[cnt["s"] % len(ps_s)]
        pt = pts[cnt["s"] % len(pts)]
        cnt["s"] += 1
        k.op("pe", lambda g: g.matmul(ps[:], lhsT=kT[:, kt * 128:(kt + 1) * 128], rhs=qT[:, q0:q0 + 512],
                                      start=True, stop=True), reads=[kT, qT], writes=[ps])
        k.op("act", lambda g: g.activation(out=pt[:], in_=ps[:], func=AF.Exp, scale=scale), reads=[ps], writes=[pt])
        for j in range(4):
            k.op("pe", lambda g: g.matmul(accs[j][:, 0:W], lhsT=pt[:, j * 128:(j + 1) * 128], rhs=vaug[:, kt, 0:W],
                                          start=(kt == 0), stop=(kt == NKT_FULL - 1)), reads=[pt, vaug], writes=[accs[j]])


def phase_attn_cd(k, pre, qc_d, kc_d, vc_d, qd_d, kd_d, vd_d, lam_d, subg_d, identb_d, oT_d, nchunks=SEQ // 512):
    scale = 1.0 / math.sqrt(128.0)
    ident_b = k.sbuf(f"{pre}ident", [128, 128], BF16)
    k.dma("sp", ident_b[:], identb_d[:, :], reads=[identb_d], writes=[ident_b])
    lam_t = k.sbuf(f"{pre}lamt", [128, 4, 128], F32)
    k.dma("sp", lam_t[:], lam_d[:, :].unsqueeze(0).to_broadcast([128, 4, 128]), reads=[lam_d], writes=[lam_t])
    lprod = k.sbuf(f"{pre}lprod", [128, 2, 128], F32)
    lsum = k.sbuf(f"{pre}lsum", [128, 2], F32)
    neglam = k.sbuf(f"{pre}neglam", [128, 1], F32)
    lam4 = lam_t[:].rearrange("p (a b) d -> p a b d", b=2)
    k.op("dve", lambda g: g.tensor_tensor(out=lprod[:], in0=lam4[:, :, 0, :], in1=lam4[:, :, 1, :], op=ALU.mult),
         reads=[lam_t], writes=[lprod])
    k.op("dve", lambda g: g.tensor_reduce(out=lsum[:], in_=lprod[:], axis=AX.X, op=ALU.add), reads=[lprod], writes=[lsum])
    k.op("act", lambda g: g.activation(out=lsum[:], in_=lsum[:], func=AF.Exp), reads=[lsum], writes=[lsum])
    k.op("dve", lambda g: g.scalar_tensor_tensor(out=neglam[:], in0=lsum[:, 1:2], scalar=-LAMBDA_INIT, in1=lsum[:, 0:1],
                                                 op0=ALU.add, op1=ALU.subtract), reads=[lsum], writes=[neglam])
    subg = k.sbuf(f"{pre}subg", [128, 256], F32)
    k.dma("sp", subg[:], subg_d[0:1, :].to_broadcast([128, 256]), reads=[subg_d], writes=[subg])
    k.op("dve", lambda g: g.tensor_scalar(out=subg[:], in0=subg[:], scalar1=1.0 - LAMBDA_INIT, scalar2=None, op0=ALU.mult),
         reads=[subg], writes=[subg])

    kTs = [k.sbuf(f"{pre}kT{i}", [128, SEQ], BF16) for i in range(2)]
    qTs = [k.sbuf(f"{pre}qT{i}", [128, SEQ], BF16) for i in range(2)]
    vcs = k.sbuf(f"{pre}vc", [128, NKT_FULL, 257], BF16)
    vds = k.sbuf(f"{pre}vd", [128, NKT_FULL, 129], BF16)
    ps_s = [k.psum(f"{pre}pss{i}", [128, 512], F32) for i in range(2)]
    pts = [k.sbuf(f"{pre}pt{i}", [128, 512], BF16) for i in range(3)]
    accs = [k.psum(f"{pre}acc{i}", [128, 512], F32) for i in range(4)]
    ps_t = [k.psum(f"{pre}pst{i}", [128, 8, 128], BF16) for i in range(2)]
    o1s = k.sbuf(f"{pre}o1s", [128, nchunks * 4, 256], F32)
    den = k.sbuf(f"{pre}den", [128, 1], F32)
    o2 = k.sbuf(f"{pre}o2", [128, 256], F32)
    junk = k.sbuf(f"{pre}junk", [128, 256], F32)
    ssq = k.sbuf(f"{pre}ssq", [128, 1], F32)
    ob = [k.sbuf(f"{pre}ob{i}", [128, 256], BF16) for i in range(2)]
    oTst = [k.sbuf(f"{pre}oTst{i}", [128, 2, 512], BF16) for i in range(2)]
    cnt = {"s": 0, "o": 0, "t": 0}

    k.dma("sp", vcs[:], vc_d[:, :].rearrange("(j p) c -> p j c", p=128), reads=[vc_d], writes=[vcs])
    k.dma("sp", vds[:], vd_d[:, :].rearrange("(j p) c -> p j c", p=128), reads=[vd_d], writes=[vds])

    def recip_den(acc, W):
        k.op("dve", lambda g: g.reciprocal(out=den[:], in_=acc[:, W - 1:W]), reads=[acc], writes=[den])

    def emit_T(obuf, nblk, blk0, q0, j, last):
        pt_ = ps_t[cnt["t"] % 2]
        cnt["t"] += 1
        for b_ in range(nblk):
            k.op("pe", lambda g: g.transpose(out=pt_[:, b_, :], in_=obuf[:, b_ * 128:(b_ + 1) * 128], identity=ident_b[:]),
                 reads=[obuf, ident_b], writes=[pt_])
        stg = oTst[(q0 // 512) % 2]
        k.copy("dve", stg, stg[:, 0:nblk, j * 128:(j + 1) * 128], pt_, pt_[:, 0:nblk, :])
        if last:
            k.dma("sp", oT_d[blk0:blk0 + nblk, :, q0:q0 + 512].rearrange("b p t -> p b t"), stg[:, 0:nblk, :],
                  reads=[stg], writes=[oT_d])

    for mp in range(2):
        kT, qT = kTs[mp], qTs[mp]
        k.dma("sp", kT[:], kc_d[mp], reads=[kc_d], writes=[kT])
        k.dma("sp", qT[:], qc_d[mp], reads=[qc_d], writes=[qT])
        for ch in range(nchunks):
            q0 = ch * 512
            dense_unit(k, kT, qT, q0, vcs, 257, accs, ps_s, pts, cnt, scale)
            for j in range(4):
                acc = accs[j]
                recip_den(acc, 257)
                if mp == 0:
                    k.op("dve", lambda g: g.tensor_scalar(out=o1s[:, ch * 4 + j, :], in0=acc[:, 0:256], scalar1=den[:, 0:1],
                                                          scalar2=None, op0=ALU.mult), reads=[acc, den], writes=[o1s])
                else:
                    k.op("dve", lambda g: g.tensor_scalar(out=o2[:], in0=acc[:, 0:256], scalar1=den[:, 0:1], scalar2=None,
                                                          op0=ALU.mult), reads=[acc, den], writes=[o2])
                    k.op("dve", lambda g: g.scalar_tensor_tensor(out=o2[:], in0=o2[:], scalar=neglam[:, 0:1],
                                                                 in1=o1s[:, ch * 4 + j, :], op0=ALU.mult, op1=ALU.add),
                         reads=[o2, neglam, o1s], writes=[o2])
                    k.op("act", lambda g: g.activation(out=junk[:], in_=o2[:], func=AF.Square, accum_out=ssq[:]),
                         reads=[o2], writes=[junk, ssq])
                    k.op("dve", lambda g: g.tensor_scalar(out=ssq[:], in0=ssq[:], scalar1=1.0 / 256, scalar2=1e-6,
                                                          op0=ALU.mult, op1=ALU.add), reads=[ssq], writes=[ssq])
                    k.op("act", lambda g: g.activation(out=ssq[:], in_=ssq[:], func=AF.Sqrt), reads=[ssq], writes=[ssq])
                    k.op("dve", lambda g: g.reciprocal(out=ssq[:], in_=ssq[:]), reads=[ssq], writes=[ssq])
                    o_ = ob[cnt["o"] % 2]
                    cnt["o"] += 1
                    k.op("dve", lambda g: g.scalar_tensor_tensor(out=o_[:], in0=o2[:], scalar=ssq[:, 0:1], in1=subg[:],
                                                                 op0=ALU.mult, op1=ALU.mult), reads=[o2, ssq, subg], writes=[o_])
                    emit_T(o_, 2, 0, q0, j, j == 3)
    kT = kTs[0]
    k.dma("sp", kT[:], kd_d[:, :], reads=[kd_d], writes=[kT])
    for hq in range(2):
        qT = qTs[hq]
        k.dma("sp", qT[:], qd_d[hq], reads=[qd_d], writes=[qT])
        for ch in range(nchunks):
            q0 = ch * 512
            dense_unit(k, kT, qT, q0, vds, 129, accs, ps_s, pts, cnt, scale)
            for j in range(4):
                acc = accs[j]
                recip_den(acc, 129)
                o_ = ob[cnt["o"] % 2]
                cnt["o"] += 1
                k.op("dve", lambda g: g.tensor_scalar(out=o_[:, 0:128], in0=acc[:, 0:128], scalar1=den[:, 0:1], scalar2=None,
                                                      op0=ALU.mult), reads=[acc, den], writes=[o_])
                emit_T(o_, 1, 2 + hq, q0, j, j == 3)


def build_attn_cd(nchunks=SEQ // 512):
    nc = bass.Bass("TRN2", target_bir_lowering=False)
    with ExitStack() as st:
        k = K(nc, st)
        qc = k.dram("qc", [2, 128, SEQ], BF16, "ExternalInput")
        kc = k.dram("kc", [2, 128, SEQ], BF16, "ExternalInput")
        vc = k.dram("vc", [SEQ, 257], BF16, "ExternalInput")
        qd = k.dram("qd", [2, 128, SEQ], BF16, "ExternalInput")
        kd = k.dram("kd", [128, SEQ], BF16, "ExternalInput")
        vd = k.dram("vd", [SEQ, 129], BF16, "ExternalInput")
        lam = k.dram("lam", [4, 128], F32, "ExternalInput")
        subg = k.dram("subg", [1, 256], F32, "ExternalInput")
        identb = k.dram("identb", [128, 128], BF16, "ExternalInput")
        oT = k.dram("oT", [4, 128, SEQ], BF16, "ExternalOutput")
        phase_attn_cd(k, "e", qc, kc, vc, qd, kd, vd, lam, subg, identb, oT, nchunks=nchunks)
        k.finish("sp")
    return nc


def _run(nc, in_maps):
    res = run_bass_kernel_spmd(nc, in_maps, core_ids=list(range(NCORES)))
    return res.results


def _aug_ones(v):
    out = np.zeros(v.shape[:-1] + (v.shape[-1] + 1,), dtype=v.dtype)
    out[..., :-1] = v
    out[..., -1] = 1.0
    return out


def kernel(x, w_in_ab, sink_a, w_out_ab, w_in_cd, lam_q1, lam_k1, lam_q2, lam_k2, subln_g, q_norm_g, k_norm_g,
           w_out_cd, ln_mix_g, ln_mix_b, peer_wq, peer_keys, peer_u, peer_v, ln_ffn_g, ln_ffn_b):
    f32 = lambda a: np.ascontiguousarray(np.asarray(a, dtype=np.float32))
    x = f32(x)[0]
    ccp, ssp, cca, ssa = rope_tables()
    identb = np.eye(128, dtype=np.float32).astype(NP_BF16)
    masks = band_masks()
    pconst = peer_consts()
    cs = [slice(c * T, (c + 1) * T) for c in range(NCORES)]

    def keysT_of(l):
        return np.ascontiguousarray(f32(peer_keys)[l].reshape(16, 128, 128).transpose(2, 0, 1).reshape(128, 2048))

    w0 = f32(w_in_ab)[0]
    r = _run(build_inproj(0), [{"xT": np.ascontiguousarray(x[cs[c]].T), "w": w0, "identb": identb,
                                "ccp": ccp[cs[c]], "ssp": ssp[cs[c]]} for c in range(NCORES)])
    qkT_all = [r[c]["qkT"] for c in range(NCORES)]
    ext = host_ext_kv(qkT_all, [r[c]["vtm"] for c in range(NCORES)], [16, 17, 18, 19] + list(range(44, 68)), None)
    sink = f32(sink_a)[0].reshape(1, 16)
    r = _run(build_attn_ab(), [{"qT": qkT_all[c], "kTe": ext[c][0], "vaug": ext[c][1], "sink": sink, "masks": masks,
                                "identb": identb} for c in range(NCORES)])
    oT = [r[c]["oT"] for c in range(NCORES)]
    del ext, qkT_all
    g0, b0 = f32(ln_mix_g)[0:1], f32(ln_mix_b)[0:1]
    wo0 = f32(w_out_ab)[0]
    r = _run(build_outproj_ln(24, True), [{"oT": oT[c], "w": wo0, "x": x[cs[c]], "g": g0, "b": b0, "identb": identb}
                                          for c in range(NCORES)])
    x1 = [r[c]["x1"] for c in range(NCORES)]
    x1T = [r[c]["x1T"] for c in range(NCORES)]
    u0, v0 = f32(peer_u)[0], f32(peer_v)[0]
    wq0 = f32(peer_wq)[0]
    kT0 = keysT_of(0)
    gf0, bf0 = f32(ln_ffn_g)[0:1], f32(ln_ffn_b)[0:1]
    r = _run(build_peer(False), [{"x1": x1[c], "x1T": x1T[c], "wq": wq0, "keysT": kT0, "u": u0, "v": v0, "g": gf0, "b": bf0,
                                  "pconst": pconst, "identb": identb} for c in range(NCORES)])
    x2 = [r[c]["x2"] for c in range(NCORES)]
    x2T = [r[c]["x2T"] for c in range(NCORES)]
    w1 = f32(w_in_cd)[0]
    qg, kg = f32(q_norm_g)[0:1], f32(k_norm_g)[0:1]
    r = _run(build_inproj(1, x_bf16=True), [{"xT": x2T[c], "w": w1, "identb": identb, "ccp": ccp[cs[c]], "ssp": ssp[cs[c]],
                                             "cca": cca[cs[c]], "ssa": ssa[cs[c]], "qg": qg, "kg": kg} for c in range(NCORES)])
    qk = np.concatenate([r[c]["qkT"] for c in range(NCORES)], axis=2)
    vt = np.concatenate([r[c]["vtm"] for c in range(NCORES)], axis=0)
    lam = np.concatenate([f32(lam_q1)[0:1], f32(lam_k1)[0:1], f32(lam_q2)[0:1], f32(lam_k2)[0:1]], 0)
    subg = f32(subln_g)[0:1]
    maps = []
    for c in range(NCORES):
        maps.append({"qc": np.ascontiguousarray(qk[2 * c:2 * c + 2]), "kc": np.ascontiguousarray(qk[16 + 2 * c:18 + 2 * c]),
                     "vc": _aug_ones(vt[:, 256 * c:256 * c + 256]), "qd": np.ascontiguousarray(qk[32 + 2 * c:34 + 2 * c]),
                     "kd": np.ascontiguousarray(qk[48 + c // 2]),
                     "vd": _aug_ones(vt[:, 2048 + 128 * (c // 2):2048 + 128 * (c // 2) + 128]),
                     "lam": lam, "subg": subg, "identb": identb})
    r = _run(build_attn_cd(), maps)
    del qk, vt, maps
    ofull = np.zeros((32, 128, SEQ), dtype=NP_BF16)
    for c in range(NCORES):
        o = r[c]["oT"]
        ofull[2 * c] = o[0]
        ofull[2 * c + 1] = o[1]
        ofull[16 + 2 * c] = o[2]
        ofull[16 + 2 * c + 1] = o[3]
    g1, b1 = f32(ln_mix_g)[1:2], f32(ln_mix_b)[1:2]
    wo1 = f32(w_out_cd)[0]
    r = _run(build_outproj_ln(32, True), [{"oT": np.ascontiguousarray(ofull[:, :, cs[c]]), "w": wo1, "x": x2[c], "g": g1,
                                           "b": b1, "identb": identb} for c in range(NCORES)])
    x3 = [r[c]["x1"] for c in range(NCORES)]
    x3T = [r[c]["x1T"] for c in range(NCORES)]
    u1, v1 = f32(peer_u)[1], f32(peer_v)[1]
    gf1, bf1 = f32(ln_ffn_g)[1:2], f32(ln_ffn_b)[1:2]
    r = _run(build_peer(True), [{"x1": x3[c], "x1T": x3T[c], "wq": f32(peer_wq)[1], "keysT": keysT_of(1), "u": u1, "v": v1,
                                 "g": gf1, "b": bf1, "pconst": pconst, "identb": identb} for c in range(NCORES)])
    out = np.concatenate([r[c]["x2"] for c in range(NCORES)], axis=0)
    return out[None].astype(np.float32)
```

```python
import math
from contextlib import ExitStack

import numpy as np
import ml_dtypes

import concourse.bass as bass
import concourse.mybir as mybir
from concourse.bass_utils import run_bass_kernel_spmd

F32 = mybir.dt.float32
BF16 = mybir.dt.bfloat16
I32 = mybir.dt.int32
U32 = mybir.dt.uint32
AF = mybir.ActivationFunctionType
ALU = mybir.AluOpType
AX = mybir.AxisListType

NCORES = 8
SEQ = 8192
D = 4096
T = SEQ // NCORES
MT = T // 128
KT = D // 128
HD = 128
SEM_LIMIT = 2000
NP_BF16 = ml_dtypes.bfloat16


class Buf:
    __slots__ = ("t", "lw", "rd", "name")

    def __init__(self, t, name=""):
        self.t = t
        self.lw = None
        self.rd = []
        self.name = name

    def __getitem__(self, idx):
        return self.t[idx]


class K:
    def __init__(self, nc, stack):
        self.nc = nc
        self.stack = stack
        self.eng = {"pe": nc.tensor, "dve": nc.vector, "act": nc.scalar, "pool": nc.gpsimd, "sp": nc.sync}
        self.cur = {}
        self.seen = {}
        self.nsem = 0
        self.dma_rr = {}
        self.same_engine_wait = {"pe": False, "dve": True, "act": True, "pool": True, "sp": False}
        self.n_inst = 0

    def sbuf(self, name, shape, dtype):
        return Buf(self.stack.enter_context(self.nc.sbuf_tensor(name, list(shape), dtype)), name)

    def psum(self, name, shape, dtype=F32):
        return Buf(self.stack.enter_context(self.nc.psum_tensor(name, list(shape), dtype)), name)

    def dram(self, name, shape, dtype, kind="Internal"):
        return Buf(self.nc.dram_tensor(name, list(shape), dtype, kind=kind).ap(), name)

    def new_sem(self, name):
        self.nsem += 1
        return self.stack.enter_context(self.nc.semaphore(f"{name}_{self.nsem}"))

    def _wait(self, e, tok):
        if tok is None:
            return
        sem, val, src = tok
        if src == e and not self.same_engine_wait[e]:
            return
        seen = self.seen.setdefault(e, {})
        kk = id(sem)
        if seen.get(kk, 0) >= val:
            return
        self.eng[e].wait_ge(sem, val)
        seen[kk] = val

    def _deps(self, e, reads, writes):
        for b in reads:
            self._wait(e, b.lw)
        for b in writes:
            self._wait(e, b.lw)
            for r in b.rd:
                self._wait(e, r)

    def _commit(self, tok, reads, writes):
        for b in reads:
            b.rd.append(tok)
            if len(b.rd) > 64:
                b.rd = b.rd[-48:]
        for b in writes:
            b.lw = tok
            b.rd = []

    def op(self, e, fn, reads=(), writes=()):
        self._deps(e, reads, writes)
        c = self.cur.get(e)
        if c is None or c[1] >= SEM_LIMIT:
            c = [self.new_sem("s" + e), 0]
            self.cur[e] = c
        ins = fn(self.eng[e])
        c[1] += 1
        ins.then_inc(c[0], 1)
        tok = (c[0], c[1], e)
        self._commit(tok, reads, writes)
        self.n_inst += 1
        return tok

    def _dma_slot(self, q):
        R = 8
        ring = self.dma_rr.setdefault(q, {"sems": [], "i": 0})
        i = ring["i"]
        ring["i"] += 1
        slot = i % R
        if len(ring["sems"]) <= slot:
            ring["sems"].append([self.new_sem("d" + q), 0, None])
        s = ring["sems"][slot]
        self._wait(q, s[2])
        if s[1] >= SEM_LIMIT:
            s[0] = self.new_sem("d" + q)
            s[1] = 0
            s[2] = None
        return s

    def dma(self, q, out, in_, reads=(), writes=(), **kw):
        s = self._dma_slot(q)
        self._deps(q, reads, writes)
        ins = self.eng[q].dma_start(out=out, in_=in_, **kw)
        s[1] += 16
        ins.then_inc(s[0], 16)
        tok = (s[0], s[1], "dma")
        s[2] = tok
        self._commit(tok, reads, writes)
        self.n_inst += 1
        return tok

    def gather(self, out, in_, idx_ap, reads=(), writes=(), **kw):
        q = "pool"
        s = self._dma_slot(q)
        self._deps(q, reads, writes)
        ins = self.eng[q].indirect_dma_start(out=out, out_offset=None, in_=in_,
                                             in_offset=bass.IndirectOffsetOnAxis(ap=idx_ap, axis=0), **kw)
        s[1] += 16
        ins.then_inc(s[0], 16)
        tok = (s[0], s[1], "dma")
        s[2] = tok
        self._commit(tok, reads, writes)
        self.n_inst += 1
        return tok

    def finish(self, e="sp"):
        for q, ring in self.dma_rr.items():
            for s in ring["sems"]:
                self._wait(e, s[2])

    def copy(self, e, out_b, out_ap, in_b, in_ap):
        if e == "act":
            return self.op("act", lambda g: g.copy(out=out_ap, in_=in_ap), reads=[in_b], writes=[out_b])
        return self.op(e, lambda g: g.tensor_copy(out=out_ap, in_=in_ap), reads=[in_b], writes=[out_b])


def bcast_free(ap, shape):
    return ap.to_broadcast(list(shape))


def load_xT(k, xT_dram, xb, stage):
    for kt in range(KT):
        s = stage[kt % len(stage)]
        k.dma("sp", s[:], xT_dram[kt * 128:(kt + 1) * 128, :], reads=[xT_dram], writes=[s])
        k.copy("act" if kt % 2 == 0 else "dve", xb[kt], xb[kt][:], s, s[:])


def stream_linear(k, xb, w_dram, N, wst, wb, ps_banks, epilogue, NCH=512, KQ=4, cast_engs=("act", "dve", "pool"),
                  state=None):
    nkt = len(xb)
    NKQ = nkt // KQ
    st = state if state is not None else {"cnt": 0, "oc": 0}
    for c in range(N // NCH):
        n0 = c * NCH
        b = c % 2
        for q in range(NKQ):
            s = wst[st["cnt"] % len(wst)]
            src = w_dram[q * KQ * 128:(q + 1) * KQ * 128, n0:n0 + NCH].rearrange("(j p) c -> p j c", p=128)
            k.dma("sp", s[:], src, reads=[w_dram], writes=[s])
            k.copy(cast_engs[st["cnt"] % len(cast_engs)], wb[b][q], wb[b][q][:], s, s[:])
            st["cnt"] += 1
        for m in range(MT):
            p = ps_banks[st["oc"] % len(ps_banks)]
            for kt in range(nkt):
                q, j = divmod(kt, KQ)
                k.op("pe", lambda g: g.matmul(p[:], lhsT=xb[kt][:, m * 128:(m + 1) * 128], rhs=wb[b][q][:, j, :],
                                              start=(kt == 0), stop=(kt == nkt - 1)),
                     reads=[xb[kt], wb[b][q]], writes=[p])
            epilogue(c, m, p)
            st["oc"] += 1
    return st


def rope_tile(k, src_b, src3, cc, ss, m, groups, rot_w, rb, tmp_t, tmp_u):
    H4 = 4
    w = rot_w
    ccb = cc[:, m, 0:w].unsqueeze(1).to_broadcast([128, H4, w])
    k.op("dve", lambda g: g.tensor_tensor(out=tmp_t[:, :, 0:w], in0=src3[:, :, 0:w], in1=ccb, op=ALU.mult),
         reads=[src_b, cc], writes=[tmp_t])
    first = True
    for (lo, half) in groups:
        hi = lo + half
        s_lo = ss[:, m, lo:lo + half].unsqueeze(1).to_broadcast([128, H4, half])
        s_hi = ss[:, m, hi:hi + half].unsqueeze(1).to_broadcast([128, H4, half])
        k.op("dve", lambda g: g.tensor_tensor(out=tmp_u[:, :, lo:lo + half], in0=src3[:, :, hi:hi + half], in1=s_lo,
                                              op=ALU.mult), reads=[src_b, ss], writes=[tmp_u])
        k.op("dve", lambda g: g.tensor_tensor(out=tmp_u[:, :, hi:hi + half], in0=src3[:, :, lo:lo + half], in1=s_hi,
                                              op=ALU.mult), reads=[src_b, ss], writes=[tmp_u])
    k.op("dve", lambda g: g.tensor_tensor(out=rb[:, :, 0:w], in0=tmp_t[:, :, 0:w], in1=tmp_u[:, :, 0:w], op=ALU.add),
         reads=[tmp_t, tmp_u], writes=[rb])
    if w < 128:
        k.op("act", lambda g: g.copy(out=rb[:, :, w:128], in_=src3[:, :, w:128]), reads=[src_b], writes=[rb])


def phase_inproj(k, pre, xT_dram, w_dram, N, chunk_types, tabs, gains, ident_b, qkT_dram, v_dram, res):
    xb = res["xb"]
    wst = res["wst"]
    wb = res["wb"]
    ps = res["ps_mm"]
    psT = res["ps_tr"]
    rbs = [k.sbuf(f"{pre}rb{i}", [128, 4, 128], BF16) for i in range(2)]
    tmp_t = k.sbuf(f"{pre}tt", [128, 4, 128], F32)
    tmp_u = k.sbuf(f"{pre}tu", [128, 4, 128], F32)
    qn = k.sbuf(f"{pre}qn", [128, 4, 128], F32)
    junk = k.sbuf(f"{pre}junk", [128, 128], F32)
    ssq = k.sbuf(f"{pre}ssq", [128, 4], F32)
    rstd = k.sbuf(f"{pre}rstd", [128, 4], F32)
    fmst = [k.sbuf(f"{pre}fm{i}", [128, 4, T], BF16) for i in range(2)]
    vst = [k.sbuf(f"{pre}vst{i}", [128, 512], BF16) for i in range(2)]
    cnt = {"e": 0}

    def epilogue(c, m, p):
        ty = chunk_types[c]
        i = cnt["e"]
        cnt["e"] += 1
        if ty[0] == "v":
            vs = vst[i % 2]
            k.copy("act", vs, vs[:], p, p[:])
            k.dma("sp", v_dram[m * 128:(m + 1) * 128, ty[1]:ty[1] + 512], vs[:], reads=[vs], writes=[v_dram])
            return
        rb = rbs[i % 2]
        p3 = p[:].rearrange("p (h d) -> p h d", h=4)
        if ty[0] == "rope":
            rope_tile(k, p, p3, tabs["ccp"], tabs["ssp"], m, [(0, 16)], 32, rb, tmp_t, tmp_u)
        else:
            gb = gains[ty[2]]
            for j in range(4):
                k.op("act", lambda g: g.activation(out=junk[:], in_=p[:, j * 128:(j + 1) * 128], func=AF.Square,
                                                   accum_out=ssq[:, j:j + 1]), reads=[p], writes=[junk, ssq])
            k.op("dve", lambda g: g.tensor_scalar(out=rstd[:], in0=ssq[:], scalar1=1.0 / 128, scalar2=1e-6,
                                                  op0=ALU.mult, op1=ALU.add), reads=[ssq], writes=[rstd])
            k.op("act", lambda g: g.activation(out=rstd[:], in_=rstd[:], func=AF.Sqrt), reads=[rstd], writes=[rstd])
            k.op("dve", lambda g: g.reciprocal(out=rstd[:], in_=rstd[:]), reads=[rstd], writes=[rstd])
            k.op("dve", lambda g: g.tensor_tensor(out=qn[:], in0=p3, in1=rstd[:].unsqueeze(2).to_broadcast([128, 4, 128]),
                                                  op=ALU.mult), reads=[p, rstd], writes=[qn])
            k.op("dve", lambda g: g.tensor_tensor(out=qn[:], in0=qn[:], in1=gb[:].unsqueeze(1).to_broadcast([128, 4, 128]),
                                                  op=ALU.mult), reads=[qn, gb], writes=[qn])
            rope_tile(k, qn, qn[:], tabs["cca"], tabs["ssa"], m, [(0, 32), (64, 32)], 128, rb, tmp_t, tmp_u)
        pt = psT[i % len(psT)]
        for j in range(4):
            k.op("pe", lambda g: g.transpose(out=pt[:, j, 0:128], in_=rb[:, j, :], identity=ident_b[:]),
                 reads=[rb, ident_b], writes=[pt])
        fm = fmst[c % 2]
        k.copy("act" if i % 2 else "dve", fm, fm[:, :, m * 128:(m + 1) * 128], pt, pt[:, :, 0:128])
        if m == MT - 1:
            fm0 = ty[1]
            k.dma("sp", qkT_dram[fm0:fm0 + 4].rearrange("j p t -> p j t"), fm[:], reads=[fm], writes=[qkT_dram])

    stream_linear(k, xb, w_dram, N, wst, wb, ps, epilogue, state=res["lin_state"])


def rope_tables():
    pos = np.arange(SEQ, dtype=np.float32)
    half = 16
    inv = (np.float32(500000.0) ** (-np.arange(half, dtype=np.float32) / half)).astype(np.float32)
    ang = pos[:, None] * inv[None, :]
    c, s = np.cos(ang).astype(np.float32), np.sin(ang).astype(np.float32)
    ccp = np.ones((SEQ, 128), np.float32)
    ssp = np.zeros((SEQ, 128), np.float32)
    ccp[:, 0:16] = c
    ccp[:, 16:32] = c
    ssp[:, 0:16] = -s
    ssp[:, 16:32] = s
    rows = (np.arange(SEQ) // 64).astype(np.float32)
    cols = (np.arange(SEQ) % 64).astype(np.float32)
    inv2 = (np.float32(10000.0) ** (-np.arange(32, dtype=np.float32) / 32)).astype(np.float32)
    ar = rows[:, None] * inv2[None, :]
    ac = cols[:, None] * inv2[None, :]
    cr, sr, cc_, sc = (np.cos(ar).astype(np.float32), np.sin(ar).astype(np.float32),
                       np.cos(ac).astype(np.float32), np.sin(ac).astype(np.float32))
    cca = np.concatenate([cr, cr, cc_, cc_], 1)
    ssa = np.concatenate([-sr, sr, -sc, sc], 1)
    return ccp, ssp, cca, ssa


def alloc_linear_res(k, pre):
    res = {}
    res["xb"] = [k.sbuf(f"{pre}xb{i}", [128, T], BF16) for i in range(KT)]
    res["xst"] = [k.sbuf(f"{pre}xst{i}", [128, T], F32) for i in range(2)]
    res["wst"] = [k.sbuf(f"{pre}wst{i}", [128, 4, 512], F32) for i in range(2)]
    res["wb"] = [[k.sbuf(f"{pre}wb{b}_{q}", [128, 4, 512], BF16) for q in range(KT // 4)] for b in range(2)]
    res["ps_mm"] = [k.psum(f"{pre}psmm{i}", [128, 512], F32) for i in range(4)]
    res["ps_tr"] = [k.psum(f"{pre}pstr{i}", [128, 4, 256], BF16) for i in range(2)]
    res["lin_state"] = {"cnt": 0, "oc": 0}
    return res


def load_tabs(k, pre, names, drams):
    tabs = {}
    for nm in names:
        t = k.sbuf(f"{pre}tab_{nm}", [128, MT, 128], F32)
        k.dma("sp", t[:], drams[nm][:, :].rearrange("(m p) d -> p m d", p=128), reads=[drams[nm]], writes=[t])
        tabs[nm] = t
    return tabs


AB_TYPES = ([("rope", c * 4) for c in range(4)] + [("rope", 16)] + [("v", 0)]
            + [("rope", 20 + c * 4) for c in range(6)] + [("rope", 44 + c * 4) for c in range(6)]
            + [("v", 512 + c * 512) for c in range(6)])
AB_NFM, AB_NV = 68, 3584
CD_TYPES = ([("rope", c * 4) for c in range(4)] + [("rope", 16 + c * 4) for c in range(4)]
            + [("v", c * 512) for c in range(4)]
            + [("axial", 32 + c * 4, "qg") for c in range(4)] + [("axial", 48, "kg")] + [("v", 2048)])
CD_NFM, CD_NV = 52, 2560


def build_inproj(layer, types=None, x_bf16=False):
    nc = bass.Bass("TRN2", target_bir_lowering=False)
    if types is None:
        types = AB_TYPES if layer == 0 else CD_TYPES
    N = 512 * len(types)
    nfm, nv = (AB_NFM, AB_NV) if layer == 0 else (CD_NFM, CD_NV)
    with ExitStack() as st:
        k = K(nc, st)
        xT = k.dram("xT", [D, T], BF16 if x_bf16 else F32, "ExternalInput")
        w = k.dram("w", [D, N], F32, "ExternalInput")
        identb = k.dram("identb", [128, 128], BF16, "ExternalInput")
        tabd = {nm: k.dram(nm, [T, 128], F32, "ExternalInput") for nm in (("ccp", "ssp") if layer == 0 else ("ccp", "ssp", "cca", "ssa"))}
        qkT = k.dram("qkT", [nfm, 128, T], BF16, "ExternalOutput")
        vtm = k.dram("vtm", [T, nv], BF16, "ExternalOutput")
        res = alloc_linear_res(k, "a")
        ident_b = k.sbuf("identb_s", [128, 128], BF16)
        k.dma("sp", ident_b[:], identb[:, :], reads=[identb], writes=[ident_b])
        tabs = load_tabs(k, "a", list(tabd.keys()), tabd)
        gains = {}
        if layer == 1:
            for nm in ("qg", "kg"):
                gd = k.dram(nm, [1, 128], F32, "ExternalInput")
                gs = k.sbuf("g_" + nm, [128, 128], F32)
                k.dma("sp", gs[:], gd[0:1, :].to_broadcast([128, 128]), reads=[gd], writes=[gs])
                gains[nm] = gs
        if x_bf16:
            for kt in range(KT):
                k.dma("sp", res["xb"][kt][:], xT[kt * 128:(kt + 1) * 128, :], reads=[xT], writes=[res["xb"][kt]])
        else:
            load_xT(k, xT, res["xb"], res["xst"])
        phase_inproj(k, "a", xT, w, N, types, tabs, gains, ident_b, qkT, vtm, res)
        k.finish("sp")
    return nc


HALO = 1024
TE = T + 2 * HALO
NTE = TE // 128
B_DIL = (1, 4, 16)
B_DT = (1, 2, 8)
MASK_OFF = {"A": 0, 0: 3, 1: 6, 2: 11}
N_MASKS = 28


def band_masks():
    m = np.zeros((128, N_MASKS, 128), np.float32)
    kl = np.arange(128)[:, None]
    ql = np.arange(128)[None, :]
    for j, dt in enumerate((-1, 0, 1)):
        diff = (ql - kl) - dt * 128
        m[:, MASK_OFF["A"] + j, :] = (np.abs(diff) <= 128)
    for g in range(3):
        d, r = B_DIL[g], B_DT[g]
        for j, dt in enumerate(range(-r, r + 1)):
            diff = (ql - kl) - dt * 128
            m[:, MASK_OFF[g] + j, :] = (np.abs(diff) <= 64 * d) & (diff % d == 0)
    return m.astype(NP_BF16)


def attn_unit(k, qT_ap, qT_b, kT_b, v_b, ktiles, mask_b, mask0, acc, first, last, ps_s, pts, cnt, scale):
    n = len(ktiles)
    i0 = 0
    while i0 < n:
        nt = min(4, n - i0)
        ps = ps_s[cnt["s"] % len(ps_s)]
        pt = pts[cnt["s"] % len(pts)]
        cnt["s"] += 1
        for j in range(nt):
            lt = ktiles[i0 + j]
            k.op("pe", lambda g: g.matmul(ps[:, j, :], lhsT=kT_b[:, lt * 128:(lt + 1) * 128], rhs=qT_ap,
                                          start=True, stop=True), reads=[kT_b, qT_b], writes=[ps])
        k.op("act", lambda g: g.activation(out=pt[:, 0:nt, :], in_=ps[:, 0:nt, :], func=AF.Exp, scale=scale),
             reads=[ps], writes=[pt])
        k.op("dve", lambda g: g.tensor_tensor(out=pt[:, 0:nt, :], in0=pt[:, 0:nt, :],
                                              in1=mask_b[:, mask0 + i0:mask0 + i0 + nt, :], op=ALU.mult),
             reads=[pt, mask_b], writes=[pt])
        for j in range(nt):
            lt = ktiles[i0 + j]
            st_ = first and (i0 + j == 0)
            sp_ = last and (i0 + j == n - 1)
            k.op("pe", lambda g: g.matmul(acc[:, 0:129], lhsT=pt[:, j, :], rhs=v_b[:, lt, :], start=st_, stop=sp_),
                 reads=[pt, v_b], writes=[acc])
        i0 += nt


def phase_attn_ab(k, pre, qT_d, kTe_d, vaug_d, sink_d, mask_d, identb_d, oT_d):
    scale = 1.0 / math.sqrt(128.0)
    mask_b = k.sbuf(f"{pre}mask", [128, N_MASKS, 128], BF16)
    k.dma("sp", mask_b[:], mask_d[:, :, :], reads=[mask_d], writes=[mask_b])
    ident_b = k.sbuf(f"{pre}ident", [128, 128], BF16)
    k.dma("sp", ident_b[:], identb_d[:, :], reads=[identb_d], writes=[ident_b])
    esink = k.sbuf(f"{pre}esink", [128, 16], F32)
    k.dma("sp", esink[:], sink_d[0:1, :].to_broadcast([128, 16]), reads=[sink_d], writes=[esink])
    k.op("act", lambda g: g.activation(out=esink[:], in_=esink[:], func=AF.Exp), reads=[esink], writes=[esink])
    NKT = 46
    kbuf = [k.sbuf(f"{pre}kb{i}", [128, NKT * 128], BF16) for i in range(2)]
    vbuf = [k.sbuf(f"{pre}vb{i}", [128, NKT, 129], BF16) for i in range(2)]
    qbuf = [k.sbuf(f"{pre}qb{i}", [128, 4, T], BF16) for i in range(2)]
    ps_s = [k.psum(f"{pre}pss{i}", [128, 4, 128], F32) for i in range(3)]
    pts = [k.sbuf(f"{pre}pt{i}", [128, 4, 128], BF16) for i in range(3)]
    accs = [k.psum(f"{pre}acc{i}", [128, 512], F32) for i in range(2)]
    ps_t = [k.psum(f"{pre}pst{i}", [128, 1024], BF16) for i in range(2)]
    den = k.sbuf(f"{pre}den", [128, 1], F32)
    ob = [k.sbuf(f"{pre}ob{i}", [128, 128], BF16) for i in range(2)]
    oTs = [k.sbuf(f"{pre}oTs{i}", [128, T], BF16) for i in range(2)]
    cnt = {"s": 0, "a": 0, "o": 0}

    def finalize(acc, sink_col, ohead, m):
        if sink_col is not None:
            k.op("dve", lambda g: g.tensor_tensor(out=den[:], in0=acc[:, 128:129], in1=esink[:, sink_col:sink_col + 1],
                                                  op=ALU.add), reads=[acc, esink], writes=[den])
        else:
            k.op("dve", lambda g: g.tensor_copy(out=den[:], in_=acc[:, 128:129]), reads=[acc], writes=[den])
        k.op("dve", lambda g: g.reciprocal(out=den[:], in_=den[:]), reads=[den], writes=[den])
        o = ob[cnt["o"] % 2]
        k.op("dve", lambda g: g.tensor_scalar(out=o[:], in0=acc[:, 0:128], scalar1=den[:, 0:1], scalar2=None,
                                              op0=ALU.mult), reads=[acc, den], writes=[o])
        pt_ = ps_t[cnt["o"] % 2]
        k.op("pe", lambda g: g.transpose(out=pt_[:, 0:128], in_=o[:], identity=ident_b[:]),
             reads=[o, ident_b], writes=[pt_])
        oT = oTs[ohead % 2]
        k.copy("act", oT, oT[:, m * 128:(m + 1) * 128], pt_, pt_[:, 0:128])
        cnt["o"] += 1
        if m == MT - 1:
            k.dma("sp", oT_d[ohead], oT[:], reads=[oT], writes=[oT_d])

    job = 0
    for kvh in range(4):
        kb_, vb_, qb_ = kbuf[job % 2], vbuf[job % 2], qbuf[job % 2]
        job += 1
        k.dma("sp", kb_[:, 0:10 * 128], kTe_d[kvh, :, 7 * 128:17 * 128], reads=[kTe_d], writes=[kb_])
        k.dma("sp", vb_[:, 0:10, :], vaug_d[7 * 128:17 * 128, kvh, :].rearrange("(j p) c -> p j c", p=128),
              reads=[vaug_d], writes=[vb_])
        k.dma("sp", qb_[:], qT_d[kvh * 4:kvh * 4 + 4].rearrange("j p t -> p j t"), reads=[qT_d], writes=[qb_])
        for g4 in range(4):
            h = kvh * 4 + g4
            for m in range(MT):
                acc = accs[cnt["a"] % 2]
                cnt["a"] += 1
                attn_unit(k, qb_[:, g4, m * 128:(m + 1) * 128], qb_, kb_, vb_, [m, m + 1, m + 2], mask_b,
                          MASK_OFF["A"], acc, True, True, ps_s, pts, cnt, scale)
                finalize(acc, h, h, m)
    base = [0, 10, 22]
    lo_ext = [7, 6, 0]
    nload = [10, 12, 24]
    for h in range(8):
        kb_, vb_, qb_ = kbuf[job % 2], vbuf[job % 2], qbuf[job % 2]
        job += 1
        for g in range(3):
            kblk = 4 + g * 8 + h
            k.dma("sp", kb_[:, base[g] * 128:(base[g] + nload[g]) * 128],
                  kTe_d[kblk, :, lo_ext[g] * 128:(lo_ext[g] + nload[g]) * 128], reads=[kTe_d], writes=[kb_])
            k.dma("sp", vb_[:, base[g]:base[g] + nload[g], :],
                  vaug_d[lo_ext[g] * 128:(lo_ext[g] + nload[g]) * 128, kblk, :].rearrange("(j p) c -> p j c", p=128),
                  reads=[vaug_d], writes=[vb_])
            k.dma("sp", qb_[:, g, :], qT_d[20 + g * 8 + h], reads=[qT_d], writes=[qb_])
        for m in range(MT):
            acc = accs[cnt["a"] % 2]
            cnt["a"] += 1
            for g in range(3):
                r = B_DT[g]
                tiles = [base[g] + (8 + m + dt) - lo_ext[g] for dt in range(-r, r + 1)]
                attn_unit(k, qb_[:, g, m * 128:(m + 1) * 128], qb_, kb_, vb_, tiles, mask_b, MASK_OFF[g], acc,
                          g == 0, g == 2, ps_s, pts, cnt, scale)
            finalize(acc, None, 16 + h, m)


def build_attn_ab():
    nc = bass.Bass("TRN2", target_bir_lowering=False)
    with ExitStack() as st:
        k = K(nc, st)
        qT = k.dram("qT", [AB_NFM, 128, T], BF16, "ExternalInput")
        kTe = k.dram("kTe", [28, 128, TE], BF16, "ExternalInput")
        vaug = k.dram("vaug", [TE, 28, 129], BF16, "ExternalInput")
        sink = k.dram("sink", [1, 16], F32, "ExternalInput")
        masks = k.dram("masks", [128, N_MASKS, 128], BF16, "ExternalInput")
        identb = k.dram("identb", [128, 128], BF16, "ExternalInput")
        oT = k.dram("oT", [24, 128, T], BF16, "ExternalOutput")
        phase_attn_ab(k, "b", qT, kTe, vaug, sink, masks, identb, oT)
        k.finish("sp")
    return nc


def host_ext_kv(qkT_all, vtm_all, kblocks, vheads_cols):
    kfull = np.concatenate([q[kblocks] for q in qkT_all], axis=2)
    vfull = np.concatenate(vtm_all, axis=0)
    nk = len(kblocks)
    kpad = np.zeros((nk, 128, SEQ + 2 * HALO), dtype=kfull.dtype)
    kpad[:, :, HALO:HALO + SEQ] = kfull
    nvh = vfull.shape[1] // 128
    vpad = np.zeros((SEQ + 2 * HALO, nvh, 129), dtype=vfull.dtype)
    vpad[HALO:HALO + SEQ, :, 0:128] = vfull.reshape(SEQ, nvh, 128)
    vpad[HALO:HALO + SEQ, :, 128] = 1.0
    outs = []
    for c in range(NCORES):
        outs.append((np.ascontiguousarray(kpad[:, :, c * T:c * T + TE]), np.ascontiguousarray(vpad[c * T:c * T + TE])))
    return outs


ALPHA = float((2 * 2) ** 0.25)


def barrier(k):
    toks = []
    for e, c in k.cur.items():
        if c[1] > 0:
            toks.append((c[0], c[1], e))
    for q, ring in k.dma_rr.items():
        for s in ring["sems"]:
            if s[2] is not None:
                toks.append(s[2])
    for e in ("pe", "dve", "act", "pool", "sp"):
        for tk in toks:
            if tk[2] == e:
                continue
            k._wait(e, tk)


def phase_outproj(k, pre, oT_d, nkt, w_d, x_d, z_d, res):
    xb = res["xb"][:nkt]
    for kt in range(nkt):
        k.dma("sp", xb[kt][:], oT_d[kt], reads=[oT_d], writes=[xb[kt]])
    xin = [k.sbuf(f"{pre}xin{i}", [128, 512], F32) for i in range(3)]
    zt = [k.sbuf(f"{pre}zt{i}", [128, 512], F32) for i in range(3)]
    cnt = {"e": 0}

    def epilogue(c, m, p):
        i = cnt["e"]
        cnt["e"] += 1
        xi, zo = xin[i % 3], zt[i % 3]
        k.dma("sp", xi[:], x_d[m * 128:(m + 1) * 128, c * 512:(c + 1) * 512], reads=[x_d], writes=[xi])
        k.op("dve", lambda g: g.scalar_tensor_tensor(out=zo[:], in0=xi[:], scalar=ALPHA, in1=p[:], op0=ALU.mult,
                                                     op1=ALU.add), reads=[xi, p], writes=[zo])
        k.dma("sp", z_d[m * 128:(m + 1) * 128, c * 512:(c + 1) * 512], zo[:], reads=[zo], writes=[z_d])

    stream_linear(k, xb, w_d, D, res["wst"], res["wb"], res["ps_mm"], epilogue, state=res["lin_state"])


def phase_ln(k, pre, z_d, g_d, b_d, out_d, outT_d, identb_d, ps_tr, mt=MT):
    gt = k.sbuf(f"{pre}g", [128, D], F32)
    bt = k.sbuf(f"{pre}b", [128, D], F32)
    k.dma("sp", gt[:], g_d[0:1, :].to_broadcast([128, D]), reads=[g_d], writes=[gt])
    k.dma("sp", bt[:], b_d[0:1, :].to_broadcast([128, D]), reads=[b_d], writes=[bt])
    ident_b = k.sbuf(f"{pre}ident", [128, 128], BF16)
    k.dma("sp", ident_b[:], identb_d[:, :], reads=[identb_d], writes=[ident_b])
    zs = [k.sbuf(f"{pre}z{i}", [128, D], F32) for i in range(2)]
    os_ = [k.sbuf(f"{pre}o{i}", [128, D], F32) for i in range(2)]
    ob16 = k.sbuf(f"{pre}o16", [128, D], BF16)
    stg = [k.sbuf(f"{pre}stg{i}", [128, KT, 128], BF16) for i in range(2)]
    stats = k.sbuf(f"{pre}stats", [128, 8, 6], F32)
    mv = k.sbuf(f"{pre}mv", [128, 2], F32)
    rstd = k.sbuf(f"{pre}rstd", [128, 1], F32)
    for m in range(mt):
        z, o = zs[m % 2], os_[m % 2]
        k.dma("sp", z[:], z_d[m * 128:(m + 1) * 128, :], reads=[z_d], writes=[z])
        for c in range(8):
            k.op("dve", lambda g: g.bn_stats(out=stats[:, c, :], in_=z[:, c * 512:(c + 1) * 512]), reads=[z], writes=[stats])
        k.op("dve", lambda g: g.bn_aggr(out=mv[:], in_=stats[:].rearrange("p a b -> p (a b)")), reads=[stats], writes=[mv])
        k.op("dve", lambda g: g.tensor_scalar(out=rstd[:], in0=mv[:, 1:2], scalar1=1e-5, scalar2=None, op0=ALU.add),
             reads=[mv], writes=[rstd])
        k.op("act", lambda g: g.activation(out=rstd[:], in_=rstd[:], func=AF.Sqrt), reads=[rstd], writes=[rstd])
        k.op("dve", lambda g: g.reciprocal(out=rstd[:], in_=rstd[:]), reads=[rstd], writes=[rstd])
        k.op("dve", lambda g: g.tensor_scalar(out=o[:], in0=z[:], scalar1=mv[:, 0:1], scalar2=rstd[:, 0:1],
                                              op0=ALU.subtract, op1=ALU.mult), reads=[z, mv, rstd], writes=[o])
        k.op("pool", lambda g: g.tensor_tensor(out=o[:], in0=o[:], in1=gt[:], op=ALU.mult), reads=[o, gt], writes=[o])
        k.op("pool", lambda g: g.tensor_tensor(out=o[:], in0=o[:], in1=bt[:], op=ALU.add), reads=[o, bt], writes=[o])
        k.dma("sp", out_d[m * 128:(m + 1) * 128, :], o[:], reads=[o], writes=[out_d])
        if outT_d is not None:
            k.op("act", lambda g: g.copy(out=ob16[:], in_=o[:]), reads=[o], writes=[ob16])
            sg = stg[m % 2]
            for q in range(4):
                pt = ps_tr[q % len(ps_tr)]
                for j in range(8):
                    kt = q * 8 + j
                    k.op("pe", lambda g: g.transpose(out=pt[:, j, :], in_=ob16[:, kt * 128:(kt + 1) * 128],
                                                     identity=ident_b[:]), reads=[ob16, ident_b], writes=[pt])
                k.copy("act" if q % 2 else "dve", sg, sg[:, q * 8:(q + 1) * 8, :], pt, pt[:])
            k.dma("sp", outT_d[:, m * 128:(m + 1) * 128].rearrange("(kt p) t -> p kt t", p=128), sg[:],
                  reads=[sg], writes=[outT_d])


def build_outproj_ln(nkt, with_T):
    nc = bass.Bass("TRN2", target_bir_lowering=False)
    with ExitStack() as st:
        k = K(nc, st)
        oT = k.dram("oT", [nkt, 128, T], BF16, "ExternalInput")
        w = k.dram("w", [nkt * 128, D], F32, "ExternalInput")
        x = k.dram("x", [T, D], F32, "ExternalInput")
        g = k.dram("g", [1, D], F32, "ExternalInput")
        b = k.dram("b", [1, D], F32, "ExternalInput")
        identb = k.dram("identb", [128, 128], BF16, "ExternalInput")
        z = k.dram("z", [T, D], F32, "Internal")
        x1 = k.dram("x1", [T, D], F32, "ExternalOutput")
        x1T = k.dram("x1T", [D, T], BF16, "ExternalOutput") if with_T else None
        with ExitStack() as ph:
            k.stack = ph
            res = alloc_linear_res(k, "c")
            phase_outproj(k, "c", oT, nkt, w, x, z, res)
            barrier(k)
        k.stack = st
        with ExitStack() as ph:
            k.stack = ph
            ps_tr = [k.psum(f"lnpt{i}", [128, 8, 128], BF16) for i in range(2)]
            phase_ln(k, "l", z, g, b, x1, x1T, identb, ps_tr)
            barrier(k)
        k.stack = st
        k.finish("sp")
    return nc


NEG = -1.0e30


def phase_peer_scores(k, pre, x1T_d, wq_d, keysT_d, sc_d, xb):
    for kt in range(KT):
        k.dma("sp", xb[kt][:], x1T_d[kt * 128:(kt + 1) * 128, :], reads=[x1T_d], writes=[xb[kt]])
    keysT = k.sbuf(f"{pre}keysT", [128, 2048], F32)
    k.dma("sp", keysT[:], keysT_d[:, :], reads=[keysT_d], writes=[keysT])
    wst = [k.sbuf(f"{pre}wst{i}", [128, KT, 128], F32) for i in range(2)]
    wqb = [k.sbuf(f"{pre}wqb{i}", [128, KT, 128], BF16) for i in range(2)]
    qg = [k.sbuf(f"{pre}qg{i}", [128, T], F32) for i in range(2)]
    scst = [k.sbuf(f"{pre}scst{i}", [128, MT, 128], F32) for i in range(2)]
    ps = [k.psum(f"{pre}ps{i}", [128, 512], F32) for i in range(4)]
    ps2 = [k.psum(f"{pre}ps2{i}", [128, 512], F32) for i in range(2)]
    for g in range(16):
        ws, wb_, q_, sc_ = wst[g % 2], wqb[g % 2], qg[g % 2], scst[g % 2]
        k.dma("sp", ws[:], wq_d[:, g * 128:(g + 1) * 128].rearrange("(kt p) c -> p kt c", p=128), reads=[wq_d], writes=[ws])
        k.copy(("act", "dve")[g % 2], wb_, wb_[:], ws, ws[:])
        for tc in range(T // 512):
            p = ps[(g * 2 + tc) % 4]
            for kt in range(KT):
                k.op("pe", lambda e: e.matmul(p[:], lhsT=wb_[:, kt, :], rhs=xb[kt][:, tc * 512:(tc + 1) * 512],
                                              start=(kt == 0), stop=(kt == KT - 1)), reads=[wb_, xb[kt]], writes=[p])
            k.copy(("dve", "act")[tc % 2], q_, q_[:, tc * 512:(tc + 1) * 512], p, p[:])
        for m in range(MT):
            p2 = ps2[m % 2]
            k.op("pe", lambda e: e.matmul(p2[:, 0:128], lhsT=q_[:, m * 128:(m + 1) * 128], rhs=keysT[:, g * 128:(g + 1) * 128],
                                          start=True, stop=True), reads=[q_, keysT], writes=[p2])
            k.copy(("act", "dve")[m % 2], sc_, sc_[:, m, :], p2, p2[:, 0:128])
        k.dma("sp", sc_d[:, g * 128:(g + 1) * 128].rearrange("(m p) n -> p m n", p=128), sc_[:], reads=[sc_], writes=[sc_d])


def phase_cast_rows(k, pre, srcs_dsts, nrows=16384):
    st = [k.sbuf(f"{pre}st{i}", [128, D], F32) for i in range(3)]
    cb = [k.sbuf(f"{pre}cb{i}", [128, D], BF16) for i in range(3)]
    jobs = [(src, dst, i) for (src, dst) in srcs_dsts for i in range(nrows // 128)]
    engs = ("act", "dve", "pool")

    def load(n):
        src, dst, i = jobs[n]
        k.dma("sp", st[n % 3][:], src[i * 128:(i + 1) * 128, :], reads=[src], writes=[st[n % 3]])

    for n in range(min(2, len(jobs))):
        load(n)
    for n in range(len(jobs)):
        if n + 2 < len(jobs):
            load(n + 2)
        src, dst, i = jobs[n]
        k.copy(engs[n % 3], cb[n % 3], cb[n % 3][:], st[n % 3], st[n % 3][:])
        k.dma("act", dst[i * 128:(i + 1) * 128, :], cb[n % 3][:], reads=[cb[n % 3]], writes=[dst])


def phase_peer_main(k, pre, x1_d, sc_d, u_d, v_d, pconst_d, identb_d, z_d, NG=4, mt=MT):
    ident_b = k.sbuf(f"{pre}ident", [128, 128], BF16)
    k.dma("sp", ident_b[:], identb_d[:, :], reads=[identb_d], writes=[ident_b])
    pc = k.sbuf(f"{pre}pc", [128, 48], F32)
    k.dma("sp", pc[:], pconst_d[:, :], reads=[pconst_d], writes=[pc])
    iota16, lo16, hi16 = pc[:, 0:16], pc[:, 16:32], pc[:, 32:48]
    sc = k.sbuf(f"{pre}sc", [128, 16, 128], F32)
    scr = k.sbuf(f"{pre}scr", [128, 16, 128], F32)
    sv = k.sbuf(f"{pre}sv", [128, 16, 16], F32)
    si = k.sbuf(f"{pre}si", [128, 16, 16], U32)
    sif = k.sbuf(f"{pre}sif", [128, 16, 16], F32)
    cand = k.sbuf(f"{pre}cand", [128, 8, 256], F32)
    cscr = k.sbuf(f"{pre}cscr", [128, 8, 256], F32)
    ts = k.sbuf(f"{pre}ts", [128, 8, 16], F32)
    tp = k.sbuf(f"{pre}tp", [128, 8, 16], U32)
    tpf = k.sbuf(f"{pre}tpf", [128, 8, 16], F32)
    w4a = k.sbuf(f"{pre}w4a", [128, 8, 16, 16], F32)
    w4b = k.sbuf(f"{pre}w4b", [128, 8, 16, 16], F32)
    r3a = k.sbuf(f"{pre}r3a", [128, 8, 16], F32)
    r3b = k.sbuf(f"{pre}r3b", [128, 8, 16], F32)
    r3c = k.sbuf(f"{pre}r3c", [128, 8, 16], F32)
    eidx_f = k.sbuf(f"{pre}eidxf", [128, 128], F32)
    eidx = k.sbuf(f"{pre}eidx", [128, 128], U32)
    gate = k.sbuf(f"{pre}gate", [128, 8, 16], F32)
    zsum = k.sbuf(f"{pre}zsum", [128, 8], F32)
    hid = k.sbuf(f"{pre}hid", [128, 128], F32)
    g1 = k.sbuf(f"{pre}g1", [128, 128], F32)
    wgt = k.sbuf(f"{pre}wgt", [128, 128], F32)
    xt = k.sbuf(f"{pre}xt", [128, D], F32)
    junk = k.sbuf(f"{pre}junk", [128, D], BF16)
    ug = [k.sbuf(f"{pre}ug{i}", [128, D], BF16) for i in range(NG)]
    vg = [k.sbuf(f"{pre}vg{i}", [128, D], BF16) for i in range(NG)]
    dg = [k.sbuf(f"{pre}dg{i}", [128, 128], BF16) for i in range(3)]
    zo = [k.sbuf(f"{pre}zo{i}", [128, 512], F32) for i in range(2)]
    pall = k.psum(f"{pre}pall", [128, 8, 512], F32)
    acc = [Buf(pall.t, f"{pre}acc{i}") for i in range(8)]
    xps = pall[:].rearrange("p c n -> p (c n)")

    def dve(fn, reads, writes):
        return k.op("dve", fn, reads=reads, writes=writes)

    for m in range(mt):
        k.dma("sp", xt[:], x1_d[m * 128:(m + 1) * 128, :], reads=[x1_d], writes=[xt])
        k.dma("sp", sc[:], sc_d[m * 128:(m + 1) * 128, :].rearrange("p (g n) -> p g n", g=16), reads=[sc_d], writes=[sc])
        for g in range(16):
            dve(lambda e: e.max(out=sv[:, g, 0:8], in_=sc[:, g, :]), [sc], [sv])
            dve(lambda e: e.max_index(out=si[:, g, 0:8], in_max=sv[:, g, 0:8], in_values=sc[:, g, :]), [sc, sv], [si])
            dve(lambda e: e.match_replace(out=scr[:, g, :], in_to_replace=sv[:, g, 0:8], in_values=sc[:, g, :],
                                          imm_value=NEG), [sc, sv], [scr])
            dve(lambda e: e.max(out=sv[:, g, 8:16], in_=scr[:, g, :]), [scr], [sv])
            dve(lambda e: e.max_index(out=si[:, g, 8:16], in_max=sv[:, g, 8:16], in_values=scr[:, g, :]), [scr, sv], [si])
        dve(lambda e: e.tensor_copy(out=sif[:], in_=si[:]), [si], [sif])
        sv4 = sv[:].rearrange("p (h c) r -> p h c r", c=2)
        sif4 = sif[:].rearrange("p (h c) r -> p h c r", c=2)
        c4 = cand[:].rearrange("p h (a b) -> p h a b", a=16)
        dve(lambda e: e.tensor_tensor(out=c4, in0=sv4[:, :, 0, :].unsqueeze(3).to_broadcast([128, 8, 16, 16]),
                                      in1=sv4[:, :, 1, :].unsqueeze(2).to_broadcast([128, 8, 16, 16]), op=ALU.add),
            [sv], [cand])
        for h in range(8):
            dve(lambda e: e.max(out=ts[:, h, 0:8], in_=cand[:, h, :]), [cand], [ts])
            dve(lambda e: e.max_index(out=tp[:, h, 0:8], in_max=ts[:, h, 0:8], in_values=cand[:, h, :]), [cand, ts], [tp])
            dve(lambda e: e.match_replace(out=cscr[:, h, :], in_to_replace=ts[:, h, 0:8], in_values=cand[:, h, :],
                                          imm_value=NEG), [cand, ts], [cscr])
            dve(lambda e: e.max(out=ts[:, h, 8:16], in_=cscr[:, h, :]), [cscr], [ts])
            dve(lambda e: e.max_index(out=tp[:, h, 8:16], in_max=ts[:, h, 8:16], in_values=cscr[:, h, :]), [cscr, ts], [tp])
        dve(lambda e: e.tensor_copy(out=tpf[:], in_=tp[:]), [tp], [tpf])
        tpf4 = tpf[:].unsqueeze(3).to_broadcast([128, 8, 16, 16])
        lo4 = lo16.unsqueeze(1).unsqueeze(1).to_broadcast([128, 8, 16, 16])
        hi4 = hi16.unsqueeze(1).unsqueeze(1).to_broadcast([128, 8, 16, 16])
        io4 = iota16.unsqueeze(1).unsqueeze(1).to_broadcast([128, 8, 16, 16])
        dve(lambda e: e.tensor_tensor(out=w4a[:], in0=tpf4, in1=lo4, op=ALU.is_ge), [tpf, pc], [w4a])
        dve(lambda e: e.tensor_tensor(out=w4b[:], in0=tpf4, in1=hi4, op=ALU.is_ge), [tpf, pc], [w4b])
        dve(lambda e: e.tensor_tensor(out=w4a[:], in0=w4a[:], in1=w4b[:], op=ALU.subtract), [w4a, w4b], [w4a])
        dve(lambda e: e.tensor_tensor(out=w4b[:], in0=w4a[:], in1=sif4[:, :, 0, :].unsqueeze(2).to_broadcast([128, 8, 16, 16]),
                                      op=ALU.mult), [w4a, sif], [w4b])
        dve(lambda e: e.tensor_reduce(out=r3a[:], in_=w4b[:], axis=AX.X, op=ALU.add), [w4b], [r3a])
        dve(lambda e: e.tensor_tensor(out=w4b[:], in0=w4a[:], in1=lo4, op=ALU.mult), [w4a, pc], [w4b])
        dve(lambda e: e.tensor_reduce(out=r3b[:], in_=w4b[:], axis=AX.X, op=ALU.add), [w4b], [r3b])
        dve(lambda e: e.tensor_tensor(out=r3b[:], in0=tpf[:], in1=r3b[:], op=ALU.subtract), [tpf, r3b], [r3b])
        dve(lambda e: e.tensor_tensor(out=w4a[:], in0=r3b[:].unsqueeze(3).to_broadcast([128, 8, 16, 16]), in1=io4,
                                      op=ALU.is_equal), [r3b, pc], [w4a])
        dve(lambda e: e.tensor_tensor(out=w4b[:], in0=w4a[:], in1=sif4[:, :, 1, :].unsqueeze(2).to_broadcast([128, 8, 16, 16]),
                                      op=ALU.mult), [w4a, sif], [w4b])
        dve(lambda e: e.tensor_reduce(out=r3c[:], in_=w4b[:], axis=AX.X, op=ALU.add), [w4b], [r3c])
        ef3 = eidx_f[:].rearrange("p (h r) -> p h r", h=8)
        dve(lambda e: e.scalar_tensor_tensor(out=ef3, in0=r3a[:], scalar=128.0, in1=r3c[:], op0=ALU.mult, op1=ALU.add),
            [r3a, r3c], [eidx_f])
        dve(lambda e: e.tensor_copy(out=eidx[:], in_=eidx_f[:]), [eidx_f], [eidx])
        dve(lambda e: e.tensor_tensor(out=gate[:], in0=ts[:], in1=ts[:, :, 0:1].to_broadcast([128, 8, 16]), op=ALU.subtract),
            [ts], [gate])
        k.op("act", lambda e: e.activation(out=gate[:], in_=gate[:], func=AF.Exp), reads=[gate], writes=[gate])
        dve(lambda e: e.tensor_reduce(out=zsum[:], in_=gate[:], axis=AX.X, op=ALU.add), [gate], [zsum])
        dve(lambda e: e.reciprocal(out=zsum[:], in_=zsum[:]), [zsum], [zsum])
        dve(lambda e: e.tensor_tensor(out=gate[:], in0=gate[:], in1=zsum[:].unsqueeze(2).to_broadcast([128, 8, 16]), op=ALU.mult),
            [gate, zsum], [gate])
        for c in range(8):
            k.op("act", lambda e: e.copy(out=pall[:, c, :], in_=xt[:, c * 512:(c + 1) * 512]), reads=[xt], writes=[acc[c]])
        for s in range(128):
            ub = ug[s % NG]
            k.gather(ub[:], u_d[:, :], eidx[:, s:s + 1], reads=[eidx, u_d], writes=[ub])
            dve(lambda e: e.scalar_tensor_tensor(out=junk[:], in0=ub[:], scalar=1.0, in1=xps, op0=ALU.mult, op1=ALU.mult,
                                                 accum_out=hid[:, s:s + 1]), [ub] + acc, [junk, hid])
        dve(lambda e: e.tensor_tensor(out=g1[:], in0=hid[:], in1=hid[:], op=ALU.mult), [hid], [g1])
        dve(lambda e: e.tensor_scalar(out=g1[:], in0=g1[:], scalar1=0.044715 * 1.5957691216057308,
                                      scalar2=1.5957691216057308, op0=ALU.mult, op1=ALU.add), [g1], [g1])
        dve(lambda e: e.tensor_tensor(out=g1[:], in0=g1[:], in1=hid[:], op=ALU.mult), [g1, hid], [g1])
        k.op("act", lambda e: e.activation(out=g1[:], in_=g1[:], func=AF.Sigmoid), reads=[g1], writes=[g1])
        dve(lambda e: e.tensor_tensor(out=g1[:], in0=g1[:], in1=hid[:], op=ALU.mult), [g1, hid], [g1])
        dve(lambda e: e.tensor_tensor(out=wgt[:], in0=g1[:], in1=gate[:].rearrange("p h r -> p (h r)"), op=ALU.mult),
            [g1, gate], [wgt])
        for s in range(128):
            vb_ = vg[s % NG]
            k.gather(vb_[:], v_d[:, :], eidx[:, s:s + 1], reads=[eidx, v_d], writes=[vb_])
            d_ = dg[s % 3]
            dve(lambda e: e.tensor_scalar(out=d_[:], in0=ident_b[:], scalar1=wgt[:, s:s + 1], scalar2=None, op0=ALU.mult),
                [ident_b, wgt], [d_])
            for c in range(8):
                k.op("pe", lambda e: e.matmul(pall[:, c, :], lhsT=d_[:], rhs=vb_[:, c * 512:(c + 1) * 512],
                                              start=(s == 0), stop=(s == 127)), reads=[d_, vb_], writes=[acc[c]])
        for c in range(8):
            z_ = zo[c % 2]
            dve(lambda e: e.scalar_tensor_tensor(out=z_[:], in0=xt[:, c * 512:(c + 1) * 512], scalar=ALPHA, in1=pall[:, c, :],
                                                 op0=ALU.mult, op1=ALU.add), [xt, acc[c]], [z_])
            k.dma("sp", z_d[m * 128:(m + 1) * 128, c * 512:(c + 1) * 512], z_[:], reads=[z_], writes=[z_d])


def peer_consts():
    pc = np.zeros((128, 48), np.float32)
    pc[:, 0:16] = np.arange(16)
    pc[:, 16:32] = 16 * np.arange(16)
    pc[:, 32:48] = 16 * np.arange(16) + 16
    return pc


def build_peer(final, mt=MT):
    nc = bass.Bass("TRN2", target_bir_lowering=False)
    with ExitStack() as st:
        k = K(nc, st)
        x1 = k.dram("x1", [T, D], F32, "ExternalInput")
        x1T = k.dram("x1T", [D, T], BF16, "ExternalInput")
        wq = k.dram("wq", [D, 2048], F32, "ExternalInput")
        keysT = k.dram("keysT", [128, 2048], F32, "ExternalInput")
        u = k.dram("u", [16384, D], F32, "ExternalInput")
        v = k.dram("v", [16384, D], F32, "ExternalInput")
        g = k.dram("g", [1, D], F32, "ExternalInput")
        b = k.dram("b", [1, D], F32, "ExternalInput")
        pconst = k.dram("pconst", [128, 48], F32, "ExternalInput")
        identb = k.dram("identb", [128, 128], BF16, "ExternalInput")
        sc = k.dram("sc", [T, 2048], F32, "Internal")
        z = k.dram("z", [T, D], F32, "Internal")
        ub16 = k.dram("ub16", [16384, D], BF16, "Internal")
        vb16 = k.dram("vb16", [16384, D], BF16, "Internal")
        x2 = k.dram("x2", [T, D], F32, "ExternalOutput")
        x2T = None if final else k.dram("x2T", [D, T], BF16, "ExternalOutput")
        with ExitStack() as ph:
            k.stack = ph
            xb = [k.sbuf(f"pxb{i}", [128, T], BF16) for i in range(KT)]
            phase_peer_scores(k, "ps", x1T, wq, keysT, sc, xb)
            barrier(k)
        with ExitStack() as ph:
            k.stack = ph
            phase_cast_rows(k, "pcv", [(u, ub16), (v, vb16)])
            barrier(k)
        with ExitStack() as ph:
            k.stack = ph
            phase_peer_main(k, "pm", x1, sc, ub16, vb16, pconst, identb, z, mt=mt)
            barrier(k)
        with ExitStack() as ph:
            k.stack = ph
            ps_tr = [k.psum(f"lnpt{i}", [128, 8, 128], BF16) for i in range(2)]
            phase_ln(k, "pl", z, g, b, x2, x2T, identb, ps_tr, mt=mt)
            barrier(k)
        k.stack = st
        k.finish("sp")
    return nc


LAMBDA_INIT = 0.8 - 0.6 * math.exp(-0.3 * 1)
NKT_FULL = SEQ // 128


def dense_unit(k, kT, qT, q0, vaug, W, accs, ps_s, pts, cnt, scale, LA=2):
    slots = {}

    def score(kt):
        ps = ps_s[cnt["s"] % len(ps_s)]
        pt = pts[cnt["s"] % len(pts)]
        cnt["s"] += 1
        slots[kt] = pt
        k.op("pe", lambda g: g.matmul(ps[:], lhsT=kT[:, kt * 128:(kt + 1) * 128], rhs=qT[:, q0:q0 + 512],
                                      start=True, stop=True), reads=[kT, qT], writes=[ps])
        k.op("act", lambda g: g.activation(out=pt[:], in_=ps[:], func=AF.Exp, scale=scale), reads=[ps], writes=[pt])

    for kt in range(min(LA, NKT_FULL)):
        score(kt)
    for kt in range(NKT_FULL):
        if kt + LA < NKT_FULL:
            score(kt + LA)
        pt = slots.pop(kt)
        for j in range(4):
            k.op("pe", lambda g: g.matmul(accs[j][:, 0:W], lhsT=pt[:, j * 128:(j + 1) * 128], rhs=vaug[:, kt, 0:W],
                                          start=(kt == 0), stop=(kt == NKT_FULL - 1)), reads=[pt, vaug], writes=[accs[j]])


def phase_attn_cd(k, pre, qc_d, kc_d, vc_d, qd_d, kd_d, vd_d, lam_d, subg_d, identb_d, oT_d, nchunks=SEQ // 512):
    scale = 1.0 / math.sqrt(128.0)
    ident_b = k.sbuf(f"{pre}ident", [128, 128], BF16)
    k.dma("sp", ident_b[:], identb_d[:, :], reads=[identb_d], writes=[ident_b])
    lam_t = k.sbuf(f"{pre}lamt", [128, 4, 128], F32)
    k.dma("sp", lam_t[:], lam_d[:, :].unsqueeze(0).to_broadcast([128, 4, 128]), reads=[lam_d], writes=[lam_t])
    lprod = k.sbuf(f"{pre}lprod", [128, 2, 128], F32)
    lsum = k.sbuf(f"{pre}lsum", [128, 2], F32)
    neglam = k.sbuf(f"{pre}neglam", [128, 1], F32)
    lam4 = lam_t[:].rearrange("p (a b) d -> p a b d", b=2)
    k.op("dve", lambda g: g.tensor_tensor(out=lprod[:], in0=lam4[:, :, 0, :], in1=lam4[:, :, 1, :], op=ALU.mult),
         reads=[lam_t], writes=[lprod])
    k.op("dve", lambda g: g.tensor_reduce(out=lsum[:], in_=lprod[:], axis=AX.X, op=ALU.add), reads=[lprod], writes=[lsum])
    k.op("act", lambda g: g.activation(out=lsum[:], in_=lsum[:], func=AF.Exp), reads=[lsum], writes=[lsum])
    k.op("dve", lambda g: g.scalar_tensor_tensor(out=neglam[:], in0=lsum[:, 1:2], scalar=-LAMBDA_INIT, in1=lsum[:, 0:1],
                                                 op0=ALU.add, op1=ALU.subtract), reads=[lsum], writes=[neglam])
    subg = k.sbuf(f"{pre}subg", [128, 256], F32)
    k.dma("sp", subg[:], subg_d[0:1, :].to_broadcast([128, 256]), reads=[subg_d], writes=[subg])
    k.op("dve", lambda g: g.tensor_scalar(out=subg[:], in0=subg[:], scalar1=1.0 - LAMBDA_INIT, scalar2=None, op0=ALU.mult),
         reads=[subg], writes=[subg])

    kTs = [k.sbuf(f"{pre}kT{i}", [128, SEQ], BF16) for i in range(2)]
    qTs = [k.sbuf(f"{pre}qT{i}", [128, SEQ], BF16) for i in range(2)]
    vcs = k.sbuf(f"{pre}vc", [128, NKT_FULL, 257], BF16)
    vds = k.sbuf(f"{pre}vd", [128, NKT_FULL, 129], BF16)
    ps_s = [k.psum(f"{pre}pss{i}", [128, 512], F32) for i in range(3)]
    pts = [k.sbuf(f"{pre}pt{i}", [128, 512], BF16) for i in range(4)]
    accs = [k.psum(f"{pre}acc{i}", [128, 512], F32) for i in range(4)]
    ps_t = [k.psum(f"{pre}pst{i}", [128, 8, 128], BF16) for i in range(1)]
    o1s = k.sbuf(f"{pre}o1s", [128, nchunks * 4, 256], F32)
    den = k.sbuf(f"{pre}den", [128, 1], F32)
    o2 = k.sbuf(f"{pre}o2", [128, 256], F32)
    junk = k.sbuf(f"{pre}junk", [128, 256], F32)
    ssq = k.sbuf(f"{pre}ssq", [128, 1], F32)
    ob = [k.sbuf(f"{pre}ob{i}", [128, 256], BF16) for i in range(2)]
    oTst = [k.sbuf(f"{pre}oTst{i}", [128, 2, 512], BF16) for i in range(2)]
    cnt = {"s": 0, "o": 0, "t": 0}

    k.dma("sp", vcs[:], vc_d[:, :].rearrange("(j p) c -> p j c", p=128), reads=[vc_d], writes=[vcs])
    k.dma("sp", vds[:], vd_d[:, :].rearrange("(j p) c -> p j c", p=128), reads=[vd_d], writes=[vds])

    def recip_den(acc, W):
        k.op("dve", lambda g: g.reciprocal(out=den[:], in_=acc[:, W - 1:W]), reads=[acc], writes=[den])

    def emit_T(obuf, nblk, blk0, q0, j, last):
        pt_ = ps_t[cnt["t"] % len(ps_t)]
        cnt["t"] += 1
        for b_ in range(nblk):
            k.op("pe", lambda g: g.transpose(out=pt_[:, b_, :], in_=obuf[:, b_ * 128:(b_ + 1) * 128], identity=ident_b[:]),
                 reads=[obuf, ident_b], writes=[pt_])
        stg = oTst[(q0 // 512) % 2]
        k.copy("dve", stg, stg[:, 0:nblk, j * 128:(j + 1) * 128], pt_, pt_[:, 0:nblk, :])
        if last:
            k.dma("sp", oT_d[blk0:blk0 + nblk, :, q0:q0 + 512].rearrange("b p t -> p b t"), stg[:, 0:nblk, :],
                  reads=[stg], writes=[oT_d])

    for mp in range(2):
        kT, qT = kTs[mp], qTs[mp]
        k.dma("sp", kT[:], kc_d[mp], reads=[kc_d], writes=[kT])
        k.dma("sp", qT[:], qc_d[mp], reads=[qc_d], writes=[qT])
        for ch in range(nchunks):
            q0 = ch * 512
            dense_unit(k, kT, qT, q0, vcs, 257, accs, ps_s, pts, cnt, scale)
            for j in range(4):
                acc = accs[j]
                recip_den(acc, 257)
                if mp == 0:
                    k.op("dve", lambda g: g.tensor_scalar(out=o1s[:, ch * 4 + j, :], in0=acc[:, 0:256], scalar1=den[:, 0:1],
                                                          scalar2=None, op0=ALU.mult), reads=[acc, den], writes=[o1s])
                else:
                    k.op("dve", lambda g: g.tensor_scalar(out=o2[:], in0=acc[:, 0:256], scalar1=den[:, 0:1], scalar2=None,
                                                          op0=ALU.mult), reads=[acc, den], writes=[o2])
                    k.op("dve", lambda g: g.scalar_tensor_tensor(out=o2[:], in0=o2[:], scalar=neglam[:, 0:1],
                                                                 in1=o1s[:, ch * 4 + j, :], op0=ALU.mult, op1=ALU.add),
                         reads=[o2, neglam, o1s], writes=[o2])
                    k.op("act", lambda g: g.activation(out=junk[:], in_=o2[:], func=AF.Square, accum_out=ssq[:]),
                         reads=[o2], writes=[junk, ssq])
                    k.op("dve", lambda g: g.tensor_scalar(out=ssq[:], in0=ssq[:], scalar1=1.0 / 256, scalar2=1e-6,
                                                          op0=ALU.mult, op1=ALU.add), reads=[ssq], writes=[ssq])
                    k.op("act", lambda g: g.activation(out=ssq[:], in_=ssq[:], func=AF.Sqrt), reads=[ssq], writes=[ssq])
                    k.op("dve", lambda g: g.reciprocal(out=ssq[:], in_=ssq[:]), reads=[ssq], writes=[ssq])
                    o_ = ob[cnt["o"] % 2]
                    cnt["o"] += 1
                    k.op("dve", lambda g: g.scalar_tensor_tensor(out=o_[:], in0=o2[:], scalar=ssq[:, 0:1], in1=subg[:],
                                                                 op0=ALU.mult, op1=ALU.mult), reads=[o2, ssq, subg], writes=[o_])
                    emit_T(o_, 2, 0, q0, j, j == 3)
    kT = kTs[0]
    k.dma("sp", kT[:], kd_d[:, :], reads=[kd_d], writes=[kT])
    for hq in range(2):
        qT = qTs[hq]
        k.dma("sp", qT[:], qd_d[hq], reads=[qd_d], writes=[qT])
        for ch in range(nchunks):
            q0 = ch * 512
            dense_unit(k, kT, qT, q0, vds, 129, accs, ps_s, pts, cnt, scale)
            for j in range(4):
                acc = accs[j]
                recip_den(acc, 129)
                o_ = ob[cnt["o"] % 2]
                cnt["o"] += 1
                k.op("dve", lambda g: g.tensor_scalar(out=o_[:, 0:128], in0=acc[:, 0:128], scalar1=den[:, 0:1], scalar2=None,
                                                      op0=ALU.mult), reads=[acc, den], writes=[o_])
                emit_T(o_, 1, 2 + hq, q0, j, j == 3)


def build_attn_cd(nchunks=SEQ // 512):
    nc = bass.Bass("TRN2", target_bir_lowering=False)
    with ExitStack() as st:
        k = K(nc, st)
        qc = k.dram("qc", [2, 128, SEQ], BF16, "ExternalInput")
        kc = k.dram("kc", [2, 128, SEQ], BF16, "ExternalInput")
        vc = k.dram("vc", [SEQ, 257], BF16, "ExternalInput")
        qd = k.dram("qd", [2, 128, SEQ], BF16, "ExternalInput")
        kd = k.dram("kd", [128, SEQ], BF16, "ExternalInput")
        vd = k.dram("vd", [SEQ, 129], BF16, "ExternalInput")
        lam = k.dram("lam", [4, 128], F32, "ExternalInput")
        subg = k.dram("subg", [1, 256], F32, "ExternalInput")
        identb = k.dram("identb", [128, 128], BF16, "ExternalInput")
        oT = k.dram("oT", [4, 128, SEQ], BF16, "ExternalOutput")
        phase_attn_cd(k, "e", qc, kc, vc, qd, kd, vd, lam, subg, identb, oT, nchunks=nchunks)
        k.finish("sp")
    return nc


def _run(nc, in_maps):
    res = run_bass_kernel_spmd(nc, in_maps, core_ids=list(range(NCORES)))
    return res.results


def _aug_ones(v):
    out = np.zeros(v.shape[:-1] + (v.shape[-1] + 1,), dtype=v.dtype)
    out[..., :-1] = v
    out[..., -1] = 1.0
    return out


def kernel(x, w_in_ab, sink_a, w_out_ab, w_in_cd, lam_q1, lam_k1, lam_q2, lam_k2, subln_g, q_norm_g, k_norm_g,
           w_out_cd, ln_mix_g, ln_mix_b, peer_wq, peer_keys, peer_u, peer_v, ln_ffn_g, ln_ffn_b):
    f32 = lambda a: np.ascontiguousarray(np.asarray(a, dtype=np.float32))
    x = f32(x)[0]
    ccp, ssp, cca, ssa = rope_tables()
    identb = np.eye(128, dtype=np.float32).astype(NP_BF16)
    masks = band_masks()
    pconst = peer_consts()
    cs = [slice(c * T, (c + 1) * T) for c in range(NCORES)]

    def keysT_of(l):
        return np.ascontiguousarray(f32(peer_keys)[l].reshape(16, 128, 128).transpose(2, 0, 1).reshape(128, 2048))

    w0 = f32(w_in_ab)[0]
    r = _run(build_inproj(0), [{"xT": np.ascontiguousarray(x[cs[c]].T), "w": w0, "identb": identb,
                                "ccp": ccp[cs[c]], "ssp": ssp[cs[c]]} for c in range(NCORES)])
    qkT_all = [r[c]["qkT"] for c in range(NCORES)]
    ext = host_ext_kv(qkT_all, [r[c]["vtm"] for c in range(NCORES)], [16, 17, 18, 19] + list(range(44, 68)), None)
    sink = f32(sink_a)[0].reshape(1, 16)
    r = _run(build_attn_ab(), [{"qT": qkT_all[c], "kTe": ext[c][0], "vaug": ext[c][1], "sink": sink, "masks": masks,
                                "identb": identb} for c in range(NCORES)])
    oT = [r[c]["oT"] for c in range(NCORES)]
    del ext, qkT_all
    g0, b0 = f32(ln_mix_g)[0:1], f32(ln_mix_b)[0:1]
    wo0 = f32(w_out_ab)[0]
    r = _run(build_outproj_ln(24, True), [{"oT": oT[c], "w": wo0, "x": x[cs[c]], "g": g0, "b": b0, "identb": identb}
                                          for c in range(NCORES)])
    x1 = [r[c]["x1"] for c in range(NCORES)]
    x1T = [r[c]["x1T"] for c in range(NCORES)]
    u0, v0 = f32(peer_u)[0], f32(peer_v)[0]
    wq0 = f32(peer_wq)[0]
    kT0 = keysT_of(0)
    gf0, bf0 = f32(ln_ffn_g)[0:1], f32(ln_ffn_b)[0:1]
    r = _run(build_peer(False), [{"x1": x1[c], "x1T": x1T[c], "wq": wq0, "keysT": kT0, "u": u0, "v": v0, "g": gf0, "b": bf0,
                                  "pconst": pconst, "identb": identb} for c in range(NCORES)])
    x2 = [r[c]["x2"] for c in range(NCORES)]
    x2T = [r[c]["x2T"] for c in range(NCORES)]
    w1 = f32(w_in_cd)[0]
    qg, kg = f32(q_norm_g)[0:1], f32(k_norm_g)[0:1]
    r = _run(build_inproj(1, x_bf16=True), [{"xT": x2T[c], "w": w1, "identb": identb, "ccp": ccp[cs[c]], "ssp": ssp[cs[c]],
                                             "cca": cca[cs[c]], "ssa": ssa[cs[c]], "qg": qg, "kg": kg} for c in range(NCORES)])
    qk = np.concatenate([r[c]["qkT"] for c in range(NCORES)], axis=2)
    vt = np.concatenate([r[c]["vtm"] for c in range(NCORES)], axis=0)
    lam = np.concatenate([f32(lam_q1)[0:1], f32(lam_k1)[0:1], f32(lam_q2)[0:1], f32(lam_k2)[0:1]], 0)
    subg = f32(subln_g)[0:1]
    maps = []
    for c in range(NCORES):
        maps.append({"qc": np.ascontiguousarray(qk[2 * c:2 * c + 2]), "kc": np.ascontiguousarray(qk[16 + 2 * c:18 + 2 * c]),
                     "vc": _aug_ones(vt[:, 256 * c:256 * c + 256]), "qd": np.ascontiguousarray(qk[32 + 2 * c:34 + 2 * c]),
                     "kd": np.ascontiguousarray(qk[48 + c // 2]),
                     "vd": _aug_ones(vt[:, 2048 + 128 * (c // 2):2048 + 128 * (c // 2) + 128]),
                     "lam": lam, "subg": subg, "identb": identb})
    r = _run(build_attn_cd(), maps)
    del qk, vt, maps
    ofull = np.zeros((32, 128, SEQ), dtype=NP_BF16)
    for c in range(NCORES):
        o = r[c]["oT"]
        ofull[2 * c] = o[0]
        ofull[2 * c + 1] = o[1]
        ofull[16 + 2 * c] = o[2]
        ofull[16 + 2 * c + 1] = o[3]
    g1, b1 = f32(ln_mix_g)[1:2], f32(ln_mix_b)[1:2]
    wo1 = f32(w_out_cd)[0]
    r = _run(build_outproj_ln(32, True), [{"oT": np.ascontiguousarray(ofull[:, :, cs[c]]), "w": wo1, "x": x2[c], "g": g1,
                                           "b": b1, "identb": identb} for c in range(NCORES)])
    x3 = [r[c]["x1"] for c in range(NCORES)]
    x3T = [r[c]["x1T"] for c in range(NCORES)]
    u1, v1 = f32(peer_u)[1], f32(peer_v)[1]
    gf1, bf1 = f32(ln_ffn_g)[1:2], f32(ln_ffn_b)[1:2]
    r = _run(build_peer(True), [{"x1": x3[c], "x1T": x3T[c], "wq": f32(peer_wq)[1], "keysT": keysT_of(1), "u": u1, "v": v1,
                                 "g": gf1, "b": bf1, "pconst": pconst, "identb": identb} for c in range(NCORES)])
    out = np.concatenate([r[c]["x2"] for c in range(NCORES)], axis=0)
    return out[None].astype(np.float32)
```

```python
import math
from contextlib import ExitStack

import numpy as np
import ml_dtypes

import concourse.bass as bass
import concourse.mybir as mybir
from concourse.bass_utils import run_bass_kernel_spmd

F32 = mybir.dt.float32
BF16 = mybir.dt.bfloat16
I32 = mybir.dt.int32
U32 = mybir.dt.uint32
AF = mybir.ActivationFunctionType
ALU = mybir.AluOpType
AX = mybir.AxisListType

NCORES = 8
SEQ = 8192
D = 4096
T = SEQ // NCORES
MT = T // 128
KT = D // 128
HD = 128
SEM_LIMIT = 2000
NP_BF16 = ml_dtypes.bfloat16
SAME_ENGINE_WAIT = {"pe": False, "dve": True, "act": True, "pool": True, "sp": False}


class Buf:
    __slots__ = ("t", "lw", "rd", "name")

    def __init__(self, t, name=""):
        self.t = t
        self.lw = None
        self.rd = []
        self.name = name

    def __getitem__(self, idx):
        return self.t[idx]


class K:
    def __init__(self, nc, stack):
        self.nc = nc
        self.stack = stack
        self.eng = {"pe": nc.tensor, "dve": nc.vector, "act": nc.scalar, "pool": nc.gpsimd, "sp": nc.sync}
        self.cur = {}
        self.seen = {}
        self.nsem = 0
        self.dma_rr = {}
        self.same_engine_wait = dict(SAME_ENGINE_WAIT)
        self.n_inst = 0
        self.pending = {}

    def sbuf(self, name, shape, dtype):
        return Buf(self.stack.enter_context(self.nc.sbuf_tensor(name, list(shape), dtype)), name)

    def psum(self, name, shape, dtype=F32):
        return Buf(self.stack.enter_context(self.nc.psum_tensor(name, list(shape), dtype)), name)

    def dram(self, name, shape, dtype, kind="Internal"):
        return Buf(self.nc.dram_tensor(name, list(shape), dtype, kind=kind).ap(), name)

    def new_sem(self, name):
        self.nsem += 1
        return self.stack.enter_context(self.nc.semaphore(f"{name}_{self.nsem}"))

    def _wait(self, e, tok):
        if tok is None:
            return
        sem, val, src = tok
        if src == e and not self.same_engine_wait[e]:
            return
        seen = self.seen.setdefault(e, {})
        kk = id(sem)
        if seen.get(kk, 0) >= val:
            return
        self.eng[e].wait_ge(sem, val)
        seen[kk] = val

    def _deps(self, e, reads, writes):
        for b in reads:
            self._wait(e, b.lw)
        for b in writes:
            self._wait(e, b.lw)
            for r in b.rd:
                self._wait(e, r)

    def _commit(self, tok, reads, writes):
        for b in reads:
            b.rd.append(tok)
            if len(b.rd) > 64:
                b.rd = b.rd[-48:]
        for b in writes:
            b.lw = tok
            b.rd = []

    def op(self, e, fn, reads=(), writes=(), sig=True):
        self._deps(e, reads, writes)
        if not sig:
            pr, pw = self.pending.setdefault(e, ([], []))
            for b in reads:
                if b not in pr:
                    pr.append(b)
            for b in writes:
                if b not in pw:
                    pw.append(b)
            fn(self.eng[e])
            self.n_inst += 1
            return None
        if e in self.pending:
            pr, pw = self.pending.pop(e)
            reads = list(reads) + [b for b in pr if b not in reads]
            writes = list(writes) + [b for b in pw if b not in writes]
        c = self.cur.get(e)
        if c is None or c[1] >= SEM_LIMIT:
            c = [self.new_sem("s" + e), 0]
            self.cur[e] = c
        ins = fn(self.eng[e])
        c[1] += 1
        ins.then_inc(c[0], 1)
        tok = (c[0], c[1], e)
        self._commit(tok, reads, writes)
        self.n_inst += 1
        return tok

    def _dma_slot(self, q):
        R = 8
        ring = self.dma_rr.setdefault(q, {"sems": [], "i": 0})
        i = ring["i"]
        ring["i"] += 1
        slot = i % R
        if len(ring["sems"]) <= slot:
            ring["sems"].append([self.new_sem("d" + q), 0, None])
        s = ring["sems"][slot]
        self._wait(q, s[2])
        if s[1] >= SEM_LIMIT:
            s[0] = self.new_sem("d" + q)
            s[1] = 0
            s[2] = None
        return s

    def dma(self, q, out, in_, reads=(), writes=(), **kw):
        s = self._dma_slot(q)
        self._deps(q, reads, writes)
        ins = self.eng[q].dma_start(out=out, in_=in_, **kw)
        s[1] += 16
        ins.then_inc(s[0], 16)
        tok = (s[0], s[1], "dma")
        s[2] = tok
        self._commit(tok, reads, writes)
        self.n_inst += 1
        return tok

    def gather(self, out, in_, idx_ap, reads=(), writes=(), **kw):
        q = "pool"
        s = self._dma_slot(q)
        self._deps(q, reads, writes)
        ins = self.eng[q].indirect_dma_start(out=out, out_offset=None, in_=in_,
                                             in_offset=bass.IndirectOffsetOnAxis(ap=idx_ap, axis=0), **kw)
        s[1] += 16
        ins.then_inc(s[0], 16)
        tok = (s[0], s[1], "dma")
        s[2] = tok
        self._commit(tok, reads, writes)
        self.n_inst += 1
        return tok

    def finish(self, e="sp"):
        for q, ring in self.dma_rr.items():
            for s in ring["sems"]:
                self._wait(e, s[2])

    def copy(self, e, out_b, out_ap, in_b, in_ap):
        if e == "act":
            return self.op("act", lambda g: g.copy(out=out_ap, in_=in_ap), reads=[in_b], writes=[out_b])
        return self.op(e, lambda g: g.tensor_copy(out=out_ap, in_=in_ap), reads=[in_b], writes=[out_b])


def bcast_free(ap, shape):
    return ap.to_broadcast(list(shape))


def load_xT(k, xT_dram, xb, stage):
    for kt in range(KT):
        s = stage[kt % len(stage)]
        k.dma("sp", s[:], xT_dram[kt * 128:(kt + 1) * 128, :], reads=[xT_dram], writes=[s])
        k.copy("act" if kt % 2 == 0 else "dve", xb[kt], xb[kt][:], s, s[:])


def stream_linear(k, xb, w_dram, N, wst, wb, ps_banks, epilogue, NCH=512, KQ=4, cast_engs=("act", "dve", "pool"),
                  state=None):
    nkt = len(xb)
    NKQ = nkt // KQ
    st = state if state is not None else {"cnt": 0, "oc": 0}
    for c in range(N // NCH):
        n0 = c * NCH
        b = c % 2
        for q in range(NKQ):
            s = wst[st["cnt"] % len(wst)]
            src = w_dram[q * KQ * 128:(q + 1) * KQ * 128, n0:n0 + NCH].rearrange("(j p) c -> p j c", p=128)
            k.dma("sp", s[:], src, reads=[w_dram], writes=[s])
            k.copy(cast_engs[st["cnt"] % len(cast_engs)], wb[b][q], wb[b][q][:], s, s[:])
            st["cnt"] += 1
        for m in range(MT):
            p = ps_banks[st["oc"] % len(ps_banks)]
            for kt in range(nkt):
                q, j = divmod(kt, KQ)
                k.op("pe", lambda g: g.matmul(p[:], lhsT=xb[kt][:, m * 128:(m + 1) * 128], rhs=wb[b][q][:, j, :],
                                              start=(kt == 0), stop=(kt == nkt - 1)),
                     reads=[xb[kt], wb[b][q]], writes=[p], sig=(kt == nkt - 1))
            epilogue(c, m, p)
            st["oc"] += 1
    return st


def rope_tile(k, src_b, src3, cc, ss, m, groups, rot_w, rb, tmp_t, tmp_u):
    H4 = 4
    w = rot_w
    ccb = cc[:, m, 0:w].unsqueeze(1).to_broadcast([128, H4, w])
    k.op("dve", lambda g: g.tensor_tensor(out=tmp_t[:, :, 0:w], in0=src3[:, :, 0:w], in1=ccb, op=ALU.mult),
         reads=[src_b, cc], writes=[tmp_t])
    first = True
    for (lo, half) in groups:
        hi = lo + half
        s_lo = ss[:, m, lo:lo + half].unsqueeze(1).to_broadcast([128, H4, half])
        s_hi = ss[:, m, hi:hi + half].unsqueeze(1).to_broadcast([128, H4, half])
        k.op("dve", lambda g: g.tensor_tensor(out=tmp_u[:, :, lo:lo + half], in0=src3[:, :, hi:hi + half], in1=s_lo,
                                              op=ALU.mult), reads=[src_b, ss], writes=[tmp_u])
        k.op("dve", lambda g: g.tensor_tensor(out=tmp_u[:, :, hi:hi + half], in0=src3[:, :, lo:lo + half], in1=s_hi,
                                              op=ALU.mult), reads=[src_b, ss], writes=[tmp_u])
    k.op("dve", lambda g: g.tensor_tensor(out=rb[:, :, 0:w], in0=tmp_t[:, :, 0:w], in1=tmp_u[:, :, 0:w], op=ALU.add),
         reads=[tmp_t, tmp_u], writes=[rb])
    if w < 128:
        k.op("act", lambda g: g.copy(out=rb[:, :, w:128], in_=src3[:, :, w:128]), reads=[src_b], writes=[rb])


def phase_inproj(k, pre, xT_dram, w_dram, N, chunk_types, tabs, gains, ident_b, qkT_dram, v_dram, res):
    xb = res["xb"]
    wst = res["wst"]
    wb = res["wb"]
    ps = res["ps_mm"]
    psT = res["ps_tr"]
    rbs = [k.sbuf(f"{pre}rb{i}", [128, 4, 128], BF16) for i in range(2)]
    tmp_t = k.sbuf(f"{pre}tt", [128, 4, 128], F32)
    tmp_u = k.sbuf(f"{pre}tu", [128, 4, 128], F32)
    qn = k.sbuf(f"{pre}qn", [128, 4, 128], F32)
    junk = k.sbuf(f"{pre}junk", [128, 128], F32)
    ssq = k.sbuf(f"{pre}ssq", [128, 4], F32)
    rstd = k.sbuf(f"{pre}rstd", [128, 4], F32)
    fmst = [k.sbuf(f"{pre}fm{i}", [128, 4, T], BF16) for i in range(2)]
    vst = [k.sbuf(f"{pre}vst{i}", [128, 512], BF16) for i in range(2)]
    cnt = {"e": 0}

    def epilogue(c, m, p):
        ty = chunk_types[c]
        i = cnt["e"]
        cnt["e"] += 1
        if ty[0] == "v":
            vs = vst[i % 2]
            k.copy("act", vs, vs[:], p, p[:])
            k.dma("sp", v_dram[m * 128:(m + 1) * 128, ty[1]:ty[1] + 512], vs[:], reads=[vs], writes=[v_dram])
            return
        rb = rbs[i % 2]
        p3 = p[:].rearrange("p (h d) -> p h d", h=4)
        if ty[0] == "rope":
            rope_tile(k, p, p3, tabs["ccp"], tabs["ssp"], m, [(0, 16)], 32, rb, tmp_t, tmp_u)
        else:
            gb = gains[ty[2]]
            for j in range(4):
                k.op("act", lambda g: g.activation(out=junk[:], in_=p[:, j * 128:(j + 1) * 128], func=AF.Square,
                                                   accum_out=ssq[:, j:j + 1]), reads=[p], writes=[junk, ssq])
            k.op("dve", lambda g: g.tensor_scalar(out=rstd[:], in0=ssq[:], scalar1=1.0 / 128, scalar2=1e-6,
                                                  op0=ALU.mult, op1=ALU.add), reads=[ssq], writes=[rstd])
            k.op("act", lambda g: g.activation(out=rstd[:], in_=rstd[:], func=AF.Sqrt), reads=[rstd], writes=[rstd])
            k.op("dve", lambda g: g.reciprocal(out=rstd[:], in_=rstd[:]), reads=[rstd], writes=[rstd])
            k.op("dve", lambda g: g.tensor_tensor(out=qn[:], in0=p3, in1=rstd[:].unsqueeze(2).to_broadcast([128, 4, 128]),
                                                  op=ALU.mult), reads=[p, rstd], writes=[qn])
            k.op("dve", lambda g: g.tensor_tensor(out=qn[:], in0=qn[:], in1=gb[:].unsqueeze(1).to_broadcast([128, 4, 128]),
                                                  op=ALU.mult), reads=[qn, gb], writes=[qn])
            rope_tile(k, qn, qn[:], tabs["cca"], tabs["ssa"], m, [(0, 32), (64, 32)], 128, rb, tmp_t, tmp_u)
        pt = psT[i % len(psT)]
        for j in range(4):
            k.op("pe", lambda g: g.transpose(out=pt[:, j, 0:128], in_=rb[:, j, :], identity=ident_b[:]),
                 reads=[rb, ident_b], writes=[pt])
        fm = fmst[c % 2]
        k.copy("act" if i % 2 else "dve", fm, fm[:, :, m * 128:(m + 1) * 128], pt, pt[:, :, 0:128])
        if m == MT - 1:
            fm0 = ty[1]
            k.dma("sp", qkT_dram[fm0:fm0 + 4].rearrange("j p t -> p j t"), fm[:], reads=[fm], writes=[qkT_dram])

    stream_linear(k, xb, w_dram, N, wst, wb, ps, epilogue, state=res["lin_state"])


def rope_tables():
    pos = np.arange(SEQ, dtype=np.float32)
    half = 16
    inv = (np.float32(500000.0) ** (-np.arange(half, dtype=np.float32) / half)).astype(np.float32)
    ang = pos[:, None] * inv[None, :]
    c, s = np.cos(ang).astype(np.float32), np.sin(ang).astype(np.float32)
    ccp = np.ones((SEQ, 128), np.float32)
    ssp = np.zeros((SEQ, 128), np.float32)
    ccp[:, 0:16] = c
    ccp[:, 16:32] = c
    ssp[:, 0:16] = -s
    ssp[:, 16:32] = s
    rows = (np.arange(SEQ) // 64).astype(np.float32)
    cols = (np.arange(SEQ) % 64).astype(np.float32)
    inv2 = (np.float32(10000.0) ** (-np.arange(32, dtype=np.float32) / 32)).astype(np.float32)
    ar = rows[:, None] * inv2[None, :]
    ac = cols[:, None] * inv2[None, :]
    cr, sr, cc_, sc = (np.cos(ar).astype(np.float32), np.sin(ar).astype(np.float32),
                       np.cos(ac).astype(np.float32), np.sin(ac).astype(np.float32))
    cca = np.concatenate([cr, cr, cc_, cc_], 1)
    ssa = np.concatenate([-sr, sr, -sc, sc], 1)
    return ccp, ssp, cca, ssa


def alloc_linear_res(k, pre):
    res = {}
    res["xb"] = [k.sbuf(f"{pre}xb{i}", [128, T], BF16) for i in range(KT)]
    res["xst"] = [k.sbuf(f"{pre}xst{i}", [128, T], F32) for i in range(2)]
    res["wst"] = [k.sbuf(f"{pre}wst{i}", [128, 4, 512], F32) for i in range(2)]
    res["wb"] = [[k.sbuf(f"{pre}wb{b}_{q}", [128, 4, 512], BF16) for q in range(KT // 4)] for b in range(2)]
    res["ps_mm"] = [k.psum(f"{pre}psmm{i}", [128, 512], F32) for i in range(4)]
    res["ps_tr"] = [k.psum(f"{pre}pstr{i}", [128, 4, 256], BF16) for i in range(2)]
    res["lin_state"] = {"cnt": 0, "oc": 0}
    return res


def load_tabs(k, pre, names, drams):
    tabs = {}
    for nm in names:
        t = k.sbuf(f"{pre}tab_{nm}", [128, MT, 128], F32)
        k.dma("sp", t[:], drams[nm][:, :].rearrange("(m p) d -> p m d", p=128), reads=[drams[nm]], writes=[t])
        tabs[nm] = t
    return tabs


AB_TYPES = ([("rope", c * 4) for c in range(4)] + [("rope", 16)] + [("v", 0)]
            + [("rope", 20 + c * 4) for c in range(6)] + [("rope", 44 + c * 4) for c in range(6)]
            + [("v", 512 + c * 512) for c in range(6)])
AB_NFM, AB_NV = 68, 3584
CD_TYPES = ([("rope", c * 4) for c in range(4)] + [("rope", 16 + c * 4) for c in range(4)]
            + [("v", c * 512) for c in range(4)]
            + [("axial", 32 + c * 4, "qg") for c in range(4)] + [("axial", 48, "kg")] + [("v", 2048)])
CD_NFM, CD_NV = 52, 2560


def build_inproj(layer, types=None, x_bf16=False):
    nc = bass.Bass("TRN2", target_bir_lowering=False)
    if types is None:
        types = AB_TYPES if layer == 0 else CD_TYPES
    N = 512 * len(types)
    nfm, nv = (AB_NFM, AB_NV) if layer == 0 else (CD_NFM, CD_NV)
    with ExitStack() as st:
        k = K(nc, st)
        xT = k.dram("xT", [D, T], BF16 if x_bf16 else F32, "ExternalInput")
        w = k.dram("w", [D, N], F32, "ExternalInput")
        identb = k.dram("identb", [128, 128], BF16, "ExternalInput")
        tabd = {nm: k.dram(nm, [T, 128], F32, "ExternalInput") for nm in (("ccp", "ssp") if layer == 0 else ("ccp", "ssp", "cca", "ssa"))}
        qkT = k.dram("qkT", [nfm, 128, T], BF16, "ExternalOutput")
        vtm = k.dram("vtm", [T, nv], BF16, "ExternalOutput")
        res = alloc_linear_res(k, "a")
        ident_b = k.sbuf("identb_s", [128, 128], BF16)
        k.dma("sp", ident_b[:], identb[:, :], reads=[identb], writes=[ident_b])
        tabs = load_tabs(k, "a", list(tabd.keys()), tabd)
        gains = {}
        if layer == 1:
            for nm in ("qg", "kg"):
                gd = k.dram(nm, [1, 128], F32, "ExternalInput")
                gs = k.sbuf("g_" + nm, [128, 128], F32)
                k.dma("sp", gs[:], gd[0:1, :].to_broadcast([128, 128]), reads=[gd], writes=[gs])
                gains[nm] = gs
        if x_bf16:
            for kt in range(KT):
                k.dma("sp", res["xb"][kt][:], xT[kt * 128:(kt + 1) * 128, :], reads=[xT], writes=[res["xb"][kt]])
        else:
            load_xT(k, xT, res["xb"], res["xst"])
        phase_inproj(k, "a", xT, w, N, types, tabs, gains, ident_b, qkT, vtm, res)
        k.finish("sp")
    return nc


HALO = 1024
TE = T + 2 * HALO
NTE = TE // 128
B_DIL = (1, 4, 16)
B_DT = (1, 2, 8)
MASK_OFF = {"A": 0, 0: 3, 1: 6, 2: 11}
N_MASKS = 28


def band_masks():
    m = np.zeros((128, N_MASKS, 128), np.float32)
    kl = np.arange(128)[:, None]
    ql = np.arange(128)[None, :]
    for j, dt in enumerate((-1, 0, 1)):
        diff = (ql - kl) - dt * 128
        m[:, MASK_OFF["A"] + j, :] = (np.abs(diff) <= 128)
    for g in range(3):
        d, r = B_DIL[g], B_DT[g]
        for j, dt in enumerate(range(-r, r + 1)):
            diff = (ql - kl) - dt * 128
            m[:, MASK_OFF[g] + j, :] = (np.abs(diff) <= 64 * d) & (diff % d == 0)
    return m.astype(NP_BF16)


def attn_unit(k, qT_ap, qT_b, kT_b, v_b, ktiles, mask_b, mask0, acc, first, last, ps_s, pts, cnt, scale):
    n = len(ktiles)
    i0 = 0
    while i0 < n:
        nt = min(4, n - i0)
        ps = ps_s[cnt["s"] % len(ps_s)]
        pt = pts[cnt["s"] % len(pts)]
        cnt["s"] += 1
        for j in range(nt):
            lt = ktiles[i0 + j]
            k.op("pe", lambda g: g.matmul(ps[:, j, :], lhsT=kT_b[:, lt * 128:(lt + 1) * 128], rhs=qT_ap,
                                          start=True, stop=True), reads=[kT_b, qT_b], writes=[ps])
        k.op("act", lambda g: g.activation(out=pt[:, 0:nt, :], in_=ps[:, 0:nt, :], func=AF.Exp, scale=scale),
             reads=[ps], writes=[pt])
        k.op("dve", lambda g: g.tensor_tensor(out=pt[:, 0:nt, :], in0=pt[:, 0:nt, :],
                                              in1=mask_b[:, mask0 + i0:mask0 + i0 + nt, :], op=ALU.mult),
             reads=[pt, mask_b], writes=[pt])
        for j in range(nt):
            lt = ktiles[i0 + j]
            st_ = first and (i0 + j == 0)
            sp_ = last and (i0 + j == n - 1)
            k.op("pe", lambda g: g.matmul(acc[:, 0:129], lhsT=pt[:, j, :], rhs=v_b[:, lt, :], start=st_, stop=sp_),
                 reads=[pt, v_b], writes=[acc])
        i0 += nt


def phase_attn_ab(k, pre, qT_d, kTe_d, vaug_d, sink_d, mask_d, identb_d, oT_d, bg=None):
    scale = 1.0 / math.sqrt(128.0)
    mask_b = k.sbuf(f"{pre}mask", [128, N_MASKS, 128], BF16)
    k.dma("sp", mask_b[:], mask_d[:, :, :], reads=[mask_d], writes=[mask_b])
    ident_b = k.sbuf(f"{pre}ident", [128, 128], BF16)
    k.dma("sp", ident_b[:], identb_d[:, :], reads=[identb_d], writes=[ident_b])
    esink = k.sbuf(f"{pre}esink", [128, 16], F32)
    k.dma("sp", esink[:], sink_d[0:1, :].to_broadcast([128, 16]), reads=[sink_d], writes=[esink])
    k.op("act", lambda g: g.activation(out=esink[:], in_=esink[:], func=AF.Exp), reads=[esink], writes=[esink])
    NKT = 46
    kbuf = [k.sbuf(f"{pre}kb{i}", [128, NKT * 128], BF16) for i in range(2)]
    vbuf = [k.sbuf(f"{pre}vb{i}", [128, NKT, 129], BF16) for i in range(2)]
    qbuf = [k.sbuf(f"{pre}qb{i}", [128, 4, T], BF16) for i in range(2)]
    ps_s = [k.psum(f"{pre}pss{i}", [128, 4, 128], F32) for i in range(3)]
    pts = [k.sbuf(f"{pre}pt{i}", [128, 4, 128], BF16) for i in range(3)]
    accs = [k.psum(f"{pre}acc{i}", [128, 512], F32) for i in range(2)]
    ps_t = [k.psum(f"{pre}pst{i}", [128, 1024], BF16) for i in range(2)]
    den = k.sbuf(f"{pre}den", [128, 1], F32)
    ob = [k.sbuf(f"{pre}ob{i}", [128, 128], BF16) for i in range(2)]
    oTs = [k.sbuf(f"{pre}oTs{i}", [128, T], BF16) for i in range(2)]
    cnt = {"s": 0, "a": 0, "o": 0}

    def finalize(acc, sink_col, ohead, m):
        if sink_col is not None:
            k.op("dve", lambda g: g.tensor_tensor(out=den[:], in0=acc[:, 128:129], in1=esink[:, sink_col:sink_col + 1],
                                                  op=ALU.add), reads=[acc, esink], writes=[den])
        else:
            k.op("dve", lambda g: g.tensor_copy(out=den[:], in_=acc[:, 128:129]), reads=[acc], writes=[den])
        k.op("dve", lambda g: g.reciprocal(out=den[:], in_=den[:]), reads=[den], writes=[den])
        o = ob[cnt["o"] % 2]
        k.op("dve", lambda g: g.tensor_scalar(out=o[:], in0=acc[:, 0:128], scalar1=den[:, 0:1], scalar2=None,
                                              op0=ALU.mult), reads=[acc, den], writes=[o])
        pt_ = ps_t[cnt["o"] % 2]
        k.op("pe", lambda g: g.transpose(out=pt_[:, 0:128], in_=o[:], identity=ident_b[:]),
             reads=[o, ident_b], writes=[pt_])
        oT = oTs[ohead % 2]
        k.copy("act", oT, oT[:, m * 128:(m + 1) * 128], pt_, pt_[:, 0:128])
        cnt["o"] += 1
        if m == MT - 1:
            k.dma("sp", oT_d[ohead], oT[:], reads=[oT], writes=[oT_d])

    job = 0
    for kvh in range(4):
        kb_, vb_, qb_ = kbuf[job % 2], vbuf[job % 2], qbuf[job % 2]
        job += 1
        k.dma("sp", kb_[:, 0:10 * 128], kTe_d[kvh, :, 7 * 128:17 * 128], reads=[kTe_d], writes=[kb_])
        k.dma("sp", vb_[:, 0:10, :], vaug_d[7 * 128:17 * 128, kvh, :].rearrange("(j p) c -> p j c", p=128),
              reads=[vaug_d], writes=[vb_])
        k.dma("sp", qb_[:], qT_d[kvh * 4:kvh * 4 + 4].rearrange("j p t -> p j t"), reads=[qT_d], writes=[qb_])
        for g4 in range(4):
            h = kvh * 4 + g4
            for m in range(MT):
                acc = accs[cnt["a"] % 2]
                cnt["a"] += 1
                attn_unit(k, qb_[:, g4, m * 128:(m + 1) * 128], qb_, kb_, vb_, [m, m + 1, m + 2], mask_b,
                          MASK_OFF["A"], acc, True, True, ps_s, pts, cnt, scale)
                finalize(acc, h, h, m)
            if bg is not None:
                bg()
    base = [0, 10, 22]
    lo_ext = [7, 6, 0]
    nload = [10, 12, 24]
    for h in range(8):
        kb_, vb_, qb_ = kbuf[job % 2], vbuf[job % 2], qbuf[job % 2]
        job += 1
        for g in range(3):
            kblk = 4 + g * 8 + h
            k.dma("sp", kb_[:, base[g] * 128:(base[g] + nload[g]) * 128],
                  kTe_d[kblk, :, lo_ext[g] * 128:(lo_ext[g] + nload[g]) * 128], reads=[kTe_d], writes=[kb_])
            k.dma("sp", vb_[:, base[g]:base[g] + nload[g], :],
                  vaug_d[lo_ext[g] * 128:(lo_ext[g] + nload[g]) * 128, kblk, :].rearrange("(j p) c -> p j c", p=128),
                  reads=[vaug_d], writes=[vb_])
            k.dma("sp", qb_[:, g, :], qT_d[20 + g * 8 + h], reads=[qT_d], writes=[qb_])
        for m in range(MT):
            acc = accs[cnt["a"] % 2]
            cnt["a"] += 1
            for g in range(3):
                r = B_DT[g]
                tiles = [base[g] + (8 + m + dt) - lo_ext[g] for dt in range(-r, r + 1)]
                attn_unit(k, qb_[:, g, m * 128:(m + 1) * 128], qb_, kb_, vb_, tiles, mask_b, MASK_OFF[g], acc,
                          g == 0, g == 2, ps_s, pts, cnt, scale)
            finalize(acc, None, 16 + h, m)
            if bg is not None:
                bg()


def build_attn_ab():
    nc = bass.Bass("TRN2", target_bir_lowering=False)
    with ExitStack() as st:
        k = K(nc, st)
        qT = k.dram("qT", [AB_NFM, 128, T], BF16, "ExternalInput")
        kTe = k.dram("kTe", [28, 128, TE], BF16, "ExternalInput")
        vaug = k.dram("vaug", [TE, 28, 129], BF16, "ExternalInput")
        sink = k.dram("sink", [1, 16], F32, "ExternalInput")
        masks = k.dram("masks", [128, N_MASKS, 128], BF16, "ExternalInput")
        identb = k.dram("identb", [128, 128], BF16, "ExternalInput")
        oT = k.dram("oT", [24, 128, T], BF16, "ExternalOutput")
        RW = 16384 // NCORES
        pairs = []
        for nm in ("u0", "v0", "u1", "v1"):
            src = k.dram(nm + "p", [RW, D], F32, "ExternalInput")
            dst = k.dram(nm + "b", [RW, D], BF16, "ExternalOutput")
            pairs.append((src, dst))
        bg = cast_rows_bg(k, "cv", pairs, RW)
        phase_attn_ab(k, "b", qT, kTe, vaug, sink, masks, identb, oT, bg=bg)
        while bg():
            pass
        k.finish("sp")
    return nc


def host_ext_kv(qkT_all, vtm_all, kblocks, vheads_cols):
    kfull = np.concatenate([q[kblocks] for q in qkT_all], axis=2)
    vfull = np.concatenate(vtm_all, axis=0)
    nk = len(kblocks)
    kpad = np.zeros((nk, 128, SEQ + 2 * HALO), dtype=kfull.dtype)
    kpad[:, :, HALO:HALO + SEQ] = kfull
    nvh = vfull.shape[1] // 128
    vpad = np.zeros((SEQ + 2 * HALO, nvh, 129), dtype=vfull.dtype)
    vpad[HALO:HALO + SEQ, :, 0:128] = vfull.reshape(SEQ, nvh, 128)
    vpad[HALO:HALO + SEQ, :, 128] = 1.0
    outs = []
    for c in range(NCORES):
        outs.append((np.ascontiguousarray(kpad[:, :, c * T:c * T + TE]), np.ascontiguousarray(vpad[c * T:c * T + TE])))
    return outs


ALPHA = float((2 * 2) ** 0.25)


def barrier(k):
    toks = []
    for e, c in k.cur.items():
        if c[1] > 0:
            toks.append((c[0], c[1], e))
    for q, ring in k.dma_rr.items():
        for s in ring["sems"]:
            if s[2] is not None:
                toks.append(s[2])
    for e in ("pe", "dve", "act", "pool", "sp"):
        for tk in toks:
            if tk[2] == e:
                continue
            k._wait(e, tk)


def phase_outproj(k, pre, oT_d, nkt, w_d, x_d, z_d, res):
    xb = res["xb"][:nkt]
    for kt in range(nkt):
        k.dma("sp", xb[kt][:], oT_d[kt], reads=[oT_d], writes=[xb[kt]])
    xin = [k.sbuf(f"{pre}xin{i}", [128, 512], F32) for i in range(3)]
    zt = [k.sbuf(f"{pre}zt{i}", [128, 512], F32) for i in range(3)]
    cnt = {"e": 0}

    def epilogue(c, m, p):
        i = cnt["e"]
        cnt["e"] += 1
        xi, zo = xin[i % 3], zt[i % 3]
        k.dma("sp", xi[:], x_d[m * 128:(m + 1) * 128, c * 512:(c + 1) * 512], reads=[x_d], writes=[xi])
        k.op("dve", lambda g: g.scalar_tensor_tensor(out=zo[:], in0=xi[:], scalar=ALPHA, in1=p[:], op0=ALU.mult,
                                                     op1=ALU.add), reads=[xi, p], writes=[zo])
        k.dma("sp", z_d[m * 128:(m + 1) * 128, c * 512:(c + 1) * 512], zo[:], reads=[zo], writes=[z_d])

    stream_linear(k, xb, w_d, D, res["wst"], res["wb"], res["ps_mm"], epilogue, state=res["lin_state"])


def phase_ln(k, pre, z_d, g_d, b_d, out_d, outT_d, identb_d, ps_tr, mt=MT):
    gt = k.sbuf(f"{pre}g", [128, D], F32)
    bt = k.sbuf(f"{pre}b", [128, D], F32)
    k.dma("sp", gt[:], g_d[0:1, :].to_broadcast([128, D]), reads=[g_d], writes=[gt])
    k.dma("sp", bt[:], b_d[0:1, :].to_broadcast([128, D]), reads=[b_d], writes=[bt])
    ident_b = k.sbuf(f"{pre}ident", [128, 128], BF16)
    k.dma("sp", ident_b[:], identb_d[:, :], reads=[identb_d], writes=[ident_b])
    zs = [k.sbuf(f"{pre}z{i}", [128, D], F32) for i in range(2)]
    os_ = [k.sbuf(f"{pre}o{i}", [128, D], F32) for i in range(2)]
    ob16 = k.sbuf(f"{pre}o16", [128, D], BF16)
    stg = [k.sbuf(f"{pre}stg{i}", [128, KT, 128], BF16) for i in range(2)]
    stats = k.sbuf(f"{pre}stats", [128, 8, 6], F32)
    mv = k.sbuf(f"{pre}mv", [128, 2], F32)
    rstd = k.sbuf(f"{pre}rstd", [128, 1], F32)
    for m in range(mt):
        z, o = zs[m % 2], os_[m % 2]
        k.dma("sp", z[:], z_d[m * 128:(m + 1) * 128, :], reads=[z_d], writes=[z])
        for c in range(8):
            k.op("dve", lambda g: g.bn_stats(out=stats[:, c, :], in_=z[:, c * 512:(c + 1) * 512]), reads=[z], writes=[stats])
        k.op("dve", lambda g: g.bn_aggr(out=mv[:], in_=stats[:].rearrange("p a b -> p (a b)")), reads=[stats], writes=[mv])
        k.op("dve", lambda g: g.tensor_scalar(out=rstd[:], in0=mv[:, 1:2], scalar1=1e-5, scalar2=None, op0=ALU.add),
             reads=[mv], writes=[rstd])
        k.op("act", lambda g: g.activation(out=rstd[:], in_=rstd[:], func=AF.Sqrt), reads=[rstd], writes=[rstd])
        k.op("dve", lambda g: g.reciprocal(out=rstd[:], in_=rstd[:]), reads=[rstd], writes=[rstd])
        k.op("dve", lambda g: g.tensor_scalar(out=o[:], in0=z[:], scalar1=mv[:, 0:1], scalar2=rstd[:, 0:1],
                                              op0=ALU.subtract, op1=ALU.mult), reads=[z, mv, rstd], writes=[o])
        k.op("pool", lambda g: g.tensor_tensor(out=o[:], in0=o[:], in1=gt[:], op=ALU.mult), reads=[o, gt], writes=[o])
        k.op("pool", lambda g: g.tensor_tensor(out=o[:], in0=o[:], in1=bt[:], op=ALU.add), reads=[o, bt], writes=[o])
        k.dma("sp", out_d[m * 128:(m + 1) * 128, :], o[:], reads=[o], writes=[out_d])
        if outT_d is not None:
            k.op("act", lambda g: g.copy(out=ob16[:], in_=o[:]), reads=[o], writes=[ob16])
            sg = stg[m % 2]
            for q in range(4):
                pt = ps_tr[q % len(ps_tr)]
                for j in range(8):
                    kt = q * 8 + j
                    k.op("pe", lambda g: g.transpose(out=pt[:, j, :], in_=ob16[:, kt * 128:(kt + 1) * 128],
                                                     identity=ident_b[:]), reads=[ob16, ident_b], writes=[pt])
                k.copy("act" if q % 2 else "dve", sg, sg[:, q * 8:(q + 1) * 8, :], pt, pt[:])
            k.dma("sp", outT_d[:, m * 128:(m + 1) * 128].rearrange("(kt p) t -> p kt t", p=128), sg[:],
                  reads=[sg], writes=[outT_d])


def build_outproj_ln(nkt, with_T):
    nc = bass.Bass("TRN2", target_bir_lowering=False)
    with ExitStack() as st:
        k = K(nc, st)
        oT = k.dram("oT", [nkt, 128, T], BF16, "ExternalInput")
        w = k.dram("w", [nkt * 128, D], F32, "ExternalInput")
        x = k.dram("x", [T, D], F32, "ExternalInput")
        g = k.dram("g", [1, D], F32, "ExternalInput")
        b = k.dram("b", [1, D], F32, "ExternalInput")
        identb = k.dram("identb", [128, 128], BF16, "ExternalInput")
        z = k.dram("z", [T, D], F32, "Internal")
        x1 = k.dram("x1", [T, D], F32, "ExternalOutput")
        x1T = k.dram("x1T", [D, T], BF16, "ExternalOutput") if with_T else None
        with ExitStack() as ph:
            k.stack = ph
            res = alloc_linear_res(k, "c")
            phase_outproj(k, "c", oT, nkt, w, x, z, res)
            barrier(k)
        k.stack = st
        with ExitStack() as ph:
            k.stack = ph
            ps_tr = [k.psum(f"lnpt{i}", [128, 8, 128], BF16) for i in range(2)]
            phase_ln(k, "l", z, g, b, x1, x1T, identb, ps_tr)
            barrier(k)
        k.stack = st
        k.finish("sp")
    return nc


NEG = -1.0e30


def phase_peer_scores(k, pre, x1T_d, wq_d, keysT_d, sc_d, xb):
    for kt in range(KT):
        k.dma("sp", xb[kt][:], x1T_d[kt * 128:(kt + 1) * 128, :], reads=[x1T_d], writes=[xb[kt]])
    keysT = k.sbuf(f"{pre}keysT", [128, 2048], F32)
    k.dma("sp", keysT[:], keysT_d[:, :], reads=[keysT_d], writes=[keysT])
    wst = [k.sbuf(f"{pre}wst{i}", [128, KT, 128], F32) for i in range(2)]
    wqb = [k.sbuf(f"{pre}wqb{i}", [128, KT, 128], BF16) for i in range(2)]
    qg = [k.sbuf(f"{pre}qg{i}", [128, T], F32) for i in range(2)]
    scst = [k.sbuf(f"{pre}scst{i}", [128, MT, 128], F32) for i in range(2)]
    ps = [k.psum(f"{pre}ps{i}", [128, 512], F32) for i in range(4)]
    ps2 = [k.psum(f"{pre}ps2{i}", [128, 512], F32) for i in range(2)]
    for g in range(16):
        ws, wb_, q_, sc_ = wst[g % 2], wqb[g % 2], qg[g % 2], scst[g % 2]
        k.dma("sp", ws[:], wq_d[:, g * 128:(g + 1) * 128].rearrange("(kt p) c -> p kt c", p=128), reads=[wq_d], writes=[ws])
        k.copy(("act", "dve")[g % 2], wb_, wb_[:], ws, ws[:])
        for tc in range(T // 512):
            p = ps[(g * 2 + tc) % 4]
            for kt in range(KT):
                k.op("pe", lambda e: e.matmul(p[:], lhsT=wb_[:, kt, :], rhs=xb[kt][:, tc * 512:(tc + 1) * 512],
                                              start=(kt == 0), stop=(kt == KT - 1)), reads=[wb_, xb[kt]], writes=[p])
            k.copy(("dve", "act")[tc % 2], q_, q_[:, tc * 512:(tc + 1) * 512], p, p[:])
        for m in range(MT):
            p2 = ps2[m % 2]
            k.op("pe", lambda e: e.matmul(p2[:, 0:128], lhsT=q_[:, m * 128:(m + 1) * 128], rhs=keysT[:, g * 128:(g + 1) * 128],
                                          start=True, stop=True), reads=[q_, keysT], writes=[p2])
            k.copy(("act", "dve")[m % 2], sc_, sc_[:, m, :], p2, p2[:, 0:128])
        k.dma("sp", sc_d[:, g * 128:(g + 1) * 128].rearrange("(m p) n -> p m n", p=128), sc_[:], reads=[sc_], writes=[sc_d])


def cast_rows_bg(k, pre, srcs_dsts, nrows, nbuf=2):
    st = [k.sbuf(f"{pre}st{i}", [128, D], F32) for i in range(nbuf)]
    cb = [k.sbuf(f"{pre}cb{i}", [128, D], BF16) for i in range(nbuf)]
    jobs = [(src, dst, i) for (src, dst) in srcs_dsts for i in range(nrows // 128)]
    state = {"n": 0}

    def step():
        n = state["n"]
        if n >= len(jobs):
            return False
        state["n"] += 1
        src, dst, i = jobs[n]
        s_, c_ = st[n % nbuf], cb[n % nbuf]
        k.dma("pool", s_[:], src[i * 128:(i + 1) * 128, :], reads=[src], writes=[s_])
        k.copy("pool", c_, c_[:], s_, s_[:])
        k.dma("pool", dst[i * 128:(i + 1) * 128, :], c_[:], reads=[c_], writes=[dst])
        return True
    return step


def phase_cast_rows(k, pre, srcs_dsts, nrows=16384):
    st = [k.sbuf(f"{pre}st{i}", [128, D], F32) for i in range(3)]
    cb = [k.sbuf(f"{pre}cb{i}", [128, D], BF16) for i in range(3)]
    jobs = [(src, dst, i) for (src, dst) in srcs_dsts for i in range(nrows // 128)]
    engs = ("act", "dve", "pool")

    def load(n):
        src, dst, i = jobs[n]
        k.dma("sp", st[n % 3][:], src[i * 128:(i + 1) * 128, :], reads=[src], writes=[st[n % 3]])

    for n in range(min(2, len(jobs))):
        load(n)
    for n in range(len(jobs)):
        if n + 2 < len(jobs):
            load(n + 2)
        src, dst, i = jobs[n]
        k.copy(engs[n % 3], cb[n % 3], cb[n % 3][:], st[n % 3], st[n % 3][:])
        k.dma("act", dst[i * 128:(i + 1) * 128, :], cb[n % 3][:], reads=[cb[n % 3]], writes=[dst])


def phase_peer_main(k, pre, x1_d, sc_d, u_d, v_d, pconst_d, identb_d, z_d, NG=4, mt=MT):
    ident_b = k.sbuf(f"{pre}ident", [128, 128], BF16)
    k.dma("sp", ident_b[:], identb_d[:, :], reads=[identb_d], writes=[ident_b])
    pc = k.sbuf(f"{pre}pc", [128, 48], F32)
    k.dma("sp", pc[:], pconst_d[:, :], reads=[pconst_d], writes=[pc])
    iota16, lo16, hi16 = pc[:, 0:16], pc[:, 16:32], pc[:, 32:48]
    sc = k.sbuf(f"{pre}sc", [128, 16, 128], F32)
    scr = k.sbuf(f"{pre}scr", [128, 16, 128], F32)
    sv = k.sbuf(f"{pre}sv", [128, 16, 16], F32)
    si = k.sbuf(f"{pre}si", [128, 16, 16], U32)
    sif = k.sbuf(f"{pre}sif", [128, 16, 16], F32)
    cand = k.sbuf(f"{pre}cand", [128, 8, 256], F32)
    cscr = k.sbuf(f"{pre}cscr", [128, 8, 256], F32)
    ts = k.sbuf(f"{pre}ts", [128, 8, 16], F32)
    tp = k.sbuf(f"{pre}tp", [128, 8, 16], U32)
    tpf = k.sbuf(f"{pre}tpf", [128, 8, 16], F32)
    w4a = k.sbuf(f"{pre}w4a", [128, 8, 16, 16], F32)
    w4b = k.sbuf(f"{pre}w4b", [128, 8, 16, 16], F32)
    r3a = k.sbuf(f"{pre}r3a", [128, 8, 16], F32)
    r3b = k.sbuf(f"{pre}r3b", [128, 8, 16], F32)
    r3c = k.sbuf(f"{pre}r3c", [128, 8, 16], F32)
    eidx_f = k.sbuf(f"{pre}eidxf", [128, 128], F32)
    eidx = k.sbuf(f"{pre}eidx", [128, 128], U32)
    gate = k.sbuf(f"{pre}gate", [128, 8, 16], F32)
    zsum = k.sbuf(f"{pre}zsum", [128, 8], F32)
    hid = k.sbuf(f"{pre}hid", [128, 128], F32)
    g1 = k.sbuf(f"{pre}g1", [128, 128], F32)
    wgt = k.sbuf(f"{pre}wgt", [128, 128], F32)
    xt = k.sbuf(f"{pre}xt", [128, D], F32)
    junk = k.sbuf(f"{pre}junk", [128, D], BF16)
    ug = [k.sbuf(f"{pre}ug{i}", [128, D], BF16) for i in range(NG)]
    vg = [k.sbuf(f"{pre}vg{i}", [128, D], BF16) for i in range(NG)]
    dg = [k.sbuf(f"{pre}dg{i}", [128, 128], BF16) for i in range(3)]
    zo = [k.sbuf(f"{pre}zo{i}", [128, 512], F32) for i in range(2)]
    pall = k.psum(f"{pre}pall", [128, 8, 512], F32)
    acc = [Buf(pall.t, f"{pre}acc{i}") for i in range(8)]
    xps = pall[:].rearrange("p c n -> p (c n)")

    def dve(fn, reads, writes):
        return k.op("dve", fn, reads=reads, writes=writes)

    eidx_t = [k.sbuf(f"{pre}eidx_t{i}", [128, 128], U32) for i in range(mt)]
    gate_t = [k.sbuf(f"{pre}gate_t{i}", [128, 128], F32) for i in range(mt)]

    def route(m):
        k.dma("sp", sc[:], sc_d[m * 128:(m + 1) * 128, :].rearrange("p (g n) -> p g n", g=16), reads=[sc_d], writes=[sc])
        for g in range(16):
            yield ('dve', lambda e: e.max(out=sv[:, g, 0:8], in_=sc[:, g, :]), [sc], [sv])
            yield ('dve', lambda e: e.max_index(out=si[:, g, 0:8], in_max=sv[:, g, 0:8], in_values=sc[:, g, :]), [sc, sv], [si])
            yield ('dve', lambda e: e.match_replace(out=scr[:, g, :], in_to_replace=sv[:, g, 0:8], in_values=sc[:, g, :],
                                          imm_value=NEG), [sc, sv], [scr])
            yield ('dve', lambda e: e.max(out=sv[:, g, 8:16], in_=scr[:, g, :]), [scr], [sv])
            yield ('dve', lambda e: e.max_index(out=si[:, g, 8:16], in_max=sv[:, g, 8:16], in_values=scr[:, g, :]), [scr, sv], [si])
        yield ('dve', lambda e: e.tensor_copy(out=sif[:], in_=si[:]), [si], [sif])
        sv4 = sv[:].rearrange("p (h c) r -> p h c r", c=2)
        sif4 = sif[:].rearrange("p (h c) r -> p h c r", c=2)
        c4 = cand[:].rearrange("p h (a b) -> p h a b", a=16)
        yield ('dve', lambda e: e.tensor_tensor(out=c4, in0=sv4[:, :, 0, :].unsqueeze(3).to_broadcast([128, 8, 16, 16]),
                                      in1=sv4[:, :, 1, :].unsqueeze(2).to_broadcast([128, 8, 16, 16]), op=ALU.add),
            [sv], [cand])
        for h in range(8):
            yield ('dve', lambda e: e.max(out=ts[:, h, 0:8], in_=cand[:, h, :]), [cand], [ts])
            yield ('dve', lambda e: e.max_index(out=tp[:, h, 0:8], in_max=ts[:, h, 0:8], in_values=cand[:, h, :]), [cand, ts], [tp])
            yield ('dve', lambda e: e.match_replace(out=cscr[:, h, :], in_to_replace=ts[:, h, 0:8], in_values=cand[:, h, :],
                                          imm_value=NEG), [cand, ts], [cscr])
            yield ('dve', lambda e: e.max(out=ts[:, h, 8:16], in_=cscr[:, h, :]), [cscr], [ts])
            yield ('dve', lambda e: e.max_index(out=tp[:, h, 8:16], in_max=ts[:, h, 8:16], in_values=cscr[:, h, :]), [cscr, ts], [tp])
        yield ('dve', lambda e: e.tensor_copy(out=tpf[:], in_=tp[:]), [tp], [tpf])
        tpf4 = tpf[:].unsqueeze(3).to_broadcast([128, 8, 16, 16])
        lo4 = lo16.unsqueeze(1).unsqueeze(1).to_broadcast([128, 8, 16, 16])
        hi4 = hi16.unsqueeze(1).unsqueeze(1).to_broadcast([128, 8, 16, 16])
        io4 = iota16.unsqueeze(1).unsqueeze(1).to_broadcast([128, 8, 16, 16])
        yield ('dve', lambda e: e.tensor_tensor(out=w4a[:], in0=tpf4, in1=lo4, op=ALU.is_ge), [tpf, pc], [w4a])
        yield ('dve', lambda e: e.tensor_tensor(out=w4b[:], in0=tpf4, in1=hi4, op=ALU.is_ge), [tpf, pc], [w4b])
        yield ('dve', lambda e: e.tensor_tensor(out=w4a[:], in0=w4a[:], in1=w4b[:], op=ALU.subtract), [w4a, w4b], [w4a])
        yield ('dve', lambda e: e.tensor_tensor(out=w4b[:], in0=w4a[:], in1=sif4[:, :, 0, :].unsqueeze(2).to_broadcast([128, 8, 16, 16]),
                                      op=ALU.mult), [w4a, sif], [w4b])
        yield ('dve', lambda e: e.tensor_reduce(out=r3a[:], in_=w4b[:], axis=AX.X, op=ALU.add), [w4b], [r3a])
        yield ('dve', lambda e: e.tensor_tensor(out=w4b[:], in0=w4a[:], in1=lo4, op=ALU.mult), [w4a, pc], [w4b])
        yield ('dve', lambda e: e.tensor_reduce(out=r3b[:], in_=w4b[:], axis=AX.X, op=ALU.add), [w4b], [r3b])
        yield ('dve', lambda e: e.tensor_tensor(out=r3b[:], in0=tpf[:], in1=r3b[:], op=ALU.subtract), [tpf, r3b], [r3b])
        yield ('dve', lambda e: e.tensor_tensor(out=w4a[:], in0=r3b[:].unsqueeze(3).to_broadcast([128, 8, 16, 16]), in1=io4,
                                      op=ALU.is_equal), [r3b, pc], [w4a])
        yield ('dve', lambda e: e.tensor_tensor(out=w4b[:], in0=w4a[:], in1=sif4[:, :, 1, :].unsqueeze(2).to_broadcast([128, 8, 16, 16]),
                                      op=ALU.mult), [w4a, sif], [w4b])
        yield ('dve', lambda e: e.tensor_reduce(out=r3c[:], in_=w4b[:], axis=AX.X, op=ALU.add), [w4b], [r3c])
        ef3 = eidx_f[:].rearrange("p (h r) -> p h r", h=8)
        yield ('dve', lambda e: e.scalar_tensor_tensor(out=ef3, in0=r3a[:], scalar=128.0, in1=r3c[:], op0=ALU.mult, op1=ALU.add),
            [r3a, r3c], [eidx_f])
        yield ('dve', lambda e: e.tensor_copy(out=eidx[:], in_=eidx_f[:]), [eidx_f], [eidx])
        yield ('dve', lambda e: e.tensor_tensor(out=gate[:], in0=ts[:], in1=ts[:, :, 0:1].to_broadcast([128, 8, 16]), op=ALU.subtract),
            [ts], [gate])
        yield ('act', lambda e: e.activation(out=gate[:], in_=gate[:], func=AF.Exp), [gate], [gate])
        yield ('dve', lambda e: e.tensor_reduce(out=zsum[:], in_=gate[:], axis=AX.X, op=ALU.add), [gate], [zsum])
        yield ('dve', lambda e: e.reciprocal(out=zsum[:], in_=zsum[:]), [zsum], [zsum])
        yield ('dve', lambda e: e.tensor_tensor(out=gate[:], in0=gate[:], in1=zsum[:].unsqueeze(2).to_broadcast([128, 8, 16]), op=ALU.mult),
            [gate, zsum], [gate])
        yield ('dve', lambda e: e.tensor_copy(out=eidx_t[m][:], in_=eidx[:]), [eidx], [eidx_t[m]])
        yield ('dve', lambda e: e.tensor_copy(out=gate_t[m][:], in_=gate[:].rearrange("p h r -> p (h r)")), [gate], [gate_t[m]])


    def advance(gen, n=1):
        for _ in range(n):
            it = next(gen, None)
            if it is None:
                return False
            k.op(it[0], it[1], reads=it[2], writes=it[3])
        return True

    g0 = route(0)
    while advance(g0):
        pass
    for m in range(mt):
        nxt = route(m + 1) if m + 1 < mt else None
        k.dma("sp", xt[:], x1_d[m * 128:(m + 1) * 128, :], reads=[x1_d], writes=[xt])
        for c in range(8):
            k.op("act", lambda e: e.copy(out=pall[:, c, :], in_=xt[:, c * 512:(c + 1) * 512]), reads=[xt], writes=[acc[c]])
        for s in range(128):
            ub = ug[s % NG]
            k.gather(ub[:], u_d[:, :], eidx_t[m][:, s:s + 1], reads=[eidx_t[m], u_d], writes=[ub])
            dve(lambda e: e.scalar_tensor_tensor(out=junk[:], in0=ub[:], scalar=1.0, in1=xps, op0=ALU.mult, op1=ALU.mult,
                                                 accum_out=hid[:, s:s + 1]), [ub] + acc, [junk, hid])
        dve(lambda e: e.tensor_tensor(out=g1[:], in0=hid[:], in1=hid[:], op=ALU.mult), [hid], [g1])
        dve(lambda e: e.tensor_scalar(out=g1[:], in0=g1[:], scalar1=0.044715 * 1.5957691216057308,
                                      scalar2=1.5957691216057308, op0=ALU.mult, op1=ALU.add), [g1], [g1])
        dve(lambda e: e.tensor_tensor(out=g1[:], in0=g1[:], in1=hid[:], op=ALU.mult), [g1, hid], [g1])
        k.op("act", lambda e: e.activation(out=g1[:], in_=g1[:], func=AF.Sigmoid), reads=[g1], writes=[g1])
        dve(lambda e: e.tensor_tensor(out=g1[:], in0=g1[:], in1=hid[:], op=ALU.mult), [g1, hid], [g1])
        dve(lambda e: e.tensor_tensor(out=wgt[:], in0=g1[:], in1=gate_t[m][:], op=ALU.mult), [g1, gate_t[m]], [wgt])
        for s in range(128):
            vb_ = vg[s % NG]
            k.gather(vb_[:], v_d[:, :], eidx_t[m][:, s:s + 1], reads=[eidx_t[m], v_d], writes=[vb_])
            d_ = dg[s % 3]
            dve(lambda e: e.tensor_scalar(out=d_[:], in0=ident_b[:], scalar1=wgt[:, s:s + 1], scalar2=None, op0=ALU.mult),
                [ident_b, wgt], [d_])
            if nxt is not None:
                advance(nxt, 2)
            for c in range(8):
                k.op("pe", lambda e: e.matmul(pall[:, c, :], lhsT=d_[:], rhs=vb_[:, c * 512:(c + 1) * 512],
                                              start=(s == 0), stop=(s == 127)), reads=[d_, vb_], writes=[acc[c]])
        while nxt is not None and advance(nxt):
            pass
        for c in range(8):
            z_ = zo[c % 2]
            dve(lambda e: e.scalar_tensor_tensor(out=z_[:], in0=xt[:, c * 512:(c + 1) * 512], scalar=ALPHA, in1=pall[:, c, :],
                                                 op0=ALU.mult, op1=ALU.add), [xt, acc[c]], [z_])
            k.dma("sp", z_d[m * 128:(m + 1) * 128, c * 512:(c + 1) * 512], z_[:], reads=[z_], writes=[z_d])


def peer_consts():
    pc = np.zeros((128, 48), np.float32)
    pc[:, 0:16] = np.arange(16)
    pc[:, 16:32] = 16 * np.arange(16)
    pc[:, 32:48] = 16 * np.arange(16) + 16
    return pc


def build_peer(final, mt=MT):
    nc = bass.Bass("TRN2", target_bir_lowering=False)
    with ExitStack() as st:
        k = K(nc, st)
        x1 = k.dram("x1", [T, D], F32, "ExternalInput")
        x1T = k.dram("x1T", [D, T], BF16, "ExternalInput")
        wq = k.dram("wq", [D, 2048], F32, "ExternalInput")
        keysT = k.dram("keysT", [128, 2048], F32, "ExternalInput")
        ub16 = k.dram("u", [16384, D], BF16, "ExternalInput")
        vb16 = k.dram("v", [16384, D], BF16, "ExternalInput")
        g = k.dram("g", [1, D], F32, "ExternalInput")
        b = k.dram("b", [1, D], F32, "ExternalInput")
        pconst = k.dram("pconst", [128, 48], F32, "ExternalInput")
        identb = k.dram("identb", [128, 128], BF16, "ExternalInput")
        sc = k.dram("sc", [T, 2048], F32, "Internal")
        z = k.dram("z", [T, D], F32, "Internal")
        x2 = k.dram("x2", [T, D], F32, "ExternalOutput")
        x2T = None if final else k.dram("x2T", [D, T], BF16, "ExternalOutput")
        with ExitStack() as ph:
            k.stack = ph
            xb = [k.sbuf(f"pxb{i}", [128, T], BF16) for i in range(KT)]
            phase_peer_scores(k, "ps", x1T, wq, keysT, sc, xb)
            barrier(k)
        with ExitStack() as ph:
            k.stack = ph
            phase_peer_main(k, "pm", x1, sc, ub16, vb16, pconst, identb, z, mt=mt)
            barrier(k)
        with ExitStack() as ph:
            k.stack = ph
            ps_tr = [k.psum(f"lnpt{i}", [128, 8, 128], BF16) for i in range(2)]
            phase_ln(k, "pl", z, g, b, x2, x2T, identb, ps_tr, mt=mt)
            barrier(k)
        k.stack = st
        k.finish("sp")
    return nc


LAMBDA_INIT = 0.8 - 0.6 * math.exp(-0.3 * 1)
NKT_FULL = SEQ // 128


def dense_unit(k, kT, qT, q0, vaug, W, accs, ps_s, pts, cnt, scale, LA=2):
    slots = {}

    def score(kt):
        ps = ps_s[cnt["s"] % len(ps_s)]
        pt = pts[cnt["s"] % len(pts)]
        cnt["s"] += 1
        slots[kt] = pt
        k.op("pe", lambda g: g.matmul(ps[:], lhsT=kT[:, kt * 128:(kt + 1) * 128], rhs=qT[:, q0:q0 + 512],
                                      start=True, stop=True), reads=[kT, qT], writes=[ps])
        k.op("act", lambda g: g.activation(out=pt[:], in_=ps[:], func=AF.Exp, scale=scale), reads=[ps], writes=[pt])

    for kt in range(min(LA, NKT_FULL)):
        score(kt)
    for kt in range(NKT_FULL):
        if kt + LA < NKT_FULL:
            score(kt + LA)
        pt = slots.pop(kt)
        for j in range(4):
            k.op("pe", lambda g: g.matmul(accs[j][:, 0:W], lhsT=pt[:, j * 128:(j + 1) * 128], rhs=vaug[:, kt, 0:W],
                                          start=(kt == 0), stop=(kt == NKT_FULL - 1)), reads=[pt, vaug], writes=[accs[j]])


def phase_attn_cd(k, pre, qc_d, kc_d, vc_d, qd_d, kd_d, vd_d, lam_d, subg_d, identb_d, oT_d, nchunks=SEQ // 512):
    scale = 1.0 / math.sqrt(128.0)
    ident_b = k.sbuf(f"{pre}ident", [128, 128], BF16)
    k.dma("sp", ident_b[:], identb_d[:, :], reads=[identb_d], writes=[ident_b])
    lam_t = k.sbuf(f"{pre}lamt", [128, 4, 128], F32)
    k.dma("sp", lam_t[:], lam_d[:, :].unsqueeze(0).to_broadcast([128, 4, 128]), reads=[lam_d], writes=[lam_t])
    lprod = k.sbuf(f"{pre}lprod", [128, 2, 128], F32)
    lsum = k.sbuf(f"{pre}lsum", [128, 2], F32)
    neglam = k.sbuf(f"{pre}neglam", [128, 1], F32)
    lam4 = lam_t[:].rearrange("p (a b) d -> p a b d", b=2)
    k.op("dve", lambda g: g.tensor_tensor(out=lprod[:], in0=lam4[:, :, 0, :], in1=lam4[:, :, 1, :], op=ALU.mult),
         reads=[lam_t], writes=[lprod])
    k.op("dve", lambda g: g.tensor_reduce(out=lsum[:], in_=lprod[:], axis=AX.X, op=ALU.add), reads=[lprod], writes=[lsum])
    k.op("act", lambda g: g.activation(out=lsum[:], in_=lsum[:], func=AF.Exp), reads=[lsum], writes=[lsum])
    k.op("dve", lambda g: g.scalar_tensor_tensor(out=neglam[:], in0=lsum[:, 1:2], scalar=-LAMBDA_INIT, in1=lsum[:, 0:1],
                                                 op0=ALU.add, op1=ALU.subtract), reads=[lsum], writes=[neglam])
    subg = k.sbuf(f"{pre}subg", [128, 256], F32)
    k.dma("sp", subg[:], subg_d[0:1, :].to_broadcast([128, 256]), reads=[subg_d], writes=[subg])
    k.op("dve", lambda g: g.tensor_scalar(out=subg[:], in0=subg[:], scalar1=1.0 - LAMBDA_INIT, scalar2=None, op0=ALU.mult),
         reads=[subg], writes=[subg])

    kTs = [k.sbuf(f"{pre}kT{i}", [128, SEQ], BF16) for i in range(2)]
    qTs = [k.sbuf(f"{pre}qT{i}", [128, SEQ], BF16) for i in range(2)]
    vcs = k.sbuf(f"{pre}vc", [128, NKT_FULL, 257], BF16)
    vds = k.sbuf(f"{pre}vd", [128, NKT_FULL, 129], BF16)
    ps_s = [k.psum(f"{pre}pss{i}", [128, 512], F32) for i in range(3)]
    pts = [k.sbuf(f"{pre}pt{i}", [128, 512], BF16) for i in range(4)]
    accs = [k.psum(f"{pre}acc{i}", [128, 512], F32) for i in range(4)]
    ps_t = [k.psum(f"{pre}pst{i}", [128, 8, 128], BF16) for i in range(1)]
    o1s = k.sbuf(f"{pre}o1s", [128, nchunks * 4, 256], F32)
    den = k.sbuf(f"{pre}den", [128, 1], F32)
    o2 = k.sbuf(f"{pre}o2", [128, 256], F32)
    junk = k.sbuf(f"{pre}junk", [128, 256], F32)
    ssq = k.sbuf(f"{pre}ssq", [128, 1], F32)
    ob = [k.sbuf(f"{pre}ob{i}", [128, 256], BF16) for i in range(2)]
    oTst = [k.sbuf(f"{pre}oTst{i}", [128, 2, 512], BF16) for i in range(2)]
    cnt = {"s": 0, "o": 0, "t": 0}

    k.dma("sp", vcs[:], vc_d[:, :].rearrange("(j p) c -> p j c", p=128), reads=[vc_d], writes=[vcs])
    k.dma("sp", vds[:], vd_d[:, :].rearrange("(j p) c -> p j c", p=128), reads=[vd_d], writes=[vds])

    def recip_den(acc, W):
        k.op("dve", lambda g: g.reciprocal(out=den[:], in_=acc[:, W - 1:W]), reads=[acc], writes=[den])

    def emit_T(obuf, nblk, blk0, q0, j, last):
        pt_ = ps_t[cnt["t"] % len(ps_t)]
        cnt["t"] += 1
        for b_ in range(nblk):
            k.op("pe", lambda g: g.transpose(out=pt_[:, b_, :], in_=obuf[:, b_ * 128:(b_ + 1) * 128], identity=ident_b[:]),
                 reads=[obuf, ident_b], writes=[pt_])
        stg = oTst[(q0 // 512) % 2]
        k.copy("dve", stg, stg[:, 0:nblk, j * 128:(j + 1) * 128], pt_, pt_[:, 0:nblk, :])
        if last:
            k.dma("sp", oT_d[blk0:blk0 + nblk, :, q0:q0 + 512].rearrange("b p t -> p b t"), stg[:, 0:nblk, :],
                  reads=[stg], writes=[oT_d])

    for mp in range(2):
        kT, qT = kTs[mp], qTs[mp]
        k.dma("sp", kT[:], kc_d[mp], reads=[kc_d], writes=[kT])
        k.dma("sp", qT[:], qc_d[mp], reads=[qc_d], writes=[qT])
        for ch in range(nchunks):
            q0 = ch * 512
            dense_unit(k, kT, qT, q0, vcs, 257, accs, ps_s, pts, cnt, scale)
            for j in range(4):
                acc = accs[j]
                recip_den(acc, 257)
                if mp == 0:
                    k.op("dve", lambda g: g.tensor_scalar(out=o1s[:, ch * 4 + j, :], in0=acc[:, 0:256], scalar1=den[:, 0:1],
                                                          scalar2=None, op0=ALU.mult), reads=[acc, den], writes=[o1s])
                else:
                    k.op("dve", lambda g: g.tensor_scalar(out=o2[:], in0=acc[:, 0:256], scalar1=den[:, 0:1], scalar2=None,
                                                          op0=ALU.mult), reads=[acc, den], writes=[o2])
                    k.op("dve", lambda g: g.scalar_tensor_tensor(out=o2[:], in0=o2[:], scalar=neglam[:, 0:1],
                                                                 in1=o1s[:, ch * 4 + j, :], op0=ALU.mult, op1=ALU.add),
                         reads=[o2, neglam, o1s], writes=[o2])
                    k.op("act", lambda g: g.activation(out=junk[:], in_=o2[:], func=AF.Square, accum_out=ssq[:]),
                         reads=[o2], writes=[junk, ssq])
                    k.op("dve", lambda g: g.tensor_scalar(out=ssq[:], in0=ssq[:], scalar1=1.0 / 256, scalar2=1e-6,
                                                          op0=ALU.mult, op1=ALU.add), reads=[ssq], writes=[ssq])
                    k.op("act", lambda g: g.activation(out=ssq[:], in_=ssq[:], func=AF.Sqrt), reads=[ssq], writes=[ssq])
                    k.op("dve", lambda g: g.reciprocal(out=ssq[:], in_=ssq[:]), reads=[ssq], writes=[ssq])
                    o_ = ob[cnt["o"] % 2]
                    cnt["o"] += 1
                    k.op("dve", lambda g: g.scalar_tensor_tensor(out=o_[:], in0=o2[:], scalar=ssq[:, 0:1], in1=subg[:],
                                                                 op0=ALU.mult, op1=ALU.mult), reads=[o2, ssq, subg], writes=[o_])
                    emit_T(o_, 2, 0, q0, j, j == 3)
    kT = kTs[0]
    k.dma("sp", kT[:], kd_d[:, :], reads=[kd_d], writes=[kT])
    for hq in range(2):
        qT = qTs[hq]
        k.dma("sp", qT[:], qd_d[hq], reads=[qd_d], writes=[qT])
        for ch in range(nchunks):
            q0 = ch * 512
            dense_unit(k, kT, qT, q0, vds, 129, accs, ps_s, pts, cnt, scale)
            for j in range(4):
                acc = accs[j]
                recip_den(acc, 129)
                o_ = ob[cnt["o"] % 2]
                cnt["o"] += 1
                k.op("dve", lambda g: g.tensor_scalar(out=o_[:, 0:128], in0=acc[:, 0:128], scalar1=den[:, 0:1], scalar2=None,
                                                      op0=ALU.mult), reads=[acc, den], writes=[o_])
                emit_T(o_, 1, 2 + hq, q0, j, j == 3)


def build_attn_cd(nchunks=SEQ // 512):
    nc = bass.Bass("TRN2", target_bir_lowering=False)
    with ExitStack() as st:
        k = K(nc, st)
        qc = k.dram("qc", [2, 128, SEQ], BF16, "ExternalInput")
        kc = k.dram("kc", [2, 128, SEQ], BF16, "ExternalInput")
        vc = k.dram("vc", [SEQ, 257], BF16, "ExternalInput")
        qd = k.dram("qd", [2, 128, SEQ], BF16, "ExternalInput")
        kd = k.dram("kd", [128, SEQ], BF16, "ExternalInput")
        vd = k.dram("vd", [SEQ, 129], BF16, "ExternalInput")
        lam = k.dram("lam", [4, 128], F32, "ExternalInput")
        subg = k.dram("subg", [1, 256], F32, "ExternalInput")
        identb = k.dram("identb", [128, 128], BF16, "ExternalInput")
        oT = k.dram("oT", [4, 128, SEQ], BF16, "ExternalOutput")
        phase_attn_cd(k, "e", qc, kc, vc, qd, kd, vd, lam, subg, identb, oT, nchunks=nchunks)
        k.finish("sp")
    return nc


def _run(nc, in_maps):
    res = run_bass_kernel_spmd(nc, in_maps, core_ids=list(range(NCORES)))
    return res.results


def _aug_ones(v):
    out = np.zeros(v.shape[:-1] + (v.shape[-1] + 1,), dtype=v.dtype)
    out[..., :-1] = v
    out[..., -1] = 1.0
    return out


def kernel(x, w_in_ab, sink_a, w_out_ab, w_in_cd, lam_q1, lam_k1, lam_q2, lam_k2, subln_g, q_norm_g, k_norm_g,
           w_out_cd, ln_mix_g, ln_mix_b, peer_wq, peer_keys, peer_u, peer_v, ln_ffn_g, ln_ffn_b):
    f32 = lambda a: np.ascontiguousarray(np.asarray(a, dtype=np.float32))
    x = f32(x)[0]
    ccp, ssp, cca, ssa = rope_tables()
    identb = np.eye(128, dtype=np.float32).astype(NP_BF16)
    masks = band_masks()
    pconst = peer_consts()
    cs = [slice(c * T, (c + 1) * T) for c in range(NCORES)]

    def keysT_of(l):
        return np.ascontiguousarray(f32(peer_keys)[l].reshape(16, 128, 128).transpose(2, 0, 1).reshape(128, 2048))

    w0 = f32(w_in_ab)[0]
    r = _run(build_inproj(0), [{"xT": np.ascontiguousarray(x[cs[c]].T), "w": w0, "identb": identb,
                                "ccp": ccp[cs[c]], "ssp": ssp[cs[c]]} for c in range(NCORES)])
    qkT_all = [r[c]["qkT"] for c in range(NCORES)]
    ext = host_ext_kv(qkT_all, [r[c]["vtm"] for c in range(NCORES)], [16, 17, 18, 19] + list(range(44, 68)), None)
    sink = f32(sink_a)[0].reshape(1, 16)
    RW = 16384 // NCORES
    pu, pv = f32(peer_u), f32(peer_v)
    r = _run(build_attn_ab(), [{"qT": qkT_all[c], "kTe": ext[c][0], "vaug": ext[c][1], "sink": sink, "masks": masks,
                                "identb": identb, "u0p": pu[0, c * RW:(c + 1) * RW], "v0p": pv[0, c * RW:(c + 1) * RW],
                                "u1p": pu[1, c * RW:(c + 1) * RW], "v1p": pv[1, c * RW:(c + 1) * RW]} for c in range(NCORES)])
    oT = [r[c]["oT"] for c in range(NCORES)]
    tabs16 = {nm: np.concatenate([r[c][nm + "b"] for c in range(NCORES)], axis=0) for nm in ("u0", "v0", "u1", "v1")}
    del ext, qkT_all
    g0, b0 = f32(ln_mix_g)[0:1], f32(ln_mix_b)[0:1]
    wo0 = f32(w_out_ab)[0]
    r = _run(build_outproj_ln(24, True), [{"oT": oT[c], "w": wo0, "x": x[cs[c]], "g": g0, "b": b0, "identb": identb}
                                          for c in range(NCORES)])
    x1 = [r[c]["x1"] for c in range(NCORES)]
    x1T = [r[c]["x1T"] for c in range(NCORES)]
    u0, v0 = tabs16["u0"], tabs16["v0"]
    wq0 = f32(peer_wq)[0]
    kT0 = keysT_of(0)
    gf0, bf0 = f32(ln_ffn_g)[0:1], f32(ln_ffn_b)[0:1]
    r = _run(build_peer(False), [{"x1": x1[c], "x1T": x1T[c], "wq": wq0, "keysT": kT0, "u": u0, "v": v0, "g": gf0, "b": bf0,
                                  "pconst": pconst, "identb": identb} for c in range(NCORES)])
    x2 = [r[c]["x2"] for c in range(NCORES)]
    x2T = [r[c]["x2T"] for c in range(NCORES)]
    w1 = f32(w_in_cd)[0]
    qg, kg = f32(q_norm_g)[0:1], f32(k_norm_g)[0:1]
    r = _run(build_inproj(1, x_bf16=True), [{"xT": x2T[c], "w": w1, "identb": identb, "ccp": ccp[cs[c]], "ssp": ssp[cs[c]],
                                             "cca": cca[cs[c]], "ssa": ssa[cs[c]], "qg": qg, "kg": kg} for c in range(NCORES)])
    qk = np.concatenate([r[c]["qkT"] for c in range(NCORES)], axis=2)
    vt = np.concatenate([r[c]["vtm"] for c in range(NCORES)], axis=0)
    lam = np.concatenate([f32(lam_q1)[0:1], f32(lam_k1)[0:1], f32(lam_q2)[0:1], f32(lam_k2)[0:1]], 0)
    subg = f32(subln_g)[0:1]
    maps = []
    for c in range(NCORES):
        maps.append({"qc": np.ascontiguousarray(qk[2 * c:2 * c + 2]), "kc": np.ascontiguousarray(qk[16 + 2 * c:18 + 2 * c]),
                     "vc": _aug_ones(vt[:, 256 * c:256 * c + 256]), "qd": np.ascontiguousarray(qk[32 + 2 * c:34 + 2 * c]),
                     "kd": np.ascontiguousarray(qk[48 + c // 2]),
                     "vd": _aug_ones(vt[:, 2048 + 128 * (c // 2):2048 + 128 * (c // 2) + 128]),
                     "lam": lam, "subg": subg, "identb": identb})
    r = _run(build_attn_cd(), maps)
    del qk, vt, maps
    ofull = np.zeros((32, 128, SEQ), dtype=NP_BF16)
    for c in range(NCORES):
        o = r[c]["oT"]
        ofull[2 * c] = o[0]
        ofull[2 * c + 1] = o[1]
        ofull[16 + 2 * c] = o[2]
        ofull[16 + 2 * c + 1] = o[3]
    g1, b1 = f32(ln_mix_g)[1:2], f32(ln_mix_b)[1:2]
    wo1 = f32(w_out_cd)[0]
    r = _run(build_outproj_ln(32, True), [{"oT": np.ascontiguousarray(ofull[:, :, cs[c]]), "w": wo1, "x": x2[c], "g": g1,
                                           "b": b1, "identb": identb} for c in range(NCORES)])
    x3 = [r[c]["x1"] for c in range(NCORES)]
    x3T = [r[c]["x1T"] for c in range(NCORES)]
    u1, v1 = tabs16["u1"], tabs16["v1"]
    gf1, bf1 = f32(ln_ffn_g)[1:2], f32(ln_ffn_b)[1:2]
    r = _run(build_peer(True), [{"x1": x3[c], "x1T": x3T[c], "wq": f32(peer_wq)[1], "keysT": keysT_of(1), "u": u1, "v": v1,
                                 "g": gf1, "b": bf1, "pconst": pconst, "identb": identb} for c in range(NCORES)])
    out = np.concatenate([r[c]["x2"] for c in range(NCORES)], axis=0)
    return out[None].astype(np.float32)
```

```python
import math
from contextlib import ExitStack

import numpy as np
import ml_dtypes

import concourse.bass as bass
import concourse.mybir as mybir
from concourse.bass_utils import run_bass_kernel_spmd

F32 = mybir.dt.float32
BF16 = mybir.dt.bfloat16
I32 = mybir.dt.int32
U32 = mybir.dt.uint32
AF = mybir.ActivationFunctionType
ALU = mybir.AluOpType
AX = mybir.AxisListType

NCORES = 8
SEQ = 8192
D = 4096
T = SEQ // NCORES
MT = T // 128
KT = D // 128
HD = 128
SEM_LIMIT = 2000
NP_BF16 = ml_dtypes.bfloat16
SAME_ENGINE_WAIT = {"pe": False, "dve": True, "act": True, "pool": True, "sp": False}


class Buf:
    __slots__ = ("t", "lw", "rd", "name")

    def __init__(self, t, name=""):
        self.t = t
        self.lw = None
        self.rd = []
        self.name = name

    def __getitem__(self, idx):
        return self.t[idx]


class K:
    def __init__(self, nc, stack):
        self.nc = nc
        self.stack = stack
        self.eng = {"pe": nc.tensor, "dve": nc.vector, "act": nc.scalar, "pool": nc.gpsimd, "sp": nc.sync}
        self.cur = {}
        self.seen = {}
        self.nsem = 0
        self.dma_rr = {}
        self.same_engine_wait = dict(SAME_ENGINE_WAIT)
        self.n_inst = 0
        self.pending = {}

    def sbuf(self, name, shape, dtype):
        return Buf(self.stack.enter_context(self.nc.sbuf_tensor(name, list(shape), dtype)), name)

    def psum(self, name, shape, dtype=F32):
        return Buf(self.stack.enter_context(self.nc.psum_tensor(name, list(shape), dtype)), name)

    def dram(self, name, shape, dtype, kind="Internal"):
        return Buf(self.nc.dram_tensor(name, list(shape), dtype, kind=kind).ap(), name)

    def new_sem(self, name):
        self.nsem += 1
        return self.stack.enter_context(self.nc.semaphore(f"{name}_{self.nsem}"))

    def _wait(self, e, tok):
        if tok is None:
            return
        sem, val, src = tok
        if src == e and not self.same_engine_wait[e]:
            return
        seen = self.seen.setdefault(e, {})
        kk = id(sem)
        if seen.get(kk, 0) >= val:
            return
        self.eng[e].wait_ge(sem, val)
        seen[kk] = val

    def _deps(self, e, reads, writes):
        for b in reads:
            self._wait(e, b.lw)
        for b in writes:
            self._wait(e, b.lw)
            for r in b.rd:
                self._wait(e, r)

    def _commit(self, tok, reads, writes):
        for b in reads:
            b.rd.append(tok)
            if len(b.rd) > 64:
                b.rd = b.rd[-48:]
        for b in writes:
            b.lw = tok
            b.rd = []

    def op(self, e, fn, reads=(), writes=(), sig=True):
        self._deps(e, reads, writes)
        if not sig:
            pr, pw = self.pending.setdefault(e, ([], []))
            for b in reads:
                if b not in pr:
                    pr.append(b)
            for b in writes:
                if b not in pw:
                    pw.append(b)
            fn(self.eng[e])
            self.n_inst += 1
            return None
        if e in self.pending:
            pr, pw = self.pending.pop(e)
            reads = list(reads) + [b for b in pr if b not in reads]
            writes = list(writes) + [b for b in pw if b not in writes]
        c = self.cur.get(e)
        if c is None or c[1] >= SEM_LIMIT:
            c = [self.new_sem("s" + e), 0]
            self.cur[e] = c
        ins = fn(self.eng[e])
        c[1] += 1
        ins.then_inc(c[0], 1)
        tok = (c[0], c[1], e)
        self._commit(tok, reads, writes)
        self.n_inst += 1
        return tok

    def _dma_slot(self, q):
        R = 8
        ring = self.dma_rr.setdefault(q, {"sems": [], "i": 0})
        i = ring["i"]
        ring["i"] += 1
        slot = i % R
        if len(ring["sems"]) <= slot:
            ring["sems"].append([self.new_sem("d" + q), 0, None])
        s = ring["sems"][slot]
        self._wait(q, s[2])
        if s[1] >= SEM_LIMIT:
            s[0] = self.new_sem("d" + q)
            s[1] = 0
            s[2] = None
        return s

    def dma(self, q, out, in_, reads=(), writes=(), **kw):
        s = self._dma_slot(q)
        self._deps(q, reads, writes)
        ins = self.eng[q].dma_start(out=out, in_=in_, **kw)
        s[1] += 16
        ins.then_inc(s[0], 16)
        tok = (s[0], s[1], "dma")
        s[2] = tok
        self._commit(tok, reads, writes)
        self.n_inst += 1
        return tok

    def gather(self, out, in_, idx_ap, reads=(), writes=(), **kw):
        q = "pool"
        s = self._dma_slot(q)
        self._deps(q, reads, writes)
        ins = self.eng[q].indirect_dma_start(out=out, out_offset=None, in_=in_,
                                             in_offset=bass.IndirectOffsetOnAxis(ap=idx_ap, axis=0), **kw)
        s[1] += 16
        ins.then_inc(s[0], 16)
        tok = (s[0], s[1], "dma")
        s[2] = tok
        self._commit(tok, reads, writes)
        self.n_inst += 1
        return tok

    def finish(self, e="sp"):
        for q, ring in self.dma_rr.items():
            for s in ring["sems"]:
                self._wait(e, s[2])

    def copy(self, e, out_b, out_ap, in_b, in_ap):
        if e == "act":
            return self.op("act", lambda g: g.copy(out=out_ap, in_=in_ap), reads=[in_b], writes=[out_b])
        return self.op(e, lambda g: g.tensor_copy(out=out_ap, in_=in_ap), reads=[in_b], writes=[out_b])


def bcast_free(ap, shape):
    return ap.to_broadcast(list(shape))


def load_xT(k, xT_dram, xb, stage):
    for kt in range(KT):
        s = stage[kt % len(stage)]
        k.dma("sp", s[:], xT_dram[kt * 128:(kt + 1) * 128, :], reads=[xT_dram], writes=[s])
        k.copy("act" if kt % 2 == 0 else "dve", xb[kt], xb[kt][:], s, s[:])


def stream_linear(k, xb, w_dram, N, wst, wb, ps_banks, epilogue, NCH=512, KQ=4, cast_engs=("act", "dve", "pool"),
                  state=None):
    nkt = len(xb)
    NKQ = nkt // KQ
    st = state if state is not None else {"cnt": 0, "oc": 0}
    for c in range(N // NCH):
        n0 = c * NCH
        b = c % 2
        for q in range(NKQ):
            s = wst[st["cnt"] % len(wst)]
            src = w_dram[q * KQ * 128:(q + 1) * KQ * 128, n0:n0 + NCH].rearrange("(j p) c -> p j c", p=128)
            k.dma("sp", s[:], src, reads=[w_dram], writes=[s])
            k.copy(cast_engs[st["cnt"] % len(cast_engs)], wb[b][q], wb[b][q][:], s, s[:])
            st["cnt"] += 1
        for m in range(MT):
            p = ps_banks[st["oc"] % len(ps_banks)]
            for kt in range(nkt):
                q, j = divmod(kt, KQ)
                k.op("pe", lambda g: g.matmul(p[:], lhsT=xb[kt][:, m * 128:(m + 1) * 128], rhs=wb[b][q][:, j, :],
                                              start=(kt == 0), stop=(kt == nkt - 1)),
                     reads=[xb[kt], wb[b][q]], writes=[p], sig=(kt == nkt - 1))
            epilogue(c, m, p)
            st["oc"] += 1
    return st


def rope_tile(k, src_b, src3, cc, ss, m, groups, rot_w, rb, tmp_t, tmp_u):
    H4 = 4
    w = rot_w
    ccb = cc[:, m, 0:w].unsqueeze(1).to_broadcast([128, H4, w])
    k.op("dve", lambda g: g.tensor_tensor(out=tmp_t[:, :, 0:w], in0=src3[:, :, 0:w], in1=ccb, op=ALU.mult),
         reads=[src_b, cc], writes=[tmp_t])
    first = True
    for (lo, half) in groups:
        hi = lo + half
        s_lo = ss[:, m, lo:lo + half].unsqueeze(1).to_broadcast([128, H4, half])
        s_hi = ss[:, m, hi:hi + half].unsqueeze(1).to_broadcast([128, H4, half])
        k.op("dve", lambda g: g.tensor_tensor(out=tmp_u[:, :, lo:lo + half], in0=src3[:, :, hi:hi + half], in1=s_lo,
                                              op=ALU.mult), reads=[src_b, ss], writes=[tmp_u])
        k.op("dve", lambda g: g.tensor_tensor(out=tmp_u[:, :, hi:hi + half], in0=src3[:, :, lo:lo + half], in1=s_hi,
                                              op=ALU.mult), reads=[src_b, ss], writes=[tmp_u])
    k.op("dve", lambda g: g.tensor_tensor(out=rb[:, :, 0:w], in0=tmp_t[:, :, 0:w], in1=tmp_u[:, :, 0:w], op=ALU.add),
         reads=[tmp_t, tmp_u], writes=[rb])
    if w < 128:
        k.op("act", lambda g: g.copy(out=rb[:, :, w:128], in_=src3[:, :, w:128]), reads=[src_b], writes=[rb])


def phase_inproj(k, pre, xT_dram, w_dram, N, chunk_types, tabs, gains, ident_b, qkT_dram, v_dram, res):
    xb = res["xb"]
    wst = res["wst"]
    wb = res["wb"]
    ps = res["ps_mm"]
    psT = res["ps_tr"]
    rbs = [k.sbuf(f"{pre}rb{i}", [128, 4, 128], BF16) for i in range(2)]
    tmp_t = k.sbuf(f"{pre}tt", [128, 4, 128], F32)
    tmp_u = k.sbuf(f"{pre}tu", [128, 4, 128], F32)
    qn = k.sbuf(f"{pre}qn", [128, 4, 128], F32)
    junk = k.sbuf(f"{pre}junk", [128, 128], F32)
    ssq = k.sbuf(f"{pre}ssq", [128, 4], F32)
    rstd = k.sbuf(f"{pre}rstd", [128, 4], F32)
    fmst = [k.sbuf(f"{pre}fm{i}", [128, 4, T], BF16) for i in range(2)]
    vst = [k.sbuf(f"{pre}vst{i}", [128, 512], BF16) for i in range(2)]
    cnt = {"e": 0}

    def epilogue(c, m, p):
        ty = chunk_types[c]
        i = cnt["e"]
        cnt["e"] += 1
        if ty[0] == "v":
            vs = vst[i % 2]
            k.copy("act", vs, vs[:], p, p[:])
            k.dma("sp", v_dram[m * 128:(m + 1) * 128, ty[1]:ty[1] + 512], vs[:], reads=[vs], writes=[v_dram])
            return
        rb = rbs[i % 2]
        p3 = p[:].rearrange("p (h d) -> p h d", h=4)
        if ty[0] == "rope":
            rope_tile(k, p, p3, tabs["ccp"], tabs["ssp"], m, [(0, 16)], 32, rb, tmp_t, tmp_u)
        else:
            gb = gains[ty[2]]
            for j in range(4):
                k.op("act", lambda g: g.activation(out=junk[:], in_=p[:, j * 128:(j + 1) * 128], func=AF.Square,
                                                   accum_out=ssq[:, j:j + 1]), reads=[p], writes=[junk, ssq])
            k.op("dve", lambda g: g.tensor_scalar(out=rstd[:], in0=ssq[:], scalar1=1.0 / 128, scalar2=1e-6,
                                                  op0=ALU.mult, op1=ALU.add), reads=[ssq], writes=[rstd])
            k.op("act", lambda g: g.activation(out=rstd[:], in_=rstd[:], func=AF.Sqrt), reads=[rstd], writes=[rstd])
            k.op("dve", lambda g: g.reciprocal(out=rstd[:], in_=rstd[:]), reads=[rstd], writes=[rstd])
            k.op("dve", lambda g: g.tensor_tensor(out=qn[:], in0=p3, in1=rstd[:].unsqueeze(2).to_broadcast([128, 4, 128]),
                                                  op=ALU.mult), reads=[p, rstd], writes=[qn])
            k.op("dve", lambda g: g.tensor_tensor(out=qn[:], in0=qn[:], in1=gb[:].unsqueeze(1).to_broadcast([128, 4, 128]),
                                                  op=ALU.mult), reads=[qn, gb], writes=[qn])
            rope_tile(k, qn, qn[:], tabs["cca"], tabs["ssa"], m, [(0, 32), (64, 32)], 128, rb, tmp_t, tmp_u)
        pt = psT[i % len(psT)]
        for j in range(4):
            k.op("pe", lambda g: g.transpose(out=pt[:, j, 0:128], in_=rb[:, j, :], identity=ident_b[:]),
                 reads=[rb, ident_b], writes=[pt])
        fm = fmst[c % 2]
        k.copy("act" if i % 2 else "dve", fm, fm[:, :, m * 128:(m + 1) * 128], pt, pt[:, :, 0:128])
        if m == MT - 1:
            fm0 = ty[1]
            k.dma("sp", qkT_dram[fm0:fm0 + 4].rearrange("j p t -> p j t"), fm[:], reads=[fm], writes=[qkT_dram])

    stream_linear(k, xb, w_dram, N, wst, wb, ps, epilogue, state=res["lin_state"])


def rope_tables():
    pos = np.arange(SEQ, dtype=np.float32)
    half = 16
    inv = (np.float32(500000.0) ** (-np.arange(half, dtype=np.float32) / half)).astype(np.float32)
    ang = pos[:, None] * inv[None, :]
    c, s = np.cos(ang).astype(np.float32), np.sin(ang).astype(np.float32)
    ccp = np.ones((SEQ, 128), np.float32)
    ssp = np.zeros((SEQ, 128), np.float32)
    ccp[:, 0:16] = c
    ccp[:, 16:32] = c
    ssp[:, 0:16] = -s
    ssp[:, 16:32] = s
    rows = (np.arange(SEQ) // 64).astype(np.float32)
    cols = (np.arange(SEQ) % 64).astype(np.float32)
    inv2 = (np.float32(10000.0) ** (-np.arange(32, dtype=np.float32) / 32)).astype(np.float32)
    ar = rows[:, None] * inv2[None, :]
    ac = cols[:, None] * inv2[None, :]
    cr, sr, cc_, sc = (np.cos(ar).astype(np.float32), np.sin(ar).astype(np.float32),
                       np.cos(ac).astype(np.float32), np.sin(ac).astype(np.float32))
    cca = np.concatenate([cr, cr, cc_, cc_], 1)
    ssa = np.concatenate([-sr, sr, -sc, sc], 1)
    return ccp, ssp, cca, ssa


def alloc_linear_res(k, pre):
    res = {}
    res["xb"] = [k.sbuf(f"{pre}xb{i}", [128, T], BF16) for i in range(KT)]
    res["xst"] = [k.sbuf(f"{pre}xst{i}", [128, T], F32) for i in range(2)]
    res["wst"] = [k.sbuf(f"{pre}wst{i}", [128, 4, 512], F32) for i in range(2)]
    res["wb"] = [[k.sbuf(f"{pre}wb{b}_{q}", [128, 4, 512], BF16) for q in range(KT // 4)] for b in range(2)]
    res["ps_mm"] = [k.psum(f"{pre}psmm{i}", [128, 512], F32) for i in range(4)]
    res["ps_tr"] = [k.psum(f"{pre}pstr{i}", [128, 4, 256], BF16) for i in range(2)]
    res["lin_state"] = {"cnt": 0, "oc": 0}
    return res


def load_tabs(k, pre, names, drams):
    tabs = {}
    for nm in names:
        t = k.sbuf(f"{pre}tab_{nm}", [128, MT, 128], F32)
        k.dma("sp", t[:], drams[nm][:, :].rearrange("(m p) d -> p m d", p=128), reads=[drams[nm]], writes=[t])
        tabs[nm] = t
    return tabs


AB_TYPES = ([("rope", c * 4) for c in range(4)] + [("rope", 16)] + [("v", 0)]
            + [("rope", 20 + c * 4) for c in range(6)] + [("rope", 44 + c * 4) for c in range(6)]
            + [("v", 512 + c * 512) for c in range(6)])
AB_NFM, AB_NV = 68, 3584
CD_TYPES = ([("rope", c * 4) for c in range(4)] + [("rope", 16 + c * 4) for c in range(4)]
            + [("v", c * 512) for c in range(4)]
            + [("axial", 32 + c * 4, "qg") for c in range(4)] + [("axial", 48, "kg")] + [("v", 2048)])
CD_NFM, CD_NV = 52, 2560


def build_inproj(layer, types=None, x_bf16=False):
    nc = bass.Bass("TRN2", target_bir_lowering=False)
    if types is None:
        types = AB_TYPES if layer == 0 else CD_TYPES
    N = 512 * len(types)
    nfm, nv = (AB_NFM, AB_NV) if layer == 0 else (CD_NFM, CD_NV)
    with ExitStack() as st:
        k = K(nc, st)
        xT = k.dram("xT", [D, T], BF16 if x_bf16 else F32, "ExternalInput")
        w = k.dram("w", [D, N], F32, "ExternalInput")
        identb = k.dram("identb", [128, 128], BF16, "ExternalInput")
        tabd = {nm: k.dram(nm, [T, 128], F32, "ExternalInput") for nm in (("ccp", "ssp") if layer == 0 else ("ccp", "ssp", "cca", "ssa"))}
        qkT = k.dram("qkT", [nfm, 128, T], BF16, "ExternalOutput")
        vtm = k.dram("vtm", [T, nv], BF16, "ExternalOutput")
        res = alloc_linear_res(k, "a")
        ident_b = k.sbuf("identb_s", [128, 128], BF16)
        k.dma("sp", ident_b[:], identb[:, :], reads=[identb], writes=[ident_b])
        tabs = load_tabs(k, "a", list(tabd.keys()), tabd)
        gains = {}
        if layer == 1:
            for nm in ("qg", "kg"):
                gd = k.dram(nm, [1, 128], F32, "ExternalInput")
                gs = k.sbuf("g_" + nm, [128, 128], F32)
                k.dma("sp", gs[:], gd[0:1, :].to_broadcast([128, 128]), reads=[gd], writes=[gs])
                gains[nm] = gs
        if x_bf16:
            for kt in range(KT):
                k.dma("sp", res["xb"][kt][:], xT[kt * 128:(kt + 1) * 128, :], reads=[xT], writes=[res["xb"][kt]])
        else:
            load_xT(k, xT, res["xb"], res["xst"])
        phase_inproj(k, "a", xT, w, N, types, tabs, gains, ident_b, qkT, vtm, res)
        k.finish("sp")
    return nc


HALO = 1024
TE = T + 2 * HALO
NTE = TE // 128
B_DIL = (1, 4, 16)
B_DT = (1, 2, 8)
MASK_OFF = {"A": 0, 0: 3, 1: 6, 2: 11}
N_MASKS = 28


def band_masks():
    m = np.zeros((128, N_MASKS, 128), np.float32)
    kl = np.arange(128)[:, None]
    ql = np.arange(128)[None, :]
    for j, dt in enumerate((-1, 0, 1)):
        diff = (ql - kl) - dt * 128
        m[:, MASK_OFF["A"] + j, :] = (np.abs(diff) <= 128)
    for g in range(3):
        d, r = B_DIL[g], B_DT[g]
        for j, dt in enumerate(range(-r, r + 1)):
            diff = (ql - kl) - dt * 128
            m[:, MASK_OFF[g] + j, :] = (np.abs(diff) <= 64 * d) & (diff % d == 0)
    return m.astype(NP_BF16)


def attn_unit(k, qT_ap, qT_b, kT_b, v_b, ktiles, mask_b, mask0, acc, first, last, ps_s, pts, cnt, scale):
    n = len(ktiles)
    i0 = 0
    while i0 < n:
        nt = min(4, n - i0)
        ps = ps_s[cnt["s"] % len(ps_s)]
        pt = pts[cnt["s"] % len(pts)]
        cnt["s"] += 1
        for j in range(nt):
            lt = ktiles[i0 + j]
            k.op("pe", lambda g: g.matmul(ps[:, j, :], lhsT=kT_b[:, lt * 128:(lt + 1) * 128], rhs=qT_ap,
                                          start=True, stop=True), reads=[kT_b, qT_b], writes=[ps])
        k.op("act", lambda g: g.activation(out=pt[:, 0:nt, :], in_=ps[:, 0:nt, :], func=AF.Exp, scale=scale),
             reads=[ps], writes=[pt])
        k.op("dve", lambda g: g.tensor_tensor(out=pt[:, 0:nt, :], in0=pt[:, 0:nt, :],
                                              in1=mask_b[:, mask0 + i0:mask0 + i0 + nt, :], op=ALU.mult),
             reads=[pt, mask_b], writes=[pt])
        for j in range(nt):
            lt = ktiles[i0 + j]
            st_ = first and (i0 + j == 0)
            sp_ = last and (i0 + j == n - 1)
            k.op("pe", lambda g: g.matmul(acc[:, 0:129], lhsT=pt[:, j, :], rhs=v_b[:, lt, :], start=st_, stop=sp_),
                 reads=[pt, v_b], writes=[acc])
        i0 += nt


def phase_attn_ab(k, pre, qT_d, kTe_d, vaug_d, sink_d, mask_d, identb_d, oT_d, bg=None):
    scale = 1.0 / math.sqrt(128.0)
    mask_b = k.sbuf(f"{pre}mask", [128, N_MASKS, 128], BF16)
    k.dma("sp", mask_b[:], mask_d[:, :, :], reads=[mask_d], writes=[mask_b])
    ident_b = k.sbuf(f"{pre}ident", [128, 128], BF16)
    k.dma("sp", ident_b[:], identb_d[:, :], reads=[identb_d], writes=[ident_b])
    esink = k.sbuf(f"{pre}esink", [128, 16], F32)
    k.dma("sp", esink[:], sink_d[0:1, :].to_broadcast([128, 16]), reads=[sink_d], writes=[esink])
    k.op("act", lambda g: g.activation(out=esink[:], in_=esink[:], func=AF.Exp), reads=[esink], writes=[esink])
    NKT = 46
    kbuf = [k.sbuf(f"{pre}kb{i}", [128, NKT * 128], BF16) for i in range(2)]
    vbuf = [k.sbuf(f"{pre}vb{i}", [128, NKT, 129], BF16) for i in range(2)]
    qbuf = [k.sbuf(f"{pre}qb{i}", [128, 4, T], BF16) for i in range(2)]
    ps_s = [k.psum(f"{pre}pss{i}", [128, 4, 128], F32) for i in range(3)]
    pts = [k.sbuf(f"{pre}pt{i}", [128, 4, 128], BF16) for i in range(3)]
    accs = [k.psum(f"{pre}acc{i}", [128, 512], F32) for i in range(2)]
    ps_t = [k.psum(f"{pre}pst{i}", [128, 1024], BF16) for i in range(2)]
    den = k.sbuf(f"{pre}den", [128, 1], F32)
    ob = [k.sbuf(f"{pre}ob{i}", [128, 128], BF16) for i in range(2)]
    oTs = [k.sbuf(f"{pre}oTs{i}", [128, T], BF16) for i in range(2)]
    cnt = {"s": 0, "a": 0, "o": 0}

    def finalize(acc, sink_col, ohead, m):
        if sink_col is not None:
            k.op("dve", lambda g: g.tensor_tensor(out=den[:], in0=acc[:, 128:129], in1=esink[:, sink_col:sink_col + 1],
                                                  op=ALU.add), reads=[acc, esink], writes=[den])
        else:
            k.op("dve", lambda g: g.tensor_copy(out=den[:], in_=acc[:, 128:129]), reads=[acc], writes=[den])
        k.op("dve", lambda g: g.reciprocal(out=den[:], in_=den[:]), reads=[den], writes=[den])
        o = ob[cnt["o"] % 2]
        k.op("dve", lambda g: g.tensor_scalar(out=o[:], in0=acc[:, 0:128], scalar1=den[:, 0:1], scalar2=None,
                                              op0=ALU.mult), reads=[acc, den], writes=[o])
        pt_ = ps_t[cnt["o"] % 2]
        k.op("pe", lambda g: g.transpose(out=pt_[:, 0:128], in_=o[:], identity=ident_b[:]),
             reads=[o, ident_b], writes=[pt_])
        oT = oTs[ohead % 2]
        k.copy("act", oT, oT[:, m * 128:(m + 1) * 128], pt_, pt_[:, 0:128])
        cnt["o"] += 1
        if m == MT - 1:
            k.dma("sp", oT_d[ohead], oT[:], reads=[oT], writes=[oT_d])

    job = 0
    for kvh in range(4):
        kb_, vb_, qb_ = kbuf[job % 2], vbuf[job % 2], qbuf[job % 2]
        job += 1
        k.dma("sp", kb_[:, 0:10 * 128], kTe_d[kvh, :, 7 * 128:17 * 128], reads=[kTe_d], writes=[kb_])
        k.dma("sp", vb_[:, 0:10, :], vaug_d[7 * 128:17 * 128, kvh, :].rearrange("(j p) c -> p j c", p=128),
              reads=[vaug_d], writes=[vb_])
        k.dma("sp", qb_[:], qT_d[kvh * 4:kvh * 4 + 4].rearrange("j p t -> p j t"), reads=[qT_d], writes=[qb_])
        for g4 in range(4):
            h = kvh * 4 + g4
            for m in range(MT):
                acc = accs[cnt["a"] % 2]
                cnt["a"] += 1
                attn_unit(k, qb_[:, g4, m * 128:(m + 1) * 128], qb_, kb_, vb_, [m, m + 1, m + 2], mask_b,
                          MASK_OFF["A"], acc, True, True, ps_s, pts, cnt, scale)
                finalize(acc, h, h, m)
            if bg is not None:
                bg()
    base = [0, 10, 22]
    lo_ext = [7, 6, 0]
    nload = [10, 12, 24]
    for h in range(8):
        kb_, vb_, qb_ = kbuf[job % 2], vbuf[job % 2], qbuf[job % 2]
        job += 1
        for g in range(3):
            kblk = 4 + g * 8 + h
            k.dma("sp", kb_[:, base[g] * 128:(base[g] + nload[g]) * 128],
                  kTe_d[kblk, :, lo_ext[g] * 128:(lo_ext[g] + nload[g]) * 128], reads=[kTe_d], writes=[kb_])
            k.dma("sp", vb_[:, base[g]:base[g] + nload[g], :],
                  vaug_d[lo_ext[g] * 128:(lo_ext[g] + nload[g]) * 128, kblk, :].rearrange("(j p) c -> p j c", p=128),
                  reads=[vaug_d], writes=[vb_])
            k.dma("sp", qb_[:, g, :], qT_d[20 + g * 8 + h], reads=[qT_d], writes=[qb_])
        for m in range(MT):
            acc = accs[cnt["a"] % 2]
            cnt["a"] += 1
            for g in range(3):
                r = B_DT[g]
                tiles = [base[g] + (8 + m + dt) - lo_ext[g] for dt in range(-r, r + 1)]
                attn_unit(k, qb_[:, g, m * 128:(m + 1) * 128], qb_, kb_, vb_, tiles, mask_b, MASK_OFF[g], acc,
                          g == 0, g == 2, ps_s, pts, cnt, scale)
            finalize(acc, None, 16 + h, m)
            if bg is not None:
                bg()


def build_attn_ab():
    nc = bass.Bass("TRN2", target_bir_lowering=False)
    with ExitStack() as st:
        k = K(nc, st)
        qT = k.dram("qT", [AB_NFM, 128, T], BF16, "ExternalInput")
        kTe = k.dram("kTe", [28, 128, TE], BF16, "ExternalInput")
        vaug = k.dram("vaug", [TE, 28, 129], BF16, "ExternalInput")
        sink = k.dram("sink", [1, 16], F32, "ExternalInput")
        masks = k.dram("masks", [128, N_MASKS, 128], BF16, "ExternalInput")
        identb = k.dram("identb", [128, 128], BF16, "ExternalInput")
        oT = k.dram("oT", [24, 128, T], BF16, "ExternalOutput")
        RW = 16384 // NCORES
        pairs = []
        for nm in ("u0", "v0", "u1", "v1"):
            src = k.dram(nm + "p", [RW, D], F32, "ExternalInput")
            dst = k.dram(nm + "b", [RW, D], BF16, "ExternalOutput")
            pairs.append((src, dst))
        bg = cast_rows_bg(k, "cv", pairs, RW)
        phase_attn_ab(k, "b", qT, kTe, vaug, sink, masks, identb, oT, bg=bg)
        while bg():
            pass
        k.finish("sp")
    return nc


def host_ext_kv(qkT_all, vtm_all, kblocks, vheads_cols):
    kfull = np.concatenate([q[kblocks] for q in qkT_all], axis=2)
    vfull = np.concatenate(vtm_all, axis=0)
    nk = len(kblocks)
    kpad = np.zeros((nk, 128, SEQ + 2 * HALO), dtype=kfull.dtype)
    kpad[:, :, HALO:HALO + SEQ] = kfull
    nvh = vfull.shape[1] // 128
    vpad = np.zeros((SEQ + 2 * HALO, nvh, 129), dtype=vfull.dtype)
    vpad[HALO:HALO + SEQ, :, 0:128] = vfull.reshape(SEQ, nvh, 128)
    vpad[HALO:HALO + SEQ, :, 128] = 1.0
    outs = []
    for c in range(NCORES):
        outs.append((np.ascontiguousarray(kpad[:, :, c * T:c * T + TE]), np.ascontiguousarray(vpad[c * T:c * T + TE])))
    return outs


ALPHA = float((2 * 2) ** 0.25)


def barrier(k):
    toks = []
    for e, c in k.cur.items():
        if c[1] > 0:
            toks.append((c[0], c[1], e))
    for q, ring in k.dma_rr.items():
        for s in ring["sems"]:
            if s[2] is not None:
                toks.append(s[2])
    for e in ("pe", "dve", "act", "pool", "sp"):
        for tk in toks:
            if tk[2] == e:
                continue
            k._wait(e, tk)


def phase_outproj(k, pre, oT_d, nkt, w_d, x_d, z_d, res):
    xb = res["xb"][:nkt]
    for kt in range(nkt):
        k.dma("sp", xb[kt][:], oT_d[kt], reads=[oT_d], writes=[xb[kt]])
    xin = [k.sbuf(f"{pre}xin{i}", [128, 512], F32) for i in range(3)]
    zt = [k.sbuf(f"{pre}zt{i}", [128, 512], F32) for i in range(3)]
    cnt = {"e": 0}

    def epilogue(c, m, p):
        i = cnt["e"]
        cnt["e"] += 1
        xi, zo = xin[i % 3], zt[i % 3]
        k.dma("sp", xi[:], x_d[m * 128:(m + 1) * 128, c * 512:(c + 1) * 512], reads=[x_d], writes=[xi])
        k.op("dve", lambda g: g.scalar_tensor_tensor(out=zo[:], in0=xi[:], scalar=ALPHA, in1=p[:], op0=ALU.mult,
                                                     op1=ALU.add), reads=[xi, p], writes=[zo])
        k.dma("sp", z_d[m * 128:(m + 1) * 128, c * 512:(c + 1) * 512], zo[:], reads=[zo], writes=[z_d])

    stream_linear(k, xb, w_d, D, res["wst"], res["wb"], res["ps_mm"], epilogue, state=res["lin_state"])


def phase_ln(k, pre, z_d, g_d, b_d, out_d, outT_d, identb_d, ps_tr, mt=MT):
    gt = k.sbuf(f"{pre}g", [128, D], F32)
    bt = k.sbuf(f"{pre}b", [128, D], F32)
    k.dma("sp", gt[:], g_d[0:1, :].to_broadcast([128, D]), reads=[g_d], writes=[gt])
    k.dma("sp", bt[:], b_d[0:1, :].to_broadcast([128, D]), reads=[b_d], writes=[bt])
    ident_b = k.sbuf(f"{pre}ident", [128, 128], BF16)
    k.dma("sp", ident_b[:], identb_d[:, :], reads=[identb_d], writes=[ident_b])
    zs = [k.sbuf(f"{pre}z{i}", [128, D], F32) for i in range(2)]
    os_ = [k.sbuf(f"{pre}o{i}", [128, D], F32) for i in range(2)]
    ob16 = k.sbuf(f"{pre}o16", [128, D], BF16)
    stg = [k.sbuf(f"{pre}stg{i}", [128, KT, 128], BF16) for i in range(2)]
    stats = k.sbuf(f"{pre}stats", [128, 8, 6], F32)
    mv = k.sbuf(f"{pre}mv", [128, 2], F32)
    rstd = k.sbuf(f"{pre}rstd", [128, 1], F32)
    for m in range(mt):
        z, o = zs[m % 2], os_[m % 2]
        k.dma("sp", z[:], z_d[m * 128:(m + 1) * 128, :], reads=[z_d], writes=[z])
        for c in range(8):
            k.op("dve", lambda g: g.bn_stats(out=stats[:, c, :], in_=z[:, c * 512:(c + 1) * 512]), reads=[z], writes=[stats])
        k.op("dve", lambda g: g.bn_aggr(out=mv[:], in_=stats[:].rearrange("p a b -> p (a b)")), reads=[stats], writes=[mv])
        k.op("dve", lambda g: g.tensor_scalar(out=rstd[:], in0=mv[:, 1:2], scalar1=1e-5, scalar2=None, op0=ALU.add),
             reads=[mv], writes=[rstd])
        k.op("act", lambda g: g.activation(out=rstd[:], in_=rstd[:], func=AF.Sqrt), reads=[rstd], writes=[rstd])
        k.op("dve", lambda g: g.reciprocal(out=rstd[:], in_=rstd[:]), reads=[rstd], writes=[rstd])
        k.op("dve", lambda g: g.tensor_scalar(out=o[:], in0=z[:], scalar1=mv[:, 0:1], scalar2=rstd[:, 0:1],
                                              op0=ALU.subtract, op1=ALU.mult), reads=[z, mv, rstd], writes=[o])
        k.op("pool", lambda g: g.tensor_tensor(out=o[:], in0=o[:], in1=gt[:], op=ALU.mult), reads=[o, gt], writes=[o])
        k.op("pool", lambda g: g.tensor_tensor(out=o[:], in0=o[:], in1=bt[:], op=ALU.add), reads=[o, bt], writes=[o])
        k.dma("sp", out_d[m * 128:(m + 1) * 128, :], o[:], reads=[o], writes=[out_d])
        if outT_d is not None:
            k.op("act", lambda g: g.copy(out=ob16[:], in_=o[:]), reads=[o], writes=[ob16])
            sg = stg[m % 2]
            for q in range(4):
                pt = ps_tr[q % len(ps_tr)]
                for j in range(8):
                    kt = q * 8 + j
                    k.op("pe", lambda g: g.transpose(out=pt[:, j, :], in_=ob16[:, kt * 128:(kt + 1) * 128],
                                                     identity=ident_b[:]), reads=[ob16, ident_b], writes=[pt])
                k.copy("act" if q % 2 else "dve", sg, sg[:, q * 8:(q + 1) * 8, :], pt, pt[:])
            k.dma("sp", outT_d[:, m * 128:(m + 1) * 128].rearrange("(kt p) t -> p kt t", p=128), sg[:],
                  reads=[sg], writes=[outT_d])


def build_outproj_ln(nkt, with_T):
    nc = bass.Bass("TRN2", target_bir_lowering=False)
    with ExitStack() as st:
        k = K(nc, st)
        oT = k.dram("oT", [nkt, 128, T], BF16, "ExternalInput")
        w = k.dram("w", [nkt * 128, D], F32, "ExternalInput")
        x = k.dram("x", [T, D], F32, "ExternalInput")
        g = k.dram("g", [1, D], F32, "ExternalInput")
        b = k.dram("b", [1, D], F32, "ExternalInput")
        identb = k.dram("identb", [128, 128], BF16, "ExternalInput")
        z = k.dram("z", [T, D], F32, "Internal")
        x1 = k.dram("x1", [T, D], F32, "ExternalOutput")
        x1T = k.dram("x1T", [D, T], BF16, "ExternalOutput") if with_T else None
        with ExitStack() as ph:
            k.stack = ph
            res = alloc_linear_res(k, "c")
            phase_outproj(k, "c", oT, nkt, w, x, z, res)
            barrier(k)
        k.stack = st
        with ExitStack() as ph:
            k.stack = ph
            ps_tr = [k.psum(f"lnpt{i}", [128, 8, 128], BF16) for i in range(2)]
            phase_ln(k, "l", z, g, b, x1, x1T, identb, ps_tr)
            barrier(k)
        k.stack = st
        k.finish("sp")
    return nc


NEG = -1.0e30


def phase_peer_scores(k, pre, x1T_d, wq_d, keysT_d, sc_d, xb):
    for kt in range(KT):
        k.dma("sp", xb[kt][:], x1T_d[kt * 128:(kt + 1) * 128, :], reads=[x1T_d], writes=[xb[kt]])
    keysT = k.sbuf(f"{pre}keysT", [128, 2048], F32)
    k.dma("sp", keysT[:], keysT_d[:, :], reads=[keysT_d], writes=[keysT])
    wst = [k.sbuf(f"{pre}wst{i}", [128, KT, 128], F32) for i in range(2)]
    wqb = [k.sbuf(f"{pre}wqb{i}", [128, KT, 128], BF16) for i in range(2)]
    qg = [k.sbuf(f"{pre}qg{i}", [128, T], F32) for i in range(2)]
    scst = [k.sbuf(f"{pre}scst{i}", [128, MT, 128], F32) for i in range(2)]
    ps = [k.psum(f"{pre}ps{i}", [128, 512], F32) for i in range(4)]
    ps2 = [k.psum(f"{pre}ps2{i}", [128, 512], F32) for i in range(2)]
    for g in range(16):
        ws, wb_, q_, sc_ = wst[g % 2], wqb[g % 2], qg[g % 2], scst[g % 2]
        k.dma("sp", ws[:], wq_d[:, g * 128:(g + 1) * 128].rearrange("(kt p) c -> p kt c", p=128), reads=[wq_d], writes=[ws])
        k.copy(("act", "dve")[g % 2], wb_, wb_[:], ws, ws[:])
        for tc in range(T // 512):
            p = ps[(g * 2 + tc) % 4]
            for kt in range(KT):
                k.op("pe", lambda e: e.matmul(p[:], lhsT=wb_[:, kt, :], rhs=xb[kt][:, tc * 512:(tc + 1) * 512],
                                              start=(kt == 0), stop=(kt == KT - 1)), reads=[wb_, xb[kt]], writes=[p])
            k.copy(("dve", "act")[tc % 2], q_, q_[:, tc * 512:(tc + 1) * 512], p, p[:])
        for m in range(MT):
            p2 = ps2[m % 2]
            k.op("pe", lambda e: e.matmul(p2[:, 0:128], lhsT=q_[:, m * 128:(m + 1) * 128], rhs=keysT[:, g * 128:(g + 1) * 128],
                                          start=True, stop=True), reads=[q_, keysT], writes=[p2])
            k.copy(("act", "dve")[m % 2], sc_, sc_[:, m, :], p2, p2[:, 0:128])
        k.dma("sp", sc_d[:, g * 128:(g + 1) * 128].rearrange("(m p) n -> p m n", p=128), sc_[:], reads=[sc_], writes=[sc_d])


def cast_rows_bg(k, pre, srcs_dsts, nrows, nbuf=2):
    st = [k.sbuf(f"{pre}st{i}", [128, D], F32) for i in range(nbuf)]
    cb = [k.sbuf(f"{pre}cb{i}", [128, D], BF16) for i in range(nbuf)]
    jobs = [(src, dst, i) for (src, dst) in srcs_dsts for i in range(nrows // 128)]
    state = {"n": 0}

    def step():
        n = state["n"]
        if n >= len(jobs):
            return False
        state["n"] += 1
        src, dst, i = jobs[n]
        s_, c_ = st[n % nbuf], cb[n % nbuf]
        k.dma("pool", s_[:], src[i * 128:(i + 1) * 128, :], reads=[src], writes=[s_])
        k.copy("pool", c_, c_[:], s_, s_[:])
        k.dma("pool", dst[i * 128:(i + 1) * 128, :], c_[:], reads=[c_], writes=[dst])
        return True
    return step


def phase_cast_rows(k, pre, srcs_dsts, nrows=16384):
    st = [k.sbuf(f"{pre}st{i}", [128, D], F32) for i in range(3)]
    cb = [k.sbuf(f"{pre}cb{i}", [128, D], BF16) for i in range(3)]
    jobs = [(src, dst, i) for (src, dst) in srcs_dsts for i in range(nrows // 128)]
    engs = ("act", "dve", "pool")

    def load(n):
        src, dst, i = jobs[n]
        k.dma("sp", st[n % 3][:], src[i * 128:(i + 1) * 128, :], reads=[src], writes=[st[n % 3]])

    for n in range(min(2, len(jobs))):
        load(n)
    for n in range(len(jobs)):
        if n + 2 < len(jobs):
            load(n + 2)
        src, dst, i = jobs[n]
        k.copy(engs[n % 3], cb[n % 3], cb[n % 3][:], st[n % 3], st[n % 3][:])
        k.dma("act", dst[i * 128:(i + 1) * 128, :], cb[n % 3][:], reads=[cb[n % 3]], writes=[dst])


def phase_peer_main(k, pre, x1_d, x1T_d, sc_d, u_d, v_d, pconst_d, identb_d, z_d, NG=4, mt=MT):
    ident_b = k.sbuf(f"{pre}ident", [128, 128], BF16)
    k.dma("sp", ident_b[:], identb_d[:, :], reads=[identb_d], writes=[ident_b])
    pc = k.sbuf(f"{pre}pc", [128, 48], F32)
    k.dma("sp", pc[:], pconst_d[:, :], reads=[pconst_d], writes=[pc])
    iota16, lo16, hi16 = pc[:, 0:16], pc[:, 16:32], pc[:, 32:48]
    sc = k.sbuf(f"{pre}sc", [128, 16, 128], F32)
    scr = k.sbuf(f"{pre}scr", [128, 16, 128], F32)
    sv = k.sbuf(f"{pre}sv", [128, 16, 16], F32)
    si = k.sbuf(f"{pre}si", [128, 16, 16], U32)
    sif = k.sbuf(f"{pre}sif", [128, 16, 16], F32)
    cand = k.sbuf(f"{pre}cand", [128, 8, 256], F32)
    cscr = k.sbuf(f"{pre}cscr", [128, 8, 256], F32)
    ts = k.sbuf(f"{pre}ts", [128, 8, 16], F32)
    tp = k.sbuf(f"{pre}tp", [128, 8, 16], U32)
    tpf = k.sbuf(f"{pre}tpf", [128, 8, 16], F32)
    w4a = k.sbuf(f"{pre}w4a", [128, 8, 16, 16], F32)
    w4b = k.sbuf(f"{pre}w4b", [128, 8, 16, 16], F32)
    r3a = k.sbuf(f"{pre}r3a", [128, 8, 16], F32)
    r3b = k.sbuf(f"{pre}r3b", [128, 8, 16], F32)
    r3c = k.sbuf(f"{pre}r3c", [128, 8, 16], F32)
    eidx_f = k.sbuf(f"{pre}eidxf", [128, 128], F32)
    eidx = k.sbuf(f"{pre}eidx", [128, 128], U32)
    gate = k.sbuf(f"{pre}gate", [128, 8, 16], F32)
    zsum = k.sbuf(f"{pre}zsum", [128, 8], F32)
    hids = [k.sbuf(f"{pre}hid{i}", [128, 128], F32) for i in range(2)]
    g1 = k.sbuf(f"{pre}g1", [128, 128], F32)
    wgts = [k.sbuf(f"{pre}wgt{i}", [128, 128], F32) for i in range(2)]
    xts = [k.sbuf(f"{pre}xt{i}", [128, D], F32) for i in range(2)]
    junk = k.sbuf(f"{pre}junk", [128, D], BF16)
    ug = [k.sbuf(f"{pre}ug{i}", [128, D], BF16) for i in range(NG)]
    vg = [k.sbuf(f"{pre}vg{i}", [128, D // 2], BF16) for i in range(NG)]
    dgs = [k.sbuf(f"{pre}dg{i}", [128, 128], BF16) for i in range(8)]
    zo = [k.sbuf(f"{pre}zo{i}", [128, 512], F32) for i in range(2)]
    xps = k.psum(f"{pre}xps", [128, D], BF16)
    acc = [k.psum(f"{pre}acc{i}", [128, 512], F32) for i in range(4)]

    def dve(fn, reads, writes):
        return k.op("dve", fn, reads=reads, writes=writes)

    eidx_t = [k.sbuf(f"{pre}eidx_t{i}", [128, 128], U32) for i in range(mt)]
    gate_t = [k.sbuf(f"{pre}gate_t{i}", [128, 128], F32) for i in range(mt)]
    evidx_t = [[k.sbuf(f"{pre}evidx_t{i}_{h}", [128, 128], U32) for h in range(2)] for i in range(mt)]

    def route(m):
        k.dma("sp", sc[:], sc_d[m * 128:(m + 1) * 128, :].rearrange("p (g n) -> p g n", g=16), reads=[sc_d], writes=[sc])
        for g in range(16):
            yield ('dve', lambda e: e.max(out=sv[:, g, 0:8], in_=sc[:, g, :]), [sc], [sv])
            yield ('dve', lambda e: e.max_index(out=si[:, g, 0:8], in_max=sv[:, g, 0:8], in_values=sc[:, g, :]), [sc, sv], [si])
            yield ('dve', lambda e: e.match_replace(out=scr[:, g, :], in_to_replace=sv[:, g, 0:8], in_values=sc[:, g, :],
                                          imm_value=NEG), [sc, sv], [scr])
            yield ('dve', lambda e: e.max(out=sv[:, g, 8:16], in_=scr[:, g, :]), [scr], [sv])
            yield ('dve', lambda e: e.max_index(out=si[:, g, 8:16], in_max=sv[:, g, 8:16], in_values=scr[:, g, :]), [scr, sv], [si])
        yield ('dve', lambda e: e.tensor_copy(out=sif[:], in_=si[:]), [si], [sif])
        sv4 = sv[:].rearrange("p (h c) r -> p h c r", c=2)
        sif4 = sif[:].rearrange("p (h c) r -> p h c r", c=2)
        c4 = cand[:].rearrange("p h (a b) -> p h a b", a=16)
        yield ('dve', lambda e: e.tensor_tensor(out=c4, in0=sv4[:, :, 0, :].unsqueeze(3).to_broadcast([128, 8, 16, 16]),
                                      in1=sv4[:, :, 1, :].unsqueeze(2).to_broadcast([128, 8, 16, 16]), op=ALU.add),
            [sv], [cand])
        for h in range(8):
            yield ('dve', lambda e: e.max(out=ts[:, h, 0:8], in_=cand[:, h, :]), [cand], [ts])
            yield ('dve', lambda e: e.max_index(out=tp[:, h, 0:8], in_max=ts[:, h, 0:8], in_values=cand[:, h, :]), [cand, ts], [tp])
            yield ('dve', lambda e: e.match_replace(out=cscr[:, h, :], in_to_replace=ts[:, h, 0:8], in_values=cand[:, h, :],
                                          imm_value=NEG), [cand, ts], [cscr])
            yield ('dve', lambda e: e.max(out=ts[:, h, 8:16], in_=cscr[:, h, :]), [cscr], [ts])
            yield ('dve', lambda e: e.max_index(out=tp[:, h, 8:16], in_max=ts[:, h, 8:16], in_values=cscr[:, h, :]), [cscr, ts], [tp])
        yield ('dve', lambda e: e.tensor_copy(out=tpf[:], in_=tp[:]), [tp], [tpf])
        tpf4 = tpf[:].unsqueeze(3).to_broadcast([128, 8, 16, 16])
        lo4 = lo16.unsqueeze(1).unsqueeze(1).to_broadcast([128, 8, 16, 16])
        hi4 = hi16.unsqueeze(1).unsqueeze(1).to_broadcast([128, 8, 16, 16])
        io4 = iota16.unsqueeze(1).unsqueeze(1).to_broadcast([128, 8, 16, 16])
        yield ('dve', lambda e: e.tensor_tensor(out=w4a[:], in0=tpf4, in1=lo4, op=ALU.is_ge), [tpf, pc], [w4a])
        yield ('dve', lambda e: e.tensor_tensor(out=w4b[:], in0=tpf4, in1=hi4, op=ALU.is_ge), [tpf, pc], [w4b])
        yield ('dve', lambda e: e.tensor_tensor(out=w4a[:], in0=w4a[:], in1=w4b[:], op=ALU.subtract), [w4a, w4b], [w4a])
        yield ('dve', lambda e: e.tensor_tensor(out=w4b[:], in0=w4a[:], in1=sif4[:, :, 0, :].unsqueeze(2).to_broadcast([128, 8, 16, 16]),
                                      op=ALU.mult), [w4a, sif], [w4b])
        yield ('dve', lambda e: e.tensor_reduce(out=r3a[:], in_=w4b[:], axis=AX.X, op=ALU.add), [w4b], [r3a])
        yield ('dve', lambda e: e.tensor_tensor(out=w4b[:], in0=w4a[:], in1=lo4, op=ALU.mult), [w4a, pc], [w4b])
        yield ('dve', lambda e: e.tensor_reduce(out=r3b[:], in_=w4b[:], axis=AX.X, op=ALU.add), [w4b], [r3b])
        yield ('dve', lambda e: e.tensor_tensor(out=r3b[:], in0=tpf[:], in1=r3b[:], op=ALU.subtract), [tpf, r3b], [r3b])
        yield ('dve', lambda e: e.tensor_tensor(out=w4a[:], in0=r3b[:].unsqueeze(3).to_broadcast([128, 8, 16, 16]), in1=io4,
                                      op=ALU.is_equal), [r3b, pc], [w4a])
        yield ('dve', lambda e: e.tensor_tensor(out=w4b[:], in0=w4a[:], in1=sif4[:, :, 1, :].unsqueeze(2).to_broadcast([128, 8, 16, 16]),
                                      op=ALU.mult), [w4a, sif], [w4b])
        yield ('dve', lambda e: e.tensor_reduce(out=r3c[:], in_=w4b[:], axis=AX.X, op=ALU.add), [w4b], [r3c])
        ef3 = eidx_f[:].rearrange("p (h r) -> p h r", h=8)
        yield ('dve', lambda e: e.scalar_tensor_tensor(out=ef3, in0=r3a[:], scalar=128.0, in1=r3c[:], op0=ALU.mult, op1=ALU.add),
            [r3a, r3c], [eidx_f])
        yield ('dve', lambda e: e.tensor_copy(out=eidx[:], in_=eidx_f[:]), [eidx_f], [eidx])
        yield ('dve', lambda e: e.tensor_tensor(out=gate[:], in0=ts[:], in1=ts[:, :, 0:1].to_broadcast([128, 8, 16]), op=ALU.subtract),
            [ts], [gate])
        yield ('act', lambda e: e.activation(out=gate[:], in_=gate[:], func=AF.Exp), [gate], [gate])
        yield ('dve', lambda e: e.tensor_reduce(out=zsum[:], in_=gate[:], axis=AX.X, op=ALU.add), [gate], [zsum])
        yield ('dve', lambda e: e.reciprocal(out=zsum[:], in_=zsum[:]), [zsum], [zsum])
        yield ('dve', lambda e: e.tensor_tensor(out=gate[:], in0=gate[:], in1=zsum[:].unsqueeze(2).to_broadcast([128, 8, 16]), op=ALU.mult),
            [gate, zsum], [gate])
        yield ('dve', lambda e: e.tensor_copy(out=eidx_t[m][:], in_=eidx[:]), [eidx], [eidx_t[m]])
        yield ('dve', lambda e: e.tensor_scalar(out=eidx_f[:], in0=eidx_f[:], scalar1=2.0, scalar2=None, op0=ALU.mult), [eidx_f], [eidx_f])
        yield ('dve', lambda e: e.tensor_copy(out=evidx_t[m][0][:], in_=eidx_f[:]), [eidx_f], [evidx_t[m][0]])
        yield ('dve', lambda e: e.tensor_scalar(out=eidx_f[:], in0=eidx_f[:], scalar1=1.0, scalar2=None, op0=ALU.add), [eidx_f], [eidx_f])
        yield ('dve', lambda e: e.tensor_copy(out=evidx_t[m][1][:], in_=eidx_f[:]), [eidx_f], [evidx_t[m][1]])
        yield ('dve', lambda e: e.tensor_copy(out=gate_t[m][:], in_=gate[:].rearrange("p h r -> p (h r)")), [gate], [gate_t[m]])


    def advance(gen, n=1):
        for _ in range(n):
            it = next(gen, None)
            if it is None:
                return False
            k.op(it[0], it[1], reads=it[2], writes=it[3])
        return True

    xTt = [k.sbuf(f"{pre}xTt{i}", [128, KT, 128], BF16) for i in range(2)]

    def load_x(m):
        xt = xts[m % 2]
        k.dma("sp", xt[:], x1_d[m * 128:(m + 1) * 128, :], reads=[x1_d], writes=[xt])
        xT_ = xTt[m % 2]
        k.dma("sp", xT_[:], x1T_d[:, m * 128:(m + 1) * 128].rearrange("(kt p) t -> p kt t", p=128), reads=[x1T_d], writes=[xT_])
        for kt in range(KT):
            k.op("pe", lambda e: e.transpose(out=xps[:, kt * 128:(kt + 1) * 128], in_=xT_[:, kt, :], identity=ident_b[:]),
                 reads=[xT_, ident_b], writes=[xps], sig=(kt == KT - 1))

    def u_slot(m, s):
        ub = ug[s % NG]
        hid = hids[m % 2]
        k.gather(ub[:], u_d[:, :], eidx_t[m][:, s:s + 1], reads=[eidx_t[m], u_d], writes=[ub])
        dve(lambda e: e.scalar_tensor_tensor(out=junk[:], in0=ub[:], scalar=1.0, in1=xps[:], op0=ALU.mult, op1=ALU.mult,
                                             accum_out=hid[:, s:s + 1]), [ub, xps], [junk, hid])

    def gelu_wgt(m):
        hid, wgt = hids[m % 2], wgts[m % 2]
        dve(lambda e: e.tensor_tensor(out=g1[:], in0=hid[:], in1=hid[:], op=ALU.mult), [hid], [g1])
        dve(lambda e: e.tensor_scalar(out=g1[:], in0=g1[:], scalar1=0.044715 * 1.5957691216057308,
                                      scalar2=1.5957691216057308, op0=ALU.mult, op1=ALU.add), [g1], [g1])
        dve(lambda e: e.tensor_tensor(out=g1[:], in0=g1[:], in1=hid[:], op=ALU.mult), [g1, hid], [g1])
        k.op("act", lambda e: e.activation(out=g1[:], in_=g1[:], func=AF.Sigmoid), reads=[g1], writes=[g1])
        dve(lambda e: e.tensor_tensor(out=g1[:], in0=g1[:], in1=hid[:], op=ALU.mult), [g1, hid], [g1])
        dve(lambda e: e.tensor_tensor(out=wgt[:], in0=g1[:], in1=gate_t[m][:], op=ALU.mult), [g1, gate_t[m]], [wgt])

    def run_all(gen):
        while advance(gen):
            pass

    run_all(route(0))
    load_x(0)
    r1 = route(1) if mt > 1 else None
    for s in range(128):
        u_slot(0, s)
        if r1 is not None:
            advance(r1, 2)
    if r1 is not None:
        run_all(r1)
    gelu_wgt(0)
    HD2 = D // 2
    for m in range(mt):
        xt, wgt = xts[m % 2], wgts[m % 2]
        has_next = m + 1 < mt
        r2 = route(m + 2) if m + 2 < mt else None
        if has_next:
            load_x(m + 1)
        for half in range(2):
            for s in range(128):
                vb_ = vg[s % NG]
                k.gather(vb_[:], v_d[:, :].rearrange("e (h c) -> (e h) c", h=2), evidx_t[m][half][:, s:s + 1], reads=[evidx_t[m][half], v_d], writes=[vb_])
                d_ = dgs[s % 8]
                dve(lambda e: e.tensor_scalar(out=d_[:], in0=ident_b[:], scalar1=wgt[:, s:s + 1], scalar2=None,
                                              op0=ALU.mult), [ident_b, wgt], [d_])
                for c in range(4):
                    k.op("pe", lambda e: e.matmul(acc[c][:], lhsT=d_[:], rhs=vb_[:, c * 512:(c + 1) * 512],
                                                  start=(s == 0), stop=(s == 127)), reads=[d_, vb_], writes=[acc[c]],
                         sig=(c == 3))
                if half == 0 and has_next:
                    u_slot(m + 1, s)
                if half == 1 and r2 is not None:
                    advance(r2, 2)
            for c in range(4):
                z_ = zo[c % 2]
                col = half * HD2 + c * 512
                dve(lambda e: e.scalar_tensor_tensor(out=z_[:], in0=xt[:, col:col + 512], scalar=ALPHA, in1=acc[c][:],
                                                     op0=ALU.mult, op1=ALU.add), [xt, acc[c]], [z_])
                k.dma("sp", z_d[m * 128:(m + 1) * 128, col:col + 512], z_[:], reads=[z_], writes=[z_d])
        if r2 is not None:
            run_all(r2)
        if has_next:
            gelu_wgt(m + 1)


def peer_consts():
    pc = np.zeros((128, 48), np.float32)
    pc[:, 0:16] = np.arange(16)
    pc[:, 16:32] = 16 * np.arange(16)
    pc[:, 32:48] = 16 * np.arange(16) + 16
    return pc


def build_peer(final, mt=MT):
    nc = bass.Bass("TRN2", target_bir_lowering=False)
    with ExitStack() as st:
        k = K(nc, st)
        x1 = k.dram("x1", [T, D], F32, "ExternalInput")
        x1T = k.dram("x1T", [D, T], BF16, "ExternalInput")
        wq = k.dram("wq", [D, 2048], F32, "ExternalInput")
        keysT = k.dram("keysT", [128, 2048], F32, "ExternalInput")
        ub16 = k.dram("u", [16384, D], BF16, "ExternalInput")
        vb16 = k.dram("v", [16384, D], BF16, "ExternalInput")
        g = k.dram("g", [1, D], F32, "ExternalInput")
        b = k.dram("b", [1, D], F32, "ExternalInput")
        pconst = k.dram("pconst", [128, 48], F32, "ExternalInput")
        identb = k.dram("identb", [128, 128], BF16, "ExternalInput")
        sc = k.dram("sc", [T, 2048], F32, "Internal")
        z = k.dram("z", [T, D], F32, "Internal")
        x2 = k.dram("x2", [T, D], F32, "ExternalOutput")
        x2T = None if final else k.dram("x2T", [D, T], BF16, "ExternalOutput")
        with ExitStack() as ph:
            k.stack = ph
            xb = [k.sbuf(f"pxb{i}", [128, T], BF16) for i in range(KT)]
            phase_peer_scores(k, "ps", x1T, wq, keysT, sc, xb)
            barrier(k)
        with ExitStack() as ph:
            k.stack = ph
            phase_peer_main(k, "pm", x1, x1T, sc, ub16, vb16, pconst, identb, z, mt=mt)
            barrier(k)
        with ExitStack() as ph:
            k.stack = ph
            ps_tr = [k.psum(f"lnpt{i}", [128, 8, 128], BF16) for i in range(2)]
            phase_ln(k, "pl", z, g, b, x2, x2T, identb, ps_tr, mt=mt)
            barrier(k)
        k.stack = st
        k.finish("sp")
    return nc


LAMBDA_INIT = 0.8 - 0.6 * math.exp(-0.3 * 1)
NKT_FULL = SEQ // 128


def dense_unit(k, kT, qT, q0, vaug, W, accs, ps_s, pts, cnt, scale, LA=2):
    slots = {}

    def score(kt):
        ps = ps_s[cnt["s"] % len(ps_s)]
        pt = pts[cnt["s"] % len(pts)]
        cnt["s"] += 1
        slots[kt] = pt
        k.op("pe", lambda g: g.matmul(ps[:], lhsT=kT[:, kt * 128:(kt + 1) * 128], rhs=qT[:, q0:q0 + 512],
                                      start=True, stop=True), reads=[kT, qT], writes=[ps])
        k.op("act", lambda g: g.activation(out=pt[:], in_=ps[:], func=AF.Exp, scale=scale), reads=[ps], writes=[pt])

    for kt in range(min(LA, NKT_FULL)):
        score(kt)
    for kt in range(NKT_FULL):
        if kt + LA < NKT_FULL:
            score(kt + LA)
        pt = slots.pop(kt)
        for j in range(4):
            k.op("pe", lambda g: g.matmul(accs[j][:, 0:W], lhsT=pt[:, j * 128:(j + 1) * 128], rhs=vaug[:, kt, 0:W],
                                          start=(kt == 0), stop=(kt == NKT_FULL - 1)), reads=[pt, vaug], writes=[accs[j]])


def phase_attn_cd(k, pre, qc_d, kc_d, vc_d, qd_d, kd_d, vd_d, lam_d, subg_d, identb_d, oT_d, nchunks=SEQ // 512):
    scale = 1.0 / math.sqrt(128.0)
    ident_b = k.sbuf(f"{pre}ident", [128, 128], BF16)
    k.dma("sp", ident_b[:], identb_d[:, :], reads=[identb_d], writes=[ident_b])
    lam_t = k.sbuf(f"{pre}lamt", [128, 4, 128], F32)
    k.dma("sp", lam_t[:], lam_d[:, :].unsqueeze(0).to_broadcast([128, 4, 128]), reads=[lam_d], writes=[lam_t])
    lprod = k.sbuf(f"{pre}lprod", [128, 2, 128], F32)
    lsum = k.sbuf(f"{pre}lsum", [128, 2], F32)
    neglam = k.sbuf(f"{pre}neglam", [128, 1], F32)
    lam4 = lam_t[:].rearrange("p (a b) d -> p a b d", b=2)
    k.op("dve", lambda g: g.tensor_tensor(out=lprod[:], in0=lam4[:, :, 0, :], in1=lam4[:, :, 1, :], op=ALU.mult),
         reads=[lam_t], writes=[lprod])
    k.op("dve", lambda g: g.tensor_reduce(out=lsum[:], in_=lprod[:], axis=AX.X, op=ALU.add), reads=[lprod], writes=[lsum])
    k.op("act", lambda g: g.activation(out=lsum[:], in_=lsum[:], func=AF.Exp), reads=[lsum], writes=[lsum])
    k.op("dve", lambda g: g.scalar_tensor_tensor(out=neglam[:], in0=lsum[:, 1:2], scalar=-LAMBDA_INIT, in1=lsum[:, 0:1],
                                                 op0=ALU.add, op1=ALU.subtract), reads=[lsum], writes=[neglam])
    subg = k.sbuf(f"{pre}subg", [128, 256], F32)
    k.dma("sp", subg[:], subg_d[0:1, :].to_broadcast([128, 256]), reads=[subg_d], writes=[subg])
    k.op("dve", lambda g: g.tensor_scalar(out=subg[:], in0=subg[:], scalar1=1.0 - LAMBDA_INIT, scalar2=None, op0=ALU.mult),
         reads=[subg], writes=[subg])

    kTs = [k.sbuf(f"{pre}kT{i}", [128, SEQ], BF16) for i in range(2)]
    qTs = [k.sbuf(f"{pre}qT{i}", [128, SEQ], BF16) for i in range(2)]
    vcs = k.sbuf(f"{pre}vc", [128, NKT_FULL, 257], BF16)
    vds = k.sbuf(f"{pre}vd", [128, NKT_FULL, 129], BF16)
    ps_s = [k.psum(f"{pre}pss{i}", [128, 512], F32) for i in range(3)]
    pts = [k.sbuf(f"{pre}pt{i}", [128, 512], BF16) for i in range(4)]
    accs = [k.psum(f"{pre}acc{i}", [128, 512], F32) for i in range(4)]
    ps_t = [k.psum(f"{pre}pst{i}", [128, 8, 128], BF16) for i in range(1)]
    o1s = k.sbuf(f"{pre}o1s", [128, nchunks * 4, 256], F32)
    den = k.sbuf(f"{pre}den", [128, 1], F32)
    o2 = k.sbuf(f"{pre}o2", [128, 256], F32)
    junk = k.sbuf(f"{pre}junk", [128, 256], F32)
    ssq = k.sbuf(f"{pre}ssq", [128, 1], F32)
    ob = [k.sbuf(f"{pre}ob{i}", [128, 256], BF16) for i in range(2)]
    oTst = [k.sbuf(f"{pre}oTst{i}", [128, 2, 512], BF16) for i in range(2)]
    cnt = {"s": 0, "o": 0, "t": 0}

    k.dma("sp", vcs[:], vc_d[:, :].rearrange("(j p) c -> p j c", p=128), reads=[vc_d], writes=[vcs])
    k.dma("sp", vds[:], vd_d[:, :].rearrange("(j p) c -> p j c", p=128), reads=[vd_d], writes=[vds])

    def recip_den(acc, W):
        k.op("dve", lambda g: g.reciprocal(out=den[:], in_=acc[:, W - 1:W]), reads=[acc], writes=[den])

    def emit_T(obuf, nblk, blk0, q0, j, last):
        pt_ = ps_t[cnt["t"] % len(ps_t)]
        cnt["t"] += 1
        for b_ in range(nblk):
            k.op("pe", lambda g: g.transpose(out=pt_[:, b_, :], in_=obuf[:, b_ * 128:(b_ + 1) * 128], identity=ident_b[:]),
                 reads=[obuf, ident_b], writes=[pt_])
        stg = oTst[(q0 // 512) % 2]
        k.copy("dve", stg, stg[:, 0:nblk, j * 128:(j + 1) * 128], pt_, pt_[:, 0:nblk, :])
        if last:
            k.dma("sp", oT_d[blk0:blk0 + nblk, :, q0:q0 + 512].rearrange("b p t -> p b t"), stg[:, 0:nblk, :],
                  reads=[stg], writes=[oT_d])

    for mp in range(2):
        kT, qT = kTs[mp], qTs[mp]
        k.dma("sp", kT[:], kc_d[mp], reads=[kc_d], writes=[kT])
        k.dma("sp", qT[:], qc_d[mp], reads=[qc_d], writes=[qT])
        for ch in range(nchunks):
            q0 = ch * 512
            dense_unit(k, kT, qT, q0, vcs, 257, accs, ps_s, pts, cnt, scale)
            for j in range(4):
                acc = accs[j]
                recip_den(acc, 257)
                if mp == 0:
                    k.op("dve", lambda g: g.tensor_scalar(out=o1s[:, ch * 4 + j, :], in0=acc[:, 0:256], scalar1=den[:, 0:1],
                                                          scalar2=None, op0=ALU.mult), reads=[acc, den], writes=[o1s])
                else:
                    k.op("dve", lambda g: g.tensor_scalar(out=o2[:], in0=acc[:, 0:256], scalar1=den[:, 0:1], scalar2=None,
                                                          op0=ALU.mult), reads=[acc, den], writes=[o2])
                    k.op("dve", lambda g: g.scalar_tensor_tensor(out=o2[:], in0=o2[:], scalar=neglam[:, 0:1],
                                                                 in1=o1s[:, ch * 4 + j, :], op0=ALU.mult, op1=ALU.add),
                         reads=[o2, neglam, o1s], writes=[o2])
                    k.op("act", lambda g: g.activation(out=junk[:], in_=o2[:], func=AF.Square, accum_out=ssq[:]),
                         reads=[o2], writes=[junk, ssq])
                    k.op("dve", lambda g: g.tensor_scalar(out=ssq[:], in0=ssq[:], scalar1=1.0 / 256, scalar2=1e-6,
                                                          op0=ALU.mult, op1=ALU.add), reads=[ssq], writes=[ssq])
                    k.op("act", lambda g: g.activation(out=ssq[:], in_=ssq[:], func=AF.Sqrt), reads=[ssq], writes=[ssq])
                    k.op("dve", lambda g: g.reciprocal(out=ssq[:], in_=ssq[:]), reads=[ssq], writes=[ssq])
                    o_ = ob[cnt["o"] % 2]
                    cnt["o"] += 1
                    k.op("dve", lambda g: g.scalar_tensor_tensor(out=o_[:], in0=o2[:], scalar=ssq[:, 0:1], in1=subg[:],
                                                                 op0=ALU.mult, op1=ALU.mult), reads=[o2, ssq, subg], writes=[o_])
                    emit_T(o_, 2, 0, q0, j, j == 3)
    kT = kTs[0]
    k.dma("sp", kT[:], kd_d[:, :], reads=[kd_d], writes=[kT])
    for hq in range(2):
        qT = qTs[hq]
        k.dma("sp", qT[:], qd_d[hq], reads=[qd_d], writes=[qT])
        for ch in range(nchunks):
            q0 = ch * 512
            dense_unit(k, kT, qT, q0, vds, 129, accs, ps_s, pts, cnt, scale)
            for j in range(4):
                acc = accs[j]
                recip_den(acc, 129)
                o_ = ob[cnt["o"] % 2]
                cnt["o"] += 1
                k.op("dve", lambda g: g.tensor_scalar(out=o_[:, 0:128], in0=acc[:, 0:128], scalar1=den[:, 0:1], scalar2=None,
                                                      op0=ALU.mult), reads=[acc, den], writes=[o_])
                emit_T(o_, 1, 2 + hq, q0, j, j == 3)


def build_attn_cd(nchunks=SEQ // 512):
    nc = bass.Bass("TRN2", target_bir_lowering=False)
    with ExitStack() as st:
        k = K(nc, st)
        qc = k.dram("qc", [2, 128, SEQ], BF16, "ExternalInput")
        kc = k.dram("kc", [2, 128, SEQ], BF16, "ExternalInput")
        vc = k.dram("vc", [SEQ, 257], BF16, "ExternalInput")
        qd = k.dram("qd", [2, 128, SEQ], BF16, "ExternalInput")
        kd = k.dram("kd", [128, SEQ], BF16, "ExternalInput")
        vd = k.dram("vd", [SEQ, 129], BF16, "ExternalInput")
        lam = k.dram("lam", [4, 128], F32, "ExternalInput")
        subg = k.dram("subg", [1, 256], F32, "ExternalInput")
        identb = k.dram("identb", [128, 128], BF16, "ExternalInput")
        oT = k.dram("oT", [4, 128, SEQ], BF16, "ExternalOutput")
        phase_attn_cd(k, "e", qc, kc, vc, qd, kd, vd, lam, subg, identb, oT, nchunks=nchunks)
        k.finish("sp")
    return nc


def _run(nc, in_maps):
    res = run_bass_kernel_spmd(nc, in_maps, core_ids=list(range(NCORES)))
    return res.results


def _aug_ones(v):
    out = np.zeros(v.shape[:-1] + (v.shape[-1] + 1,), dtype=v.dtype)
    out[..., :-1] = v
    out[..., -1] = 1.0
    return out


def kernel(x, w_in_ab, sink_a, w_out_ab, w_in_cd, lam_q1, lam_k1, lam_q2, lam_k2, subln_g, q_norm_g, k_norm_g,
           w_out_cd, ln_mix_g, ln_mix_b, peer_wq, peer_keys, peer_u, peer_v, ln_ffn_g, ln_ffn_b):
    f32 = lambda a: np.ascontiguousarray(np.asarray(a, dtype=np.float32))
    x = f32(x)[0]
    ccp, ssp, cca, ssa = rope_tables()
    identb = np.eye(128, dtype=np.float32).astype(NP_BF16)
    masks = band_masks()
    pconst = peer_consts()
    cs = [slice(c * T, (c + 1) * T) for c in range(NCORES)]

    def keysT_of(l):
        return np.ascontiguousarray(f32(peer_keys)[l].reshape(16, 128, 128).transpose(2, 0, 1).reshape(128, 2048))

    w0 = f32(w_in_ab)[0]
    r = _run(build_inproj(0), [{"xT": np.ascontiguousarray(x[cs[c]].T), "w": w0, "identb": identb,
                                "ccp": ccp[cs[c]], "ssp": ssp[cs[c]]} for c in range(NCORES)])
    qkT_all = [r[c]["qkT"] for c in range(NCORES)]
    ext = host_ext_kv(qkT_all, [r[c]["vtm"] for c in range(NCORES)], [16, 17, 18, 19] + list(range(44, 68)), None)
    sink = f32(sink_a)[0].reshape(1, 16)
    RW = 16384 // NCORES
    pu, pv = f32(peer_u), f32(peer_v)
    r = _run(build_attn_ab(), [{"qT": qkT_all[c], "kTe": ext[c][0], "vaug": ext[c][1], "sink": sink, "masks": masks,
                                "identb": identb, "u0p": pu[0, c * RW:(c + 1) * RW], "v0p": pv[0, c * RW:(c + 1) * RW],
                                "u1p": pu[1, c * RW:(c + 1) * RW], "v1p": pv[1, c * RW:(c + 1) * RW]} for c in range(NCORES)])
    oT = [r[c]["oT"] for c in range(NCORES)]
    tabs16 = {nm: np.concatenate([r[c][nm + "b"] for c in range(NCORES)], axis=0) for nm in ("u0", "v0", "u1", "v1")}
    del ext, qkT_all
    g0, b0 = f32(ln_mix_g)[0:1], f32(ln_mix_b)[0:1]
    wo0 = f32(w_out_ab)[0]
    r = _run(build_outproj_ln(24, True), [{"oT": oT[c], "w": wo0, "x": x[cs[c]], "g": g0, "b": b0, "identb": identb}
                                          for c in range(NCORES)])
    x1 = [r[c]["x1"] for c in range(NCORES)]
    x1T = [r[c]["x1T"] for c in range(NCORES)]
    u0, v0 = tabs16["u0"], tabs16["v0"]
    wq0 = f32(peer_wq)[0]
    kT0 = keysT_of(0)
    gf0, bf0 = f32(ln_ffn_g)[0:1], f32(ln_ffn_b)[0:1]
    r = _run(build_peer(False), [{"x1": x1[c], "x1T": x1T[c], "wq": wq0, "keysT": kT0, "u": u0, "v": v0, "g": gf0, "b": bf0,
                                  "pconst": pconst, "identb": identb} for c in range(NCORES)])
    x2 = [r[c]["x2"] for c in range(NCORES)]
    x2T = [r[c]["x2T"] for c in range(NCORES)]
    w1 = f32(w_in_cd)[0]
    qg, kg = f32(q_norm_g)[0:1], f32(k_norm_g)[0:1]
    r = _run(build_inproj(1, x_bf16=True), [{"xT": x2T[c], "w": w1, "identb": identb, "ccp": ccp[cs[c]], "ssp": ssp[cs[c]],
                                             "cca": cca[cs[c]], "ssa": ssa[cs[c]], "qg": qg, "kg": kg} for c in range(NCORES)])
    qk = np.concatenate([r[c]["qkT"] for c in range(NCORES)], axis=2)
    vt = np.concatenate([r[c]["vtm"] for c in range(NCORES)], axis=0)
    lam = np.concatenate([f32(lam_q1)[0:1], f32(lam_k1)[0:1], f32(lam_q2)[0:1], f32(lam_k2)[0:1]], 0)
    subg = f32(subln_g)[0:1]
    maps = []
    for c in range(NCORES):
        maps.append({"qc": np.ascontiguousarray(qk[2 * c:2 * c + 2]), "kc": np.ascontiguousarray(qk[16 + 2 * c:18 + 2 * c]),
                     "vc": _aug_ones(vt[:, 256 * c:256 * c + 256]), "qd": np.ascontiguousarray(qk[32 + 2 * c:34 + 2 * c]),
                     "kd": np.ascontiguousarray(qk[48 + c // 2]),
                     "vd": _aug_ones(vt[:, 2048 + 128 * (c // 2):2048 + 128 * (c // 2) + 128]),
                     "lam": lam, "subg": subg, "identb": identb})
    r = _run(build_attn_cd(), maps)
    del qk, vt, maps
    ofull = np.zeros((32, 128, SEQ), dtype=NP_BF16)
    for c in range(NCORES):
        o = r[c]["oT"]
        ofull[2 * c] = o[0]
        ofull[2 * c + 1] = o[1]
        ofull[16 + 2 * c] = o[2]
        ofull[16 + 2 * c + 1] = o[3]
    g1, b1 = f32(ln_mix_g)[1:2], f32(ln_mix_b)[1:2]
    wo1 = f32(w_out_cd)[0]
    r = _run(build_outproj_ln(32, True), [{"oT": np.ascontiguousarray(ofull[:, :, cs[c]]), "w": wo1, "x": x2[c], "g": g1,
                                           "b": b1, "identb": identb} for c in range(NCORES)])
    x3 = [r[c]["x1"] for c in range(NCORES)]
    x3T = [r[c]["x1T"] for c in range(NCORES)]
    u1, v1 = tabs16["u1"], tabs16["v1"]
    gf1, bf1 = f32(ln_ffn_g)[1:2], f32(ln_ffn_b)[1:2]
    r = _run(build_peer(True), [{"x1": x3[c], "x1T": x3T[c], "wq": f32(peer_wq)[1], "keysT": keysT_of(1), "u": u1, "v": v1,
                                 "g": gf1, "b": bf1, "pconst": pconst, "identb": identb} for c in range(NCORES)])
    out = np.concatenate([r[c]["x2"] for c in range(NCORES)], axis=0)
    return out[None].astype(np.float32)
```

```python
import math
from contextlib import ExitStack

import numpy as np
import ml_dtypes

import concourse.bass as bass
import concourse.mybir as mybir
from concourse.bass_utils import run_bass_kernel_spmd

F32 = mybir.dt.float32
BF16 = mybir.dt.bfloat16
I32 = mybir.dt.int32
U32 = mybir.dt.uint32
AF = mybir.ActivationFunctionType
ALU = mybir.AluOpType
AX = mybir.AxisListType

NCORES = 8
SEQ = 8192
D = 4096
T = SEQ // NCORES
MT = T // 128
KT = D // 128
HD = 128
SEM_LIMIT = 2000
NP_BF16 = ml_dtypes.bfloat16
SAME_ENGINE_WAIT = {"pe": False, "dve": True, "act": True, "pool": True, "sp": False}


class Buf:
    __slots__ = ("t", "lw", "rd", "name")

    def __init__(self, t, name=""):
        self.t = t
        self.lw = None
        self.rd = []
        self.name = name

    def __getitem__(self, idx):
        return self.t[idx]


class K:
    def __init__(self, nc, stack):
        self.nc = nc
        self.stack = stack
        self.eng = {"pe": nc.tensor, "dve": nc.vector, "act": nc.scalar, "pool": nc.gpsimd, "sp": nc.sync}
        self.cur = {}
        self.seen = {}
        self.nsem = 0
        self.dma_rr = {}
        self.same_engine_wait = dict(SAME_ENGINE_WAIT)
        self.n_inst = 0
        self.pending = {}

    def sbuf(self, name, shape, dtype):
        return Buf(self.stack.enter_context(self.nc.sbuf_tensor(name, list(shape), dtype)), name)

    def psum(self, name, shape, dtype=F32):
        return Buf(self.stack.enter_context(self.nc.psum_tensor(name, list(shape), dtype)), name)

    def dram(self, name, shape, dtype, kind="Internal"):
        return Buf(self.nc.dram_tensor(name, list(shape), dtype, kind=kind).ap(), name)

    def new_sem(self, name):
        self.nsem += 1
        return self.stack.enter_context(self.nc.semaphore(f"{name}_{self.nsem}"))

    def _wait(self, e, tok):
        if tok is None:
            return
        sem, val, src = tok
        if src == e and not self.same_engine_wait[e]:
            return
        seen = self.seen.setdefault(e, {})
        kk = id(sem)
        if seen.get(kk, 0) >= val:
            return
        self.eng[e].wait_ge(sem, val)
        seen[kk] = val

    def _deps(self, e, reads, writes):
        for b in reads:
            self._wait(e, b.lw)
        for b in writes:
            self._wait(e, b.lw)
            for r in b.rd:
                self._wait(e, r)

    def _commit(self, tok, reads, writes):
        for b in reads:
            b.rd.append(tok)
            if len(b.rd) > 64:
                b.rd = b.rd[-48:]
        for b in writes:
            b.lw = tok
            b.rd = []

    def op(self, e, fn, reads=(), writes=(), sig=True):
        self._deps(e, reads, writes)
        if not sig:
            pr, pw = self.pending.setdefault(e, ([], []))
            for b in reads:
                if b not in pr:
                    pr.append(b)
            for b in writes:
                if b not in pw:
                    pw.append(b)
            fn(self.eng[e])
            self.n_inst += 1
            return None
        if e in self.pending:
            pr, pw = self.pending.pop(e)
            reads = list(reads) + [b for b in pr if b not in reads]
            writes = list(writes) + [b for b in pw if b not in writes]
        c = self.cur.get(e)
        if c is None or c[1] >= SEM_LIMIT:
            c = [self.new_sem("s" + e), 0]
            self.cur[e] = c
        ins = fn(self.eng[e])
        c[1] += 1
        ins.then_inc(c[0], 1)
        tok = (c[0], c[1], e)
        self._commit(tok, reads, writes)
        self.n_inst += 1
        return tok

    def _dma_slot(self, q):
        R = 8
        ring = self.dma_rr.setdefault(q, {"sems": [], "i": 0})
        i = ring["i"]
        ring["i"] += 1
        slot = i % R
        if len(ring["sems"]) <= slot:
            ring["sems"].append([self.new_sem("d" + q), 0, None])
        s = ring["sems"][slot]
        self._wait(q, s[2])
        if s[1] >= SEM_LIMIT:
            s[0] = self.new_sem("d" + q)
            s[1] = 0
            s[2] = None
        return s

    def dma(self, q, out, in_, reads=(), writes=(), **kw):
        s = self._dma_slot(q)
        self._deps(q, reads, writes)
        ins = self.eng[q].dma_start(out=out, in_=in_, **kw)
        s[1] += 16
        ins.then_inc(s[0], 16)
        tok = (s[0], s[1], "dma")
        s[2] = tok
        self._commit(tok, reads, writes)
        self.n_inst += 1
        return tok

    def gather(self, out, in_, idx_ap, reads=(), writes=(), **kw):
        q = "pool"
        s = self._dma_slot(q)
        self._deps(q, reads, writes)
        ins = self.eng[q].indirect_dma_start(out=out, out_offset=None, in_=in_,
                                             in_offset=bass.IndirectOffsetOnAxis(ap=idx_ap, axis=0), **kw)
        s[1] += 16
        ins.then_inc(s[0], 16)
        tok = (s[0], s[1], "dma")
        s[2] = tok
        self._commit(tok, reads, writes)
        self.n_inst += 1
        return tok

    def finish(self, e="sp"):
        for q, ring in self.dma_rr.items():
            for s in ring["sems"]:
                self._wait(e, s[2])

    def copy(self, e, out_b, out_ap, in_b, in_ap):
        if e == "act":
            return self.op("act", lambda g: g.copy(out=out_ap, in_=in_ap), reads=[in_b], writes=[out_b])
        return self.op(e, lambda g: g.tensor_copy(out=out_ap, in_=in_ap), reads=[in_b], writes=[out_b])


def bcast_free(ap, shape):
    return ap.to_broadcast(list(shape))


def load_xT(k, xT_dram, xb, stage):
    for kt in range(KT):
        s = stage[kt % len(stage)]
        k.dma("sp", s[:], xT_dram[kt * 128:(kt + 1) * 128, :], reads=[xT_dram], writes=[s])
        k.copy("act" if kt % 2 == 0 else "dve", xb[kt], xb[kt][:], s, s[:])


def stream_linear(k, xb, w_dram, N, wst, wb, ps_banks, epilogue, NCH=512, KQ=4, cast_engs=("act", "dve"),
                  state=None):
    nkt = len(xb)
    NKQ = nkt // KQ
    st = state if state is not None else {"cnt": 0, "oc": 0}
    for c in range(N // NCH):
        n0 = c * NCH
        b = c % 2
        for q in range(NKQ):
            s = wst[st["cnt"] % len(wst)]
            src = w_dram[q * KQ * 128:(q + 1) * KQ * 128, n0:n0 + NCH].rearrange("(j p) c -> p j c", p=128)
            k.dma("sp", s[:], src, reads=[w_dram], writes=[s])
            k.copy(cast_engs[st["cnt"] % len(cast_engs)], wb[b][q], wb[b][q][:], s, s[:])
            st["cnt"] += 1
        for m in range(MT):
            p = ps_banks[st["oc"] % len(ps_banks)]
            for kt in range(nkt):
                q, j = divmod(kt, KQ)
                k.op("pe", lambda g: g.matmul(p[:], lhsT=xb[kt][:, m * 128:(m + 1) * 128], rhs=wb[b][q][:, j, :],
                                              start=(kt == 0), stop=(kt == nkt - 1)),
                     reads=[xb[kt], wb[b][q]], writes=[p], sig=(kt == nkt - 1))
            epilogue(c, m, p)
            st["oc"] += 1
    return st


def rope_tile(k, src_b, src3, cc, ss, m, groups, rot_w, rb, tmp_t, tmp_u):
    H4 = 4
    w = rot_w
    ccb = cc[:, m, 0:w].unsqueeze(1).to_broadcast([128, H4, w])
    k.op("dve", lambda g: g.tensor_tensor(out=tmp_t[:, :, 0:w], in0=src3[:, :, 0:w], in1=ccb, op=ALU.mult),
         reads=[src_b, cc], writes=[tmp_t])
    first = True
    for (lo, half) in groups:
        hi = lo + half
        s_lo = ss[:, m, lo:lo + half].unsqueeze(1).to_broadcast([128, H4, half])
        s_hi = ss[:, m, hi:hi + half].unsqueeze(1).to_broadcast([128, H4, half])
        k.op("dve", lambda g: g.tensor_tensor(out=tmp_u[:, :, lo:lo + half], in0=src3[:, :, hi:hi + half], in1=s_lo,
                                              op=ALU.mult), reads=[src_b, ss], writes=[tmp_u])
        k.op("dve", lambda g: g.tensor_tensor(out=tmp_u[:, :, hi:hi + half], in0=src3[:, :, lo:lo + half], in1=s_hi,
                                              op=ALU.mult), reads=[src_b, ss], writes=[tmp_u])
    k.op("dve", lambda g: g.tensor_tensor(out=rb[:, :, 0:w], in0=tmp_t[:, :, 0:w], in1=tmp_u[:, :, 0:w], op=ALU.add),
         reads=[tmp_t, tmp_u], writes=[rb])
    if w < 128:
        k.op("act", lambda g: g.copy(out=rb[:, :, w:128], in_=src3[:, :, w:128]), reads=[src_b], writes=[rb])


def phase_inproj(k, pre, xT_dram, w_dram, N, chunk_types, tabs, gains, ident_b, qkT_dram, v_dram, res, bg=None):
    xb = res["xb"]
    wst = res["wst"]
    wb = res["wb"]
    ps = res["ps_mm"]
    psT = res["ps_tr"]
    rbs = [k.sbuf(f"{pre}rb{i}", [128, 4, 128], BF16) for i in range(2)]
    tmp_t = k.sbuf(f"{pre}tt", [128, 4, 128], F32)
    tmp_u = k.sbuf(f"{pre}tu", [128, 4, 128], F32)
    qn = k.sbuf(f"{pre}qn", [128, 4, 128], F32)
    junk = k.sbuf(f"{pre}junk", [128, 128], F32)
    ssq = k.sbuf(f"{pre}ssq", [128, 4], F32)
    rstd = k.sbuf(f"{pre}rstd", [128, 4], F32)
    fmst = [k.sbuf(f"{pre}fm{i}", [128, 4, T], BF16) for i in range(2)]
    vst = [k.sbuf(f"{pre}vst{i}", [128, 512], BF16) for i in range(2)]
    cnt = {"e": 0}

    def epilogue(c, m, p):
        ty = chunk_types[c]
        i = cnt["e"]
        cnt["e"] += 1
        if bg is not None:
            bg()
            bg()
        if ty[0] == "v":
            vs = vst[i % 2]
            k.copy("act", vs, vs[:], p, p[:])
            k.dma("sp", v_dram[m * 128:(m + 1) * 128, ty[1]:ty[1] + 512], vs[:], reads=[vs], writes=[v_dram])
            return
        rb = rbs[i % 2]
        p3 = p[:].rearrange("p (h d) -> p h d", h=4)
        if ty[0] == "rope":
            rope_tile(k, p, p3, tabs["ccp"], tabs["ssp"], m, [(0, 16)], 32, rb, tmp_t, tmp_u)
        else:
            gb = gains[ty[2]]
            for j in range(4):
                k.op("act", lambda g: g.activation(out=junk[:], in_=p[:, j * 128:(j + 1) * 128], func=AF.Square,
                                                   accum_out=ssq[:, j:j + 1]), reads=[p], writes=[junk, ssq])
            k.op("dve", lambda g: g.tensor_scalar(out=rstd[:], in0=ssq[:], scalar1=1.0 / 128, scalar2=1e-6,
                                                  op0=ALU.mult, op1=ALU.add), reads=[ssq], writes=[rstd])
            k.op("act", lambda g: g.activation(out=rstd[:], in_=rstd[:], func=AF.Sqrt), reads=[rstd], writes=[rstd])
            k.op("dve", lambda g: g.reciprocal(out=rstd[:], in_=rstd[:]), reads=[rstd], writes=[rstd])
            k.op("dve", lambda g: g.tensor_tensor(out=qn[:], in0=p3, in1=rstd[:].unsqueeze(2).to_broadcast([128, 4, 128]),
                                                  op=ALU.mult), reads=[p, rstd], writes=[qn])
            k.op("dve", lambda g: g.tensor_tensor(out=qn[:], in0=qn[:], in1=gb[:].unsqueeze(1).to_broadcast([128, 4, 128]),
                                                  op=ALU.mult), reads=[qn, gb], writes=[qn])
            rope_tile(k, qn, qn[:], tabs["cca"], tabs["ssa"], m, [(0, 32), (64, 32)], 128, rb, tmp_t, tmp_u)
        pt = psT[i % len(psT)]
        for j in range(4):
            k.op("pe", lambda g: g.transpose(out=pt[:, j, 0:128], in_=rb[:, j, :], identity=ident_b[:]),
                 reads=[rb, ident_b], writes=[pt])
        fm = fmst[c % 2]
        k.copy("act" if i % 2 else "dve", fm, fm[:, :, m * 128:(m + 1) * 128], pt, pt[:, :, 0:128])
        if m == MT - 1:
            fm0 = ty[1]
            k.dma("sp", qkT_dram[fm0:fm0 + 4].rearrange("j p t -> p j t"), fm[:], reads=[fm], writes=[qkT_dram])

    stream_linear(k, xb, w_dram, N, wst, wb, ps, epilogue, state=res["lin_state"],
                  cast_engs=("act", "dve"))


def rope_tables():
    pos = np.arange(SEQ, dtype=np.float32)
    half = 16
    inv = (np.float32(500000.0) ** (-np.arange(half, dtype=np.float32) / half)).astype(np.float32)
    ang = pos[:, None] * inv[None, :]
    c, s = np.cos(ang).astype(np.float32), np.sin(ang).astype(np.float32)
    ccp = np.ones((SEQ, 128), np.float32)
    ssp = np.zeros((SEQ, 128), np.float32)
    ccp[:, 0:16] = c
    ccp[:, 16:32] = c
    ssp[:, 0:16] = -s
    ssp[:, 16:32] = s
    rows = (np.arange(SEQ) // 64).astype(np.float32)
    cols = (np.arange(SEQ) % 64).astype(np.float32)
    inv2 = (np.float32(10000.0) ** (-np.arange(32, dtype=np.float32) / 32)).astype(np.float32)
    ar = rows[:, None] * inv2[None, :]
    ac = cols[:, None] * inv2[None, :]
    cr, sr, cc_, sc = (np.cos(ar).astype(np.float32), np.sin(ar).astype(np.float32),
                       np.cos(ac).astype(np.float32), np.sin(ac).astype(np.float32))
    cca = np.concatenate([cr, cr, cc_, cc_], 1)
    ssa = np.concatenate([-sr, sr, -sc, sc], 1)
    return ccp, ssp, cca, ssa


def alloc_linear_res(k, pre):
    res = {}
    res["xb"] = [k.sbuf(f"{pre}xb{i}", [128, T], BF16) for i in range(KT)]
    res["xst"] = [k.sbuf(f"{pre}xst{i}", [128, T], F32) for i in range(2)]
    res["wst"] = [k.sbuf(f"{pre}wst{i}", [128, 4, 512], F32) for i in range(2)]
    res["wb"] = [[k.sbuf(f"{pre}wb{b}_{q}", [128, 4, 512], BF16) for q in range(KT // 4)] for b in range(2)]
    res["ps_mm"] = [k.psum(f"{pre}psmm{i}", [128, 512], F32) for i in range(4)]
    res["ps_tr"] = [k.psum(f"{pre}pstr{i}", [128, 4, 256], BF16) for i in range(2)]
    res["lin_state"] = {"cnt": 0, "oc": 0}
    return res


def load_tabs(k, pre, names, drams):
    tabs = {}
    for nm in names:
        t = k.sbuf(f"{pre}tab_{nm}", [128, MT, 128], F32)
        k.dma("sp", t[:], drams[nm][:, :].rearrange("(m p) d -> p m d", p=128), reads=[drams[nm]], writes=[t])
        tabs[nm] = t
    return tabs


AB_TYPES = ([("rope", c * 4) for c in range(4)] + [("rope", 16)] + [("v", 0)]
            + [("rope", 20 + c * 4) for c in range(6)] + [("rope", 44 + c * 4) for c in range(6)]
            + [("v", 512 + c * 512) for c in range(6)])
AB_NFM, AB_NV = 68, 3584
CD_TYPES = ([("rope", c * 4) for c in range(4)] + [("rope", 16 + c * 4) for c in range(4)]
            + [("v", c * 512) for c in range(4)]
            + [("axial", 32 + c * 4, "qg") for c in range(4)] + [("axial", 48, "kg")] + [("v", 2048)])
CD_NFM, CD_NV = 52, 2560


def build_inproj(layer, types=None, x_bf16=False, conv=False):
    nc = bass.Bass("TRN2", target_bir_lowering=False)
    if types is None:
        types = AB_TYPES if layer == 0 else CD_TYPES
    N = 512 * len(types)
    nfm, nv = (AB_NFM, AB_NV) if layer == 0 else (CD_NFM, CD_NV)
    with ExitStack() as st:
        k = K(nc, st)
        xT = k.dram("xT", [D, T], BF16 if x_bf16 else F32, "ExternalInput")
        w = k.dram("w", [D, N], F32, "ExternalInput")
        identb = k.dram("identb", [128, 128], BF16, "ExternalInput")
        tabd = {nm: k.dram(nm, [T, 128], F32, "ExternalInput") for nm in (("ccp", "ssp") if layer == 0 else ("ccp", "ssp", "cca", "ssa"))}
        qkT = k.dram("qkT", [nfm, 128, T], BF16, "ExternalOutput")
        vtm = k.dram("vtm", [T, nv], BF16, "ExternalOutput")
        res = alloc_linear_res(k, "a")
        ident_b = k.sbuf("identb_s", [128, 128], BF16)
        k.dma("sp", ident_b[:], identb[:, :], reads=[identb], writes=[ident_b])
        tabs = load_tabs(k, "a", list(tabd.keys()), tabd)
        gains = {}
        if layer == 1:
            for nm in ("qg", "kg"):
                gd = k.dram(nm, [1, 128], F32, "ExternalInput")
                gs = k.sbuf("g_" + nm, [128, 128], F32)
                k.dma("sp", gs[:], gd[0:1, :].to_broadcast([128, 128]), reads=[gd], writes=[gs])
                gains[nm] = gs
        if x_bf16:
            for kt in range(KT):
                k.dma("sp", res["xb"][kt][:], xT[kt * 128:(kt + 1) * 128, :], reads=[xT], writes=[res["xb"][kt]])
        else:
            load_xT(k, xT, res["xb"], res["xst"])
        bg = None
        if conv:
            RW = 16384 // NCORES
            pairs = []
            for nm in ("u0", "v0", "u1", "v1"):
                src = k.dram(nm + "p", [RW, D], F32, "ExternalInput")
                dst = k.dram(nm + "b", [RW, D], BF16, "ExternalOutput")
                pairs.append((src, dst))
            bg = cast_rows_bg(k, "cv", pairs, RW, ncol=4)
        phase_inproj(k, "a", xT, w, N, types, tabs, gains, ident_b, qkT, vtm, res, bg=bg)
        while bg is not None and bg():
            pass
        k.finish("sp")
    return nc


HALO = 1024
TE = T + 2 * HALO
NTE = TE // 128
B_DIL = (1, 4, 16)
B_DT = (1, 2, 8)
MASK_OFF = {"A": 0, 0: 3, 1: 6, 2: 11}
N_MASKS = 28


def band_masks():
    m = np.zeros((128, N_MASKS, 128), np.float32)
    kl = np.arange(128)[:, None]
    ql = np.arange(128)[None, :]
    for j, dt in enumerate((-1, 0, 1)):
        diff = (ql - kl) - dt * 128
        m[:, MASK_OFF["A"] + j, :] = (np.abs(diff) <= 128)
    for g in range(3):
        d, r = B_DIL[g], B_DT[g]
        for j, dt in enumerate(range(-r, r + 1)):
            diff = (ql - kl) - dt * 128
            m[:, MASK_OFF[g] + j, :] = (np.abs(diff) <= 64 * d) & (diff % d == 0)
    return m.astype(NP_BF16)


def attn_chunks(qT_ap, qT_b, kT_b, v_b, ktiles, mask0, acc, first, last):
    out = []
    n = len(ktiles)
    i0 = 0
    while i0 < n:
        nt = min(4, n - i0)
        out.append({"q": qT_ap, "qb": qT_b, "kT": kT_b, "v": v_b, "tiles": ktiles[i0:i0 + nt], "mask0": mask0 + i0, "acc": acc,
                    "start": first and i0 == 0, "stop": last and i0 + nt == n, "post": None})
        i0 += nt
    return out


def run_chunks(k, chunks, mask_b, ps_s, pts, cnt, scale):
    def S(ch):
        ps = ps_s[cnt["s"] % len(ps_s)]
        pt = pts[cnt["s"] % len(pts)]
        cnt["s"] += 1
        ch["pt"] = pt
        nt = len(ch["tiles"])
        for j, lt in enumerate(ch["tiles"]):
            k.op("pe", lambda g: g.matmul(ps[:, j, :], lhsT=ch["kT"][:, lt * 128:(lt + 1) * 128], rhs=ch["q"],
                                          start=True, stop=True), reads=[ch["kT"], ch["qb"]], writes=[ps], sig=(j == nt - 1))
        k.op("act", lambda g: g.activation(out=pt[:, 0:nt, :], in_=ps[:, 0:nt, :], func=AF.Exp, scale=scale),
             reads=[ps], writes=[pt])
        k.op("dve", lambda g: g.tensor_tensor(out=pt[:, 0:nt, :], in0=pt[:, 0:nt, :],
                                              in1=mask_b[:, ch["mask0"]:ch["mask0"] + nt, :], op=ALU.mult),
             reads=[pt, mask_b], writes=[pt])

    def P(ch):
        pt = ch["pt"]
        nt = len(ch["tiles"])
        for j, lt in enumerate(ch["tiles"]):
            k.op("pe", lambda g: g.matmul(ch["acc"][:, 0:129], lhsT=pt[:, j, :], rhs=ch["v"][:, lt, :],
                                          start=(ch["start"] and j == 0), stop=(ch["stop"] and j == nt - 1)),
                 reads=[pt, ch["v"]], writes=[ch["acc"]], sig=(j == nt - 1))
        if ch["post"] is not None:
            ch["post"]()

    if not chunks:
        return
    S(chunks[0])
    for i, ch in enumerate(chunks):
        if i + 1 < len(chunks):
            S(chunks[i + 1])
        P(ch)


def phase_attn_ab(k, pre, qT_d, kTe_d, vaug_d, sink_d, mask_d, identb_d, oT_d, bg=None):
    scale = 1.0 / math.sqrt(128.0)
    mask_b = k.sbuf(f"{pre}mask", [128, N_MASKS, 128], BF16)
    k.dma("sp", mask_b[:], mask_d[:, :, :], reads=[mask_d], writes=[mask_b])
    ident_b = k.sbuf(f"{pre}ident", [128, 128], BF16)
    k.dma("sp", ident_b[:], identb_d[:, :], reads=[identb_d], writes=[ident_b])
    esink = k.sbuf(f"{pre}esink", [128, 16], F32)
    k.dma("sp", esink[:], sink_d[0:1, :].to_broadcast([128, 16]), reads=[sink_d], writes=[esink])
    k.op("act", lambda g: g.activation(out=esink[:], in_=esink[:], func=AF.Exp), reads=[esink], writes=[esink])
    NKT = 46
    kbuf = [k.sbuf(f"{pre}kb{i}", [128, NKT * 128], BF16) for i in range(2)]
    vbuf = [k.sbuf(f"{pre}vb{i}", [128, NKT, 129], BF16) for i in range(2)]
    qbuf = [k.sbuf(f"{pre}qb{i}", [128, 4, T], BF16) for i in range(2)]
    ps_s = [k.psum(f"{pre}pss{i}", [128, 4, 128], F32) for i in range(3)]
    pts = [k.sbuf(f"{pre}pt{i}", [128, 4, 128], BF16) for i in range(4)]
    accs = [k.psum(f"{pre}acc{i}", [128, 512], F32) for i in range(2)]
    ps_t = [k.psum(f"{pre}pst{i}", [128, 1024], BF16) for i in range(2)]
    den = k.sbuf(f"{pre}den", [128, 1], F32)
    ob = [k.sbuf(f"{pre}ob{i}", [128, 128], BF16) for i in range(2)]
    oTs = [k.sbuf(f"{pre}oTs{i}", [128, T], BF16) for i in range(2)]
    cnt = {"s": 0, "a": 0, "o": 0}

    def finalize(acc, sink_col, ohead, m):
        if sink_col is not None:
            k.op("dve", lambda g: g.tensor_tensor(out=den[:], in0=acc[:, 128:129], in1=esink[:, sink_col:sink_col + 1],
                                                  op=ALU.add), reads=[acc, esink], writes=[den])
        else:
            k.op("dve", lambda g: g.tensor_copy(out=den[:], in_=acc[:, 128:129]), reads=[acc], writes=[den])
        k.op("dve", lambda g: g.reciprocal(out=den[:], in_=den[:]), reads=[den], writes=[den])
        o = ob[cnt["o"] % 2]
        k.op("dve", lambda g: g.tensor_scalar(out=o[:], in0=acc[:, 0:128], scalar1=den[:, 0:1], scalar2=None,
                                              op0=ALU.mult), reads=[acc, den], writes=[o])
        pt_ = ps_t[cnt["o"] % 2]
        k.op("pe", lambda g: g.transpose(out=pt_[:, 0:128], in_=o[:], identity=ident_b[:]),
             reads=[o, ident_b], writes=[pt_])
        oT = oTs[ohead % 2]
        k.copy("act", oT, oT[:, m * 128:(m + 1) * 128], pt_, pt_[:, 0:128])
        cnt["o"] += 1
        if m == MT - 1:
            k.dma("sp", oT_d[ohead], oT[:], reads=[oT], writes=[oT_d])

    job = 0
    for kvh in range(4):
        kb_, vb_, qb_ = kbuf[job % 2], vbuf[job % 2], qbuf[job % 2]
        job += 1
        k.dma("sp", kb_[:, 0:10 * 128], kTe_d[kvh, :, 7 * 128:17 * 128], reads=[kTe_d], writes=[kb_])
        k.dma("sp", vb_[:, 0:10, :], vaug_d[7 * 128:17 * 128, kvh, :].rearrange("(j p) c -> p j c", p=128),
              reads=[vaug_d], writes=[vb_])
        k.dma("sp", qb_[:], qT_d[kvh * 4:kvh * 4 + 4].rearrange("j p t -> p j t"), reads=[qT_d], writes=[qb_])
        chunks = []
        for g4 in range(4):
            h = kvh * 4 + g4
            for m in range(MT):
                acc = accs[cnt["a"] % 2]
                cnt["a"] += 1
                cs_ = attn_chunks(qb_[:, g4, m * 128:(m + 1) * 128], qb_, kb_, vb_, [m, m + 1, m + 2], MASK_OFF["A"], acc,
                                  True, True)
                cs_[-1]["post"] = (lambda acc=acc, h=h, m=m: finalize(acc, h, h, m))
                chunks += cs_
        run_chunks(k, chunks, mask_b, ps_s, pts, cnt, scale)
    base = [0, 10, 22]
    lo_ext = [7, 6, 0]
    nload = [10, 12, 24]
    for h in range(8):
        kb_, vb_, qb_ = kbuf[job % 2], vbuf[job % 2], qbuf[job % 2]
        job += 1
        for g in range(3):
            kblk = 4 + g * 8 + h
            k.dma("sp", kb_[:, base[g] * 128:(base[g] + nload[g]) * 128],
                  kTe_d[kblk, :, lo_ext[g] * 128:(lo_ext[g] + nload[g]) * 128], reads=[kTe_d], writes=[kb_])
            k.dma("sp", vb_[:, base[g]:base[g] + nload[g], :],
                  vaug_d[lo_ext[g] * 128:(lo_ext[g] + nload[g]) * 128, kblk, :].rearrange("(j p) c -> p j c", p=128),
                  reads=[vaug_d], writes=[vb_])
            k.dma("sp", qb_[:, g, :], qT_d[20 + g * 8 + h], reads=[qT_d], writes=[qb_])
        chunks = []
        for m in range(MT):
            acc = accs[cnt["a"] % 2]
            cnt["a"] += 1
            for g in range(3):
                r = B_DT[g]
                tiles = [base[g] + (8 + m + dt) - lo_ext[g] for dt in range(-r, r + 1)]
                chunks += attn_chunks(qb_[:, g, m * 128:(m + 1) * 128], qb_, kb_, vb_, tiles, MASK_OFF[g], acc, g == 0, g == 2)
            chunks[-1]["post"] = (lambda acc=acc, h=h, m=m: finalize(acc, None, 16 + h, m))
        run_chunks(k, chunks, mask_b, ps_s, pts, cnt, scale)


def build_attn_ab():
    nc = bass.Bass("TRN2", target_bir_lowering=False)
    with ExitStack() as st:
        k = K(nc, st)
        qT = k.dram("qT", [AB_NFM, 128, T], BF16, "ExternalInput")
        kTe = k.dram("kTe", [28, 128, TE], BF16, "ExternalInput")
        vaug = k.dram("vaug", [TE, 28, 129], BF16, "ExternalInput")
        sink = k.dram("sink", [1, 16], F32, "ExternalInput")
        masks = k.dram("masks", [128, N_MASKS, 128], BF16, "ExternalInput")
        identb = k.dram("identb", [128, 128], BF16, "ExternalInput")
        oT = k.dram("oT", [24, 128, T], BF16, "ExternalOutput")
        phase_attn_ab(k, "b", qT, kTe, vaug, sink, masks, identb, oT)
        k.finish("sp")
    return nc


def host_ext_kv(qkT_all, vtm_all, kblocks, vheads_cols):
    kfull = np.concatenate([q[kblocks] for q in qkT_all], axis=2)
    vfull = np.concatenate(vtm_all, axis=0)
    nk = len(kblocks)
    kpad = np.zeros((nk, 128, SEQ + 2 * HALO), dtype=kfull.dtype)
    kpad[:, :, HALO:HALO + SEQ] = kfull
    nvh = vfull.shape[1] // 128
    vpad = np.zeros((SEQ + 2 * HALO, nvh, 129), dtype=vfull.dtype)
    vpad[HALO:HALO + SEQ, :, 0:128] = vfull.reshape(SEQ, nvh, 128)
    vpad[HALO:HALO + SEQ, :, 128] = 1.0
    outs = []
    for c in range(NCORES):
        outs.append((np.ascontiguousarray(kpad[:, :, c * T:c * T + TE]), np.ascontiguousarray(vpad[c * T:c * T + TE])))
    return outs


ALPHA = float((2 * 2) ** 0.25)


def barrier(k):
    toks = []
    for e, c in k.cur.items():
        if c[1] > 0:
            toks.append((c[0], c[1], e))
    for q, ring in k.dma_rr.items():
        for s in ring["sems"]:
            if s[2] is not None:
                toks.append(s[2])
    for e in ("pe", "dve", "act", "pool", "sp"):
        for tk in toks:
            if tk[2] == e:
                continue
            k._wait(e, tk)


def phase_outproj(k, pre, oT_d, nkt, w_d, x_d, z_d, res):
    xb = res["xb"][:nkt]
    for kt in range(nkt):
        k.dma("sp", xb[kt][:], oT_d[kt], reads=[oT_d], writes=[xb[kt]])
    xin = [k.sbuf(f"{pre}xin{i}", [128, 512], F32) for i in range(3)]
    zt = [k.sbuf(f"{pre}zt{i}", [128, 512], F32) for i in range(3)]
    cnt = {"e": 0}

    def epilogue(c, m, p):
        i = cnt["e"]
        cnt["e"] += 1
        xi, zo = xin[i % 3], zt[i % 3]
        k.dma("sp", xi[:], x_d[m * 128:(m + 1) * 128, c * 512:(c + 1) * 512], reads=[x_d], writes=[xi])
        k.op("dve", lambda g: g.scalar_tensor_tensor(out=zo[:], in0=xi[:], scalar=ALPHA, in1=p[:], op0=ALU.mult,
                                                     op1=ALU.add), reads=[xi, p], writes=[zo])
        k.dma("sp", z_d[m * 128:(m + 1) * 128, c * 512:(c + 1) * 512], zo[:], reads=[zo], writes=[z_d])

    stream_linear(k, xb, w_d, D, res["wst"], res["wb"], res["ps_mm"], epilogue, state=res["lin_state"])


def phase_ln(k, pre, z_d, g_d, b_d, out_d, outT_d, identb_d, ps_tr, mt=MT):
    gt = k.sbuf(f"{pre}g", [128, D], F32)
    bt = k.sbuf(f"{pre}b", [128, D], F32)
    k.dma("sp", gt[:], g_d[0:1, :].to_broadcast([128, D]), reads=[g_d], writes=[gt])
    k.dma("sp", bt[:], b_d[0:1, :].to_broadcast([128, D]), reads=[b_d], writes=[bt])
    ident_b = k.sbuf(f"{pre}ident", [128, 128], BF16)
    k.dma("sp", ident_b[:], identb_d[:, :], reads=[identb_d], writes=[ident_b])
    zs = [k.sbuf(f"{pre}z{i}", [128, D], F32) for i in range(2)]
    os_ = [k.sbuf(f"{pre}o{i}", [128, D], F32) for i in range(2)]
    ob16 = k.sbuf(f"{pre}o16", [128, D], BF16)
    stg = [k.sbuf(f"{pre}stg{i}", [128, KT, 128], BF16) for i in range(2)]
    stats = k.sbuf(f"{pre}stats", [128, 8, 6], F32)
    mv = k.sbuf(f"{pre}mv", [128, 2], F32)
    rstd = k.sbuf(f"{pre}rstd", [128, 1], F32)
    for m in range(mt):
        z, o = zs[m % 2], os_[m % 2]
        k.dma("sp", z[:], z_d[m * 128:(m + 1) * 128, :], reads=[z_d], writes=[z])
        for c in range(8):
            k.op("dve", lambda g: g.bn_stats(out=stats[:, c, :], in_=z[:, c * 512:(c + 1) * 512]), reads=[z], writes=[stats])
        k.op("dve", lambda g: g.bn_aggr(out=mv[:], in_=stats[:].rearrange("p a b -> p (a b)")), reads=[stats], writes=[mv])
        k.op("dve", lambda g: g.tensor_scalar(out=rstd[:], in0=mv[:, 1:2], scalar1=1e-5, scalar2=None, op0=ALU.add),
             reads=[mv], writes=[rstd])
        k.op("act", lambda g: g.activation(out=rstd[:], in_=rstd[:], func=AF.Sqrt), reads=[rstd], writes=[rstd])
        k.op("dve", lambda g: g.reciprocal(out=rstd[:], in_=rstd[:]), reads=[rstd], writes=[rstd])
        k.op("dve", lambda g: g.tensor_scalar(out=o[:], in0=z[:], scalar1=mv[:, 0:1], scalar2=rstd[:, 0:1],
                                              op0=ALU.subtract, op1=ALU.mult), reads=[z, mv, rstd], writes=[o])
        k.op("pool", lambda g: g.tensor_tensor(out=o[:], in0=o[:], in1=gt[:], op=ALU.mult), reads=[o, gt], writes=[o])
        k.op("pool", lambda g: g.tensor_tensor(out=o[:], in0=o[:], in1=bt[:], op=ALU.add), reads=[o, bt], writes=[o])
        k.dma("sp", out_d[m * 128:(m + 1) * 128, :], o[:], reads=[o], writes=[out_d])
        if outT_d is not None:
            k.op("act", lambda g: g.copy(out=ob16[:], in_=o[:]), reads=[o], writes=[ob16])
            sg = stg[m % 2]
            for q in range(4):
                pt = ps_tr[q % len(ps_tr)]
                for j in range(8):
                    kt = q * 8 + j
                    k.op("pe", lambda g: g.transpose(out=pt[:, j, :], in_=ob16[:, kt * 128:(kt + 1) * 128],
                                                     identity=ident_b[:]), reads=[ob16, ident_b], writes=[pt])
                k.copy("act" if q % 2 else "dve", sg, sg[:, q * 8:(q + 1) * 8, :], pt, pt[:])
            k.dma("sp", outT_d[:, m * 128:(m + 1) * 128].rearrange("(kt p) t -> p kt t", p=128), sg[:],
                  reads=[sg], writes=[outT_d])


def build_outproj_ln(nkt, with_T):
    nc = bass.Bass("TRN2", target_bir_lowering=False)
    with ExitStack() as st:
        k = K(nc, st)
        oT = k.dram("oT", [nkt, 128, T], BF16, "ExternalInput")
        w = k.dram("w", [nkt * 128, D], F32, "ExternalInput")
        x = k.dram("x", [T, D], F32, "ExternalInput")
        g = k.dram("g", [1, D], F32, "ExternalInput")
        b = k.dram("b", [1, D], F32, "ExternalInput")
        identb = k.dram("identb", [128, 128], BF16, "ExternalInput")
        z = k.dram("z", [T, D], F32, "Internal")
        x1 = k.dram("x1", [T, D], F32, "ExternalOutput")
        x1T = k.dram("x1T", [D, T], BF16, "ExternalOutput") if with_T else None
        with ExitStack() as ph:
            k.stack = ph
            res = alloc_linear_res(k, "c")
            phase_outproj(k, "c", oT, nkt, w, x, z, res)
            barrier(k)
        k.stack = st
        with ExitStack() as ph:
            k.stack = ph
            ps_tr = [k.psum(f"lnpt{i}", [128, 8, 128], BF16) for i in range(2)]
            phase_ln(k, "l", z, g, b, x1, x1T, identb, ps_tr)
            barrier(k)
        k.stack = st
        k.finish("sp")
    return nc


NEG = -1.0e30


def phase_peer_scores(k, pre, x1T_d, wq_d, keysT_d, sc_d, xb):
    for kt in range(KT):
        k.dma("sp", xb[kt][:], x1T_d[kt * 128:(kt + 1) * 128, :], reads=[x1T_d], writes=[xb[kt]])
    keysT = k.sbuf(f"{pre}keysT", [128, 2048], F32)
    k.dma("sp", keysT[:], keysT_d[:, :], reads=[keysT_d], writes=[keysT])
    wst = [k.sbuf(f"{pre}wst{i}", [128, KT, 128], F32) for i in range(2)]
    wqb = [k.sbuf(f"{pre}wqb{i}", [128, KT, 128], BF16) for i in range(2)]
    qg = [k.sbuf(f"{pre}qg{i}", [128, T], F32) for i in range(2)]
    scst = [k.sbuf(f"{pre}scst{i}", [128, MT, 128], F32) for i in range(2)]
    ps = [k.psum(f"{pre}ps{i}", [128, 512], F32) for i in range(4)]
    ps2 = [k.psum(f"{pre}ps2{i}", [128, 512], F32) for i in range(2)]
    for g in range(16):
        ws, wb_, q_, sc_ = wst[g % 2], wqb[g % 2], qg[g % 2], scst[g % 2]
        k.dma("sp", ws[:], wq_d[:, g * 128:(g + 1) * 128].rearrange("(kt p) c -> p kt c", p=128), reads=[wq_d], writes=[ws])
        k.copy(("act", "dve")[g % 2], wb_, wb_[:], ws, ws[:])
        for tc in range(T // 512):
            p = ps[(g * 2 + tc) % 4]
            for kt in range(KT):
                k.op("pe", lambda e: e.matmul(p[:], lhsT=wb_[:, kt, :], rhs=xb[kt][:, tc * 512:(tc + 1) * 512],
                                              start=(kt == 0), stop=(kt == KT - 1)), reads=[wb_, xb[kt]], writes=[p])
            k.copy(("dve", "act")[tc % 2], q_, q_[:, tc * 512:(tc + 1) * 512], p, p[:])
        for m in range(MT):
            p2 = ps2[m % 2]
            k.op("pe", lambda e: e.matmul(p2[:, 0:128], lhsT=q_[:, m * 128:(m + 1) * 128], rhs=keysT[:, g * 128:(g + 1) * 128],
                                          start=True, stop=True), reads=[q_, keysT], writes=[p2])
            k.copy(("act", "dve")[m % 2], sc_, sc_[:, m, :], p2, p2[:, 0:128])
        k.dma("sp", sc_d[:, g * 128:(g + 1) * 128].rearrange("(m p) n -> p m n", p=128), sc_[:], reads=[sc_], writes=[sc_d])


def cast_rows_bg(k, pre, srcs_dsts, nrows, nbuf=2, ncol=1):
    W_ = D // ncol
    st = [k.sbuf(f"{pre}st{i}", [128, W_], F32) for i in range(nbuf)]
    cb = [k.sbuf(f"{pre}cb{i}", [128, W_], BF16) for i in range(nbuf)]
    jobs = [(src, dst, i, j) for (src, dst) in srcs_dsts for i in range(nrows // 128) for j in range(ncol)]
    state = {"n": 0}

    def step():
        n = state["n"]
        if n >= len(jobs):
            return False
        state["n"] += 1
        src, dst, i, j = jobs[n]
        s_, c_ = st[n % nbuf], cb[n % nbuf]
        k.dma("pool", s_[:], src[i * 128:(i + 1) * 128, j * W_:(j + 1) * W_], reads=[src], writes=[s_])
        k.copy("pool", c_, c_[:], s_, s_[:])
        k.dma("pool", dst[i * 128:(i + 1) * 128, j * W_:(j + 1) * W_], c_[:], reads=[c_], writes=[dst])
        return True
    return step


def phase_cast_rows(k, pre, srcs_dsts, nrows=16384):
    st = [k.sbuf(f"{pre}st{i}", [128, D], F32) for i in range(3)]
    cb = [k.sbuf(f"{pre}cb{i}", [128, D], BF16) for i in range(3)]
    jobs = [(src, dst, i) for (src, dst) in srcs_dsts for i in range(nrows // 128)]
    engs = ("act", "dve", "pool")

    def load(n):
        src, dst, i = jobs[n]
        k.dma("sp", st[n % 3][:], src[i * 128:(i + 1) * 128, :], reads=[src], writes=[st[n % 3]])

    for n in range(min(2, len(jobs))):
        load(n)
    for n in range(len(jobs)):
        if n + 2 < len(jobs):
            load(n + 2)
        src, dst, i = jobs[n]
        k.copy(engs[n % 3], cb[n % 3], cb[n % 3][:], st[n % 3], st[n % 3][:])
        k.dma("act", dst[i * 128:(i + 1) * 128, :], cb[n % 3][:], reads=[cb[n % 3]], writes=[dst])


def phase_peer_main(k, pre, x1_d, x1T_d, sc_d, u_d, v_d, pconst_d, identb_d, z_d, NG=4, mt=MT):
    ident_b = k.sbuf(f"{pre}ident", [128, 128], BF16)
    k.dma("sp", ident_b[:], identb_d[:, :], reads=[identb_d], writes=[ident_b])
    pc = k.sbuf(f"{pre}pc", [128, 48], F32)
    k.dma("sp", pc[:], pconst_d[:, :], reads=[pconst_d], writes=[pc])
    iota16, lo16, hi16 = pc[:, 0:16], pc[:, 16:32], pc[:, 32:48]
    sc = k.sbuf(f"{pre}sc", [128, 16, 128], F32)
    scr = k.sbuf(f"{pre}scr", [128, 16, 128], F32)
    sv = k.sbuf(f"{pre}sv", [128, 16, 16], F32)
    si = k.sbuf(f"{pre}si", [128, 16, 16], U32)
    sif = k.sbuf(f"{pre}sif", [128, 16, 16], F32)
    cand = k.sbuf(f"{pre}cand", [128, 8, 256], F32)
    cscr = k.sbuf(f"{pre}cscr", [128, 8, 256], F32)
    ts = k.sbuf(f"{pre}ts", [128, 8, 16], F32)
    tp = k.sbuf(f"{pre}tp", [128, 8, 16], U32)
    tpf = k.sbuf(f"{pre}tpf", [128, 8, 16], F32)
    w4a = k.sbuf(f"{pre}w4a", [128, 8, 16, 16], F32)
    w4b = k.sbuf(f"{pre}w4b", [128, 8, 16, 16], F32)
    r3a = k.sbuf(f"{pre}r3a", [128, 8, 16], F32)
    r3b = k.sbuf(f"{pre}r3b", [128, 8, 16], F32)
    r3c = k.sbuf(f"{pre}r3c", [128, 8, 16], F32)
    eidx_f = k.sbuf(f"{pre}eidxf", [128, 128], F32)
    eidx = k.sbuf(f"{pre}eidx", [128, 128], U32)
    gate = k.sbuf(f"{pre}gate", [128, 8, 16], F32)
    zsum = k.sbuf(f"{pre}zsum", [128, 8], F32)
    hids = [k.sbuf(f"{pre}hid{i}", [128, 128], F32) for i in range(2)]
    g1 = k.sbuf(f"{pre}g1", [128, 128], F32)
    wgts = [k.sbuf(f"{pre}wgt{i}", [128, 128], F32) for i in range(2)]
    xts = [k.sbuf(f"{pre}xt{i}", [128, D], F32) for i in range(2)]
    junk = k.sbuf(f"{pre}junk", [128, D], BF16)
    ug = [k.sbuf(f"{pre}ug{i}", [128, D], BF16) for i in range(NG)]
    vg = [k.sbuf(f"{pre}vg{i}", [128, D // 2], BF16) for i in range(NG)]
    dgs = [k.sbuf(f"{pre}dg{i}", [128, 128], BF16) for i in range(8)]
    zo = [k.sbuf(f"{pre}zo{i}", [128, 512], F32) for i in range(2)]
    xps = k.psum(f"{pre}xps", [128, D], BF16)
    acc = [k.psum(f"{pre}acc{i}", [128, 512], F32) for i in range(4)]

    def dve(fn, reads, writes):
        return k.op("dve", fn, reads=reads, writes=writes)

    eidx_t = [k.sbuf(f"{pre}eidx_t{i}", [128, 128], U32) for i in range(mt)]
    gate_t = [k.sbuf(f"{pre}gate_t{i}", [128, 128], F32) for i in range(mt)]
    evidx_t = [[k.sbuf(f"{pre}evidx_t{i}_{h}", [128, 128], U32) for h in range(2)] for i in range(mt)]

    def route(m):
        k.dma("sp", sc[:], sc_d[m * 128:(m + 1) * 128, :].rearrange("p (g n) -> p g n", g=16), reads=[sc_d], writes=[sc])
        for g in range(16):
            yield ('dve', lambda e: e.max(out=sv[:, g, 0:8], in_=sc[:, g, :]), [sc], [sv])
            yield ('dve', lambda e: e.max_index(out=si[:, g, 0:8], in_max=sv[:, g, 0:8], in_values=sc[:, g, :]), [sc, sv], [si])
            yield ('dve', lambda e: e.match_replace(out=scr[:, g, :], in_to_replace=sv[:, g, 0:8], in_values=sc[:, g, :],
                                          imm_value=NEG), [sc, sv], [scr])
            yield ('dve', lambda e: e.max(out=sv[:, g, 8:16], in_=scr[:, g, :]), [scr], [sv])
            yield ('dve', lambda e: e.max_index(out=si[:, g, 8:16], in_max=sv[:, g, 8:16], in_values=scr[:, g, :]), [scr, sv], [si])
        yield ('dve', lambda e: e.tensor_copy(out=sif[:], in_=si[:]), [si], [sif])
        sv4 = sv[:].rearrange("p (h c) r -> p h c r", c=2)
        sif4 = sif[:].rearrange("p (h c) r -> p h c r", c=2)
        c4 = cand[:].rearrange("p h (a b) -> p h a b", a=16)
        yield ('dve', lambda e: e.tensor_tensor(out=c4, in0=sv4[:, :, 0, :].unsqueeze(3).to_broadcast([128, 8, 16, 16]),
                                      in1=sv4[:, :, 1, :].unsqueeze(2).to_broadcast([128, 8, 16, 16]), op=ALU.add),
            [sv], [cand])
        for h in range(8):
            yield ('dve', lambda e: e.max(out=ts[:, h, 0:8], in_=cand[:, h, :]), [cand], [ts])
            yield ('dve', lambda e: e.max_index(out=tp[:, h, 0:8], in_max=ts[:, h, 0:8], in_values=cand[:, h, :]), [cand, ts], [tp])
            yield ('dve', lambda e: e.match_replace(out=cscr[:, h, :], in_to_replace=ts[:, h, 0:8], in_values=cand[:, h, :],
                                          imm_value=NEG), [cand, ts], [cscr])
            yield ('dve', lambda e: e.max(out=ts[:, h, 8:16], in_=cscr[:, h, :]), [cscr], [ts])
            yield ('dve', lambda e: e.max_index(out=tp[:, h, 8:16], in_max=ts[:, h, 8:16], in_values=cscr[:, h, :]), [cscr, ts], [tp])
        yield ('dve', lambda e: e.tensor_copy(out=tpf[:], in_=tp[:]), [tp], [tpf])
        tpf4 = tpf[:].unsqueeze(3).to_broadcast([128, 8, 16, 16])
        lo4 = lo16.unsqueeze(1).unsqueeze(1).to_broadcast([128, 8, 16, 16])
        hi4 = hi16.unsqueeze(1).unsqueeze(1).to_broadcast([128, 8, 16, 16])
        io4 = iota16.unsqueeze(1).unsqueeze(1).to_broadcast([128, 8, 16, 16])
        yield ('dve', lambda e: e.tensor_tensor(out=w4a[:], in0=tpf4, in1=lo4, op=ALU.is_ge), [tpf, pc], [w4a])
        yield ('dve', lambda e: e.tensor_tensor(out=w4b[:], in0=tpf4, in1=hi4, op=ALU.is_ge), [tpf, pc], [w4b])
        yield ('dve', lambda e: e.tensor_tensor(out=w4a[:], in0=w4a[:], in1=w4b[:], op=ALU.subtract), [w4a, w4b], [w4a])
        yield ('dve', lambda e: e.tensor_tensor(out=w4b[:], in0=w4a[:], in1=sif4[:, :, 0, :].unsqueeze(2).to_broadcast([128, 8, 16, 16]),
                                      op=ALU.mult), [w4a, sif], [w4b])
        yield ('dve', lambda e: e.tensor_reduce(out=r3a[:], in_=w4b[:], axis=AX.X, op=ALU.add), [w4b], [r3a])
        yield ('dve', lambda e: e.tensor_tensor(out=w4b[:], in0=w4a[:], in1=lo4, op=ALU.mult), [w4a, pc], [w4b])
        yield ('dve', lambda e: e.tensor_reduce(out=r3b[:], in_=w4b[:], axis=AX.X, op=ALU.add), [w4b], [r3b])
        yield ('dve', lambda e: e.tensor_tensor(out=r3b[:], in0=tpf[:], in1=r3b[:], op=ALU.subtract), [tpf, r3b], [r3b])
        yield ('dve', lambda e: e.tensor_tensor(out=w4a[:], in0=r3b[:].unsqueeze(3).to_broadcast([128, 8, 16, 16]), in1=io4,
                                      op=ALU.is_equal), [r3b, pc], [w4a])
        yield ('dve', lambda e: e.tensor_tensor(out=w4b[:], in0=w4a[:], in1=sif4[:, :, 1, :].unsqueeze(2).to_broadcast([128, 8, 16, 16]),
                                      op=ALU.mult), [w4a, sif], [w4b])
        yield ('dve', lambda e: e.tensor_reduce(out=r3c[:], in_=w4b[:], axis=AX.X, op=ALU.add), [w4b], [r3c])
        ef3 = eidx_f[:].rearrange("p (h r) -> p h r", h=8)
        yield ('dve', lambda e: e.scalar_tensor_tensor(out=ef3, in0=r3a[:], scalar=128.0, in1=r3c[:], op0=ALU.mult, op1=ALU.add),
            [r3a, r3c], [eidx_f])
        yield ('dve', lambda e: e.tensor_copy(out=eidx[:], in_=eidx_f[:]), [eidx_f], [eidx])
        yield ('dve', lambda e: e.tensor_tensor(out=gate[:], in0=ts[:], in1=ts[:, :, 0:1].to_broadcast([128, 8, 16]), op=ALU.subtract),
            [ts], [gate])
        yield ('act', lambda e: e.activation(out=gate[:], in_=gate[:], func=AF.Exp), [gate], [gate])
        yield ('dve', lambda e: e.tensor_reduce(out=zsum[:], in_=gate[:], axis=AX.X, op=ALU.add), [gate], [zsum])
        yield ('dve', lambda e: e.reciprocal(out=zsum[:], in_=zsum[:]), [zsum], [zsum])
        yield ('dve', lambda e: e.tensor_tensor(out=gate[:], in0=gate[:], in1=zsum[:].unsqueeze(2).to_broadcast([128, 8, 16]), op=ALU.mult),
            [gate, zsum], [gate])
        yield ('dve', lambda e: e.tensor_copy(out=eidx_t[m][:], in_=eidx[:]), [eidx], [eidx_t[m]])
        yield ('dve', lambda e: e.tensor_scalar(out=eidx_f[:], in0=eidx_f[:], scalar1=2.0, scalar2=None, op0=ALU.mult), [eidx_f], [eidx_f])
        yield ('dve', lambda e: e.tensor_copy(out=evidx_t[m][0][:], in_=eidx_f[:]), [eidx_f], [evidx_t[m][0]])
        yield ('dve', lambda e: e.tensor_scalar(out=eidx_f[:], in0=eidx_f[:], scalar1=1.0, scalar2=None, op0=ALU.add), [eidx_f], [eidx_f])
        yield ('dve', lambda e: e.tensor_copy(out=evidx_t[m][1][:], in_=eidx_f[:]), [eidx_f], [evidx_t[m][1]])
        yield ('dve', lambda e: e.tensor_copy(out=gate_t[m][:], in_=gate[:].rearrange("p h r -> p (h r)")), [gate], [gate_t[m]])


    def advance(gen, n=1):
        for _ in range(n):
            it = next(gen, None)
            if it is None:
                return False
            k.op(it[0], it[1], reads=it[2], writes=it[3])
        return True

    xTt = [k.sbuf(f"{pre}xTt{i}", [128, KT, 128], BF16) for i in range(2)]

    def load_x(m):
        xt = xts[m % 2]
        k.dma("sp", xt[:], x1_d[m * 128:(m + 1) * 128, :], reads=[x1_d], writes=[xt])
        xT_ = xTt[m % 2]
        k.dma("sp", xT_[:], x1T_d[:, m * 128:(m + 1) * 128].rearrange("(kt p) t -> p kt t", p=128), reads=[x1T_d], writes=[xT_])
        for kt in range(KT):
            k.op("pe", lambda e: e.transpose(out=xps[:, kt * 128:(kt + 1) * 128], in_=xT_[:, kt, :], identity=ident_b[:]),
                 reads=[xT_, ident_b], writes=[xps], sig=(kt == KT - 1))

    def u_slot(m, s):
        ub = ug[s % NG]
        hid = hids[m % 2]
        k.gather(ub[:], u_d[:, :], eidx_t[m][:, s:s + 1], reads=[eidx_t[m], u_d], writes=[ub])
        dve(lambda e: e.scalar_tensor_tensor(out=junk[:], in0=ub[:], scalar=1.0, in1=xps[:], op0=ALU.mult, op1=ALU.mult,
                                             accum_out=hid[:, s:s + 1]), [ub, xps], [junk, hid])

    def gelu_wgt(m):
        hid, wgt = hids[m % 2], wgts[m % 2]
        dve(lambda e: e.tensor_tensor(out=g1[:], in0=hid[:], in1=hid[:], op=ALU.mult), [hid], [g1])
        dve(lambda e: e.tensor_scalar(out=g1[:], in0=g1[:], scalar1=0.044715 * 1.5957691216057308,
                                      scalar2=1.5957691216057308, op0=ALU.mult, op1=ALU.add), [g1], [g1])
        dve(lambda e: e.tensor_tensor(out=g1[:], in0=g1[:], in1=hid[:], op=ALU.mult), [g1, hid], [g1])
        k.op("act", lambda e: e.activation(out=g1[:], in_=g1[:], func=AF.Sigmoid), reads=[g1], writes=[g1])
        dve(lambda e: e.tensor_tensor(out=g1[:], in0=g1[:], in1=hid[:], op=ALU.mult), [g1, hid], [g1])
        dve(lambda e: e.tensor_tensor(out=wgt[:], in0=g1[:], in1=gate_t[m][:], op=ALU.mult), [g1, gate_t[m]], [wgt])

    def run_all(gen):
        while advance(gen):
            pass

    run_all(route(0))
    load_x(0)
    r1 = route(1) if mt > 1 else None
    for s in range(128):
        u_slot(0, s)
        if r1 is not None:
            advance(r1, 2)
    if r1 is not None:
        run_all(r1)
    gelu_wgt(0)
    HD2 = D // 2
    for m in range(mt):
        xt, wgt = xts[m % 2], wgts[m % 2]
        has_next = m + 1 < mt
        r2 = route(m + 2) if m + 2 < mt else None
        if has_next:
            load_x(m + 1)
        for half in range(2):
            for s in range(128):
                vb_ = vg[s % NG]
                k.gather(vb_[:], v_d[:, :].rearrange("e (h c) -> (e h) c", h=2), evidx_t[m][half][:, s:s + 1], reads=[evidx_t[m][half], v_d], writes=[vb_])
                d_ = dgs[s % 8]
                dve(lambda e: e.tensor_scalar(out=d_[:], in0=ident_b[:], scalar1=wgt[:, s:s + 1], scalar2=None,
                                              op0=ALU.mult), [ident_b, wgt], [d_])
                for c in range(4):
                    k.op("pe", lambda e: e.matmul(acc[c][:], lhsT=d_[:], rhs=vb_[:, c * 512:(c + 1) * 512],
                                                  start=(s == 0), stop=(s == 127)), reads=[d_, vb_], writes=[acc[c]],
                         sig=(c == 3))
                if half == 0 and has_next:
                    u_slot(m + 1, s)
                if half == 1 and r2 is not None:
                    advance(r2, 2)
            for c in range(4):
                z_ = zo[c % 2]
                col = half * HD2 + c * 512
                dve(lambda e: e.scalar_tensor_tensor(out=z_[:], in0=xt[:, col:col + 512], scalar=ALPHA, in1=acc[c][:],
                                                     op0=ALU.mult, op1=ALU.add), [xt, acc[c]], [z_])
                k.dma("sp", z_d[m * 128:(m + 1) * 128, col:col + 512], z_[:], reads=[z_], writes=[z_d])
        if r2 is not None:
            run_all(r2)
        if has_next:
            gelu_wgt(m + 1)


def peer_consts():
    pc = np.zeros((128, 48), np.float32)
    pc[:, 0:16] = np.arange(16)
    pc[:, 16:32] = 16 * np.arange(16)
    pc[:, 32:48] = 16 * np.arange(16) + 16
    return pc


def build_peer(final, mt=MT):
    nc = bass.Bass("TRN2", target_bir_lowering=False)
    with ExitStack() as st:
        k = K(nc, st)
        x1 = k.dram("x1", [T, D], F32, "ExternalInput")
        x1T = k.dram("x1T", [D, T], BF16, "ExternalInput")
        wq = k.dram("wq", [D, 2048], F32, "ExternalInput")
        keysT = k.dram("keysT", [128, 2048], F32, "ExternalInput")
        ub16 = k.dram("u", [16384, D], BF16, "ExternalInput")
        vb16 = k.dram("v", [16384, D], BF16, "ExternalInput")
        g = k.dram("g", [1, D], F32, "ExternalInput")
        b = k.dram("b", [1, D], F32, "ExternalInput")
        pconst = k.dram("pconst", [128, 48], F32, "ExternalInput")
        identb = k.dram("identb", [128, 128], BF16, "ExternalInput")
        sc = k.dram("sc", [T, 2048], F32, "Internal")
        z = k.dram("z", [T, D], F32, "Internal")
        x2 = k.dram("x2", [T, D], F32, "ExternalOutput")
        x2T = None if final else k.dram("x2T", [D, T], BF16, "ExternalOutput")
        with ExitStack() as ph:
            k.stack = ph
            xb = [k.sbuf(f"pxb{i}", [128, T], BF16) for i in range(KT)]
            phase_peer_scores(k, "ps", x1T, wq, keysT, sc, xb)
            barrier(k)
        with ExitStack() as ph:
            k.stack = ph
            phase_peer_main(k, "pm", x1, x1T, sc, ub16, vb16, pconst, identb, z, mt=mt)
            barrier(k)
        with ExitStack() as ph:
            k.stack = ph
            ps_tr = [k.psum(f"lnpt{i}", [128, 8, 128], BF16) for i in range(2)]
            phase_ln(k, "pl", z, g, b, x2, x2T, identb, ps_tr, mt=mt)
            barrier(k)
        k.stack = st
        k.finish("sp")
    return nc


LAMBDA_INIT = 0.8 - 0.6 * math.exp(-0.3 * 1)
NKT_FULL = SEQ // 128


def dense_unit(k, kT, qT, q0, vaug, W, accs, ps_s, pts, cnt, scale, LA=2):
    slots = {}

    def score(kt):
        ps = ps_s[cnt["s"] % len(ps_s)]
        pt = pts[cnt["s"] % len(pts)]
        cnt["s"] += 1
        slots[kt] = pt
        k.op("pe", lambda g: g.matmul(ps[:], lhsT=kT[:, kt * 128:(kt + 1) * 128], rhs=qT[:, q0:q0 + 512],
                                      start=True, stop=True), reads=[kT, qT], writes=[ps])
        k.op("act", lambda g: g.activation(out=pt[:], in_=ps[:], func=AF.Exp, scale=scale), reads=[ps], writes=[pt])

    for kt in range(min(LA, NKT_FULL)):
        score(kt)
    for kt in range(NKT_FULL):
        if kt + LA < NKT_FULL:
            score(kt + LA)
        pt = slots.pop(kt)
        for j in range(4):
            k.op("pe", lambda g: g.matmul(accs[j][:, 0:W], lhsT=pt[:, j * 128:(j + 1) * 128], rhs=vaug[:, kt, 0:W],
                                          start=(kt == 0), stop=(kt == NKT_FULL - 1)), reads=[pt, vaug], writes=[accs[j]])


def phase_attn_cd(k, pre, qc_d, kc_d, vc_d, qd_d, kd_d, vd_d, lam_d, subg_d, identb_d, oT_d, nchunks=SEQ // 512):
    scale = 1.0 / math.sqrt(128.0)
    ident_b = k.sbuf(f"{pre}ident", [128, 128], BF16)
    k.dma("sp", ident_b[:], identb_d[:, :], reads=[identb_d], writes=[ident_b])
    lam_t = k.sbuf(f"{pre}lamt", [128, 4, 128], F32)
    k.dma("sp", lam_t[:], lam_d[:, :].unsqueeze(0).to_broadcast([128, 4, 128]), reads=[lam_d], writes=[lam_t])
    lprod = k.sbuf(f"{pre}lprod", [128, 2, 128], F32)
    lsum = k.sbuf(f"{pre}lsum", [128, 2], F32)
    neglam = k.sbuf(f"{pre}neglam", [128, 1], F32)
    lam4 = lam_t[:].rearrange("p (a b) d -> p a b d", b=2)
    k.op("dve", lambda g: g.tensor_tensor(out=lprod[:], in0=lam4[:, :, 0, :], in1=lam4[:, :, 1, :], op=ALU.mult),
         reads=[lam_t], writes=[lprod])
    k.op("dve", lambda g: g.tensor_reduce(out=lsum[:], in_=lprod[:], axis=AX.X, op=ALU.add), reads=[lprod], writes=[lsum])
    k.op("act", lambda g: g.activation(out=lsum[:], in_=lsum[:], func=AF.Exp), reads=[lsum], writes=[lsum])
    k.op("dve", lambda g: g.scalar_tensor_tensor(out=neglam[:], in0=lsum[:, 1:2], scalar=-LAMBDA_INIT, in1=lsum[:, 0:1],
                                                 op0=ALU.add, op1=ALU.subtract), reads=[lsum], writes=[neglam])
    subg = k.sbuf(f"{pre}subg", [128, 256], F32)
    k.dma("sp", subg[:], subg_d[0:1, :].to_broadcast([128, 256]), reads=[subg_d], writes=[subg])
    k.op("dve", lambda g: g.tensor_scalar(out=subg[:], in0=subg[:], scalar1=1.0 - LAMBDA_INIT, scalar2=None, op0=ALU.mult),
         reads=[subg], writes=[subg])

    kTs = [k.sbuf(f"{pre}kT{i}", [128, SEQ], BF16) for i in range(2)]
    qTs = [k.sbuf(f"{pre}qT{i}", [128, SEQ], BF16) for i in range(2)]
    vcs = k.sbuf(f"{pre}vc", [128, NKT_FULL, 257], BF16)
    vds = k.sbuf(f"{pre}vd", [128, NKT_FULL, 129], BF16)
    ps_s = [k.psum(f"{pre}pss{i}", [128, 512], F32) for i in range(3)]
    pts = [k.sbuf(f"{pre}pt{i}", [128, 512], BF16) for i in range(4)]
    accs = [k.psum(f"{pre}acc{i}", [128, 512], F32) for i in range(4)]
    ps_t = [k.psum(f"{pre}pst{i}", [128, 8, 128], BF16) for i in range(1)]
    o1s = k.sbuf(f"{pre}o1s", [128, nchunks * 4, 256], F32)
    den = k.sbuf(f"{pre}den", [128, 1], F32)
    o2 = k.sbuf(f"{pre}o2", [128, 256], F32)
    junk = k.sbuf(f"{pre}junk", [128, 256], F32)
    ssq = k.sbuf(f"{pre}ssq", [128, 1], F32)
    ob = [k.sbuf(f"{pre}ob{i}", [128, 256], BF16) for i in range(2)]
    oTst = [k.sbuf(f"{pre}oTst{i}", [128, 2, 512], BF16) for i in range(2)]
    cnt = {"s": 0, "o": 0, "t": 0}

    k.dma("sp", vcs[:], vc_d[:, :].rearrange("(j p) c -> p j c", p=128), reads=[vc_d], writes=[vcs])
    k.dma("sp", vds[:], vd_d[:, :].rearrange("(j p) c -> p j c", p=128), reads=[vd_d], writes=[vds])

    def recip_den(acc, W):
        k.op("dve", lambda g: g.reciprocal(out=den[:], in_=acc[:, W - 1:W]), reads=[acc], writes=[den])

    def emit_T(obuf, nblk, blk0, q0, j, last):
        pt_ = ps_t[cnt["t"] % len(ps_t)]
        cnt["t"] += 1
        for b_ in range(nblk):
            k.op("pe", lambda g: g.transpose(out=pt_[:, b_, :], in_=obuf[:, b_ * 128:(b_ + 1) * 128], identity=ident_b[:]),
                 reads=[obuf, ident_b], writes=[pt_])
        stg = oTst[(q0 // 512) % 2]
        k.copy("dve", stg, stg[:, 0:nblk, j * 128:(j + 1) * 128], pt_, pt_[:, 0:nblk, :])
        if last:
            k.dma("sp", oT_d[blk0:blk0 + nblk, :, q0:q0 + 512].rearrange("b p t -> p b t"), stg[:, 0:nblk, :],
                  reads=[stg], writes=[oT_d])

    for mp in range(2):
        kT, qT = kTs[mp], qTs[mp]
        k.dma("sp", kT[:], kc_d[mp], reads=[kc_d], writes=[kT])
        k.dma("sp", qT[:], qc_d[mp], reads=[qc_d], writes=[qT])
        for ch in range(nchunks):
            q0 = ch * 512
            dense_unit(k, kT, qT, q0, vcs, 257, accs, ps_s, pts, cnt, scale)
            for j in range(4):
                acc = accs[j]
                recip_den(acc, 257)
                if mp == 0:
                    k.op("dve", lambda g: g.tensor_scalar(out=o1s[:, ch * 4 + j, :], in0=acc[:, 0:256], scalar1=den[:, 0:1],
                                                          scalar2=None, op0=ALU.mult), reads=[acc, den], writes=[o1s])
                else:
                    k.op("dve", lambda g: g.tensor_scalar(out=o2[:], in0=acc[:, 0:256], scalar1=den[:, 0:1], scalar2=None,
                                                          op0=ALU.mult), reads=[acc, den], writes=[o2])
                    k.op("dve", lambda g: g.scalar_tensor_tensor(out=o2[:], in0=o2[:], scalar=neglam[:, 0:1],
                                                                 in1=o1s[:, ch * 4 + j, :], op0=ALU.mult, op1=ALU.add),
                         reads=[o2, neglam, o1s], writes=[o2])
                    k.op("act", lambda g: g.activation(out=junk[:], in_=o2[:], func=AF.Square, accum_out=ssq[:]),
                         reads=[o2], writes=[junk, ssq])
                    k.op("dve", lambda g: g.tensor_scalar(out=ssq[:], in0=ssq[:], scalar1=1.0 / 256, scalar2=1e-6,
                                                          op0=ALU.mult, op1=ALU.add), reads=[ssq], writes=[ssq])
                    k.op("act", lambda g: g.activation(out=ssq[:], in_=ssq[:], func=AF.Sqrt), reads=[ssq], writes=[ssq])
                    k.op("dve", lambda g: g.reciprocal(out=ssq[:], in_=ssq[:]), reads=[ssq], writes=[ssq])
                    o_ = ob[cnt["o"] % 2]
                    cnt["o"] += 1
                    k.op("dve", lambda g: g.scalar_tensor_tensor(out=o_[:], in0=o2[:], scalar=ssq[:, 0:1], in1=subg[:],
                                                                 op0=ALU.mult, op1=ALU.mult), reads=[o2, ssq, subg], writes=[o_])
                    emit_T(o_, 2, 0, q0, j, j == 3)
    kT = kTs[0]
    k.dma("sp", kT[:], kd_d[:, :], reads=[kd_d], writes=[kT])
    for hq in range(2):
        qT = qTs[hq]
        k.dma("sp", qT[:], qd_d[hq], reads=[qd_d], writes=[qT])
        for ch in range(nchunks):
            q0 = ch * 512
            dense_unit(k, kT, qT, q0, vds, 129, accs, ps_s, pts, cnt, scale)
            for j in range(4):
                acc = accs[j]
                recip_den(acc, 129)
                o_ = ob[cnt["o"] % 2]
                cnt["o"] += 1
                k.op("dve", lambda g: g.tensor_scalar(out=o_[:, 0:128], in0=acc[:, 0:128], scalar1=den[:, 0:1], scalar2=None,
                                                      op0=ALU.mult), reads=[acc, den], writes=[o_])
                emit_T(o_, 1, 2 + hq, q0, j, j == 3)


def build_attn_cd(nchunks=SEQ // 512):
    nc = bass.Bass("TRN2", target_bir_lowering=False)
    with ExitStack() as st:
        k = K(nc, st)
        qc = k.dram("qc", [2, 128, SEQ], BF16, "ExternalInput")
        kc = k.dram("kc", [2, 128, SEQ], BF16, "ExternalInput")
        vc = k.dram("vc", [SEQ, 257], BF16, "ExternalInput")
        qd = k.dram("qd", [2, 128, SEQ], BF16, "ExternalInput")
        kd = k.dram("kd", [128, SEQ], BF16, "ExternalInput")
        vd = k.dram("vd", [SEQ, 129], BF16, "ExternalInput")
        lam = k.dram("lam", [4, 128], F32, "ExternalInput")
        subg = k.dram("subg", [1, 256], F32, "ExternalInput")
        identb = k.dram("identb", [128, 128], BF16, "ExternalInput")
        oT = k.dram("oT", [4, 128, SEQ], BF16, "ExternalOutput")
        phase_attn_cd(k, "e", qc, kc, vc, qd, kd, vd, lam, subg, identb, oT, nchunks=nchunks)
        k.finish("sp")
    return nc


def _run(nc, in_maps):
    res = run_bass_kernel_spmd(nc, in_maps, core_ids=list(range(NCORES)))
    return res.results


def _aug_ones(v):
    out = np.zeros(v.shape[:-1] + (v.shape[-1] + 1,), dtype=v.dtype)
    out[..., :-1] = v
    out[..., -1] = 1.0
    return out


def kernel(x, w_in_ab, sink_a, w_out_ab, w_in_cd, lam_q1, lam_k1, lam_q2, lam_k2, subln_g, q_norm_g, k_norm_g,
           w_out_cd, ln_mix_g, ln_mix_b, peer_wq, peer_keys, peer_u, peer_v, ln_ffn_g, ln_ffn_b):
    f32 = lambda a: np.ascontiguousarray(np.asarray(a, dtype=np.float32))
    x = f32(x)[0]
    ccp, ssp, cca, ssa = rope_tables()
    identb = np.eye(128, dtype=np.float32).astype(NP_BF16)
    masks = band_masks()
    pconst = peer_consts()
    cs = [slice(c * T, (c + 1) * T) for c in range(NCORES)]

    def keysT_of(l):
        return np.ascontiguousarray(f32(peer_keys)[l].reshape(16, 128, 128).transpose(2, 0, 1).reshape(128, 2048))

    w0 = f32(w_in_ab)[0]
    RW = 16384 // NCORES
    pu, pv = f32(peer_u), f32(peer_v)
    r = _run(build_inproj(0, conv=True), [{"xT": np.ascontiguousarray(x[cs[c]].T), "w": w0, "identb": identb,
                                           "ccp": ccp[cs[c]], "ssp": ssp[cs[c]],
                                           "u0p": pu[0, c * RW:(c + 1) * RW], "v0p": pv[0, c * RW:(c + 1) * RW],
                                           "u1p": pu[1, c * RW:(c + 1) * RW], "v1p": pv[1, c * RW:(c + 1) * RW]}
                                          for c in range(NCORES)])
    qkT_all = [r[c]["qkT"] for c in range(NCORES)]
    tabs16 = {nm: np.concatenate([r[c][nm + "b"] for c in range(NCORES)], axis=0) for nm in ("u0", "v0", "u1", "v1")}
    ext = host_ext_kv(qkT_all, [r[c]["vtm"] for c in range(NCORES)], [16, 17, 18, 19] + list(range(44, 68)), None)
    sink = f32(sink_a)[0].reshape(1, 16)
    r = _run(build_attn_ab(), [{"qT": qkT_all[c], "kTe": ext[c][0], "vaug": ext[c][1], "sink": sink, "masks": masks,
                                "identb": identb} for c in range(NCORES)])
    oT = [r[c]["oT"] for c in range(NCORES)]
    del ext, qkT_all
    g0, b0 = f32(ln_mix_g)[0:1], f32(ln_mix_b)[0:1]
    wo0 = f32(w_out_ab)[0]
    r = _run(build_outproj_ln(24, True), [{"oT": oT[c], "w": wo0, "x": x[cs[c]], "g": g0, "b": b0, "identb": identb}
                                          for c in range(NCORES)])
    x1 = [r[c]["x1"] for c in range(NCORES)]
    x1T = [r[c]["x1T"] for c in range(NCORES)]
    u0, v0 = tabs16["u0"], tabs16["v0"]
    wq0 = f32(peer_wq)[0]
    kT0 = keysT_of(0)
    gf0, bf0 = f32(ln_ffn_g)[0:1], f32(ln_ffn_b)[0:1]
    r = _run(build_peer(False), [{"x1": x1[c], "x1T": x1T[c], "wq": wq0, "keysT": kT0, "u": u0, "v": v0, "g": gf0, "b": bf0,
                                  "pconst": pconst, "identb": identb} for c in range(NCORES)])
    x2 = [r[c]["x2"] for c in range(NCORES)]
    x2T = [r[c]["x2T"] for c in range(NCORES)]
    w1 = f32(w_in_cd)[0]
    qg, kg = f32(q_norm_g)[0:1], f32(k_norm_g)[0:1]
    r = _run(build_inproj(1, x_bf16=True), [{"xT": x2T[c], "w": w1, "identb": identb, "ccp": ccp[cs[c]], "ssp": ssp[cs[c]],
                                             "cca": cca[cs[c]], "ssa": ssa[cs[c]], "qg": qg, "kg": kg} for c in range(NCORES)])
    qk = np.concatenate([r[c]["qkT"] for c in range(NCORES)], axis=2)
    vt = np.concatenate([r[c]["vtm"] for c in range(NCORES)], axis=0)
    lam = np.concatenate([f32(lam_q1)[0:1], f32(lam_k1)[0:1], f32(lam_q2)[0:1], f32(lam_k2)[0:1]], 0)
    subg = f32(subln_g)[0:1]
    maps = []
    for c in range(NCORES):
        maps.append({"qc": np.ascontiguousarray(qk[2 * c:2 * c + 2]), "kc": np.ascontiguousarray(qk[16 + 2 * c:18 + 2 * c]),
                     "vc": _aug_ones(vt[:, 256 * c:256 * c + 256]), "qd": np.ascontiguousarray(qk[32 + 2 * c:34 + 2 * c]),
                     "kd": np.ascontiguousarray(qk[48 + c // 2]),
                     "vd": _aug_ones(vt[:, 2048 + 128 * (c // 2):2048 + 128 * (c // 2) + 128]),
                     "lam": lam, "subg": subg, "identb": identb})
    r = _run(build_attn_cd(), maps)
    del qk, vt, maps
    ofull = np.zeros((32, 128, SEQ), dtype=NP_BF16)
    for c in range(NCORES):
        o = r[c]["oT"]
        ofull[2 * c] = o[0]
        ofull[2 * c + 1] = o[1]
        ofull[16 + 2 * c] = o[2]
        ofull[16 + 2 * c + 1] = o[3]
    g1, b1 = f32(ln_mix_g)[1:2], f32(ln_mix_b)[1:2]
    wo1 = f32(w_out_cd)[0]
    r = _run(build_outproj_ln(32, True), [{"oT": np.ascontiguousarray(ofull[:, :, cs[c]]), "w": wo1, "x": x2[c], "g": g1,
                                           "b": b1, "identb": identb} for c in range(NCORES)])
    x3 = [r[c]["x1"] for c in range(NCORES)]
    x3T = [r[c]["x1T"] for c in range(NCORES)]
    u1, v1 = tabs16["u1"], tabs16["v1"]
    gf1, bf1 = f32(ln_ffn_g)[1:2], f32(ln_ffn_b)[1:2]
    r = _run(build_peer(True), [{"x1": x3[c], "x1T": x3T[c], "wq": f32(peer_wq)[1], "keysT": keysT_of(1), "u": u1, "v": v1,
                                 "g": gf1, "b": bf1, "pconst": pconst, "identb": identb} for c in range(NCORES)])
    out = np.concatenate([r[c]["x2"] for c in range(NCORES)], axis=0)
    return out[None].astype(np.float32)
```

```python
import math
from contextlib import ExitStack

import numpy as np
import ml_dtypes

import concourse.bass as bass
import concourse.mybir as mybir
from concourse.bass_utils import run_bass_kernel_spmd

F32 = mybir.dt.float32
BF16 = mybir.dt.bfloat16
I32 = mybir.dt.int32
U32 = mybir.dt.uint32
AF = mybir.ActivationFunctionType
ALU = mybir.AluOpType
AX = mybir.AxisListType

NCORES = 8
SEQ = 8192
D = 4096
T = SEQ // NCORES
MT = T // 128
KT = D // 128
HD = 128
SEM_LIMIT = 2000
NP_BF16 = ml_dtypes.bfloat16
SAME_ENGINE_WAIT = {"pe": False, "dve": True, "act": True, "pool": True, "sp": False}


class Buf:
    __slots__ = ("t", "lw", "rd", "name")

    def __init__(self, t, name=""):
        self.t = t
        self.lw = None
        self.rd = []
        self.name = name

    def __getitem__(self, idx):
        return self.t[idx]


class K:
    def __init__(self, nc, stack):
        self.nc = nc
        self.stack = stack
        self.sem_stack = stack
        self.eng = {"pe": nc.tensor, "dve": nc.vector, "act": nc.scalar, "pool": nc.gpsimd, "sp": nc.sync}
        self.cur = {}
        self.seen = {}
        self.nsem = 0
        self.dma_rr = {}
        self.same_engine_wait = dict(SAME_ENGINE_WAIT)
        self.n_inst = 0
        self.pending = {}

    def sbuf(self, name, shape, dtype):
        return Buf(self.stack.enter_context(self.nc.sbuf_tensor(name, list(shape), dtype)), name)

    def psum(self, name, shape, dtype=F32):
        return Buf(self.stack.enter_context(self.nc.psum_tensor(name, list(shape), dtype)), name)

    def dram(self, name, shape, dtype, kind="Internal"):
        return Buf(self.nc.dram_tensor(name, list(shape), dtype, kind=kind).ap(), name)

    def new_sem(self, name):
        self.nsem += 1
        return self.sem_stack.enter_context(self.nc.semaphore(f"{name}_{self.nsem}"))

    def _wait(self, e, tok):
        if tok is None:
            return
        sem, val, src = tok
        if src == e and not self.same_engine_wait[e]:
            return
        seen = self.seen.setdefault(e, {})
        kk = id(sem)
        if seen.get(kk, 0) >= val:
            return
        self.eng[e].wait_ge(sem, val)
        seen[kk] = val

    def _deps(self, e, reads, writes):
        for b in reads:
            self._wait(e, b.lw)
        for b in writes:
            self._wait(e, b.lw)
            for r in b.rd:
                self._wait(e, r)

    def _commit(self, tok, reads, writes):
        for b in reads:
            b.rd.append(tok)
            if len(b.rd) > 64:
                b.rd = b.rd[-48:]
        for b in writes:
            b.lw = tok
            b.rd = []

    def op(self, e, fn, reads=(), writes=(), sig=True):
        self._deps(e, reads, writes)
        if not sig:
            pr, pw = self.pending.setdefault(e, ([], []))
            for b in reads:
                if b not in pr:
                    pr.append(b)
            for b in writes:
                if b not in pw:
                    pw.append(b)
            fn(self.eng[e])
            self.n_inst += 1
            return None
        if e in self.pending:
            pr, pw = self.pending.pop(e)
            reads = list(reads) + [b for b in pr if b not in reads]
            writes = list(writes) + [b for b in pw if b not in writes]
        c = self.cur.get(e)
        if c is None or c[1] >= SEM_LIMIT:
            c = [self.new_sem("s" + e), 0]
            self.cur[e] = c
        ins = fn(self.eng[e])
        c[1] += 1
        ins.then_inc(c[0], 1)
        tok = (c[0], c[1], e)
        self._commit(tok, reads, writes)
        self.n_inst += 1
        return tok

    def _dma_slot(self, q):
        R = 8
        ring = self.dma_rr.setdefault(q, {"sems": [], "i": 0})
        i = ring["i"]
        ring["i"] += 1
        slot = i % R
        if len(ring["sems"]) <= slot:
            ring["sems"].append([self.new_sem("d" + q), 0, None])
        s = ring["sems"][slot]
        self._wait(q, s[2])
        if s[1] >= SEM_LIMIT:
            s[0] = self.new_sem("d" + q)
            s[1] = 0
            s[2] = None
        return s

    def dma(self, q, out, in_, reads=(), writes=(), **kw):
        s = self._dma_slot(q)
        self._deps(q, reads, writes)
        ins = self.eng[q].dma_start(out=out, in_=in_, **kw)
        s[1] += 16
        ins.then_inc(s[0], 16)
        tok = (s[0], s[1], "dma")
        s[2] = tok
        self._commit(tok, reads, writes)
        self.n_inst += 1
        return tok

    def gather(self, out, in_, idx_ap, reads=(), writes=(), **kw):
        q = "pool"
        s = self._dma_slot(q)
        self._deps(q, reads, writes)
        ins = self.eng[q].indirect_dma_start(out=out, out_offset=None, in_=in_,
                                             in_offset=bass.IndirectOffsetOnAxis(ap=idx_ap, axis=0), **kw)
        s[1] += 16
        ins.then_inc(s[0], 16)
        tok = (s[0], s[1], "dma")
        s[2] = tok
        self._commit(tok, reads, writes)
        self.n_inst += 1
        return tok

    def finish(self, e="sp"):
        for q, ring in self.dma_rr.items():
            for s in ring["sems"]:
                self._wait(e, s[2])

    def copy(self, e, out_b, out_ap, in_b, in_ap):
        if e == "act":
            return self.op("act", lambda g: g.copy(out=out_ap, in_=in_ap), reads=[in_b], writes=[out_b])
        return self.op(e, lambda g: g.tensor_copy(out=out_ap, in_=in_ap), reads=[in_b], writes=[out_b])


def bcast_free(ap, shape):
    return ap.to_broadcast(list(shape))


def load_xT(k, xT_dram, xb, stage):
    for kt in range(KT):
        s = stage[kt % len(stage)]
        k.dma("sp", s[:], xT_dram[kt * 128:(kt + 1) * 128, :], reads=[xT_dram], writes=[s])
        k.copy("act" if kt % 2 == 0 else "dve", xb[kt], xb[kt][:], s, s[:])


def stream_linear(k, xb, w_dram, N, wst, wb, ps_banks, epilogue, NCH=512, KQ=4, cast_engs=("act", "dve"),
                  state=None):
    nkt = len(xb)
    NKQ = nkt // KQ
    st = state if state is not None else {"cnt": 0, "oc": 0}
    for c in range(N // NCH):
        n0 = c * NCH
        b = c % 2
        for q in range(NKQ):
            s = wst[st["cnt"] % len(wst)]
            src = w_dram[q * KQ * 128:(q + 1) * KQ * 128, n0:n0 + NCH].rearrange("(j p) c -> p j c", p=128)
            k.dma("sp", s[:], src, reads=[w_dram], writes=[s])
            k.copy(cast_engs[st["cnt"] % len(cast_engs)], wb[b][q], wb[b][q][:], s, s[:])
            st["cnt"] += 1
        for m in range(MT):
            p = ps_banks[st["oc"] % len(ps_banks)]
            for kt in range(nkt):
                q, j = divmod(kt, KQ)
                k.op("pe", lambda g: g.matmul(p[:], lhsT=xb[kt][:, m * 128:(m + 1) * 128], rhs=wb[b][q][:, j, :],
                                              start=(kt == 0), stop=(kt == nkt - 1)),
                     reads=[xb[kt], wb[b][q]], writes=[p], sig=(kt == nkt - 1))
            epilogue(c, m, p)
            st["oc"] += 1
    return st


def rope_tile(k, src_b, src3, cc, ss, m, groups, rot_w, rb, tmp_t, tmp_u):
    H4 = 4
    w = rot_w
    ccb = cc[:, m, 0:w].unsqueeze(1).to_broadcast([128, H4, w])
    k.op("dve", lambda g: g.tensor_tensor(out=tmp_t[:, :, 0:w], in0=src3[:, :, 0:w], in1=ccb, op=ALU.mult),
         reads=[src_b, cc], writes=[tmp_t])
    first = True
    for (lo, half) in groups:
        hi = lo + half
        s_lo = ss[:, m, lo:lo + half].unsqueeze(1).to_broadcast([128, H4, half])
        s_hi = ss[:, m, hi:hi + half].unsqueeze(1).to_broadcast([128, H4, half])
        k.op("dve", lambda g: g.tensor_tensor(out=tmp_u[:, :, lo:lo + half], in0=src3[:, :, hi:hi + half], in1=s_lo,
                                              op=ALU.mult), reads=[src_b, ss], writes=[tmp_u])
        k.op("dve", lambda g: g.tensor_tensor(out=tmp_u[:, :, hi:hi + half], in0=src3[:, :, lo:lo + half], in1=s_hi,
                                              op=ALU.mult), reads=[src_b, ss], writes=[tmp_u])
    k.op("dve", lambda g: g.tensor_tensor(out=rb[:, :, 0:w], in0=tmp_t[:, :, 0:w], in1=tmp_u[:, :, 0:w], op=ALU.add),
         reads=[tmp_t, tmp_u], writes=[rb])
    if w < 128:
        k.op("act", lambda g: g.copy(out=rb[:, :, w:128], in_=src3[:, :, w:128]), reads=[src_b], writes=[rb])


def phase_inproj(k, pre, xT_dram, w_dram, N, chunk_types, tabs, gains, ident_b, qkT_dram, v_dram, res, bg=None):
    xb = res["xb"]
    wst = res["wst"]
    wb = res["wb"]
    ps = res["ps_mm"]
    psT = res["ps_tr"]
    rbs = [k.sbuf(f"{pre}rb{i}", [128, 4, 128], BF16) for i in range(2)]
    tmp_t = k.sbuf(f"{pre}tt", [128, 4, 128], F32)
    tmp_u = k.sbuf(f"{pre}tu", [128, 4, 128], F32)
    qn = k.sbuf(f"{pre}qn", [128, 4, 128], F32)
    junk = k.sbuf(f"{pre}junk", [128, 128], F32)
    ssq = k.sbuf(f"{pre}ssq", [128, 4], F32)
    rstd = k.sbuf(f"{pre}rstd", [128, 4], F32)
    fmst = [k.sbuf(f"{pre}fm{i}", [128, 4, T], BF16) for i in range(2)]
    vst = [k.sbuf(f"{pre}vst{i}", [128, 512], BF16) for i in range(2)]
    cnt = {"e": 0}

    def epilogue(c, m, p):
        ty = chunk_types[c]
        i = cnt["e"]
        cnt["e"] += 1
        if bg is not None:
            bg()
            bg()
        if ty[0] == "v":
            vs = vst[i % 2]
            k.copy("act", vs, vs[:], p, p[:])
            k.dma("sp", v_dram[m * 128:(m + 1) * 128, ty[1]:ty[1] + 512], vs[:], reads=[vs], writes=[v_dram])
            return
        rb = rbs[i % 2]
        p3 = p[:].rearrange("p (h d) -> p h d", h=4)
        if ty[0] == "rope":
            rope_tile(k, p, p3, tabs["ccp"], tabs["ssp"], m, [(0, 16)], 32, rb, tmp_t, tmp_u)
        else:
            gb = gains[ty[2]]
            for j in range(4):
                k.op("act", lambda g: g.activation(out=junk[:], in_=p[:, j * 128:(j + 1) * 128], func=AF.Square,
                                                   accum_out=ssq[:, j:j + 1]), reads=[p], writes=[junk, ssq])
            k.op("dve", lambda g: g.tensor_scalar(out=rstd[:], in0=ssq[:], scalar1=1.0 / 128, scalar2=1e-6,
                                                  op0=ALU.mult, op1=ALU.add), reads=[ssq], writes=[rstd])
            k.op("act", lambda g: g.activation(out=rstd[:], in_=rstd[:], func=AF.Sqrt), reads=[rstd], writes=[rstd])
            k.op("dve", lambda g: g.reciprocal(out=rstd[:], in_=rstd[:]), reads=[rstd], writes=[rstd])
            k.op("dve", lambda g: g.tensor_tensor(out=qn[:], in0=p3, in1=rstd[:].unsqueeze(2).to_broadcast([128, 4, 128]),
                                                  op=ALU.mult), reads=[p, rstd], writes=[qn])
            k.op("dve", lambda g: g.tensor_tensor(out=qn[:], in0=qn[:], in1=gb[:].unsqueeze(1).to_broadcast([128, 4, 128]),
                                                  op=ALU.mult), reads=[qn, gb], writes=[qn])
            rope_tile(k, qn, qn[:], tabs["cca"], tabs["ssa"], m, [(0, 32), (64, 32)], 128, rb, tmp_t, tmp_u)
        pt = psT[i % len(psT)]
        for j in range(4):
            k.op("pe", lambda g: g.transpose(out=pt[:, j, 0:128], in_=rb[:, j, :], identity=ident_b[:]),
                 reads=[rb, ident_b], writes=[pt])
        fm = fmst[c % 2]
        k.copy("act" if i % 2 else "dve", fm, fm[:, :, m * 128:(m + 1) * 128], pt, pt[:, :, 0:128])
        if m == MT - 1:
            fm0 = ty[1]
            k.dma("sp", qkT_dram[fm0:fm0 + 4].rearrange("j p t -> p j t"), fm[:], reads=[fm], writes=[qkT_dram])

    stream_linear(k, xb, w_dram, N, wst, wb, ps, epilogue, state=res["lin_state"],
                  cast_engs=("act", "dve"))


def rope_tables():
    pos = np.arange(SEQ, dtype=np.float32)
    half = 16
    inv = (np.float32(500000.0) ** (-np.arange(half, dtype=np.float32) / half)).astype(np.float32)
    ang = pos[:, None] * inv[None, :]
    c, s = np.cos(ang).astype(np.float32), np.sin(ang).astype(np.float32)
    ccp = np.ones((SEQ, 128), np.float32)
    ssp = np.zeros((SEQ, 128), np.float32)
    ccp[:, 0:16] = c
    ccp[:, 16:32] = c
    ssp[:, 0:16] = -s
    ssp[:, 16:32] = s
    rows = (np.arange(SEQ) // 64).astype(np.float32)
    cols = (np.arange(SEQ) % 64).astype(np.float32)
    inv2 = (np.float32(10000.0) ** (-np.arange(32, dtype=np.float32) / 32)).astype(np.float32)
    ar = rows[:, None] * inv2[None, :]
    ac = cols[:, None] * inv2[None, :]
    cr, sr, cc_, sc = (np.cos(ar).astype(np.float32), np.sin(ar).astype(np.float32),
                       np.cos(ac).astype(np.float32), np.sin(ac).astype(np.float32))
    cca = np.concatenate([cr, cr, cc_, cc_], 1)
    ssa = np.concatenate([-sr, sr, -sc, sc], 1)
    return ccp, ssp, cca, ssa


def alloc_linear_res(k, pre):
    res = {}
    res["xb"] = [k.sbuf(f"{pre}xb{i}", [128, T], BF16) for i in range(KT)]
    res["xst"] = [k.sbuf(f"{pre}xst{i}", [128, T], F32) for i in range(2)]
    res["wst"] = [k.sbuf(f"{pre}wst{i}", [128, 4, 512], F32) for i in range(2)]
    res["wb"] = [[k.sbuf(f"{pre}wb{b}_{q}", [128, 4, 512], BF16) for q in range(KT // 4)] for b in range(2)]
    res["ps_mm"] = [k.psum(f"{pre}psmm{i}", [128, 512], F32) for i in range(4)]
    res["ps_tr"] = [k.psum(f"{pre}pstr{i}", [128, 4, 256], BF16) for i in range(2)]
    res["lin_state"] = {"cnt": 0, "oc": 0}
    return res


def load_tabs(k, pre, names, drams):
    tabs = {}
    for nm in names:
        t = k.sbuf(f"{pre}tab_{nm}", [128, MT, 128], F32)
        k.dma("sp", t[:], drams[nm][:, :].rearrange("(m p) d -> p m d", p=128), reads=[drams[nm]], writes=[t])
        tabs[nm] = t
    return tabs


AB_TYPES = ([("rope", c * 4) for c in range(4)] + [("rope", 16)] + [("v", 0)]
            + [("rope", 20 + c * 4) for c in range(6)] + [("rope", 44 + c * 4) for c in range(6)]
            + [("v", 512 + c * 512) for c in range(6)])
AB_NFM, AB_NV = 68, 3584
CD_TYPES = ([("rope", c * 4) for c in range(4)] + [("rope", 16 + c * 4) for c in range(4)]
            + [("v", c * 512) for c in range(4)]
            + [("axial", 32 + c * 4, "qg") for c in range(4)] + [("axial", 48, "kg")] + [("v", 2048)])
CD_NFM, CD_NV = 52, 2560


def build_inproj(layer, types=None, x_bf16=False, conv=False):
    nc = bass.Bass("TRN2", target_bir_lowering=False)
    if types is None:
        types = AB_TYPES if layer == 0 else CD_TYPES
    N = 512 * len(types)
    nfm, nv = (AB_NFM, AB_NV) if layer == 0 else (CD_NFM, CD_NV)
    with ExitStack() as st:
        k = K(nc, st)
        xT = k.dram("xT", [D, T], BF16 if x_bf16 else F32, "ExternalInput")
        w = k.dram("w", [D, N], F32, "ExternalInput")
        identb = k.dram("identb", [128, 128], BF16, "ExternalInput")
        tabd = {nm: k.dram(nm, [T, 128], F32, "ExternalInput") for nm in (("ccp", "ssp") if layer == 0 else ("ccp", "ssp", "cca", "ssa"))}
        qkT = k.dram("qkT", [nfm, 128, T], BF16, "ExternalOutput")
        vtm = k.dram("vtm", [T, nv], BF16, "ExternalOutput")
        res = alloc_linear_res(k, "a")
        ident_b = k.sbuf("identb_s", [128, 128], BF16)
        k.dma("sp", ident_b[:], identb[:, :], reads=[identb], writes=[ident_b])
        tabs = load_tabs(k, "a", list(tabd.keys()), tabd)
        gains = {}
        if layer == 1:
            for nm in ("qg", "kg"):
                gd = k.dram(nm, [1, 128], F32, "ExternalInput")
                gs = k.sbuf("g_" + nm, [128, 128], F32)
                k.dma("sp", gs[:], gd[0:1, :].to_broadcast([128, 128]), reads=[gd], writes=[gs])
                gains[nm] = gs
        if x_bf16:
            for kt in range(KT):
                k.dma("sp", res["xb"][kt][:], xT[kt * 128:(kt + 1) * 128, :], reads=[xT], writes=[res["xb"][kt]])
        else:
            load_xT(k, xT, res["xb"], res["xst"])
        bg = None
        if conv:
            RW = 16384 // NCORES
            pairs = []
            for nm in ("u0", "v0", "u1", "v1"):
                src = k.dram(nm + "p", [RW, D], F32, "ExternalInput")
                dst = k.dram(nm + "b", [RW, D], BF16, "ExternalOutput")
                pairs.append((src, dst))
            bg = cast_rows_bg(k, "cv", pairs, RW, ncol=4)
        phase_inproj(k, "a", xT, w, N, types, tabs, gains, ident_b, qkT, vtm, res, bg=bg)
        while bg is not None and bg():
            pass
        k.finish("sp")
    return nc


HALO = 1024
TE = T + 2 * HALO
NTE = TE // 128
B_DIL = (1, 4, 16)
B_DT = (1, 2, 8)
MASK_OFF = {"A": 0, 0: 3, 1: 6, 2: 11}
N_MASKS = 28


def band_masks():
    m = np.zeros((128, N_MASKS, 128), np.float32)
    kl = np.arange(128)[:, None]
    ql = np.arange(128)[None, :]
    for j, dt in enumerate((-1, 0, 1)):
        diff = (ql - kl) - dt * 128
        m[:, MASK_OFF["A"] + j, :] = (np.abs(diff) <= 128)
    for g in range(3):
        d, r = B_DIL[g], B_DT[g]
        for j, dt in enumerate(range(-r, r + 1)):
            diff = (ql - kl) - dt * 128
            m[:, MASK_OFF[g] + j, :] = (np.abs(diff) <= 64 * d) & (diff % d == 0)
    return m.astype(NP_BF16)


def attn_chunks(qT_ap, qT_b, kT_b, v_b, ktiles, mask0, acc, first, last):
    out = []
    n = len(ktiles)
    i0 = 0
    while i0 < n:
        nt = min(4, n - i0)
        out.append({"q": qT_ap, "qb": qT_b, "kT": kT_b, "v": v_b, "tiles": ktiles[i0:i0 + nt], "mask0": mask0 + i0, "acc": acc,
                    "start": first and i0 == 0, "stop": last and i0 + nt == n, "post": None})
        i0 += nt
    return out


def run_chunks(k, chunks, mask_b, ps_s, pts, cnt, scale):
    def S(ch):
        ps = ps_s[cnt["s"] % len(ps_s)]
        pt = pts[cnt["s"] % len(pts)]
        cnt["s"] += 1
        ch["pt"] = pt
        nt = len(ch["tiles"])
        for j, lt in enumerate(ch["tiles"]):
            k.op("pe", lambda g: g.matmul(ps[:, j, :], lhsT=ch["kT"][:, lt * 128:(lt + 1) * 128], rhs=ch["q"],
                                          start=True, stop=True), reads=[ch["kT"], ch["qb"]], writes=[ps], sig=(j == nt - 1))
        k.op("act", lambda g: g.activation(out=pt[:, 0:nt, :], in_=ps[:, 0:nt, :], func=AF.Exp, scale=scale),
             reads=[ps], writes=[pt])
        k.op("dve", lambda g: g.tensor_tensor(out=pt[:, 0:nt, :], in0=pt[:, 0:nt, :],
                                              in1=mask_b[:, ch["mask0"]:ch["mask0"] + nt, :], op=ALU.mult),
             reads=[pt, mask_b], writes=[pt])

    def P(ch):
        pt = ch["pt"]
        nt = len(ch["tiles"])
        for j, lt in enumerate(ch["tiles"]):
            k.op("pe", lambda g: g.matmul(ch["acc"][:, 0:129], lhsT=pt[:, j, :], rhs=ch["v"][:, lt, :],
                                          start=(ch["start"] and j == 0), stop=(ch["stop"] and j == nt - 1)),
                 reads=[pt, ch["v"]], writes=[ch["acc"]], sig=(j == nt - 1))
        if ch["post"] is not None:
            ch["post"]()

    if not chunks:
        return
    S(chunks[0])
    for i, ch in enumerate(chunks):
        if i + 1 < len(chunks):
            S(chunks[i + 1])
        P(ch)


def phase_attn_ab(k, pre, qT_d, kTe_d, vaug_d, sink_d, mask_d, identb_d, oT_d, bg=None):
    scale = 1.0 / math.sqrt(128.0)
    mask_b = k.sbuf(f"{pre}mask", [128, N_MASKS, 128], BF16)
    k.dma("sp", mask_b[:], mask_d[:, :, :], reads=[mask_d], writes=[mask_b])
    ident_b = k.sbuf(f"{pre}ident", [128, 128], BF16)
    k.dma("sp", ident_b[:], identb_d[:, :], reads=[identb_d], writes=[ident_b])
    esink = k.sbuf(f"{pre}esink", [128, 16], F32)
    k.dma("sp", esink[:], sink_d[0:1, :].to_broadcast([128, 16]), reads=[sink_d], writes=[esink])
    k.op("act", lambda g: g.activation(out=esink[:], in_=esink[:], func=AF.Exp), reads=[esink], writes=[esink])
    NKT = 46
    kbuf = [k.sbuf(f"{pre}kb{i}", [128, NKT * 128], BF16) for i in range(2)]
    vbuf = [k.sbuf(f"{pre}vb{i}", [128, NKT, 129], BF16) for i in range(2)]
    qbuf = [k.sbuf(f"{pre}qb{i}", [128, 4, T], BF16) for i in range(2)]
    ps_s = [k.psum(f"{pre}pss{i}", [128, 4, 128], F32) for i in range(3)]
    pts = [k.sbuf(f"{pre}pt{i}", [128, 4, 128], BF16) for i in range(4)]
    accs = [k.psum(f"{pre}acc{i}", [128, 512], F32) for i in range(2)]
    ps_t = [k.psum(f"{pre}pst{i}", [128, 1024], BF16) for i in range(2)]
    den = k.sbuf(f"{pre}den", [128, 1], F32)
    ob = [k.sbuf(f"{pre}ob{i}", [128, 128], BF16) for i in range(2)]
    oTs = [k.sbuf(f"{pre}oTs{i}", [128, T], BF16) for i in range(2)]
    cnt = {"s": 0, "a": 0, "o": 0}

    def finalize(acc, sink_col, ohead, m):
        if sink_col is not None:
            k.op("dve", lambda g: g.tensor_tensor(out=den[:], in0=acc[:, 128:129], in1=esink[:, sink_col:sink_col + 1],
                                                  op=ALU.add), reads=[acc, esink], writes=[den])
        else:
            k.op("dve", lambda g: g.tensor_copy(out=den[:], in_=acc[:, 128:129]), reads=[acc], writes=[den])
        k.op("dve", lambda g: g.reciprocal(out=den[:], in_=den[:]), reads=[den], writes=[den])
        o = ob[cnt["o"] % 2]
        k.op("dve", lambda g: g.tensor_scalar(out=o[:], in0=acc[:, 0:128], scalar1=den[:, 0:1], scalar2=None,
                                              op0=ALU.mult), reads=[acc, den], writes=[o])
        pt_ = ps_t[cnt["o"] % 2]
        k.op("pe", lambda g: g.transpose(out=pt_[:, 0:128], in_=o[:], identity=ident_b[:]),
             reads=[o, ident_b], writes=[pt_])
        oT = oTs[ohead % 2]
        k.copy("act", oT, oT[:, m * 128:(m + 1) * 128], pt_, pt_[:, 0:128])
        cnt["o"] += 1
        if m == MT - 1:
            k.dma("sp", oT_d[ohead], oT[:], reads=[oT], writes=[oT_d])

    job = 0
    for kvh in range(4):
        kb_, vb_, qb_ = kbuf[job % 2], vbuf[job % 2], qbuf[job % 2]
        job += 1
        k.dma("sp", kb_[:, 0:10 * 128], kTe_d[kvh, :, 7 * 128:17 * 128], reads=[kTe_d], writes=[kb_])
        k.dma("sp", vb_[:, 0:10, :], vaug_d[7 * 128:17 * 128, kvh, :].rearrange("(j p) c -> p j c", p=128),
              reads=[vaug_d], writes=[vb_])
        k.dma("sp", qb_[:], qT_d[kvh * 4:kvh * 4 + 4].rearrange("j p t -> p j t"), reads=[qT_d], writes=[qb_])
        chunks = []
        for g4 in range(4):
            h = kvh * 4 + g4
            for m in range(MT):
                acc = accs[cnt["a"] % 2]
                cnt["a"] += 1
                cs_ = attn_chunks(qb_[:, g4, m * 128:(m + 1) * 128], qb_, kb_, vb_, [m, m + 1, m + 2], MASK_OFF["A"], acc,
                                  True, True)
                cs_[-1]["post"] = (lambda acc=acc, h=h, m=m: finalize(acc, h, h, m))
                chunks += cs_
        run_chunks(k, chunks, mask_b, ps_s, pts, cnt, scale)
    base = [0, 10, 22]
    lo_ext = [7, 6, 0]
    nload = [10, 12, 24]
    for h in range(8):
        kb_, vb_, qb_ = kbuf[job % 2], vbuf[job % 2], qbuf[job % 2]
        job += 1
        for g in range(3):
            kblk = 4 + g * 8 + h
            k.dma("sp", kb_[:, base[g] * 128:(base[g] + nload[g]) * 128],
                  kTe_d[kblk, :, lo_ext[g] * 128:(lo_ext[g] + nload[g]) * 128], reads=[kTe_d], writes=[kb_])
            k.dma("sp", vb_[:, base[g]:base[g] + nload[g], :],
                  vaug_d[lo_ext[g] * 128:(lo_ext[g] + nload[g]) * 128, kblk, :].rearrange("(j p) c -> p j c", p=128),
                  reads=[vaug_d], writes=[vb_])
            k.dma("sp", qb_[:, g, :], qT_d[20 + g * 8 + h], reads=[qT_d], writes=[qb_])
        chunks = []
        for m in range(MT):
            acc = accs[cnt["a"] % 2]
            cnt["a"] += 1
            for g in range(3):
                r = B_DT[g]
                tiles = [base[g] + (8 + m + dt) - lo_ext[g] for dt in range(-r, r + 1)]
                chunks += attn_chunks(qb_[:, g, m * 128:(m + 1) * 128], qb_, kb_, vb_, tiles, MASK_OFF[g], acc, g == 0, g == 2)
            chunks[-1]["post"] = (lambda acc=acc, h=h, m=m: finalize(acc, None, 16 + h, m))
        run_chunks(k, chunks, mask_b, ps_s, pts, cnt, scale)


def build_attn_ab():
    nc = bass.Bass("TRN2", target_bir_lowering=False)
    with ExitStack() as st:
        k = K(nc, st)
        qT = k.dram("qT", [AB_NFM, 128, T], BF16, "ExternalInput")
        kTe = k.dram("kTe", [28, 128, TE], BF16, "ExternalInput")
        vaug = k.dram("vaug", [TE, 28, 129], BF16, "ExternalInput")
        sink = k.dram("sink", [1, 16], F32, "ExternalInput")
        masks = k.dram("masks", [128, N_MASKS, 128], BF16, "ExternalInput")
        identb = k.dram("identb", [128, 128], BF16, "ExternalInput")
        oT = k.dram("oT", [24, 128, T], BF16, "ExternalOutput")
        phase_attn_ab(k, "b", qT, kTe, vaug, sink, masks, identb, oT)
        k.finish("sp")
    return nc


def host_ext_kv(qkT_all, vtm_all, kblocks, vheads_cols):
    kfull = np.concatenate([q[kblocks] for q in qkT_all], axis=2)
    vfull = np.concatenate(vtm_all, axis=0)
    nk = len(kblocks)
    kpad = np.zeros((nk, 128, SEQ + 2 * HALO), dtype=kfull.dtype)
    kpad[:, :, HALO:HALO + SEQ] = kfull
    nvh = vfull.shape[1] // 128
    vpad = np.zeros((SEQ + 2 * HALO, nvh, 129), dtype=vfull.dtype)
    vpad[HALO:HALO + SEQ, :, 0:128] = vfull.reshape(SEQ, nvh, 128)
    vpad[HALO:HALO + SEQ, :, 128] = 1.0
    outs = []
    for c in range(NCORES):
        outs.append((np.ascontiguousarray(kpad[:, :, c * T:c * T + TE]), np.ascontiguousarray(vpad[c * T:c * T + TE])))
    return outs


ALPHA = float((2 * 2) ** 0.25)


def barrier(k):
    toks = []
    for e, c in k.cur.items():
        if c[1] > 0:
            toks.append((c[0], c[1], e))
    for q, ring in k.dma_rr.items():
        for s in ring["sems"]:
            if s[2] is not None:
                toks.append(s[2])
    for e in ("pe", "dve", "act", "pool", "sp"):
        for tk in toks:
            if tk[2] == e:
                continue
            k._wait(e, tk)


def phase_outproj(k, pre, oT_d, nkt, w_d, x_d, z_d, res):
    xb = res["xb"][:nkt]
    for kt in range(nkt):
        k.dma("sp", xb[kt][:], oT_d[kt], reads=[oT_d], writes=[xb[kt]])
    xin = [k.sbuf(f"{pre}xin{i}", [128, 512], F32) for i in range(3)]
    zt = [k.sbuf(f"{pre}zt{i}", [128, 512], F32) for i in range(3)]
    cnt = {"e": 0}

    def epilogue(c, m, p):
        i = cnt["e"]
        cnt["e"] += 1
        xi, zo = xin[i % 3], zt[i % 3]
        k.dma("sp", xi[:], x_d[m * 128:(m + 1) * 128, c * 512:(c + 1) * 512], reads=[x_d], writes=[xi])
        k.op("dve", lambda g: g.scalar_tensor_tensor(out=zo[:], in0=xi[:], scalar=ALPHA, in1=p[:], op0=ALU.mult,
                                                     op1=ALU.add), reads=[xi, p], writes=[zo])
        k.dma("sp", z_d[m * 128:(m + 1) * 128, c * 512:(c + 1) * 512], zo[:], reads=[zo], writes=[z_d])

    stream_linear(k, xb, w_d, D, res["wst"], res["wb"], res["ps_mm"], epilogue, state=res["lin_state"])


def phase_ln(k, pre, z_d, g_d, b_d, out_d, outT_d, identb_d, ps_tr, mt=MT):
    gt = k.sbuf(f"{pre}g", [128, D], F32)
    bt = k.sbuf(f"{pre}b", [128, D], F32)
    k.dma("sp", gt[:], g_d[0:1, :].to_broadcast([128, D]), reads=[g_d], writes=[gt])
    k.dma("sp", bt[:], b_d[0:1, :].to_broadcast([128, D]), reads=[b_d], writes=[bt])
    ident_b = k.sbuf(f"{pre}ident", [128, 128], BF16)
    k.dma("sp", ident_b[:], identb_d[:, :], reads=[identb_d], writes=[ident_b])
    zs = [k.sbuf(f"{pre}z{i}", [128, D], F32) for i in range(2)]
    os_ = [k.sbuf(f"{pre}o{i}", [128, D], F32) for i in range(2)]
    ob16 = k.sbuf(f"{pre}o16", [128, D], BF16)
    stg = [k.sbuf(f"{pre}stg{i}", [128, KT, 128], BF16) for i in range(2)]
    stats = k.sbuf(f"{pre}stats", [128, 8, 6], F32)
    mv = k.sbuf(f"{pre}mv", [128, 2], F32)
    rstd = k.sbuf(f"{pre}rstd", [128, 1], F32)
    for m in range(mt):
        z, o = zs[m % 2], os_[m % 2]
        k.dma("sp", z[:], z_d[m * 128:(m + 1) * 128, :], reads=[z_d], writes=[z])
        for c in range(8):
            k.op("dve", lambda g: g.bn_stats(out=stats[:, c, :], in_=z[:, c * 512:(c + 1) * 512]), reads=[z], writes=[stats])
        k.op("dve", lambda g: g.bn_aggr(out=mv[:], in_=stats[:].rearrange("p a b -> p (a b)")), reads=[stats], writes=[mv])
        k.op("dve", lambda g: g.tensor_scalar(out=rstd[:], in0=mv[:, 1:2], scalar1=1e-5, scalar2=None, op0=ALU.add),
             reads=[mv], writes=[rstd])
        k.op("act", lambda g: g.activation(out=rstd[:], in_=rstd[:], func=AF.Sqrt), reads=[rstd], writes=[rstd])
        k.op("dve", lambda g: g.reciprocal(out=rstd[:], in_=rstd[:]), reads=[rstd], writes=[rstd])
        k.op("dve", lambda g: g.tensor_scalar(out=o[:], in0=z[:], scalar1=mv[:, 0:1], scalar2=rstd[:, 0:1],
                                              op0=ALU.subtract, op1=ALU.mult), reads=[z, mv, rstd], writes=[o])
        k.op("pool", lambda g: g.tensor_tensor(out=o[:], in0=o[:], in1=gt[:], op=ALU.mult), reads=[o, gt], writes=[o])
        k.op("pool", lambda g: g.tensor_tensor(out=o[:], in0=o[:], in1=bt[:], op=ALU.add), reads=[o, bt], writes=[o])
        k.dma("sp", out_d[m * 128:(m + 1) * 128, :], o[:], reads=[o], writes=[out_d])
        if outT_d is not None:
            k.op("act", lambda g: g.copy(out=ob16[:], in_=o[:]), reads=[o], writes=[ob16])
            sg = stg[m % 2]
            for q in range(4):
                pt = ps_tr[q % len(ps_tr)]
                for j in range(8):
                    kt = q * 8 + j
                    k.op("pe", lambda g: g.transpose(out=pt[:, j, :], in_=ob16[:, kt * 128:(kt + 1) * 128],
                                                     identity=ident_b[:]), reads=[ob16, ident_b], writes=[pt])
                k.copy("act" if q % 2 else "dve", sg, sg[:, q * 8:(q + 1) * 8, :], pt, pt[:])
            k.dma("sp", outT_d[:, m * 128:(m + 1) * 128].rearrange("(kt p) t -> p kt t", p=128), sg[:],
                  reads=[sg], writes=[outT_d])


def build_outproj_ln(nkt, with_T):
    nc = bass.Bass("TRN2", target_bir_lowering=False)
    with ExitStack() as st:
        k = K(nc, st)
        oT = k.dram("oT", [nkt, 128, T], BF16, "ExternalInput")
        w = k.dram("w", [nkt * 128, D], F32, "ExternalInput")
        x = k.dram("x", [T, D], F32, "ExternalInput")
        g = k.dram("g", [1, D], F32, "ExternalInput")
        b = k.dram("b", [1, D], F32, "ExternalInput")
        identb = k.dram("identb", [128, 128], BF16, "ExternalInput")
        z = k.dram("z", [T, D], F32, "Internal")
        x1 = k.dram("x1", [T, D], F32, "ExternalOutput")
        x1T = k.dram("x1T", [D, T], BF16, "ExternalOutput") if with_T else None
        with ExitStack() as ph:
            k.stack = ph
            res = alloc_linear_res(k, "c")
            phase_outproj(k, "c", oT, nkt, w, x, z, res)
            barrier(k)
        k.stack = st
        with ExitStack() as ph:
            k.stack = ph
            ps_tr = [k.psum(f"lnpt{i}", [128, 8, 128], BF16) for i in range(2)]
            phase_ln(k, "l", z, g, b, x1, x1T, identb, ps_tr)
            barrier(k)
        k.stack = st
        k.finish("sp")
    return nc


NEG = -1.0e30


def phase_peer_scores(k, pre, x1T_d, wq_d, keysT_d, sc_d, xb):
    for kt in range(KT):
        k.dma("sp", xb[kt][:], x1T_d[kt * 128:(kt + 1) * 128, :], reads=[x1T_d], writes=[xb[kt]])
    keysT = k.sbuf(f"{pre}keysT", [128, 2048], F32)
    k.dma("sp", keysT[:], keysT_d[:, :], reads=[keysT_d], writes=[keysT])
    wst = [k.sbuf(f"{pre}wst{i}", [128, KT, 128], F32) for i in range(2)]
    wqb = [k.sbuf(f"{pre}wqb{i}", [128, KT, 128], BF16) for i in range(2)]
    qg = [k.sbuf(f"{pre}qg{i}", [128, T], F32) for i in range(2)]
    scst = [k.sbuf(f"{pre}scst{i}", [128, MT, 128], F32) for i in range(2)]
    ps = [k.psum(f"{pre}ps{i}", [128, 512], F32) for i in range(4)]
    ps2 = [k.psum(f"{pre}ps2{i}", [128, 512], F32) for i in range(2)]
    for g in range(16):
        ws, wb_, q_, sc_ = wst[g % 2], wqb[g % 2], qg[g % 2], scst[g % 2]
        k.dma("sp", ws[:], wq_d[:, g * 128:(g + 1) * 128].rearrange("(kt p) c -> p kt c", p=128), reads=[wq_d], writes=[ws])
        k.copy(("act", "dve")[g % 2], wb_, wb_[:], ws, ws[:])
        for tc in range(T // 512):
            p = ps[(g * 2 + tc) % 4]
            for kt in range(KT):
                k.op("pe", lambda e: e.matmul(p[:], lhsT=wb_[:, kt, :], rhs=xb[kt][:, tc * 512:(tc + 1) * 512],
                                              start=(kt == 0), stop=(kt == KT - 1)), reads=[wb_, xb[kt]], writes=[p])
            k.copy(("dve", "act")[tc % 2], q_, q_[:, tc * 512:(tc + 1) * 512], p, p[:])
        for m in range(MT):
            p2 = ps2[m % 2]
            k.op("pe", lambda e: e.matmul(p2[:, 0:128], lhsT=q_[:, m * 128:(m + 1) * 128], rhs=keysT[:, g * 128:(g + 1) * 128],
                                          start=True, stop=True), reads=[q_, keysT], writes=[p2])
            k.copy(("act", "dve")[m % 2], sc_, sc_[:, m, :], p2, p2[:, 0:128])
        k.dma("sp", sc_d[:, g * 128:(g + 1) * 128].rearrange("(m p) n -> p m n", p=128), sc_[:], reads=[sc_], writes=[sc_d])


def cast_rows_bg(k, pre, srcs_dsts, nrows, nbuf=2, ncol=1):
    W_ = D // ncol
    st = [k.sbuf(f"{pre}st{i}", [128, W_], F32) for i in range(nbuf)]
    cb = [k.sbuf(f"{pre}cb{i}", [128, W_], BF16) for i in range(nbuf)]
    jobs = [(src, dst, i, j) for (src, dst) in srcs_dsts for i in range(nrows // 128) for j in range(ncol)]
    state = {"n": 0}

    def step():
        n = state["n"]
        if n >= len(jobs):
            return False
        state["n"] += 1
        src, dst, i, j = jobs[n]
        s_, c_ = st[n % nbuf], cb[n % nbuf]
        k.dma("pool", s_[:], src[i * 128:(i + 1) * 128, j * W_:(j + 1) * W_], reads=[src], writes=[s_])
        k.copy("pool", c_, c_[:], s_, s_[:])
        k.dma("pool", dst[i * 128:(i + 1) * 128, j * W_:(j + 1) * W_], c_[:], reads=[c_], writes=[dst])
        return True
    return step


def phase_cast_rows(k, pre, srcs_dsts, nrows=16384):
    st = [k.sbuf(f"{pre}st{i}", [128, D], F32) for i in range(3)]
    cb = [k.sbuf(f"{pre}cb{i}", [128, D], BF16) for i in range(3)]
    jobs = [(src, dst, i) for (src, dst) in srcs_dsts for i in range(nrows // 128)]
    engs = ("act", "dve", "pool")

    def load(n):
        src, dst, i = jobs[n]
        k.dma("sp", st[n % 3][:], src[i * 128:(i + 1) * 128, :], reads=[src], writes=[st[n % 3]])

    for n in range(min(2, len(jobs))):
        load(n)
    for n in range(len(jobs)):
        if n + 2 < len(jobs):
            load(n + 2)
        src, dst, i = jobs[n]
        k.copy(engs[n % 3], cb[n % 3], cb[n % 3][:], st[n % 3], st[n % 3][:])
        k.dma("act", dst[i * 128:(i + 1) * 128, :], cb[n % 3][:], reads=[cb[n % 3]], writes=[dst])


def phase_peer_main(k, pre, x1_d, x1T_d, sc_d, u_d, v_d, pconst_d, identb_d, z_d, NG=4, mt=MT):
    ident_b = k.sbuf(f"{pre}ident", [128, 128], BF16)
    k.dma("sp", ident_b[:], identb_d[:, :], reads=[identb_d], writes=[ident_b])
    pc = k.sbuf(f"{pre}pc", [128, 48], F32)
    k.dma("sp", pc[:], pconst_d[:, :], reads=[pconst_d], writes=[pc])
    iota16, lo16, hi16 = pc[:, 0:16], pc[:, 16:32], pc[:, 32:48]
    sc = k.sbuf(f"{pre}sc", [128, 16, 128], F32)
    scr = k.sbuf(f"{pre}scr", [128, 16, 128], F32)
    sv = k.sbuf(f"{pre}sv", [128, 16, 16], F32)
    si = k.sbuf(f"{pre}si", [128, 16, 16], U32)
    sif = k.sbuf(f"{pre}sif", [128, 16, 16], F32)
    cand = k.sbuf(f"{pre}cand", [128, 8, 256], F32)
    cscr = k.sbuf(f"{pre}cscr", [128, 8, 256], F32)
    ts = k.sbuf(f"{pre}ts", [128, 8, 16], F32)
    tp = k.sbuf(f"{pre}tp", [128, 8, 16], U32)
    tpf = k.sbuf(f"{pre}tpf", [128, 8, 16], F32)
    w4a = k.sbuf(f"{pre}w4a", [128, 8, 16, 16], F32)
    w4b = k.sbuf(f"{pre}w4b", [128, 8, 16, 16], F32)
    r3a = k.sbuf(f"{pre}r3a", [128, 8, 16], F32)
    r3b = k.sbuf(f"{pre}r3b", [128, 8, 16], F32)
    r3c = k.sbuf(f"{pre}r3c", [128, 8, 16], F32)
    eidx_f = k.sbuf(f"{pre}eidxf", [128, 128], F32)
    eidx = k.sbuf(f"{pre}eidx", [128, 128], U32)
    gate = k.sbuf(f"{pre}gate", [128, 8, 16], F32)
    zsum = k.sbuf(f"{pre}zsum", [128, 8], F32)
    hids = [k.sbuf(f"{pre}hid{i}", [128, 128], F32) for i in range(2)]
    g1 = k.sbuf(f"{pre}g1", [128, 128], F32)
    wgts = [k.sbuf(f"{pre}wgt{i}", [128, 128], F32) for i in range(2)]
    xts = [k.sbuf(f"{pre}xt{i}", [128, D], F32) for i in range(2)]
    junk = k.sbuf(f"{pre}junk", [128, D], BF16)
    ug = [k.sbuf(f"{pre}ug{i}", [128, D], BF16) for i in range(NG)]
    vg = [k.sbuf(f"{pre}vg{i}", [128, D // 2], BF16) for i in range(NG)]
    dgs = [k.sbuf(f"{pre}dg{i}", [128, 128], BF16) for i in range(8)]
    zo = [k.sbuf(f"{pre}zo{i}", [128, 512], F32) for i in range(2)]
    xps = k.psum(f"{pre}xps", [128, D], BF16)
    acc = [k.psum(f"{pre}acc{i}", [128, 512], F32) for i in range(4)]

    def dve(fn, reads, writes):
        return k.op("dve", fn, reads=reads, writes=writes)

    eidx_t = [k.sbuf(f"{pre}eidx_t{i}", [128, 128], U32) for i in range(mt)]
    gate_t = [k.sbuf(f"{pre}gate_t{i}", [128, 128], F32) for i in range(mt)]
    evidx_t = [[k.sbuf(f"{pre}evidx_t{i}_{h}", [128, 128], U32) for h in range(2)] for i in range(mt)]

    def route(m):
        k.dma("sp", sc[:], sc_d[m * 128:(m + 1) * 128, :].rearrange("p (g n) -> p g n", g=16), reads=[sc_d], writes=[sc])
        for g in range(16):
            yield ('dve', lambda e: e.max(out=sv[:, g, 0:8], in_=sc[:, g, :]), [sc], [sv])
            yield ('dve', lambda e: e.max_index(out=si[:, g, 0:8], in_max=sv[:, g, 0:8], in_values=sc[:, g, :]), [sc, sv], [si])
            yield ('dve', lambda e: e.match_replace(out=scr[:, g, :], in_to_replace=sv[:, g, 0:8], in_values=sc[:, g, :],
                                          imm_value=NEG), [sc, sv], [scr])
            yield ('dve', lambda e: e.max(out=sv[:, g, 8:16], in_=scr[:, g, :]), [scr], [sv])
            yield ('dve', lambda e: e.max_index(out=si[:, g, 8:16], in_max=sv[:, g, 8:16], in_values=scr[:, g, :]), [scr, sv], [si])
        yield ('dve', lambda e: e.tensor_copy(out=sif[:], in_=si[:]), [si], [sif])
        sv4 = sv[:].rearrange("p (h c) r -> p h c r", c=2)
        sif4 = sif[:].rearrange("p (h c) r -> p h c r", c=2)
        c4 = cand[:].rearrange("p h (a b) -> p h a b", a=16)
        yield ('dve', lambda e: e.tensor_tensor(out=c4, in0=sv4[:, :, 0, :].unsqueeze(3).to_broadcast([128, 8, 16, 16]),
                                      in1=sv4[:, :, 1, :].unsqueeze(2).to_broadcast([128, 8, 16, 16]), op=ALU.add),
            [sv], [cand])
        for h in range(8):
            yield ('dve', lambda e: e.max(out=ts[:, h, 0:8], in_=cand[:, h, :]), [cand], [ts])
            yield ('dve', lambda e: e.max_index(out=tp[:, h, 0:8], in_max=ts[:, h, 0:8], in_values=cand[:, h, :]), [cand, ts], [tp])
            yield ('dve', lambda e: e.match_replace(out=cscr[:, h, :], in_to_replace=ts[:, h, 0:8], in_values=cand[:, h, :],
                                          imm_value=NEG), [cand, ts], [cscr])
            yield ('dve', lambda e: e.max(out=ts[:, h, 8:16], in_=cscr[:, h, :]), [cscr], [ts])
            yield ('dve', lambda e: e.max_index(out=tp[:, h, 8:16], in_max=ts[:, h, 8:16], in_values=cscr[:, h, :]), [cscr, ts], [tp])
        yield ('dve', lambda e: e.tensor_copy(out=tpf[:], in_=tp[:]), [tp], [tpf])
        tpf4 = tpf[:].unsqueeze(3).to_broadcast([128, 8, 16, 16])
        lo4 = lo16.unsqueeze(1).unsqueeze(1).to_broadcast([128, 8, 16, 16])
        hi4 = hi16.unsqueeze(1).unsqueeze(1).to_broadcast([128, 8, 16, 16])
        io4 = iota16.unsqueeze(1).unsqueeze(1).to_broadcast([128, 8, 16, 16])
        yield ('dve', lambda e: e.tensor_tensor(out=w4a[:], in0=tpf4, in1=lo4, op=ALU.is_ge), [tpf, pc], [w4a])
        yield ('dve', lambda e: e.tensor_tensor(out=w4b[:], in0=tpf4, in1=hi4, op=ALU.is_ge), [tpf, pc], [w4b])
        yield ('dve', lambda e: e.tensor_tensor(out=w4a[:], in0=w4a[:], in1=w4b[:], op=ALU.subtract), [w4a, w4b], [w4a])
        yield ('dve', lambda e: e.tensor_tensor(out=w4b[:], in0=w4a[:], in1=sif4[:, :, 0, :].unsqueeze(2).to_broadcast([128, 8, 16, 16]),
                                      op=ALU.mult), [w4a, sif], [w4b])
        yield ('dve', lambda e: e.tensor_reduce(out=r3a[:], in_=w4b[:], axis=AX.X, op=ALU.add), [w4b], [r3a])
        yield ('dve', lambda e: e.tensor_tensor(out=w4b[:], in0=w4a[:], in1=lo4, op=ALU.mult), [w4a, pc], [w4b])
        yield ('dve', lambda e: e.tensor_reduce(out=r3b[:], in_=w4b[:], axis=AX.X, op=ALU.add), [w4b], [r3b])
        yield ('dve', lambda e: e.tensor_tensor(out=r3b[:], in0=tpf[:], in1=r3b[:], op=ALU.subtract), [tpf, r3b], [r3b])
        yield ('dve', lambda e: e.tensor_tensor(out=w4a[:], in0=r3b[:].unsqueeze(3).to_broadcast([128, 8, 16, 16]), in1=io4,
                                      op=ALU.is_equal), [r3b, pc], [w4a])
        yield ('dve', lambda e: e.tensor_tensor(out=w4b[:], in0=w4a[:], in1=sif4[:, :, 1, :].unsqueeze(2).to_broadcast([128, 8, 16, 16]),
                                      op=ALU.mult), [w4a, sif], [w4b])
        yield ('dve', lambda e: e.tensor_reduce(out=r3c[:], in_=w4b[:], axis=AX.X, op=ALU.add), [w4b], [r3c])
        ef3 = eidx_f[:].rearrange("p (h r) -> p h r", h=8)
        yield ('dve', lambda e: e.scalar_tensor_tensor(out=ef3, in0=r3a[:], scalar=128.0, in1=r3c[:], op0=ALU.mult, op1=ALU.add),
            [r3a, r3c], [eidx_f])
        yield ('dve', lambda e: e.tensor_copy(out=eidx[:], in_=eidx_f[:]), [eidx_f], [eidx])
        yield ('dve', lambda e: e.tensor_tensor(out=gate[:], in0=ts[:], in1=ts[:, :, 0:1].to_broadcast([128, 8, 16]), op=ALU.subtract),
            [ts], [gate])
        yield ('act', lambda e: e.activation(out=gate[:], in_=gate[:], func=AF.Exp), [gate], [gate])
        yield ('dve', lambda e: e.tensor_reduce(out=zsum[:], in_=gate[:], axis=AX.X, op=ALU.add), [gate], [zsum])
        yield ('dve', lambda e: e.reciprocal(out=zsum[:], in_=zsum[:]), [zsum], [zsum])
        yield ('dve', lambda e: e.tensor_tensor(out=gate[:], in0=gate[:], in1=zsum[:].unsqueeze(2).to_broadcast([128, 8, 16]), op=ALU.mult),
            [gate, zsum], [gate])
        yield ('dve', lambda e: e.tensor_copy(out=eidx_t[m][:], in_=eidx[:]), [eidx], [eidx_t[m]])
        yield ('dve', lambda e: e.tensor_scalar(out=eidx_f[:], in0=eidx_f[:], scalar1=2.0, scalar2=None, op0=ALU.mult), [eidx_f], [eidx_f])
        yield ('dve', lambda e: e.tensor_copy(out=evidx_t[m][0][:], in_=eidx_f[:]), [eidx_f], [evidx_t[m][0]])
        yield ('dve', lambda e: e.tensor_scalar(out=eidx_f[:], in0=eidx_f[:], scalar1=1.0, scalar2=None, op0=ALU.add), [eidx_f], [eidx_f])
        yield ('dve', lambda e: e.tensor_copy(out=evidx_t[m][1][:], in_=eidx_f[:]), [eidx_f], [evidx_t[m][1]])
        yield ('dve', lambda e: e.tensor_copy(out=gate_t[m][:], in_=gate[:].rearrange("p h r -> p (h r)")), [gate], [gate_t[m]])


    def advance(gen, n=1):
        for _ in range(n):
            it = next(gen, None)
            if it is None:
                return False
            k.op(it[0], it[1], reads=it[2], writes=it[3])
        return True

    xTt = [k.sbuf(f"{pre}xTt{i}", [128, KT, 128], BF16) for i in range(2)]

    def load_x(m):
        xt = xts[m % 2]
        k.dma("sp", xt[:], x1_d[m * 128:(m + 1) * 128, :], reads=[x1_d], writes=[xt])
        xT_ = xTt[m % 2]
        k.dma("sp", xT_[:], x1T_d[:, m * 128:(m + 1) * 128].rearrange("(kt p) t -> p kt t", p=128), reads=[x1T_d], writes=[xT_])
        for kt in range(KT):
            k.op("pe", lambda e: e.transpose(out=xps[:, kt * 128:(kt + 1) * 128], in_=xT_[:, kt, :], identity=ident_b[:]),
                 reads=[xT_, ident_b], writes=[xps], sig=(kt == KT - 1))

    def u_slot(m, s):
        ub = ug[s % NG]
        hid = hids[m % 2]
        k.gather(ub[:], u_d[:, :], eidx_t[m][:, s:s + 1], reads=[eidx_t[m], u_d], writes=[ub])
        dve(lambda e: e.scalar_tensor_tensor(out=junk[:], in0=ub[:], scalar=1.0, in1=xps[:], op0=ALU.mult, op1=ALU.mult,
                                             accum_out=hid[:, s:s + 1]), [ub, xps], [junk, hid])

    def gelu_wgt(m):
        hid, wgt = hids[m % 2], wgts[m % 2]
        dve(lambda e: e.tensor_tensor(out=g1[:], in0=hid[:], in1=hid[:], op=ALU.mult), [hid], [g1])
        dve(lambda e: e.tensor_scalar(out=g1[:], in0=g1[:], scalar1=0.044715 * 1.5957691216057308,
                                      scalar2=1.5957691216057308, op0=ALU.mult, op1=ALU.add), [g1], [g1])
        dve(lambda e: e.tensor_tensor(out=g1[:], in0=g1[:], in1=hid[:], op=ALU.mult), [g1, hid], [g1])
        k.op("act", lambda e: e.activation(out=g1[:], in_=g1[:], func=AF.Sigmoid), reads=[g1], writes=[g1])
        dve(lambda e: e.tensor_tensor(out=g1[:], in0=g1[:], in1=hid[:], op=ALU.mult), [g1, hid], [g1])
        dve(lambda e: e.tensor_tensor(out=wgt[:], in0=g1[:], in1=gate_t[m][:], op=ALU.mult), [g1, gate_t[m]], [wgt])

    def run_all(gen):
        while advance(gen):
            pass

    run_all(route(0))
    load_x(0)
    r1 = route(1) if mt > 1 else None
    for s in range(128):
        u_slot(0, s)
        if r1 is not None:
            advance(r1, 2)
    if r1 is not None:
        run_all(r1)
    gelu_wgt(0)
    HD2 = D // 2
    for m in range(mt):
        xt, wgt = xts[m % 2], wgts[m % 2]
        has_next = m + 1 < mt
        r2 = route(m + 2) if m + 2 < mt else None
        if has_next:
            load_x(m + 1)
        for half in range(2):
            for s in range(128):
                vb_ = vg[s % NG]
                k.gather(vb_[:], v_d[:, :].rearrange("e (h c) -> (e h) c", h=2), evidx_t[m][half][:, s:s + 1], reads=[evidx_t[m][half], v_d], writes=[vb_])
                d_ = dgs[s % 8]
                dve(lambda e: e.tensor_scalar(out=d_[:], in0=ident_b[:], scalar1=wgt[:, s:s + 1], scalar2=None,
                                              op0=ALU.mult), [ident_b, wgt], [d_])
                for c in range(4):
                    k.op("pe", lambda e: e.matmul(acc[c][:], lhsT=d_[:], rhs=vb_[:, c * 512:(c + 1) * 512],
                                                  start=(s == 0), stop=(s == 127)), reads=[d_, vb_], writes=[acc[c]],
                         sig=(c == 3))
                if half == 0 and has_next:
                    u_slot(m + 1, s)
                if half == 1 and r2 is not None:
                    advance(r2, 2)
            for c in range(4):
                z_ = zo[c % 2]
                col = half * HD2 + c * 512
                dve(lambda e: e.scalar_tensor_tensor(out=z_[:], in0=xt[:, col:col + 512], scalar=ALPHA, in1=acc[c][:],
                                                     op0=ALU.mult, op1=ALU.add), [xt, acc[c]], [z_])
                k.dma("sp", z_d[m * 128:(m + 1) * 128, col:col + 512], z_[:], reads=[z_], writes=[z_d])
        if r2 is not None:
            run_all(r2)
        if has_next:
            gelu_wgt(m + 1)


def peer_consts():
    pc = np.zeros((128, 48), np.float32)
    pc[:, 0:16] = np.arange(16)
    pc[:, 16:32] = 16 * np.arange(16)
    pc[:, 32:48] = 16 * np.arange(16) + 16
    return pc


def build_peer(final, mt=MT):
    nc = bass.Bass("TRN2", target_bir_lowering=False)
    with ExitStack() as st:
        k = K(nc, st)
        x1 = k.dram("x1", [T, D], F32, "ExternalInput")
        x1T = k.dram("x1T", [D, T], BF16, "ExternalInput")
        wq = k.dram("wq", [D, 2048], F32, "ExternalInput")
        keysT = k.dram("keysT", [128, 2048], F32, "ExternalInput")
        ub16 = k.dram("u", [16384, D], BF16, "ExternalInput")
        vb16 = k.dram("v", [16384, D], BF16, "ExternalInput")
        g = k.dram("g", [1, D], F32, "ExternalInput")
        b = k.dram("b", [1, D], F32, "ExternalInput")
        pconst = k.dram("pconst", [128, 48], F32, "ExternalInput")
        identb = k.dram("identb", [128, 128], BF16, "ExternalInput")
        sc = k.dram("sc", [T, 2048], F32, "Internal")
        z = k.dram("z", [T, D], F32, "Internal")
        x2 = k.dram("x2", [T, D], F32, "ExternalOutput")
        x2T = None if final else k.dram("x2T", [D, T], BF16, "ExternalOutput")
        with ExitStack() as ph:
            k.stack = ph
            xb = [k.sbuf(f"pxb{i}", [128, T], BF16) for i in range(KT)]
            phase_peer_scores(k, "ps", x1T, wq, keysT, sc, xb)
            barrier(k)
        with ExitStack() as ph:
            k.stack = ph
            phase_peer_main(k, "pm", x1, x1T, sc, ub16, vb16, pconst, identb, z, mt=mt)
            barrier(k)
        with ExitStack() as ph:
            k.stack = ph
            ps_tr = [k.psum(f"lnpt{i}", [128, 8, 128], BF16) for i in range(2)]
            phase_ln(k, "pl", z, g, b, x2, x2T, identb, ps_tr, mt=mt)
            barrier(k)
        k.stack = st
        k.finish("sp")
    return nc


LAMBDA_INIT = 0.8 - 0.6 * math.exp(-0.3 * 1)
NKT_FULL = SEQ // 128


def dense_unit(k, kT, qT, q0, vaug, W, accs, ps_s, pts, cnt, scale, LA=2):
    slots = {}

    def score(kt):
        ps = ps_s[cnt["s"] % len(ps_s)]
        pt = pts[cnt["s"] % len(pts)]
        cnt["s"] += 1
        slots[kt] = pt
        k.op("pe", lambda g: g.matmul(ps[:], lhsT=kT[:, kt * 128:(kt + 1) * 128], rhs=qT[:, q0:q0 + 512],
                                      start=True, stop=True), reads=[kT, qT], writes=[ps])
        k.op("act", lambda g: g.activation(out=pt[:], in_=ps[:], func=AF.Exp, scale=scale), reads=[ps], writes=[pt])

    for kt in range(min(LA, NKT_FULL)):
        score(kt)
    for kt in range(NKT_FULL):
        if kt + LA < NKT_FULL:
            score(kt + LA)
        pt = slots.pop(kt)
        for j in range(4):
            k.op("pe", lambda g: g.matmul(accs[j][:, 0:W], lhsT=pt[:, j * 128:(j + 1) * 128], rhs=vaug[:, kt, 0:W],
                                          start=(kt == 0), stop=(kt == NKT_FULL - 1)), reads=[pt, vaug], writes=[accs[j]])


def phase_attn_cd(k, pre, qc_d, kc_d, vc_d, qd_d, kd_d, vd_d, lam_d, subg_d, identb_d, oT_d, nchunks=SEQ // 512):
    scale = 1.0 / math.sqrt(128.0)
    ident_b = k.sbuf(f"{pre}ident", [128, 128], BF16)
    k.dma("sp", ident_b[:], identb_d[:, :], reads=[identb_d], writes=[ident_b])
    lam_t = k.sbuf(f"{pre}lamt", [128, 4, 128], F32)
    k.dma("sp", lam_t[:], lam_d[:, :].unsqueeze(0).to_broadcast([128, 4, 128]), reads=[lam_d], writes=[lam_t])
    lprod = k.sbuf(f"{pre}lprod", [128, 2, 128], F32)
    lsum = k.sbuf(f"{pre}lsum", [128, 2], F32)
    neglam = k.sbuf(f"{pre}neglam", [128, 1], F32)
    lam4 = lam_t[:].rearrange("p (a b) d -> p a b d", b=2)
    k.op("dve", lambda g: g.tensor_tensor(out=lprod[:], in0=lam4[:, :, 0, :], in1=lam4[:, :, 1, :], op=ALU.mult),
         reads=[lam_t], writes=[lprod])
    k.op("dve", lambda g: g.tensor_reduce(out=lsum[:], in_=lprod[:], axis=AX.X, op=ALU.add), reads=[lprod], writes=[lsum])
    k.op("act", lambda g: g.activation(out=lsum[:], in_=lsum[:], func=AF.Exp), reads=[lsum], writes=[lsum])
    k.op("dve", lambda g: g.scalar_tensor_tensor(out=neglam[:], in0=lsum[:, 1:2], scalar=-LAMBDA_INIT, in1=lsum[:, 0:1],
                                                 op0=ALU.add, op1=ALU.subtract), reads=[lsum], writes=[neglam])
    subg = k.sbuf(f"{pre}subg", [128, 256], F32)
    k.dma("sp", subg[:], subg_d[0:1, :].to_broadcast([128, 256]), reads=[subg_d], writes=[subg])
    k.op("dve", lambda g: g.tensor_scalar(out=subg[:], in0=subg[:], scalar1=1.0 - LAMBDA_INIT, scalar2=None, op0=ALU.mult),
         reads=[subg], writes=[subg])

    kTs = [k.sbuf(f"{pre}kT{i}", [128, SEQ], BF16) for i in range(2)]
    qTs = [k.sbuf(f"{pre}qT{i}", [128, SEQ], BF16) for i in range(2)]
    vcs = k.sbuf(f"{pre}vc", [128, NKT_FULL, 257], BF16)
    vds = k.sbuf(f"{pre}vd", [128, NKT_FULL, 129], BF16)
    ps_s = [k.psum(f"{pre}pss{i}", [128, 512], F32) for i in range(3)]
    pts = [k.sbuf(f"{pre}pt{i}", [128, 512], BF16) for i in range(4)]
    accs = [k.psum(f"{pre}acc{i}", [128, 512], F32) for i in range(4)]
    ps_t = [k.psum(f"{pre}pst{i}", [128, 8, 128], BF16) for i in range(1)]
    o1s = k.sbuf(f"{pre}o1s", [128, nchunks * 4, 256], F32)
    den = k.sbuf(f"{pre}den", [128, 1], F32)
    o2 = k.sbuf(f"{pre}o2", [128, 256], F32)
    junk = k.sbuf(f"{pre}junk", [128, 256], F32)
    ssq = k.sbuf(f"{pre}ssq", [128, 1], F32)
    ob = [k.sbuf(f"{pre}ob{i}", [128, 256], BF16) for i in range(2)]
    oTst = [k.sbuf(f"{pre}oTst{i}", [128, 2, 512], BF16) for i in range(2)]
    cnt = {"s": 0, "o": 0, "t": 0}

    k.dma("sp", vcs[:], vc_d[:, :].rearrange("(j p) c -> p j c", p=128), reads=[vc_d], writes=[vcs])
    k.dma("sp", vds[:], vd_d[:, :].rearrange("(j p) c -> p j c", p=128), reads=[vd_d], writes=[vds])

    def recip_den(acc, W):
        k.op("dve", lambda g: g.reciprocal(out=den[:], in_=acc[:, W - 1:W]), reads=[acc], writes=[den])

    def emit_T(obuf, nblk, blk0, q0, j, last):
        pt_ = ps_t[cnt["t"] % len(ps_t)]
        cnt["t"] += 1
        for b_ in range(nblk):
            k.op("pe", lambda g: g.transpose(out=pt_[:, b_, :], in_=obuf[:, b_ * 128:(b_ + 1) * 128], identity=ident_b[:]),
                 reads=[obuf, ident_b], writes=[pt_])
        stg = oTst[(q0 // 512) % 2]
        k.copy("dve", stg, stg[:, 0:nblk, j * 128:(j + 1) * 128], pt_, pt_[:, 0:nblk, :])
        if last:
            k.dma("sp", oT_d[blk0:blk0 + nblk, :, q0:q0 + 512].rearrange("b p t -> p b t"), stg[:, 0:nblk, :],
                  reads=[stg], writes=[oT_d])

    for mp in range(2):
        kT, qT = kTs[mp], qTs[mp]
        k.dma("sp", kT[:], kc_d[mp], reads=[kc_d], writes=[kT])
        k.dma("sp", qT[:], qc_d[mp], reads=[qc_d], writes=[qT])
        for ch in range(nchunks):
            q0 = ch * 512
            dense_unit(k, kT, qT, q0, vcs, 257, accs, ps_s, pts, cnt, scale)
            for j in range(4):
                acc = accs[j]
                recip_den(acc, 257)
                if mp == 0:
                    k.op("dve", lambda g: g.tensor_scalar(out=o1s[:, ch * 4 + j, :], in0=acc[:, 0:256], scalar1=den[:, 0:1],
                                                          scalar2=None, op0=ALU.mult), reads=[acc, den], writes=[o1s])
                else:
                    k.op("dve", lambda g: g.tensor_scalar(out=o2[:], in0=acc[:, 0:256], scalar1=den[:, 0:1], scalar2=None,
                                                          op0=ALU.mult), reads=[acc, den], writes=[o2])
                    k.op("dve", lambda g: g.scalar_tensor_tensor(out=o2[:], in0=o2[:], scalar=neglam[:, 0:1],
                                                                 in1=o1s[:, ch * 4 + j, :], op0=ALU.mult, op1=ALU.add),
                         reads=[o2, neglam, o1s], writes=[o2])
                    k.op("act", lambda g: g.activation(out=junk[:], in_=o2[:], func=AF.Square, accum_out=ssq[:]),
                         reads=[o2], writes=[junk, ssq])
                    k.op("dve", lambda g: g.tensor_scalar(out=ssq[:], in0=ssq[:], scalar1=1.0 / 256, scalar2=1e-6,
                                                          op0=ALU.mult, op1=ALU.add), reads=[ssq], writes=[ssq])
                    k.op("act", lambda g: g.activation(out=ssq[:], in_=ssq[:], func=AF.Sqrt), reads=[ssq], writes=[ssq])
                    k.op("dve", lambda g: g.reciprocal(out=ssq[:], in_=ssq[:]), reads=[ssq], writes=[ssq])
                    o_ = ob[cnt["o"] % 2]
                    cnt["o"] += 1
                    k.op("dve", lambda g: g.scalar_tensor_tensor(out=o_[:], in0=o2[:], scalar=ssq[:, 0:1], in1=subg[:],
                                                                 op0=ALU.mult, op1=ALU.mult), reads=[o2, ssq, subg], writes=[o_])
                    emit_T(o_, 2, 0, q0, j, j == 3)
    kT = kTs[0]
    k.dma("sp", kT[:], kd_d[:, :], reads=[kd_d], writes=[kT])
    for hq in range(2):
        qT = qTs[hq]
        k.dma("sp", qT[:], qd_d[hq], reads=[qd_d], writes=[qT])
        for ch in range(nchunks):
            q0 = ch * 512
            dense_unit(k, kT, qT, q0, vds, 129, accs, ps_s, pts, cnt, scale)
            for j in range(4):
                acc = accs[j]
                recip_den(acc, 129)
                o_ = ob[cnt["o"] % 2]
                cnt["o"] += 1
                k.op("dve", lambda g: g.tensor_scalar(out=o_[:, 0:128], in0=acc[:, 0:128], scalar1=den[:, 0:1], scalar2=None,
                                                      op0=ALU.mult), reads=[acc, den], writes=[o_])
                emit_T(o_, 1, 2 + hq, q0, j, j == 3)


def build_attn_cd(nchunks=SEQ // 512):
    nc = bass.Bass("TRN2", target_bir_lowering=False)
    with ExitStack() as st:
        k = K(nc, st)
        qc = k.dram("qc", [2, 128, SEQ], BF16, "ExternalInput")
        kc = k.dram("kc", [2, 128, SEQ], BF16, "ExternalInput")
        vc = k.dram("vc", [SEQ, 257], BF16, "ExternalInput")
        qd = k.dram("qd", [2, 128, SEQ], BF16, "ExternalInput")
        kd = k.dram("kd", [128, SEQ], BF16, "ExternalInput")
        vd = k.dram("vd", [SEQ, 129], BF16, "ExternalInput")
        lam = k.dram("lam", [4, 128], F32, "ExternalInput")
        subg = k.dram("subg", [1, 256], F32, "ExternalInput")
        identb = k.dram("identb", [128, 128], BF16, "ExternalInput")
        oT = k.dram("oT", [4, 128, SEQ], BF16, "ExternalOutput")
        phase_attn_cd(k, "e", qc, kc, vc, qd, kd, vd, lam, subg, identb, oT, nchunks=nchunks)
        k.finish("sp")
    return nc


def _phase(k, outer):
    class _P:
        def __enter__(self_):
            self_.ph = ExitStack()
            self_.ph.__enter__()
            k.stack = self_.ph
            return self_.ph

        def __exit__(self_, *a):
            barrier(k)
            k.stack = outer
            return self_.ph.__exit__(*a)
    return _P()


def _peer_phases(k, st, tag, x1, x1T, wq, keysT, ub16, vb16, pconst, identb, sc, z, g, b, x2, x2T):
    with _phase(k, st):
        xb = [k.sbuf(f"{tag}pxb{i}", [128, T], BF16) for i in range(KT)]
        phase_peer_scores(k, tag + "ps", x1T, wq, keysT, sc, xb)
    with _phase(k, st):
        phase_peer_main(k, tag + "pm", x1, x1T, sc, ub16, vb16, pconst, identb, z)
    with _phase(k, st):
        ps_tr = [k.psum(f"{tag}lnpt{i}", [128, 8, 128], BF16) for i in range(2)]
        phase_ln(k, tag + "pl", z, g, b, x2, x2T, identb, ps_tr)


def build_mid():
    nc = bass.Bass("TRN2", target_bir_lowering=False)
    with ExitStack() as st:
        k = K(nc, st)
        EI, EO = "ExternalInput", "ExternalOutput"
        qT = k.dram("qT", [AB_NFM, 128, T], BF16, EI)
        kTe = k.dram("kTe", [28, 128, TE], BF16, EI)
        vaug = k.dram("vaug", [TE, 28, 129], BF16, EI)
        sink = k.dram("sink", [1, 16], F32, EI)
        masks = k.dram("masks", [128, N_MASKS, 128], BF16, EI)
        identb = k.dram("identb", [128, 128], BF16, EI)
        wo = k.dram("wo", [24 * 128, D], F32, EI)
        x = k.dram("x", [T, D], F32, EI)
        gm, bm = k.dram("gm", [1, D], F32, EI), k.dram("bm", [1, D], F32, EI)
        wq = k.dram("wq", [D, 2048], F32, EI)
        keysT = k.dram("keysT", [128, 2048], F32, EI)
        u16 = k.dram("u", [16384, D], BF16, EI)
        v16 = k.dram("v", [16384, D], BF16, EI)
        gf, bf = k.dram("gf", [1, D], F32, EI), k.dram("bf", [1, D], F32, EI)
        pconst = k.dram("pconst", [128, 48], F32, EI)
        w1 = k.dram("w1", [D, 9216], F32, EI)
        tabd = {nm: k.dram(nm, [T, 128], F32, EI) for nm in ("ccp", "ssp", "cca", "ssa")}
        qgd, kgd = k.dram("qg", [1, 128], F32, EI), k.dram("kg", [1, 128], F32, EI)
        oT = k.dram("oT", [24, 128, T], BF16)
        z = k.dram("z", [T, D], F32)
        x1 = k.dram("x1", [T, D], F32)
        x1T = k.dram("x1T", [D, T], BF16)
        sc = k.dram("sc", [T, 2048], F32)
        z2 = k.dram("z2", [T, D], F32)
        x2 = k.dram("x2", [T, D], F32, EO)
        x2T = k.dram("x2T", [D, T], BF16)
        qkT1 = k.dram("qkT1", [CD_NFM, 128, T], BF16, EO)
        vtm1 = k.dram("vtm1", [T, CD_NV], BF16, EO)
        with _phase(k, st):
            phase_attn_ab(k, "b", qT, kTe, vaug, sink, masks, identb, oT)
        with _phase(k, st):
            res = alloc_linear_res(k, "c")
            phase_outproj(k, "c", oT, 24, wo, x, z, res)
        with _phase(k, st):
            ps_tr = [k.psum(f"lnpt{i}", [128, 8, 128], BF16) for i in range(2)]
            phase_ln(k, "l", z, gm, bm, x1, x1T, identb, ps_tr)
        _peer_phases(k, st, "p", x1, x1T, wq, keysT, u16, v16, pconst, identb, sc, z2, gf, bf, x2, x2T)
        with _phase(k, st):
            res = alloc_linear_res(k, "a")
            ident_b = k.sbuf("identb_s", [128, 128], BF16)
            k.dma("sp", ident_b[:], identb[:, :], reads=[identb], writes=[ident_b])
            tabs = load_tabs(k, "a", list(tabd.keys()), tabd)
            gains = {}
            for nm, gd in (("qg", qgd), ("kg", kgd)):
                gs = k.sbuf("g_" + nm, [128, 128], F32)
                k.dma("sp", gs[:], gd[0:1, :].to_broadcast([128, 128]), reads=[gd], writes=[gs])
                gains[nm] = gs
            for kt in range(KT):
                k.dma("sp", res["xb"][kt][:], x2T[kt * 128:(kt + 1) * 128, :], reads=[x2T], writes=[res["xb"][kt]])
            phase_inproj(k, "a", x2T, w1, 9216, CD_TYPES, tabs, gains, ident_b, qkT1, vtm1, res)
        k.finish("sp")
    return nc


def build_tail():
    nc = bass.Bass("TRN2", target_bir_lowering=False)
    with ExitStack() as st:
        k = K(nc, st)
        EI, EO = "ExternalInput", "ExternalOutput"
        oT = k.dram("oT", [32, 128, T], BF16, EI)
        wo = k.dram("wo", [D, D], F32, EI)
        x = k.dram("x", [T, D], F32, EI)
        gm, bm = k.dram("gm", [1, D], F32, EI), k.dram("bm", [1, D], F32, EI)
        identb = k.dram("identb", [128, 128], BF16, EI)
        wq = k.dram("wq", [D, 2048], F32, EI)
        keysT = k.dram("keysT", [128, 2048], F32, EI)
        u16 = k.dram("u", [16384, D], BF16, EI)
        v16 = k.dram("v", [16384, D], BF16, EI)
        gf, bf = k.dram("gf", [1, D], F32, EI), k.dram("bf", [1, D], F32, EI)
        pconst = k.dram("pconst", [128, 48], F32, EI)
        z = k.dram("z", [T, D], F32)
        x1 = k.dram("x1", [T, D], F32)
        x1T = k.dram("x1T", [D, T], BF16)
        sc = k.dram("sc", [T, 2048], F32)
        z2 = k.dram("z2", [T, D], F32)
        out = k.dram("out", [T, D], F32, EO)
        with _phase(k, st):
            res = alloc_linear_res(k, "c")
            phase_outproj(k, "c", oT, 32, wo, x, z, res)
        with _phase(k, st):
            ps_tr = [k.psum(f"lnpt{i}", [128, 8, 128], BF16) for i in range(2)]
            phase_ln(k, "l", z, gm, bm, x1, x1T, identb, ps_tr)
        _peer_phases(k, st, "p", x1, x1T, wq, keysT, u16, v16, pconst, identb, sc, z2, gf, bf, out, None)
        k.finish("sp")
    return nc


def _run(nc, in_maps):
    res = run_bass_kernel_spmd(nc, in_maps, core_ids=list(range(NCORES)))
    return res.results


def _aug_ones(v):
    out = np.zeros(v.shape[:-1] + (v.shape[-1] + 1,), dtype=v.dtype)
    out[..., :-1] = v
    out[..., -1] = 1.0
    return out


def kernel(x, w_in_ab, sink_a, w_out_ab, w_in_cd, lam_q1, lam_k1, lam_q2, lam_k2, subln_g, q_norm_g, k_norm_g,
           w_out_cd, ln_mix_g, ln_mix_b, peer_wq, peer_keys, peer_u, peer_v, ln_ffn_g, ln_ffn_b):
    f32 = lambda a: np.ascontiguousarray(np.asarray(a, dtype=np.float32))
    x = f32(x)[0]
    ccp, ssp, cca, ssa = rope_tables()
    identb = np.eye(128, dtype=np.float32).astype(NP_BF16)
    masks = band_masks()
    pconst = peer_consts()
    cs = [slice(c * T, (c + 1) * T) for c in range(NCORES)]

    def keysT_of(l):
        return np.ascontiguousarray(f32(peer_keys)[l].reshape(16, 128, 128).transpose(2, 0, 1).reshape(128, 2048))

    w0 = f32(w_in_ab)[0]
    RW = 16384 // NCORES
    pu, pv = f32(peer_u), f32(peer_v)
    r = _run(build_inproj(0, conv=True), [{"xT": np.ascontiguousarray(x[cs[c]].T), "w": w0, "identb": identb,
                                           "ccp": ccp[cs[c]], "ssp": ssp[cs[c]],
                                           "u0p": pu[0, c * RW:(c + 1) * RW], "v0p": pv[0, c * RW:(c + 1) * RW],
                                           "u1p": pu[1, c * RW:(c + 1) * RW], "v1p": pv[1, c * RW:(c + 1) * RW]}
                                          for c in range(NCORES)])
    qkT_all = [r[c]["qkT"] for c in range(NCORES)]
    tabs16 = {nm: np.concatenate([r[c][nm + "b"] for c in range(NCORES)], axis=0) for nm in ("u0", "v0", "u1", "v1")}
    ext = host_ext_kv(qkT_all, [r[c]["vtm"] for c in range(NCORES)], [16, 17, 18, 19] + list(range(44, 68)), None)
    sink = f32(sink_a)[0].reshape(1, 16)
    lmg, lmb, lfg, lfb = f32(ln_mix_g), f32(ln_mix_b), f32(ln_ffn_g), f32(ln_ffn_b)
    pwq = f32(peer_wq)
    qg, kg = f32(q_norm_g)[0:1], f32(k_norm_g)[0:1]
    wo0, w1 = f32(w_out_ab)[0], f32(w_in_cd)[0]
    kT0 = keysT_of(0)
    r = _run(build_mid(), [{"qT": qkT_all[c], "kTe": ext[c][0], "vaug": ext[c][1], "sink": sink, "masks": masks, "identb": identb,
                            "wo": wo0, "x": x[cs[c]], "gm": lmg[0:1], "bm": lmb[0:1], "wq": pwq[0], "keysT": kT0,
                            "u": tabs16["u0"], "v": tabs16["v0"], "gf": lfg[0:1], "bf": lfb[0:1], "pconst": pconst, "w1": w1,
                            "ccp": ccp[cs[c]], "ssp": ssp[cs[c]], "cca": cca[cs[c]], "ssa": ssa[cs[c]], "qg": qg, "kg": kg}
                           for c in range(NCORES)])
    del ext, qkT_all
    x2 = [r[c]["x2"] for c in range(NCORES)]
    qk = np.concatenate([r[c]["qkT1"] for c in range(NCORES)], axis=2)
    vt = np.concatenate([r[c]["vtm1"] for c in range(NCORES)], axis=0)
    lam = np.concatenate([f32(lam_q1)[0:1], f32(lam_k1)[0:1], f32(lam_q2)[0:1], f32(lam_k2)[0:1]], 0)
    subg = f32(subln_g)[0:1]
    maps = []
    for c in range(NCORES):
        maps.append({"qc": np.ascontiguousarray(qk[2 * c:2 * c + 2]), "kc": np.ascontiguousarray(qk[16 + 2 * c:18 + 2 * c]),
                     "vc": _aug_ones(vt[:, 256 * c:256 * c + 256]), "qd": np.ascontiguousarray(qk[32 + 2 * c:34 + 2 * c]),
                     "kd": np.ascontiguousarray(qk[48 + c // 2]),
                     "vd": _aug_ones(vt[:, 2048 + 128 * (c // 2):2048 + 128 * (c // 2) + 128]),
                     "lam": lam, "subg": subg, "identb": identb})
    r = _run(build_attn_cd(), maps)
    del qk, vt, maps
    ofull = np.zeros((32, 128, SEQ), dtype=NP_BF16)
    for c in range(NCORES):
        o = r[c]["oT"]
        ofull[2 * c] = o[0]
        ofull[2 * c + 1] = o[1]
        ofull[16 + 2 * c] = o[2]
        ofull[16 + 2 * c + 1] = o[3]
    wo1 = f32(w_out_cd)[0]
    kT1 = keysT_of(1)
    r = _run(build_tail(), [{"oT": np.ascontiguousarray(ofull[:, :, cs[c]]), "wo": wo1, "x": x2[c], "gm": lmg[1:2], "bm": lmb[1:2],
                             "identb": identb, "wq": pwq[1], "keysT": kT1, "u": tabs16["u1"], "v": tabs16["v1"],
                             "gf": lfg[1:2], "bf": lfb[1:2], "pconst": pconst} for c in range(NCORES)])
    out = np.concatenate([r[c]["out"] for c in range(NCORES)], axis=0)
    return out[None].astype(np.float32)
```

```python
import math
from contextlib import ExitStack

import numpy as np
import ml_dtypes

import concourse.bass as bass
import concourse.mybir as mybir
from concourse.bass_utils import run_bass_kernel_spmd

F32 = mybir.dt.float32
BF16 = mybir.dt.bfloat16
I32 = mybir.dt.int32
U32 = mybir.dt.uint32
AF = mybir.ActivationFunctionType
ALU = mybir.AluOpType
AX = mybir.AxisListType

NCORES = 8
SEQ = 8192
D = 4096
T = SEQ // NCORES
MT = T // 128
KT = D // 128
HD = 128
SEM_LIMIT = 2000
NP_BF16 = ml_dtypes.bfloat16
SAME_ENGINE_WAIT = {"pe": False, "dve": True, "act": True, "pool": True, "sp": False}


class Buf:
    __slots__ = ("t", "lw", "rd", "name")

    def __init__(self, t, name=""):
        self.t = t
        self.lw = None
        self.rd = []
        self.name = name

    def __getitem__(self, idx):
        return self.t[idx]


class K:
    def __init__(self, nc, stack):
        self.nc = nc
        self.stack = stack
        self.sem_stack = stack
        self.eng = {"pe": nc.tensor, "dve": nc.vector, "act": nc.scalar, "pool": nc.gpsimd, "sp": nc.sync}
        self.cur = {}
        self.seen = {}
        self.nsem = 0
        self.dma_rr = {}
        self.same_engine_wait = dict(SAME_ENGINE_WAIT)
        self.n_inst = 0
        self.pending = {}

    def sbuf(self, name, shape, dtype):
        return Buf(self.stack.enter_context(self.nc.sbuf_tensor(name, list(shape), dtype)), name)

    def psum(self, name, shape, dtype=F32):
        return Buf(self.stack.enter_context(self.nc.psum_tensor(name, list(shape), dtype)), name)

    def dram(self, name, shape, dtype, kind="Internal"):
        return Buf(self.nc.dram_tensor(name, list(shape), dtype, kind=kind).ap(), name)

    def new_sem(self, name):
        self.nsem += 1
        return self.sem_stack.enter_context(self.nc.semaphore(f"{name}_{self.nsem}"))

    def _wait(self, e, tok):
        if tok is None:
            return
        sem, val, src = tok
        if src == e and not self.same_engine_wait[e]:
            return
        seen = self.seen.setdefault(e, {})
        kk = id(sem)
        if seen.get(kk, 0) >= val:
            return
        self.eng[e].wait_ge(sem, val)
        seen[kk] = val

    def _deps(self, e, reads, writes):
        for b in reads:
            self._wait(e, b.lw)
        for b in writes:
            self._wait(e, b.lw)
            for r in b.rd:
                self._wait(e, r)

    def _commit(self, tok, reads, writes):
        for b in reads:
            b.rd.append(tok)
            if len(b.rd) > 64:
                b.rd = b.rd[-48:]
        for b in writes:
            b.lw = tok
            b.rd = []

    def op(self, e, fn, reads=(), writes=(), sig=True):
        self._deps(e, reads, writes)
        if not sig:
            pr, pw = self.pending.setdefault(e, ([], []))
            for b in reads:
                if b not in pr:
                    pr.append(b)
            for b in writes:
                if b not in pw:
                    pw.append(b)
            fn(self.eng[e])
            self.n_inst += 1
            return None
        if e in self.pending:
            pr, pw = self.pending.pop(e)
            reads = list(reads) + [b for b in pr if b not in reads]
            writes = list(writes) + [b for b in pw if b not in writes]
        c = self.cur.get(e)
        if c is None or c[1] >= SEM_LIMIT:
            c = [self.new_sem("s" + e), 0]
            self.cur[e] = c
        ins = fn(self.eng[e])
        c[1] += 1
        ins.then_inc(c[0], 1)
        tok = (c[0], c[1], e)
        self._commit(tok, reads, writes)
        self.n_inst += 1
        return tok

    def _dma_slot(self, q):
        R = 8
        ring = self.dma_rr.setdefault(q, {"sems": [], "i": 0})
        i = ring["i"]
        ring["i"] += 1
        slot = i % R
        if len(ring["sems"]) <= slot:
            ring["sems"].append([self.new_sem("d" + q), 0, None])
        s = ring["sems"][slot]
        self._wait(q, s[2])
        if s[1] >= SEM_LIMIT:
            s[0] = self.new_sem("d" + q)
            s[1] = 0
            s[2] = None
        return s

    def dma(self, q, out, in_, reads=(), writes=(), **kw):
        s = self._dma_slot(q)
        self._deps(q, reads, writes)
        ins = self.eng[q].dma_start(out=out, in_=in_, **kw)
        s[1] += 16
        ins.then_inc(s[0], 16)
        tok = (s[0], s[1], "dma")
        s[2] = tok
        self._commit(tok, reads, writes)
        self.n_inst += 1
        return tok

    def gather(self, out, in_, idx_ap, reads=(), writes=(), **kw):
        q = "pool"
        s = self._dma_slot(q)
        self._deps(q, reads, writes)
        ins = self.eng[q].indirect_dma_start(out=out, out_offset=None, in_=in_,
                                             in_offset=bass.IndirectOffsetOnAxis(ap=idx_ap, axis=0), **kw)
        s[1] += 16
        ins.then_inc(s[0], 16)
        tok = (s[0], s[1], "dma")
        s[2] = tok
        self._commit(tok, reads, writes)
        self.n_inst += 1
        return tok

    def finish(self, e="sp"):
        for q, ring in self.dma_rr.items():
            for s in ring["sems"]:
                self._wait(e, s[2])

    def copy(self, e, out_b, out_ap, in_b, in_ap):
        if e == "act":
            return self.op("act", lambda g: g.copy(out=out_ap, in_=in_ap), reads=[in_b], writes=[out_b])
        return self.op(e, lambda g: g.tensor_copy(out=out_ap, in_=in_ap), reads=[in_b], writes=[out_b])


def bcast_free(ap, shape):
    return ap.to_broadcast(list(shape))


def load_xT(k, xT_dram, xb, stage):
    for kt in range(KT):
        s = stage[kt % len(stage)]
        k.dma("sp", s[:], xT_dram[kt * 128:(kt + 1) * 128, :], reads=[xT_dram], writes=[s])
        k.copy("act" if kt % 2 == 0 else "dve", xb[kt], xb[kt][:], s, s[:])


def stream_linear(k, xb, w_dram, N, wst, wb, ps_banks, epilogue, NCH=512, KQ=4, cast_engs=("act", "dve"),
                  state=None):
    nkt = len(xb)
    NKQ = nkt // KQ
    st = state if state is not None else {"cnt": 0, "oc": 0}
    for c in range(N // NCH):
        n0 = c * NCH
        b = c % 2
        for q in range(NKQ):
            s = wst[st["cnt"] % len(wst)]
            src = w_dram[q * KQ * 128:(q + 1) * KQ * 128, n0:n0 + NCH].rearrange("(j p) c -> p j c", p=128)
            k.dma("sp", s[:], src, reads=[w_dram], writes=[s])
            k.copy(cast_engs[st["cnt"] % len(cast_engs)], wb[b][q], wb[b][q][:], s, s[:])
            st["cnt"] += 1
        for m in range(MT):
            p = ps_banks[st["oc"] % len(ps_banks)]
            for kt in range(nkt):
                q, j = divmod(kt, KQ)
                k.op("pe", lambda g: g.matmul(p[:], lhsT=xb[kt][:, m * 128:(m + 1) * 128], rhs=wb[b][q][:, j, :],
                                              start=(kt == 0), stop=(kt == nkt - 1)),
                     reads=[xb[kt], wb[b][q]], writes=[p], sig=(kt == nkt - 1))
            epilogue(c, m, p)
            st["oc"] += 1
    return st


def rope_tile(k, src_b, src3, cc, ss, m, groups, rot_w, rb, tmp_t, tmp_u):
    H4 = 4
    w = rot_w
    ccb = cc[:, m, 0:w].unsqueeze(1).to_broadcast([128, H4, w])
    k.op("dve", lambda g: g.tensor_tensor(out=tmp_t[:, :, 0:w], in0=src3[:, :, 0:w], in1=ccb, op=ALU.mult),
         reads=[src_b, cc], writes=[tmp_t])
    first = True
    for (lo, half) in groups:
        hi = lo + half
        s_lo = ss[:, m, lo:lo + half].unsqueeze(1).to_broadcast([128, H4, half])
        s_hi = ss[:, m, hi:hi + half].unsqueeze(1).to_broadcast([128, H4, half])
        k.op("dve", lambda g: g.tensor_tensor(out=tmp_u[:, :, lo:lo + half], in0=src3[:, :, hi:hi + half], in1=s_lo,
                                              op=ALU.mult), reads=[src_b, ss], writes=[tmp_u])
        k.op("dve", lambda g: g.tensor_tensor(out=tmp_u[:, :, hi:hi + half], in0=src3[:, :, lo:lo + half], in1=s_hi,
                                              op=ALU.mult), reads=[src_b, ss], writes=[tmp_u])
    k.op("dve", lambda g: g.tensor_tensor(out=rb[:, :, 0:w], in0=tmp_t[:, :, 0:w], in1=tmp_u[:, :, 0:w], op=ALU.add),
         reads=[tmp_t, tmp_u], writes=[rb])
    if w < 128:
        k.op("act", lambda g: g.copy(out=rb[:, :, w:128], in_=src3[:, :, w:128]), reads=[src_b], writes=[rb])


def phase_inproj(k, pre, xT_dram, w_dram, N, chunk_types, tabs, gains, ident_b, qkT_dram, v_dram, res, bg=None):
    xb = res["xb"]
    wst = res["wst"]
    wb = res["wb"]
    ps = res["ps_mm"]
    psT = res["ps_tr"]
    rbs = [k.sbuf(f"{pre}rb{i}", [128, 4, 128], BF16) for i in range(2)]
    tmp_t = k.sbuf(f"{pre}tt", [128, 4, 128], F32)
    tmp_u = k.sbuf(f"{pre}tu", [128, 4, 128], F32)
    qn = k.sbuf(f"{pre}qn", [128, 4, 128], F32)
    junk = k.sbuf(f"{pre}junk", [128, 128], F32)
    ssq = k.sbuf(f"{pre}ssq", [128, 4], F32)
    rstd = k.sbuf(f"{pre}rstd", [128, 4], F32)
    fmst = [k.sbuf(f"{pre}fm{i}", [128, 4, T], BF16) for i in range(2)]
    vst = [k.sbuf(f"{pre}vst{i}", [128, 512], BF16) for i in range(2)]
    cnt = {"e": 0}

    def epilogue(c, m, p):
        ty = chunk_types[c]
        i = cnt["e"]
        cnt["e"] += 1
        if bg is not None:
            bg()
            bg()
        if ty[0] == "v":
            vs = vst[i % 2]
            k.copy("act", vs, vs[:], p, p[:])
            k.dma("sp", v_dram[m * 128:(m + 1) * 128, ty[1]:ty[1] + 512], vs[:], reads=[vs], writes=[v_dram])
            return
        rb = rbs[i % 2]
        p3 = p[:].rearrange("p (h d) -> p h d", h=4)
        if ty[0] == "rope":
            rope_tile(k, p, p3, tabs["ccp"], tabs["ssp"], m, [(0, 16)], 32, rb, tmp_t, tmp_u)
        else:
            gb = gains[ty[2]]
            for j in range(4):
                k.op("act", lambda g: g.activation(out=junk[:], in_=p[:, j * 128:(j + 1) * 128], func=AF.Square,
                                                   accum_out=ssq[:, j:j + 1]), reads=[p], writes=[junk, ssq])
            k.op("dve", lambda g: g.tensor_scalar(out=rstd[:], in0=ssq[:], scalar1=1.0 / 128, scalar2=1e-6,
                                                  op0=ALU.mult, op1=ALU.add), reads=[ssq], writes=[rstd])
            k.op("act", lambda g: g.activation(out=rstd[:], in_=rstd[:], func=AF.Sqrt), reads=[rstd], writes=[rstd])
            k.op("dve", lambda g: g.reciprocal(out=rstd[:], in_=rstd[:]), reads=[rstd], writes=[rstd])
            k.op("dve", lambda g: g.tensor_tensor(out=qn[:], in0=p3, in1=rstd[:].unsqueeze(2).to_broadcast([128, 4, 128]),
                                                  op=ALU.mult), reads=[p, rstd], writes=[qn])
            k.op("dve", lambda g: g.tensor_tensor(out=qn[:], in0=qn[:], in1=gb[:].unsqueeze(1).to_broadcast([128, 4, 128]),
                                                  op=ALU.mult), reads=[qn, gb], writes=[qn])
            rope_tile(k, qn, qn[:], tabs["cca"], tabs["ssa"], m, [(0, 32), (64, 32)], 128, rb, tmp_t, tmp_u)
        pt = psT[i % len(psT)]
        for j in range(4):
            k.op("pe", lambda g: g.transpose(out=pt[:, j, 0:128], in_=rb[:, j, :], identity=ident_b[:]),
                 reads=[rb, ident_b], writes=[pt])
        fm = fmst[c % 2]
        k.copy("act" if i % 2 else "dve", fm, fm[:, :, m * 128:(m + 1) * 128], pt, pt[:, :, 0:128])
        if m == MT - 1:
            fm0 = ty[1]
            k.dma("sp", qkT_dram[fm0:fm0 + 4].rearrange("j p t -> p j t"), fm[:], reads=[fm], writes=[qkT_dram])

    stream_linear(k, xb, w_dram, N, wst, wb, ps, epilogue, state=res["lin_state"],
                  cast_engs=("act", "dve"))


def rope_tables():
    pos = np.arange(SEQ, dtype=np.float32)
    half = 16
    inv = (np.float32(500000.0) ** (-np.arange(half, dtype=np.float32) / half)).astype(np.float32)
    ang = pos[:, None] * inv[None, :]
    c, s = np.cos(ang).astype(np.float32), np.sin(ang).astype(np.float32)
    ccp = np.ones((SEQ, 128), np.float32)
    ssp = np.zeros((SEQ, 128), np.float32)
    ccp[:, 0:16] = c
    ccp[:, 16:32] = c
    ssp[:, 0:16] = -s
    ssp[:, 16:32] = s
    rows = (np.arange(SEQ) // 64).astype(np.float32)
    cols = (np.arange(SEQ) % 64).astype(np.float32)
    inv2 = (np.float32(10000.0) ** (-np.arange(32, dtype=np.float32) / 32)).astype(np.float32)
    ar = rows[:, None] * inv2[None, :]
    ac = cols[:, None] * inv2[None, :]
    cr, sr, cc_, sc = (np.cos(ar).astype(np.float32), np.sin(ar).astype(np.float32),
                       np.cos(ac).astype(np.float32), np.sin(ac).astype(np.float32))
    cca = np.concatenate([cr, cr, cc_, cc_], 1)
    ssa = np.concatenate([-sr, sr, -sc, sc], 1)
    return ccp, ssp, cca, ssa


def alloc_linear_res(k, pre):
    res = {}
    res["xb"] = [k.sbuf(f"{pre}xb{i}", [128, T], BF16) for i in range(KT)]
    res["xst"] = [k.sbuf(f"{pre}xst{i}", [128, T], F32) for i in range(2)]
    res["wst"] = [k.sbuf(f"{pre}wst{i}", [128, 4, 512], F32) for i in range(2)]
    res["wb"] = [[k.sbuf(f"{pre}wb{b}_{q}", [128, 4, 512], BF16) for q in range(KT // 4)] for b in range(2)]
    res["ps_mm"] = [k.psum(f"{pre}psmm{i}", [128, 512], F32) for i in range(4)]
    res["ps_tr"] = [k.psum(f"{pre}pstr{i}", [128, 4, 256], BF16) for i in range(2)]
    res["lin_state"] = {"cnt": 0, "oc": 0}
    return res


def load_tabs(k, pre, names, drams):
    tabs = {}
    for nm in names:
        t = k.sbuf(f"{pre}tab_{nm}", [128, MT, 128], F32)
        k.dma("sp", t[:], drams[nm][:, :].rearrange("(m p) d -> p m d", p=128), reads=[drams[nm]], writes=[t])
        tabs[nm] = t
    return tabs


AB_TYPES = ([("rope", c * 4) for c in range(4)] + [("rope", 16)] + [("v", 0)]
            + [("rope", 20 + c * 4) for c in range(6)] + [("rope", 44 + c * 4) for c in range(6)]
            + [("v", 512 + c * 512) for c in range(6)])
AB_NFM, AB_NV = 68, 3584
CD_TYPES = ([("rope", c * 4) for c in range(4)] + [("rope", 16 + c * 4) for c in range(4)]
            + [("v", c * 512) for c in range(4)]
            + [("axial", 32 + c * 4, "qg") for c in range(4)] + [("axial", 48, "kg")] + [("v", 2048)])
CD_NFM, CD_NV = 52, 2560


def build_inproj(layer, types=None, x_bf16=False, conv=False):
    nc = bass.Bass("TRN2", target_bir_lowering=False)
    if types is None:
        types = AB_TYPES if layer == 0 else CD_TYPES
    N = 512 * len(types)
    nfm, nv = (AB_NFM, AB_NV) if layer == 0 else (CD_NFM, CD_NV)
    with ExitStack() as st:
        k = K(nc, st)
        xT = k.dram("xT", [D, T], BF16 if x_bf16 else F32, "ExternalInput")
        w = k.dram("w", [D, N], F32, "ExternalInput")
        identb = k.dram("identb", [128, 128], BF16, "ExternalInput")
        tabd = {nm: k.dram(nm, [T, 128], F32, "ExternalInput") for nm in (("ccp", "ssp") if layer == 0 else ("ccp", "ssp", "cca", "ssa"))}
        qkT = k.dram("qkT", [nfm, 128, T], BF16, "ExternalOutput")
        vtm = k.dram("vtm", [T, nv], BF16, "ExternalOutput")
        res = alloc_linear_res(k, "a")
        ident_b = k.sbuf("identb_s", [128, 128], BF16)
        k.dma("sp", ident_b[:], identb[:, :], reads=[identb], writes=[ident_b])
        tabs = load_tabs(k, "a", list(tabd.keys()), tabd)
        gains = {}
        if layer == 1:
            for nm in ("qg", "kg"):
                gd = k.dram(nm, [1, 128], F32, "ExternalInput")
                gs = k.sbuf("g_" + nm, [128, 128], F32)
                k.dma("sp", gs[:], gd[0:1, :].to_broadcast([128, 128]), reads=[gd], writes=[gs])
                gains[nm] = gs
        if x_bf16:
            for kt in range(KT):
                k.dma("sp", res["xb"][kt][:], xT[kt * 128:(kt + 1) * 128, :], reads=[xT], writes=[res["xb"][kt]])
        else:
            load_xT(k, xT, res["xb"], res["xst"])
        bg = None
        if conv:
            RW = 16384 // NCORES
            pairs = []
            for nm in ("u0", "v0", "u1", "v1"):
                src = k.dram(nm + "p", [RW, D], F32, "ExternalInput")
                dst = k.dram(nm + "b", [RW, D], BF16, "ExternalOutput")
                pairs.append((src, dst))
            bg = cast_rows_bg(k, "cv", pairs, RW, ncol=4)
        phase_inproj(k, "a", xT, w, N, types, tabs, gains, ident_b, qkT, vtm, res, bg=bg)
        while bg is not None and bg():
            pass
        k.finish("sp")
    return nc


HALO = 1024
TE = T + 2 * HALO
NTE = TE // 128
B_DIL = (1, 4, 16)
B_DT = (1, 2, 8)
MASK_OFF = {"A": 0, 0: 3, 1: 6, 2: 11}
N_MASKS = 28


def band_masks():
    m = np.zeros((128, N_MASKS, 128), np.float32)
    kl = np.arange(128)[:, None]
    ql = np.arange(128)[None, :]
    for j, dt in enumerate((-1, 0, 1)):
        diff = (ql - kl) - dt * 128
        m[:, MASK_OFF["A"] + j, :] = (np.abs(diff) <= 128)
    for g in range(3):
        d, r = B_DIL[g], B_DT[g]
        for j, dt in enumerate(range(-r, r + 1)):
            diff = (ql - kl) - dt * 128
            m[:, MASK_OFF[g] + j, :] = (np.abs(diff) <= 64 * d) & (diff % d == 0)
    return m.astype(NP_BF16)


def attn_chunks(qT_ap, qT_b, kT_b, v_b, ktiles, mask0, acc, first, last):
    out = []
    n = len(ktiles)
    i0 = 0
    while i0 < n:
        nt = min(4, n - i0)
        out.append({"q": qT_ap, "qb": qT_b, "kT": kT_b, "v": v_b, "tiles": ktiles[i0:i0 + nt], "mask0": mask0 + i0, "acc": acc,
                    "start": first and i0 == 0, "stop": last and i0 + nt == n, "post": None})
        i0 += nt
    return out


def run_chunks(k, chunks, mask_b, ps_s, pts, cnt, scale):
    def S(ch):
        ps = ps_s[cnt["s"] % len(ps_s)]
        pt = pts[cnt["s"] % len(pts)]
        cnt["s"] += 1
        ch["pt"] = pt
        nt = len(ch["tiles"])
        for j, lt in enumerate(ch["tiles"]):
            k.op("pe", lambda g: g.matmul(ps[:, j, :], lhsT=ch["kT"][:, lt * 128:(lt + 1) * 128], rhs=ch["q"],
                                          start=True, stop=True), reads=[ch["kT"], ch["qb"]], writes=[ps], sig=(j == nt - 1))
        k.op("act", lambda g: g.activation(out=pt[:, 0:nt, :], in_=ps[:, 0:nt, :], func=AF.Exp, scale=scale),
             reads=[ps], writes=[pt])
        k.op("dve", lambda g: g.tensor_tensor(out=pt[:, 0:nt, :], in0=pt[:, 0:nt, :],
                                              in1=mask_b[:, ch["mask0"]:ch["mask0"] + nt, :], op=ALU.mult),
             reads=[pt, mask_b], writes=[pt])

    def P(ch):
        pt = ch["pt"]
        nt = len(ch["tiles"])
        for j, lt in enumerate(ch["tiles"]):
            k.op("pe", lambda g: g.matmul(ch["acc"][:, 0:129], lhsT=pt[:, j, :], rhs=ch["v"][:, lt, :],
                                          start=(ch["start"] and j == 0), stop=(ch["stop"] and j == nt - 1)),
                 reads=[pt, ch["v"]], writes=[ch["acc"]], sig=(j == nt - 1))
        if ch["post"] is not None:
            ch["post"]()

    if not chunks:
        return
    S(chunks[0])
    for i, ch in enumerate(chunks):
        if i + 1 < len(chunks):
            S(chunks[i + 1])
        P(ch)


def phase_attn_ab(k, pre, qT_d, kTe_d, vaug_d, sink_d, mask_d, identb_d, oT_d, bg=None):
    scale = 1.0 / math.sqrt(128.0)
    mask_b = k.sbuf(f"{pre}mask", [128, N_MASKS, 128], BF16)
    k.dma("sp", mask_b[:], mask_d[:, :, :], reads=[mask_d], writes=[mask_b])
    ident_b = k.sbuf(f"{pre}ident", [128, 128], BF16)
    k.dma("sp", ident_b[:], identb_d[:, :], reads=[identb_d], writes=[ident_b])
    esink = k.sbuf(f"{pre}esink", [128, 16], F32)
    k.dma("sp", esink[:], sink_d[0:1, :].to_broadcast([128, 16]), reads=[sink_d], writes=[esink])
    k.op("act", lambda g: g.activation(out=esink[:], in_=esink[:], func=AF.Exp), reads=[esink], writes=[esink])
    NKT = 46
    kbuf = [k.sbuf(f"{pre}kb{i}", [128, NKT * 128], BF16) for i in range(2)]
    vbuf = [k.sbuf(f"{pre}vb{i}", [128, NKT, 129], BF16) for i in range(2)]
    qbuf = [k.sbuf(f"{pre}qb{i}", [128, 4, T], BF16) for i in range(2)]
    ps_s = [k.psum(f"{pre}pss{i}", [128, 4, 128], F32) for i in range(3)]
    pts = [k.sbuf(f"{pre}pt{i}", [128, 4, 128], BF16) for i in range(4)]
    accs = [k.psum(f"{pre}acc{i}", [128, 512], F32) for i in range(2)]
    ps_t = [k.psum(f"{pre}pst{i}", [128, 1024], BF16) for i in range(2)]
    den = k.sbuf(f"{pre}den", [128, 1], F32)
    ob = [k.sbuf(f"{pre}ob{i}", [128, 128], BF16) for i in range(2)]
    oTs = [k.sbuf(f"{pre}oTs{i}", [128, T], BF16) for i in range(2)]
    cnt = {"s": 0, "a": 0, "o": 0}

    def finalize(acc, sink_col, ohead, m):
        if sink_col is not None:
            k.op("dve", lambda g: g.tensor_tensor(out=den[:], in0=acc[:, 128:129], in1=esink[:, sink_col:sink_col + 1],
                                                  op=ALU.add), reads=[acc, esink], writes=[den])
        else:
            k.op("dve", lambda g: g.tensor_copy(out=den[:], in_=acc[:, 128:129]), reads=[acc], writes=[den])
        k.op("dve", lambda g: g.reciprocal(out=den[:], in_=den[:]), reads=[den], writes=[den])
        o = ob[cnt["o"] % 2]
        k.op("dve", lambda g: g.tensor_scalar(out=o[:], in0=acc[:, 0:128], scalar1=den[:, 0:1], scalar2=None,
                                              op0=ALU.mult), reads=[acc, den], writes=[o])
        pt_ = ps_t[cnt["o"] % 2]
        k.op("pe", lambda g: g.transpose(out=pt_[:, 0:128], in_=o[:], identity=ident_b[:]),
             reads=[o, ident_b], writes=[pt_])
        oT = oTs[ohead % 2]
        k.copy("act", oT, oT[:, m * 128:(m + 1) * 128], pt_, pt_[:, 0:128])
        cnt["o"] += 1
        if m == MT - 1:
            k.dma("sp", oT_d[ohead], oT[:], reads=[oT], writes=[oT_d])

    job = 0
    for kvh in range(4):
        kb_, vb_, qb_ = kbuf[job % 2], vbuf[job % 2], qbuf[job % 2]
        job += 1
        k.dma("sp", kb_[:, 0:10 * 128], kTe_d[kvh, :, 7 * 128:17 * 128], reads=[kTe_d], writes=[kb_])
        k.dma("sp", vb_[:, 0:10, :], vaug_d[7 * 128:17 * 128, kvh, :].rearrange("(j p) c -> p j c", p=128),
              reads=[vaug_d], writes=[vb_])
        k.dma("sp", qb_[:], qT_d[kvh * 4:kvh * 4 + 4].rearrange("j p t -> p j t"), reads=[qT_d], writes=[qb_])
        chunks = []
        for g4 in range(4):
            h = kvh * 4 + g4
            for m in range(MT):
                acc = accs[cnt["a"] % 2]
                cnt["a"] += 1
                cs_ = attn_chunks(qb_[:, g4, m * 128:(m + 1) * 128], qb_, kb_, vb_, [m, m + 1, m + 2], MASK_OFF["A"], acc,
                                  True, True)
                cs_[-1]["post"] = (lambda acc=acc, h=h, m=m: finalize(acc, h, h, m))
                chunks += cs_
        run_chunks(k, chunks, mask_b, ps_s, pts, cnt, scale)
    base = [0, 10, 22]
    lo_ext = [7, 6, 0]
    nload = [10, 12, 24]
    for h in range(8):
        kb_, vb_, qb_ = kbuf[job % 2], vbuf[job % 2], qbuf[job % 2]
        job += 1
        for g in range(3):
            kblk = 4 + g * 8 + h
            k.dma("sp", kb_[:, base[g] * 128:(base[g] + nload[g]) * 128],
                  kTe_d[kblk, :, lo_ext[g] * 128:(lo_ext[g] + nload[g]) * 128], reads=[kTe_d], writes=[kb_])
            k.dma("sp", vb_[:, base[g]:base[g] + nload[g], :],
                  vaug_d[lo_ext[g] * 128:(lo_ext[g] + nload[g]) * 128, kblk, :].rearrange("(j p) c -> p j c", p=128),
                  reads=[vaug_d], writes=[vb_])
            k.dma("sp", qb_[:, g, :], qT_d[20 + g * 8 + h], reads=[qT_d], writes=[qb_])
        chunks = []
        for m in range(MT):
            acc = accs[cnt["a"] % 2]
            cnt["a"] += 1
            for g in range(3):
                r = B_DT[g]
                tiles = [base[g] + (8 + m + dt) - lo_ext[g] for dt in range(-r, r + 1)]
                chunks += attn_chunks(qb_[:, g, m * 128:(m + 1) * 128], qb_, kb_, vb_, tiles, MASK_OFF[g], acc, g == 0, g == 2)
            chunks[-1]["post"] = (lambda acc=acc, h=h, m=m: finalize(acc, None, 16 + h, m))
        run_chunks(k, chunks, mask_b, ps_s, pts, cnt, scale)


def build_attn_ab():
    nc = bass.Bass("TRN2", target_bir_lowering=False)
    with ExitStack() as st:
        k = K(nc, st)
        qT = k.dram("qT", [AB_NFM, 128, T], BF16, "ExternalInput")
        kTe = k.dram("kTe", [28, 128, TE], BF16, "ExternalInput")
        vaug = k.dram("vaug", [TE, 28, 129], BF16, "ExternalInput")
        sink = k.dram("sink", [1, 16], F32, "ExternalInput")
        masks = k.dram("masks", [128, N_MASKS, 128], BF16, "ExternalInput")
        identb = k.dram("identb", [128, 128], BF16, "ExternalInput")
        oT = k.dram("oT", [24, 128, T], BF16, "ExternalOutput")
        phase_attn_ab(k, "b", qT, kTe, vaug, sink, masks, identb, oT)
        k.finish("sp")
    return nc


def host_ext_kv(qkT_all, vtm_all, kblocks, vheads_cols):
    kfull = np.concatenate([q[kblocks] for q in qkT_all], axis=2)
    vfull = np.concatenate(vtm_all, axis=0)
    nk = len(kblocks)
    kpad = np.zeros((nk, 128, SEQ + 2 * HALO), dtype=kfull.dtype)
    kpad[:, :, HALO:HALO + SEQ] = kfull
    nvh = vfull.shape[1] // 128
    vpad = np.zeros((SEQ + 2 * HALO, nvh, 129), dtype=vfull.dtype)
    vpad[HALO:HALO + SEQ, :, 0:128] = vfull.reshape(SEQ, nvh, 128)
    vpad[HALO:HALO + SEQ, :, 128] = 1.0
    outs = []
    for c in range(NCORES):
        outs.append((np.ascontiguousarray(kpad[:, :, c * T:c * T + TE]), np.ascontiguousarray(vpad[c * T:c * T + TE])))
    return outs


ALPHA = float((2 * 2) ** 0.25)


def barrier(k):
    toks = []
    for e, c in k.cur.items():
        if c[1] > 0:
            toks.append((c[0], c[1], e))
    for q, ring in k.dma_rr.items():
        for s in ring["sems"]:
            if s[2] is not None:
                toks.append(s[2])
    for e in ("pe", "dve", "act", "pool", "sp"):
        for tk in toks:
            if tk[2] == e:
                continue
            k._wait(e, tk)


def phase_outproj(k, pre, oT_d, nkt, w_d, x_d, z_d, res):
    xb = res["xb"][:nkt]
    for kt in range(nkt):
        k.dma("sp", xb[kt][:], oT_d[kt], reads=[oT_d], writes=[xb[kt]])
    xin = [k.sbuf(f"{pre}xin{i}", [128, 512], F32) for i in range(3)]
    zt = [k.sbuf(f"{pre}zt{i}", [128, 512], F32) for i in range(3)]
    cnt = {"e": 0}

    def epilogue(c, m, p):
        i = cnt["e"]
        cnt["e"] += 1
        xi, zo = xin[i % 3], zt[i % 3]
        k.dma("sp", xi[:], x_d[m * 128:(m + 1) * 128, c * 512:(c + 1) * 512], reads=[x_d], writes=[xi])
        k.op("dve", lambda g: g.scalar_tensor_tensor(out=zo[:], in0=xi[:], scalar=ALPHA, in1=p[:], op0=ALU.mult,
                                                     op1=ALU.add), reads=[xi, p], writes=[zo])
        k.dma("sp", z_d[m * 128:(m + 1) * 128, c * 512:(c + 1) * 512], zo[:], reads=[zo], writes=[z_d])

    stream_linear(k, xb, w_d, D, res["wst"], res["wb"], res["ps_mm"], epilogue, state=res["lin_state"])


def phase_ln(k, pre, z_d, g_d, b_d, out_d, outT_d, identb_d, ps_tr, mt=MT):
    gt = k.sbuf(f"{pre}g", [128, D], F32)
    bt = k.sbuf(f"{pre}b", [128, D], F32)
    k.dma("sp", gt[:], g_d[0:1, :].to_broadcast([128, D]), reads=[g_d], writes=[gt])
    k.dma("sp", bt[:], b_d[0:1, :].to_broadcast([128, D]), reads=[b_d], writes=[bt])
    ident_b = k.sbuf(f"{pre}ident", [128, 128], BF16)
    k.dma("sp", ident_b[:], identb_d[:, :], reads=[identb_d], writes=[ident_b])
    zs = [k.sbuf(f"{pre}z{i}", [128, D], F32) for i in range(2)]
    os_ = [k.sbuf(f"{pre}o{i}", [128, D], F32) for i in range(2)]
    ob16 = k.sbuf(f"{pre}o16", [128, D], BF16)
    stg = [k.sbuf(f"{pre}stg{i}", [128, KT, 128], BF16) for i in range(2)]
    stats = k.sbuf(f"{pre}stats", [128, 8, 6], F32)
    mv = k.sbuf(f"{pre}mv", [128, 2], F32)
    rstd = k.sbuf(f"{pre}rstd", [128, 1], F32)
    for m in range(mt):
        z, o = zs[m % 2], os_[m % 2]
        k.dma("sp", z[:], z_d[m * 128:(m + 1) * 128, :], reads=[z_d], writes=[z])
        for c in range(8):
            k.op("dve", lambda g: g.bn_stats(out=stats[:, c, :], in_=z[:, c * 512:(c + 1) * 512]), reads=[z], writes=[stats])
        k.op("dve", lambda g: g.bn_aggr(out=mv[:], in_=stats[:].rearrange("p a b -> p (a b)")), reads=[stats], writes=[mv])
        k.op("dve", lambda g: g.tensor_scalar(out=rstd[:], in0=mv[:, 1:2], scalar1=1e-5, scalar2=None, op0=ALU.add),
             reads=[mv], writes=[rstd])
        k.op("act", lambda g: g.activation(out=rstd[:], in_=rstd[:], func=AF.Sqrt), reads=[rstd], writes=[rstd])
        k.op("dve", lambda g: g.reciprocal(out=rstd[:], in_=rstd[:]), reads=[rstd], writes=[rstd])
        k.op("dve", lambda g: g.tensor_scalar(out=o[:], in0=z[:], scalar1=mv[:, 0:1], scalar2=rstd[:, 0:1],
                                              op0=ALU.subtract, op1=ALU.mult), reads=[z, mv, rstd], writes=[o])
        k.op("pool", lambda g: g.tensor_tensor(out=o[:], in0=o[:], in1=gt[:], op=ALU.mult), reads=[o, gt], writes=[o])
        k.op("pool", lambda g: g.tensor_tensor(out=o[:], in0=o[:], in1=bt[:], op=ALU.add), reads=[o, bt], writes=[o])
        k.dma("sp", out_d[m * 128:(m + 1) * 128, :], o[:], reads=[o], writes=[out_d])
        if outT_d is not None:
            k.op("act", lambda g: g.copy(out=ob16[:], in_=o[:]), reads=[o], writes=[ob16])
            sg = stg[m % 2]
            for q in range(4):
                pt = ps_tr[q % len(ps_tr)]
                for j in range(8):
                    kt = q * 8 + j
                    k.op("pe", lambda g: g.transpose(out=pt[:, j, :], in_=ob16[:, kt * 128:(kt + 1) * 128],
                                                     identity=ident_b[:]), reads=[ob16, ident_b], writes=[pt])
                k.copy("act" if q % 2 else "dve", sg, sg[:, q * 8:(q + 1) * 8, :], pt, pt[:])
            k.dma("sp", outT_d[:, m * 128:(m + 1) * 128].rearrange("(kt p) t -> p kt t", p=128), sg[:],
                  reads=[sg], writes=[outT_d])


def build_outproj_ln(nkt, with_T):
    nc = bass.Bass("TRN2", target_bir_lowering=False)
    with ExitStack() as st:
        k = K(nc, st)
        oT = k.dram("oT", [nkt, 128, T], BF16, "ExternalInput")
        w = k.dram("w", [nkt * 128, D], F32, "ExternalInput")
        x = k.dram("x", [T, D], F32, "ExternalInput")
        g = k.dram("g", [1, D], F32, "ExternalInput")
        b = k.dram("b", [1, D], F32, "ExternalInput")
        identb = k.dram("identb", [128, 128], BF16, "ExternalInput")
        z = k.dram("z", [T, D], F32, "Internal")
        x1 = k.dram("x1", [T, D], F32, "ExternalOutput")
        x1T = k.dram("x1T", [D, T], BF16, "ExternalOutput") if with_T else None
        with ExitStack() as ph:
            k.stack = ph
            res = alloc_linear_res(k, "c")
            phase_outproj(k, "c", oT, nkt, w, x, z, res)
            barrier(k)
        k.stack = st
        with ExitStack() as ph:
            k.stack = ph
            ps_tr = [k.psum(f"lnpt{i}", [128, 8, 128], BF16) for i in range(2)]
            phase_ln(k, "l", z, g, b, x1, x1T, identb, ps_tr)
            barrier(k)
        k.stack = st
        k.finish("sp")
    return nc


NEG = -1.0e30


def phase_peer_scores(k, pre, x1T_d, wq_d, keysT_d, sc_d, xb):
    for kt in range(KT):
        k.dma("sp", xb[kt][:], x1T_d[kt * 128:(kt + 1) * 128, :], reads=[x1T_d], writes=[xb[kt]])
    keysT = k.sbuf(f"{pre}keysT", [128, 2048], F32)
    k.dma("sp", keysT[:], keysT_d[:, :], reads=[keysT_d], writes=[keysT])
    wst = [k.sbuf(f"{pre}wst{i}", [128, KT, 128], F32) for i in range(2)]
    wqb = [k.sbuf(f"{pre}wqb{i}", [128, KT, 128], BF16) for i in range(2)]
    qg = [k.sbuf(f"{pre}qg{i}", [128, T], F32) for i in range(2)]
    scst = [k.sbuf(f"{pre}scst{i}", [128, MT, 128], F32) for i in range(2)]
    ps = [k.psum(f"{pre}ps{i}", [128, 512], F32) for i in range(4)]
    ps2 = [k.psum(f"{pre}ps2{i}", [128, 512], F32) for i in range(2)]
    for g in range(16):
        ws, wb_, q_, sc_ = wst[g % 2], wqb[g % 2], qg[g % 2], scst[g % 2]
        k.dma("sp", ws[:], wq_d[:, g * 128:(g + 1) * 128].rearrange("(kt p) c -> p kt c", p=128), reads=[wq_d], writes=[ws])
        k.copy(("act", "dve")[g % 2], wb_, wb_[:], ws, ws[:])
        for tc in range(T // 512):
            p = ps[(g * 2 + tc) % 4]
            for kt in range(KT):
                k.op("pe", lambda e: e.matmul(p[:], lhsT=wb_[:, kt, :], rhs=xb[kt][:, tc * 512:(tc + 1) * 512],
                                              start=(kt == 0), stop=(kt == KT - 1)), reads=[wb_, xb[kt]], writes=[p])
            k.copy(("dve", "act")[tc % 2], q_, q_[:, tc * 512:(tc + 1) * 512], p, p[:])
        for m in range(MT):
            p2 = ps2[m % 2]
            k.op("pe", lambda e: e.matmul(p2[:, 0:128], lhsT=q_[:, m * 128:(m + 1) * 128], rhs=keysT[:, g * 128:(g + 1) * 128],
                                          start=True, stop=True), reads=[q_, keysT], writes=[p2])
            k.copy(("act", "dve")[m % 2], sc_, sc_[:, m, :], p2, p2[:, 0:128])
        k.dma("sp", sc_d[:, g * 128:(g + 1) * 128].rearrange("(m p) n -> p m n", p=128), sc_[:], reads=[sc_], writes=[sc_d])


def cast_rows_bg(k, pre, srcs_dsts, nrows, nbuf=2, ncol=1):
    W_ = D // ncol
    st = [k.sbuf(f"{pre}st{i}", [128, W_], F32) for i in range(nbuf)]
    cb = [k.sbuf(f"{pre}cb{i}", [128, W_], BF16) for i in range(nbuf)]
    jobs = [(src, dst, i, j) for (src, dst) in srcs_dsts for i in range(nrows // 128) for j in range(ncol)]
    state = {"n": 0}

    def step():
        n = state["n"]
        if n >= len(jobs):
            return False
        state["n"] += 1
        src, dst, i, j = jobs[n]
        s_, c_ = st[n % nbuf], cb[n % nbuf]
        k.dma("pool", s_[:], src[i * 128:(i + 1) * 128, j * W_:(j + 1) * W_], reads=[src], writes=[s_])
        k.copy("pool", c_, c_[:], s_, s_[:])
        k.dma("pool", dst[i * 128:(i + 1) * 128, j * W_:(j + 1) * W_], c_[:], reads=[c_], writes=[dst])
        return True
    return step


def phase_cast_rows(k, pre, srcs_dsts, nrows=16384):
    st = [k.sbuf(f"{pre}st{i}", [128, D], F32) for i in range(3)]
    cb = [k.sbuf(f"{pre}cb{i}", [128, D], BF16) for i in range(3)]
    jobs = [(src, dst, i) for (src, dst) in srcs_dsts for i in range(nrows // 128)]
    engs = ("act", "dve", "pool")

    def load(n):
        src, dst, i = jobs[n]
        k.dma("sp", st[n % 3][:], src[i * 128:(i + 1) * 128, :], reads=[src], writes=[st[n % 3]])

    for n in range(min(2, len(jobs))):
        load(n)
    for n in range(len(jobs)):
        if n + 2 < len(jobs):
            load(n + 2)
        src, dst, i = jobs[n]
        k.copy(engs[n % 3], cb[n % 3], cb[n % 3][:], st[n % 3], st[n % 3][:])
        k.dma("act", dst[i * 128:(i + 1) * 128, :], cb[n % 3][:], reads=[cb[n % 3]], writes=[dst])


def phase_peer_main(k, pre, x1_d, x1T_d, sc_d, u_d, v_d, pconst_d, identb_d, z_d, NG=4, mt=MT):
    ident_b = k.sbuf(f"{pre}ident", [128, 128], BF16)
    k.dma("sp", ident_b[:], identb_d[:, :], reads=[identb_d], writes=[ident_b])
    pc = k.sbuf(f"{pre}pc", [128, 48], F32)
    k.dma("sp", pc[:], pconst_d[:, :], reads=[pconst_d], writes=[pc])
    iota16, lo16, hi16 = pc[:, 0:16], pc[:, 16:32], pc[:, 32:48]
    sc = k.sbuf(f"{pre}sc", [128, 16, 128], F32)
    scr = k.sbuf(f"{pre}scr", [128, 16, 128], F32)
    sv = k.sbuf(f"{pre}sv", [128, 16, 16], F32)
    si = k.sbuf(f"{pre}si", [128, 16, 16], U32)
    sif = k.sbuf(f"{pre}sif", [128, 16, 16], F32)
    cand = k.sbuf(f"{pre}cand", [128, 8, 256], F32)
    cscr = k.sbuf(f"{pre}cscr", [128, 8, 256], F32)
    ts = k.sbuf(f"{pre}ts", [128, 8, 16], F32)
    tp = k.sbuf(f"{pre}tp", [128, 8, 16], U32)
    tpf = k.sbuf(f"{pre}tpf", [128, 8, 16], F32)
    w4a = k.sbuf(f"{pre}w4a", [128, 8, 16, 16], F32)
    w4b = k.sbuf(f"{pre}w4b", [128, 8, 16, 16], F32)
    r3a = k.sbuf(f"{pre}r3a", [128, 8, 16], F32)
    r3b = k.sbuf(f"{pre}r3b", [128, 8, 16], F32)
    r3c = k.sbuf(f"{pre}r3c", [128, 8, 16], F32)
    eidx_f = k.sbuf(f"{pre}eidxf", [128, 128], F32)
    eidx = k.sbuf(f"{pre}eidx", [128, 128], U32)
    gate = k.sbuf(f"{pre}gate", [128, 8, 16], F32)
    zsum = k.sbuf(f"{pre}zsum", [128, 8], F32)
    hids = [k.sbuf(f"{pre}hid{i}", [128, 128], F32) for i in range(2)]
    g1 = k.sbuf(f"{pre}g1", [128, 128], F32)
    wgts = [k.sbuf(f"{pre}wgt{i}", [128, 128], F32) for i in range(2)]
    xts = [k.sbuf(f"{pre}xt{i}", [128, D], F32) for i in range(2)]
    junk = k.sbuf(f"{pre}junk", [128, D], BF16)
    ug = [k.sbuf(f"{pre}ug{i}", [128, D], BF16) for i in range(NG)]
    vg = [k.sbuf(f"{pre}vg{i}", [128, D // 2], BF16) for i in range(NG)]
    dgs = [k.sbuf(f"{pre}dg{i}", [128, 128], BF16) for i in range(8)]
    zo = [k.sbuf(f"{pre}zo{i}", [128, 512], F32) for i in range(2)]
    xps = k.psum(f"{pre}xps", [128, D], BF16)
    acc = [k.psum(f"{pre}acc{i}", [128, 512], F32) for i in range(4)]

    def dve(fn, reads, writes):
        return k.op("dve", fn, reads=reads, writes=writes)

    eidx_t = [k.sbuf(f"{pre}eidx_t{i}", [128, 128], U32) for i in range(mt)]
    gate_t = [k.sbuf(f"{pre}gate_t{i}", [128, 128], F32) for i in range(mt)]
    evidx_t = [[k.sbuf(f"{pre}evidx_t{i}_{h}", [128, 128], U32) for h in range(2)] for i in range(mt)]

    svg = [Buf(sv.t, f"{pre}svg{i}") for i in range(16)]
    sig_ = [Buf(si.t, f"{pre}sig{i}") for i in range(16)]
    scrg = [Buf(scr.t, f"{pre}scrg{i}") for i in range(16)]
    tsg = [Buf(ts.t, f"{pre}tsg{i}") for i in range(8)]
    tpg = [Buf(tp.t, f"{pre}tpg{i}") for i in range(8)]
    cscrg = [Buf(cscr.t, f"{pre}cscrg{i}") for i in range(8)]

    def route(m):
        k.dma("sp", sc[:], sc_d[m * 128:(m + 1) * 128, :].rearrange("p (g n) -> p g n", g=16), reads=[sc_d], writes=[sc])
        def chain5(val, idx, scratch, src, j, tv, ti, tsr, tsrc):
            return [
                ('dve', lambda e: e.max(out=val[:, j, 0:8], in_=src[:, j, :]), [tsrc], [tv]),
                ('dve', lambda e: e.max_index(out=idx[:, j, 0:8], in_max=val[:, j, 0:8], in_values=src[:, j, :]), [tsrc, tv], [ti]),
                ('dve', lambda e: e.match_replace(out=scratch[:, j, :], in_to_replace=val[:, j, 0:8], in_values=src[:, j, :],
                                                  imm_value=NEG), [tsrc, tv], [tsr]),
                ('dve', lambda e: e.max(out=val[:, j, 8:16], in_=scratch[:, j, :]), [tsr], [tv]),
                ('dve', lambda e: e.max_index(out=idx[:, j, 8:16], in_max=val[:, j, 8:16], in_values=scratch[:, j, :]),
                 [tsr, tv], [ti]),
            ]
        for g0 in range(4):
            chains = [chain5(sv, si, scr, sc, g0 + 4 * q, svg[g0 + 4 * q], sig_[g0 + 4 * q], scrg[g0 + 4 * q], sc) for q in range(4)]
            for step in range(5):
                for ch in chains:
                    yield ch[step]
        yield ('dve', lambda e: e.tensor_copy(out=sif[:], in_=si[:]), list(sig_), [sif])
        sv4 = sv[:].rearrange("p (h c) r -> p h c r", c=2)
        sif4 = sif[:].rearrange("p (h c) r -> p h c r", c=2)
        c4 = cand[:].rearrange("p h (a b) -> p h a b", a=16)
        yield ('dve', lambda e: e.tensor_tensor(out=c4, in0=sv4[:, :, 0, :].unsqueeze(3).to_broadcast([128, 8, 16, 16]),
                                      in1=sv4[:, :, 1, :].unsqueeze(2).to_broadcast([128, 8, 16, 16]), op=ALU.add),
            list(svg), [cand])
        for h0 in range(2):
            chains = [chain5(ts, tp, cscr, cand, h0 + 2 * q, tsg[h0 + 2 * q], tpg[h0 + 2 * q], cscrg[h0 + 2 * q], cand)
                      for q in range(4)]
            for step in range(5):
                for ch in chains:
                    yield ch[step]
        yield ('dve', lambda e: e.tensor_copy(out=tpf[:], in_=tp[:]), list(tpg), [tpf])
        tpf4 = tpf[:].unsqueeze(3).to_broadcast([128, 8, 16, 16])
        lo4 = lo16.unsqueeze(1).unsqueeze(1).to_broadcast([128, 8, 16, 16])
        hi4 = hi16.unsqueeze(1).unsqueeze(1).to_broadcast([128, 8, 16, 16])
        io4 = iota16.unsqueeze(1).unsqueeze(1).to_broadcast([128, 8, 16, 16])
        yield ('dve', lambda e: e.tensor_tensor(out=w4a[:], in0=tpf4, in1=lo4, op=ALU.is_ge), [tpf, pc], [w4a])
        yield ('dve', lambda e: e.tensor_tensor(out=w4b[:], in0=tpf4, in1=hi4, op=ALU.is_ge), [tpf, pc], [w4b])
        yield ('dve', lambda e: e.tensor_tensor(out=w4a[:], in0=w4a[:], in1=w4b[:], op=ALU.subtract), [w4a, w4b], [w4a])
        yield ('dve', lambda e: e.tensor_tensor(out=w4b[:], in0=w4a[:], in1=sif4[:, :, 0, :].unsqueeze(2).to_broadcast([128, 8, 16, 16]),
                                      op=ALU.mult), [w4a, sif], [w4b])
        yield ('dve', lambda e: e.tensor_reduce(out=r3a[:], in_=w4b[:], axis=AX.X, op=ALU.add), [w4b], [r3a])
        yield ('dve', lambda e: e.tensor_tensor(out=w4b[:], in0=w4a[:], in1=lo4, op=ALU.mult), [w4a, pc], [w4b])
        yield ('dve', lambda e: e.tensor_reduce(out=r3b[:], in_=w4b[:], axis=AX.X, op=ALU.add), [w4b], [r3b])
        yield ('dve', lambda e: e.tensor_tensor(out=r3b[:], in0=tpf[:], in1=r3b[:], op=ALU.subtract), [tpf, r3b], [r3b])
        yield ('dve', lambda e: e.tensor_tensor(out=w4a[:], in0=r3b[:].unsqueeze(3).to_broadcast([128, 8, 16, 16]), in1=io4,
                                      op=ALU.is_equal), [r3b, pc], [w4a])
        yield ('dve', lambda e: e.tensor_tensor(out=w4b[:], in0=w4a[:], in1=sif4[:, :, 1, :].unsqueeze(2).to_broadcast([128, 8, 16, 16]),
                                      op=ALU.mult), [w4a, sif], [w4b])
        yield ('dve', lambda e: e.tensor_reduce(out=r3c[:], in_=w4b[:], axis=AX.X, op=ALU.add), [w4b], [r3c])
        ef3 = eidx_f[:].rearrange("p (h r) -> p h r", h=8)
        yield ('dve', lambda e: e.scalar_tensor_tensor(out=ef3, in0=r3a[:], scalar=128.0, in1=r3c[:], op0=ALU.mult, op1=ALU.add),
            [r3a, r3c], [eidx_f])
        yield ('dve', lambda e: e.tensor_copy(out=eidx[:], in_=eidx_f[:]), [eidx_f], [eidx])
        yield ('dve', lambda e: e.tensor_tensor(out=gate[:], in0=ts[:], in1=ts[:, :, 0:1].to_broadcast([128, 8, 16]), op=ALU.subtract),
            list(tsg), [gate])
        yield ('act', lambda e: e.activation(out=gate[:], in_=gate[:], func=AF.Exp), [gate], [gate])
        yield ('dve', lambda e: e.tensor_reduce(out=zsum[:], in_=gate[:], axis=AX.X, op=ALU.add), [gate], [zsum])
        yield ('dve', lambda e: e.reciprocal(out=zsum[:], in_=zsum[:]), [zsum], [zsum])
        yield ('dve', lambda e: e.tensor_tensor(out=gate[:], in0=gate[:], in1=zsum[:].unsqueeze(2).to_broadcast([128, 8, 16]), op=ALU.mult),
            [gate, zsum], [gate])
        yield ('dve', lambda e: e.tensor_copy(out=eidx_t[m][:], in_=eidx[:]), [eidx], [eidx_t[m]])
        yield ('dve', lambda e: e.tensor_scalar(out=eidx_f[:], in0=eidx_f[:], scalar1=2.0, scalar2=None, op0=ALU.mult), [eidx_f], [eidx_f])
        yield ('dve', lambda e: e.tensor_copy(out=evidx_t[m][0][:], in_=eidx_f[:]), [eidx_f], [evidx_t[m][0]])
        yield ('dve', lambda e: e.tensor_scalar(out=eidx_f[:], in0=eidx_f[:], scalar1=1.0, scalar2=None, op0=ALU.add), [eidx_f], [eidx_f])
        yield ('dve', lambda e: e.tensor_copy(out=evidx_t[m][1][:], in_=eidx_f[:]), [eidx_f], [evidx_t[m][1]])
        yield ('dve', lambda e: e.tensor_copy(out=gate_t[m][:], in_=gate[:].rearrange("p h r -> p (h r)")), [gate], [gate_t[m]])


    def advance(gen, n=1):
        for _ in range(n):
            it = next(gen, None)
            if it is None:
                return False
            k.op(it[0], it[1], reads=it[2], writes=it[3])
        return True

    xTt = [k.sbuf(f"{pre}xTt{i}", [128, KT, 128], BF16) for i in range(2)]

    def load_x(m):
        xt = xts[m % 2]
        k.dma("sp", xt[:], x1_d[m * 128:(m + 1) * 128, :], reads=[x1_d], writes=[xt])
        xT_ = xTt[m % 2]
        k.dma("sp", xT_[:], x1T_d[:, m * 128:(m + 1) * 128].rearrange("(kt p) t -> p kt t", p=128), reads=[x1T_d], writes=[xT_])
        for kt in range(KT):
            k.op("pe", lambda e: e.transpose(out=xps[:, kt * 128:(kt + 1) * 128], in_=xT_[:, kt, :], identity=ident_b[:]),
                 reads=[xT_, ident_b], writes=[xps], sig=(kt == KT - 1))

    def u_slot(m, s):
        ub = ug[s % NG]
        hid = hids[m % 2]
        k.gather(ub[:], u_d[:, :], eidx_t[m][:, s:s + 1], reads=[eidx_t[m], u_d], writes=[ub])
        dve(lambda e: e.scalar_tensor_tensor(out=junk[:], in0=ub[:], scalar=1.0, in1=xps[:], op0=ALU.mult, op1=ALU.mult,
                                             accum_out=hid[:, s:s + 1]), [ub, xps], [hid] if s == 127 else [])

    def gelu_wgt(m):
        hid, wgt = hids[m % 2], wgts[m % 2]
        dve(lambda e: e.tensor_tensor(out=g1[:], in0=hid[:], in1=hid[:], op=ALU.mult), [hid], [g1])
        dve(lambda e: e.tensor_scalar(out=g1[:], in0=g1[:], scalar1=0.044715 * 1.5957691216057308,
                                      scalar2=1.5957691216057308, op0=ALU.mult, op1=ALU.add), [g1], [g1])
        dve(lambda e: e.tensor_tensor(out=g1[:], in0=g1[:], in1=hid[:], op=ALU.mult), [g1, hid], [g1])
        k.op("act", lambda e: e.activation(out=g1[:], in_=g1[:], func=AF.Sigmoid), reads=[g1], writes=[g1])
        dve(lambda e: e.tensor_tensor(out=g1[:], in0=g1[:], in1=hid[:], op=ALU.mult), [g1, hid], [g1])
        dve(lambda e: e.tensor_tensor(out=wgt[:], in0=g1[:], in1=gate_t[m][:], op=ALU.mult), [g1, gate_t[m]], [wgt])

    def run_all(gen):
        while advance(gen):
            pass

    run_all(route(0))
    load_x(0)
    r1 = route(1) if mt > 1 else None
    for s in range(128):
        u_slot(0, s)
        if r1 is not None:
            advance(r1, 2)
    if r1 is not None:
        run_all(r1)
    gelu_wgt(0)
    HD2 = D // 2
    for m in range(mt):
        xt, wgt = xts[m % 2], wgts[m % 2]
        has_next = m + 1 < mt
        r2 = route(m + 2) if m + 2 < mt else None
        if has_next:
            load_x(m + 1)
        for half in range(2):
            for s in range(128):
                vb_ = vg[s % NG]
                k.gather(vb_[:], v_d[:, :].rearrange("e (h c) -> (e h) c", h=2), evidx_t[m][half][:, s:s + 1], reads=[evidx_t[m][half], v_d], writes=[vb_])
                d_ = dgs[s % 8]
                dve(lambda e: e.tensor_scalar(out=d_[:], in0=ident_b[:], scalar1=wgt[:, s:s + 1], scalar2=None,
                                              op0=ALU.mult), [ident_b, wgt], [d_])
                for c in range(4):
                    k.op("pe", lambda e: e.matmul(acc[c][:], lhsT=d_[:], rhs=vb_[:, c * 512:(c + 1) * 512],
                                                  start=(s == 0), stop=(s == 127)), reads=[d_, vb_], writes=[acc[c]],
                         sig=(c == 3))
                if half == 0 and has_next:
                    u_slot(m + 1, s)
                if half == 1 and r2 is not None:
                    advance(r2, 2)
            for c in range(4):
                z_ = zo[c % 2]
                col = half * HD2 + c * 512
                dve(lambda e: e.scalar_tensor_tensor(out=z_[:], in0=xt[:, col:col + 512], scalar=ALPHA, in1=acc[c][:],
                                                     op0=ALU.mult, op1=ALU.add), [xt, acc[c]], [z_])
                k.dma("sp", z_d[m * 128:(m + 1) * 128, col:col + 512], z_[:], reads=[z_], writes=[z_d])
        if r2 is not None:
            run_all(r2)
        if has_next:
            gelu_wgt(m + 1)


def peer_consts():
    pc = np.zeros((128, 48), np.float32)
    pc[:, 0:16] = np.arange(16)
    pc[:, 16:32] = 16 * np.arange(16)
    pc[:, 32:48] = 16 * np.arange(16) + 16
    return pc


def build_peer(final, mt=MT):
    nc = bass.Bass("TRN2", target_bir_lowering=False)
    with ExitStack() as st:
        k = K(nc, st)
        x1 = k.dram("x1", [T, D], F32, "ExternalInput")
        x1T = k.dram("x1T", [D, T], BF16, "ExternalInput")
        wq = k.dram("wq", [D, 2048], F32, "ExternalInput")
        keysT = k.dram("keysT", [128, 2048], F32, "ExternalInput")
        ub16 = k.dram("u", [16384, D], BF16, "ExternalInput")
        vb16 = k.dram("v", [16384, D], BF16, "ExternalInput")
        g = k.dram("g", [1, D], F32, "ExternalInput")
        b = k.dram("b", [1, D], F32, "ExternalInput")
        pconst = k.dram("pconst", [128, 48], F32, "ExternalInput")
        identb = k.dram("identb", [128, 128], BF16, "ExternalInput")
        sc = k.dram("sc", [T, 2048], F32, "Internal")
        z = k.dram("z", [T, D], F32, "Internal")
        x2 = k.dram("x2", [T, D], F32, "ExternalOutput")
        x2T = None if final else k.dram("x2T", [D, T], BF16, "ExternalOutput")
        with ExitStack() as ph:
            k.stack = ph
            xb = [k.sbuf(f"pxb{i}", [128, T], BF16) for i in range(KT)]
            phase_peer_scores(k, "ps", x1T, wq, keysT, sc, xb)
            barrier(k)
        with ExitStack() as ph:
            k.stack = ph
            phase_peer_main(k, "pm", x1, x1T, sc, ub16, vb16, pconst, identb, z, mt=mt)
            barrier(k)
        with ExitStack() as ph:
            k.stack = ph
            ps_tr = [k.psum(f"lnpt{i}", [128, 8, 128], BF16) for i in range(2)]
            phase_ln(k, "pl", z, g, b, x2, x2T, identb, ps_tr, mt=mt)
            barrier(k)
        k.stack = st
        k.finish("sp")
    return nc


LAMBDA_INIT = 0.8 - 0.6 * math.exp(-0.3 * 1)
NKT_FULL = SEQ // 128


def dense_unit(k, kT, qT, q0, vaug, W, accs, ps_s, pts, cnt, scale, LA=2):
    slots = {}

    def score(kt):
        ps = ps_s[cnt["s"] % len(ps_s)]
        pt = pts[cnt["s"] % len(pts)]
        cnt["s"] += 1
        slots[kt] = pt
        k.op("pe", lambda g: g.matmul(ps[:], lhsT=kT[:, kt * 128:(kt + 1) * 128], rhs=qT[:, q0:q0 + 512],
                                      start=True, stop=True), reads=[kT, qT], writes=[ps])
        k.op("act", lambda g: g.activation(out=pt[:], in_=ps[:], func=AF.Exp, scale=scale), reads=[ps], writes=[pt])

    for kt in range(min(LA, NKT_FULL)):
        score(kt)
    for kt in range(NKT_FULL):
        if kt + LA < NKT_FULL:
            score(kt + LA)
        pt = slots.pop(kt)
        for j in range(4):
            k.op("pe", lambda g: g.matmul(accs[j][:, 0:W], lhsT=pt[:, j * 128:(j + 1) * 128], rhs=vaug[:, kt, 0:W],
                                          start=(kt == 0), stop=(kt == NKT_FULL - 1)), reads=[pt, vaug], writes=[accs[j]])


def phase_attn_cd(k, pre, qc_d, kc_d, vc_d, qd_d, kd_d, vd_d, lam_d, subg_d, identb_d, oT_d, nchunks=SEQ // 512):
    scale = 1.0 / math.sqrt(128.0)
    ident_b = k.sbuf(f"{pre}ident", [128, 128], BF16)
    k.dma("sp", ident_b[:], identb_d[:, :], reads=[identb_d], writes=[ident_b])
    lam_t = k.sbuf(f"{pre}lamt", [128, 4, 128], F32)
    k.dma("sp", lam_t[:], lam_d[:, :].unsqueeze(0).to_broadcast([128, 4, 128]), reads=[lam_d], writes=[lam_t])
    lprod = k.sbuf(f"{pre}lprod", [128, 2, 128], F32)
    lsum = k.sbuf(f"{pre}lsum", [128, 2], F32)
    neglam = k.sbuf(f"{pre}neglam", [128, 1], F32)
    lam4 = lam_t[:].rearrange("p (a b) d -> p a b d", b=2)
    k.op("dve", lambda g: g.tensor_tensor(out=lprod[:], in0=lam4[:, :, 0, :], in1=lam4[:, :, 1, :], op=ALU.mult),
         reads=[lam_t], writes=[lprod])
    k.op("dve", lambda g: g.tensor_reduce(out=lsum[:], in_=lprod[:], axis=AX.X, op=ALU.add), reads=[lprod], writes=[lsum])
    k.op("act", lambda g: g.activation(out=lsum[:], in_=lsum[:], func=AF.Exp), reads=[lsum], writes=[lsum])
    k.op("dve", lambda g: g.scalar_tensor_tensor(out=neglam[:], in0=lsum[:, 1:2], scalar=-LAMBDA_INIT, in1=lsum[:, 0:1],
                                                 op0=ALU.add, op1=ALU.subtract), reads=[lsum], writes=[neglam])
    subg = k.sbuf(f"{pre}subg", [128, 256], F32)
    k.dma("sp", subg[:], subg_d[0:1, :].to_broadcast([128, 256]), reads=[subg_d], writes=[subg])
    k.op("dve", lambda g: g.tensor_scalar(out=subg[:], in0=subg[:], scalar1=1.0 - LAMBDA_INIT, scalar2=None, op0=ALU.mult),
         reads=[subg], writes=[subg])

    kTs = [k.sbuf(f"{pre}kT{i}", [128, SEQ], BF16) for i in range(2)]
    qTs = [k.sbuf(f"{pre}qT{i}", [128, SEQ], BF16) for i in range(2)]
    vcs = k.sbuf(f"{pre}vc", [128, NKT_FULL, 257], BF16)
    vds = k.sbuf(f"{pre}vd", [128, NKT_FULL, 129], BF16)
    ps_s = [k.psum(f"{pre}pss{i}", [128, 512], F32) for i in range(3)]
    pts = [k.sbuf(f"{pre}pt{i}", [128, 512], BF16) for i in range(4)]
    accs = [k.psum(f"{pre}acc{i}", [128, 512], F32) for i in range(4)]
    ps_t = [k.psum(f"{pre}pst{i}", [128, 8, 128], BF16) for i in range(1)]
    o1s = k.sbuf(f"{pre}o1s", [128, nchunks * 4, 256], F32)
    den = k.sbuf(f"{pre}den", [128, 1], F32)
    o2 = k.sbuf(f"{pre}o2", [128, 256], F32)
    junk = k.sbuf(f"{pre}junk", [128, 256], F32)
    ssq = k.sbuf(f"{pre}ssq", [128, 1], F32)
    ob = [k.sbuf(f"{pre}ob{i}", [128, 256], BF16) for i in range(2)]
    oTst = [k.sbuf(f"{pre}oTst{i}", [128, 2, 512], BF16) for i in range(2)]
    cnt = {"s": 0, "o": 0, "t": 0}

    k.dma("sp", vcs[:], vc_d[:, :].rearrange("(j p) c -> p j c", p=128), reads=[vc_d], writes=[vcs])
    k.dma("sp", vds[:], vd_d[:, :].rearrange("(j p) c -> p j c", p=128), reads=[vd_d], writes=[vds])

    def recip_den(acc, W):
        k.op("dve", lambda g: g.reciprocal(out=den[:], in_=acc[:, W - 1:W]), reads=[acc], writes=[den])

    def emit_T(obuf, nblk, blk0, q0, j, last):
        pt_ = ps_t[cnt["t"] % len(ps_t)]
        cnt["t"] += 1
        for b_ in range(nblk):
            k.op("pe", lambda g: g.transpose(out=pt_[:, b_, :], in_=obuf[:, b_ * 128:(b_ + 1) * 128], identity=ident_b[:]),
                 reads=[obuf, ident_b], writes=[pt_])
        stg = oTst[(q0 // 512) % 2]
        k.copy("dve", stg, stg[:, 0:nblk, j * 128:(j + 1) * 128], pt_, pt_[:, 0:nblk, :])
        if last:
            k.dma("sp", oT_d[blk0:blk0 + nblk, :, q0:q0 + 512].rearrange("b p t -> p b t"), stg[:, 0:nblk, :],
                  reads=[stg], writes=[oT_d])

    for mp in range(2):
        kT, qT = kTs[mp], qTs[mp]
        k.dma("sp", kT[:], kc_d[mp], reads=[kc_d], writes=[kT])
        k.dma("sp", qT[:], qc_d[mp], reads=[qc_d], writes=[qT])
        for ch in range(nchunks):
            q0 = ch * 512
            dense_unit(k, kT, qT, q0, vcs, 257, accs, ps_s, pts, cnt, scale)
            for j in range(4):
                acc = accs[j]
                recip_den(acc, 257)
                if mp == 0:
                    k.op("dve", lambda g: g.tensor_scalar(out=o1s[:, ch * 4 + j, :], in0=acc[:, 0:256], scalar1=den[:, 0:1],
                                                          scalar2=None, op0=ALU.mult), reads=[acc, den], writes=[o1s])
                else:
                    k.op("dve", lambda g: g.tensor_scalar(out=o2[:], in0=acc[:, 0:256], scalar1=den[:, 0:1], scalar2=None,
                                                          op0=ALU.mult), reads=[acc, den], writes=[o2])
                    k.op("dve", lambda g: g.scalar_tensor_tensor(out=o2[:], in0=o2[:], scalar=neglam[:, 0:1],
                                                                 in1=o1s[:, ch * 4 + j, :], op0=ALU.mult, op1=ALU.add),
                         reads=[o2, neglam, o1s], writes=[o2])
                    k.op("act", lambda g: g.activation(out=junk[:], in_=o2[:], func=AF.Square, accum_out=ssq[:]),
                         reads=[o2], writes=[junk, ssq])
                    k.op("dve", lambda g: g.tensor_scalar(out=ssq[:], in0=ssq[:], scalar1=1.0 / 256, scalar2=1e-6,
                                                          op0=ALU.mult, op1=ALU.add), reads=[ssq], writes=[ssq])
                    k.op("act", lambda g: g.activation(out=ssq[:], in_=ssq[:], func=AF.Sqrt), reads=[ssq], writes=[ssq])
                    k.op("dve", lambda g: g.reciprocal(out=ssq[:], in_=ssq[:]), reads=[ssq], writes=[ssq])
                    o_ = ob[cnt["o"] % 2]
                    cnt["o"] += 1
                    k.op("dve", lambda g: g.scalar_tensor_tensor(out=o_[:], in0=o2[:], scalar=ssq[:, 0:1], in1=subg[:],
                                                                 op0=ALU.mult, op1=ALU.mult), reads=[o2, ssq, subg], writes=[o_])
                    emit_T(o_, 2, 0, q0, j, j == 3)
    kT = kTs[0]
    k.dma("sp", kT[:], kd_d[:, :], reads=[kd_d], writes=[kT])
    for hq in range(2):
        qT = qTs[hq]
        k.dma("sp", qT[:], qd_d[hq], reads=[qd_d], writes=[qT])
        for ch in range(nchunks):
            q0 = ch * 512
            dense_unit(k, kT, qT, q0, vds, 129, accs, ps_s, pts, cnt, scale)
            for j in range(4):
                acc = accs[j]
                recip_den(acc, 129)
                o_ = ob[cnt["o"] % 2]
                cnt["o"] += 1
                k.op("dve", lambda g: g.tensor_scalar(out=o_[:, 0:128], in0=acc[:, 0:128], scalar1=den[:, 0:1], scalar2=None,
                                                      op0=ALU.mult), reads=[acc, den], writes=[o_])
                emit_T(o_, 1, 2 + hq, q0, j, j == 3)


def build_attn_cd(nchunks=SEQ // 512):
    nc = bass.Bass("TRN2", target_bir_lowering=False)
    with ExitStack() as st:
        k = K(nc, st)
        qc = k.dram("qc", [2, 128, SEQ], BF16, "ExternalInput")
        kc = k.dram("kc", [2, 128, SEQ], BF16, "ExternalInput")
        vc = k.dram("vc", [SEQ, 257], BF16, "ExternalInput")
        qd = k.dram("qd", [2, 128, SEQ], BF16, "ExternalInput")
        kd = k.dram("kd", [128, SEQ], BF16, "ExternalInput")
        vd = k.dram("vd", [SEQ, 129], BF16, "ExternalInput")
        lam = k.dram("lam", [4, 128], F32, "ExternalInput")
        subg = k.dram("subg", [1, 256], F32, "ExternalInput")
        identb = k.dram("identb", [128, 128], BF16, "ExternalInput")
        oT = k.dram("oT", [4, 128, SEQ], BF16, "ExternalOutput")
        phase_attn_cd(k, "e", qc, kc, vc, qd, kd, vd, lam, subg, identb, oT, nchunks=nchunks)
        k.finish("sp")
    return nc


def _phase(k, outer):
    class _P:
        def __enter__(self_):
            self_.ph = ExitStack()
            self_.ph.__enter__()
            k.stack = self_.ph
            return self_.ph

        def __exit__(self_, *a):
            barrier(k)
            k.stack = outer
            return self_.ph.__exit__(*a)
    return _P()


def _peer_phases(k, st, tag, x1, x1T, wq, keysT, ub16, vb16, pconst, identb, sc, z, g, b, x2, x2T):
    with _phase(k, st):
        xb = [k.sbuf(f"{tag}pxb{i}", [128, T], BF16) for i in range(KT)]
        phase_peer_scores(k, tag + "ps", x1T, wq, keysT, sc, xb)
    with _phase(k, st):
        phase_peer_main(k, tag + "pm", x1, x1T, sc, ub16, vb16, pconst, identb, z)
    with _phase(k, st):
        ps_tr = [k.psum(f"{tag}lnpt{i}", [128, 8, 128], BF16) for i in range(2)]
        phase_ln(k, tag + "pl", z, g, b, x2, x2T, identb, ps_tr)


def build_mid():
    nc = bass.Bass("TRN2", target_bir_lowering=False)
    with ExitStack() as st:
        k = K(nc, st)
        EI, EO = "ExternalInput", "ExternalOutput"
        qT = k.dram("qT", [AB_NFM, 128, T], BF16, EI)
        kTe = k.dram("kTe", [28, 128, TE], BF16, EI)
        vaug = k.dram("vaug", [TE, 28, 129], BF16, EI)
        sink = k.dram("sink", [1, 16], F32, EI)
        masks = k.dram("masks", [128, N_MASKS, 128], BF16, EI)
        identb = k.dram("identb", [128, 128], BF16, EI)
        wo = k.dram("wo", [24 * 128, D], F32, EI)
        x = k.dram("x", [T, D], F32, EI)
        gm, bm = k.dram("gm", [1, D], F32, EI), k.dram("bm", [1, D], F32, EI)
        wq = k.dram("wq", [D, 2048], F32, EI)
        keysT = k.dram("keysT", [128, 2048], F32, EI)
        u16 = k.dram("u", [16384, D], BF16, EI)
        v16 = k.dram("v", [16384, D], BF16, EI)
        gf, bf = k.dram("gf", [1, D], F32, EI), k.dram("bf", [1, D], F32, EI)
        pconst = k.dram("pconst", [128, 48], F32, EI)
        w1 = k.dram("w1", [D, 9216], F32, EI)
        tabd = {nm: k.dram(nm, [T, 128], F32, EI) for nm in ("ccp", "ssp", "cca", "ssa")}
        qgd, kgd = k.dram("qg", [1, 128], F32, EI), k.dram("kg", [1, 128], F32, EI)
        oT = k.dram("oT", [24, 128, T], BF16)
        z = k.dram("z", [T, D], F32)
        x1 = k.dram("x1", [T, D], F32)
        x1T = k.dram("x1T", [D, T], BF16)
        sc = k.dram("sc", [T, 2048], F32)
        z2 = k.dram("z2", [T, D], F32)
        x2 = k.dram("x2", [T, D], F32, EO)
        x2T = k.dram("x2T", [D, T], BF16)
        qkT1 = k.dram("qkT1", [CD_NFM, 128, T], BF16, EO)
        vtm1 = k.dram("vtm1", [T, CD_NV], BF16, EO)
        with _phase(k, st):
            phase_attn_ab(k, "b", qT, kTe, vaug, sink, masks, identb, oT)
        with _phase(k, st):
            res = alloc_linear_res(k, "c")
            phase_outproj(k, "c", oT, 24, wo, x, z, res)
        with _phase(k, st):
            ps_tr = [k.psum(f"lnpt{i}", [128, 8, 128], BF16) for i in range(2)]
            phase_ln(k, "l", z, gm, bm, x1, x1T, identb, ps_tr)
        _peer_phases(k, st, "p", x1, x1T, wq, keysT, u16, v16, pconst, identb, sc, z2, gf, bf, x2, x2T)
        with _phase(k, st):
            res = alloc_linear_res(k, "a")
            ident_b = k.sbuf("identb_s", [128, 128], BF16)
            k.dma("sp", ident_b[:], identb[:, :], reads=[identb], writes=[ident_b])
            tabs = load_tabs(k, "a", list(tabd.keys()), tabd)
            gains = {}
            for nm, gd in (("qg", qgd), ("kg", kgd)):
                gs = k.sbuf("g_" + nm, [128, 128], F32)
                k.dma("sp", gs[:], gd[0:1, :].to_broadcast([128, 128]), reads=[gd], writes=[gs])
                gains[nm] = gs
            for kt in range(KT):
                k.dma("sp", res["xb"][kt][:], x2T[kt * 128:(kt + 1) * 128, :], reads=[x2T], writes=[res["xb"][kt]])
            phase_inproj(k, "a", x2T, w1, 9216, CD_TYPES, tabs, gains, ident_b, qkT1, vtm1, res)
        k.finish("sp")
    return nc


def build_tail():
    nc = bass.Bass("TRN2", target_bir_lowering=False)
    with ExitStack() as st:
        k = K(nc, st)
        EI, EO = "ExternalInput", "ExternalOutput"
        oT = k.dram("oT", [32, 128, T], BF16, EI)
        wo = k.dram("wo", [D, D], F32, EI)
        x = k.dram("x", [T, D], F32, EI)
        gm, bm = k.dram("gm", [1, D], F32, EI), k.dram("bm", [1, D], F32, EI)
        identb = k.dram("identb", [128, 128], BF16, EI)
        wq = k.dram("wq", [D, 2048], F32, EI)
        keysT = k.dram("keysT", [128, 2048], F32, EI)
        u16 = k.dram("u", [16384, D], BF16, EI)
        v16 = k.dram("v", [16384, D], BF16, EI)
        gf, bf = k.dram("gf", [1, D], F32, EI), k.dram("bf", [1, D], F32, EI)
        pconst = k.dram("pconst", [128, 48], F32, EI)
        z = k.dram("z", [T, D], F32)
        x1 = k.dram("x1", [T, D], F32)
        x1T = k.dram("x1T", [D, T], BF16)
        sc = k.dram("sc", [T, 2048], F32)
        z2 = k.dram("z2", [T, D], F32)
        out = k.dram("out", [T, D], F32, EO)
        with _phase(k, st):
            res = alloc_linear_res(k, "c")
            phase_outproj(k, "c", oT, 32, wo, x, z, res)
        with _phase(k, st):
            ps_tr = [k.psum(f"lnpt{i}", [128, 8, 128], BF16) for i in range(2)]
            phase_ln(k, "l", z, gm, bm, x1, x1T, identb, ps_tr)
        _peer_phases(k, st, "p", x1, x1T, wq, keysT, u16, v16, pconst, identb, sc, z2, gf, bf, out, None)
        k.finish("sp")
    return nc


def _run(nc, in_maps):
    res = run_bass_kernel_spmd(nc, in_maps, core_ids=list(range(NCORES)))
    return res.results


def _aug_ones(v):
    out = np.zeros(v.shape[:-1] + (v.shape[-1] + 1,), dtype=v.dtype)
    out[..., :-1] = v
    out[..., -1] = 1.0
    return out


def kernel(x, w_in_ab, sink_a, w_out_ab, w_in_cd, lam_q1, lam_k1, lam_q2, lam_k2, subln_g, q_norm_g, k_norm_g,
           w_out_cd, ln_mix_g, ln_mix_b, peer_wq, peer_keys, peer_u, peer_v, ln_ffn_g, ln_ffn_b):
    f32 = lambda a: np.ascontiguousarray(np.asarray(a, dtype=np.float32))
    x = f32(x)[0]
    ccp, ssp, cca, ssa = rope_tables()
    identb = np.eye(128, dtype=np.float32).astype(NP_BF16)
    masks = band_masks()
    pconst = peer_consts()
    cs = [slice(c * T, (c + 1) * T) for c in range(NCORES)]

    def keysT_of(l):
        return np.ascontiguousarray(f32(peer_keys)[l].reshape(16, 128, 128).transpose(2, 0, 1).reshape(128, 2048))

    w0 = f32(w_in_ab)[0]
    RW = 16384 // NCORES
    pu, pv = f32(peer_u), f32(peer_v)
    r = _run(build_inproj(0, conv=True), [{"xT": np.ascontiguousarray(x[cs[c]].T), "w": w0, "identb": identb,
                                           "ccp": ccp[cs[c]], "ssp": ssp[cs[c]],
                                           "u0p": pu[0, c * RW:(c + 1) * RW], "v0p": pv[0, c * RW:(c + 1) * RW],
                                           "u1p": pu[1, c * RW:(c + 1) * RW], "v1p": pv[1, c * RW:(c + 1) * RW]}
                                          for c in range(NCORES)])
    qkT_all = [r[c]["qkT"] for c in range(NCORES)]
    tabs16 = {nm: np.concatenate([r[c][nm + "b"] for c in range(NCORES)], axis=0) for nm in ("u0", "v0", "u1", "v1")}
    ext = host_ext_kv(qkT_all, [r[c]["vtm"] for c in range(NCORES)], [16, 17, 18, 19] + list(range(44, 68)), None)
    sink = f32(sink_a)[0].reshape(1, 16)
    lmg, lmb, lfg, lfb = f32(ln_mix_g), f32(ln_mix_b), f32(ln_ffn_g), f32(ln_ffn_b)
    pwq = f32(peer_wq)
    qg, kg = f32(q_norm_g)[0:1], f32(k_norm_g)[0:1]
    wo0, w1 = f32(w_out_ab)[0], f32(w_in_cd)[0]
    kT0 = keysT_of(0)
    r = _run(build_mid(), [{"qT": qkT_all[c], "kTe": ext[c][0], "vaug": ext[c][1], "sink": sink, "masks": masks, "identb": identb,
                            "wo": wo0, "x": x[cs[c]], "gm": lmg[0:1], "bm": lmb[0:1], "wq": pwq[0], "keysT": kT0,
                            "u": tabs16["u0"], "v": tabs16["v0"], "gf": lfg[0:1], "bf": lfb[0:1], "pconst": pconst, "w1": w1,
                            "ccp": ccp[cs[c]], "ssp": ssp[cs[c]], "cca": cca[cs[c]], "ssa": ssa[cs[c]], "qg": qg, "kg": kg}
                           for c in range(NCORES)])
    del ext, qkT_all
    x2 = [r[c]["x2"] for c in range(NCORES)]
    qk = np.concatenate([r[c]["qkT1"] for c in range(NCORES)], axis=2)
    vt = np.concatenate([r[c]["vtm1"] for c in range(NCORES)], axis=0)
    lam = np.concatenate([f32(lam_q1)[0:1], f32(lam_k1)[0:1], f32(lam_q2)[0:1], f32(lam_k2)[0:1]], 0)
    subg = f32(subln_g)[0:1]
    maps = []
    for c in range(NCORES):
        maps.append({"qc": np.ascontiguousarray(qk[2 * c:2 * c + 2]), "kc": np.ascontiguousarray(qk[16 + 2 * c:18 + 2 * c]),
                     "vc": _aug_ones(vt[:, 256 * c:256 * c + 256]), "qd": np.ascontiguousarray(qk[32 + 2 * c:34 + 2 * c]),
                     "kd": np.ascontiguousarray(qk[48 + c // 2]),
                     "vd": _aug_ones(vt[:, 2048 + 128 * (c // 2):2048 + 128 * (c // 2) + 128]),
                     "lam": lam, "subg": subg, "identb": identb})
    r = _run(build_attn_cd(), maps)
    del qk, vt, maps
    ofull = np.zeros((32, 128, SEQ), dtype=NP_BF16)
    for c in range(NCORES):
        o = r[c]["oT"]
        ofull[2 * c] = o[0]
        ofull[2 * c + 1] = o[1]
        ofull[16 + 2 * c] = o[2]
        ofull[16 + 2 * c + 1] = o[3]
    wo1 = f32(w_out_cd)[0]
    kT1 = keysT_of(1)
    r = _run(build_tail(), [{"oT": np.ascontiguousarray(ofull[:, :, cs[c]]), "wo": wo1, "x": x2[c], "gm": lmg[1:2], "bm": lmb[1:2],
                             "identb": identb, "wq": pwq[1], "keysT": kT1, "u": tabs16["u1"], "v": tabs16["v1"],
                             "gf": lfg[1:2], "bf": lfb[1:2], "pconst": pconst} for c in range(NCORES)])
    out = np.concatenate([r[c]["out"] for c in range(NCORES)], axis=0)
    return out[None].astype(np.float32)
```

```python
import math
from contextlib import ExitStack

import numpy as np
import ml_dtypes

import concourse.bass as bass
import concourse.mybir as mybir
from concourse.bass_utils import run_bass_kernel_spmd

F32 = mybir.dt.float32
BF16 = mybir.dt.bfloat16
I32 = mybir.dt.int32
U32 = mybir.dt.uint32
AF = mybir.ActivationFunctionType
ALU = mybir.AluOpType
AX = mybir.AxisListType

NCORES = 8
SEQ = 8192
D = 4096
T = SEQ // NCORES
MT = T // 128
KT = D // 128
HD = 128
SEM_LIMIT = 2000
NP_BF16 = ml_dtypes.bfloat16
SAME_ENGINE_WAIT = {"pe": False, "dve": True, "act": True, "pool": True, "sp": False}


class Buf:
    __slots__ = ("t", "lw", "rd", "name")

    def __init__(self, t, name=""):
        self.t = t
        self.lw = None
        self.rd = []
        self.name = name

    def __getitem__(self, idx):
        return self.t[idx]


class K:
    def __init__(self, nc, stack):
        self.nc = nc
        self.stack = stack
        self.sem_stack = stack
        self.eng = {"pe": nc.tensor, "dve": nc.vector, "act": nc.scalar, "pool": nc.gpsimd, "sp": nc.sync}
        self.cur = {}
        self.seen = {}
        self.nsem = 0
        self.dma_rr = {}
        self.same_engine_wait = dict(SAME_ENGINE_WAIT)
        self.n_inst = 0
        self.pending = {}

    def sbuf(self, name, shape, dtype):
        return Buf(self.stack.enter_context(self.nc.sbuf_tensor(name, list(shape), dtype)), name)

    def psum(self, name, shape, dtype=F32):
        return Buf(self.stack.enter_context(self.nc.psum_tensor(name, list(shape), dtype)), name)

    def dram(self, name, shape, dtype, kind="Internal"):
        return Buf(self.nc.dram_tensor(name, list(shape), dtype, kind=kind).ap(), name)

    def new_sem(self, name):
        self.nsem += 1
        return self.sem_stack.enter_context(self.nc.semaphore(f"{name}_{self.nsem}"))

    def _wait(self, e, tok):
        if tok is None:
            return
        sem, val, src = tok
        if src == e and not self.same_engine_wait[e]:
            return
        seen = self.seen.setdefault(e, {})
        kk = id(sem)
        if seen.get(kk, 0) >= val:
            return
        self.eng[e].wait_ge(sem, val)
        seen[kk] = val

    def _deps(self, e, reads, writes):
        for b in reads:
            self._wait(e, b.lw)
        for b in writes:
            self._wait(e, b.lw)
            for r in b.rd:
                self._wait(e, r)

    def _commit(self, tok, reads, writes):
        for b in reads:
            b.rd.append(tok)
            if len(b.rd) > 64:
                b.rd = b.rd[-48:]
        for b in writes:
            b.lw = tok
            b.rd = []

    def op(self, e, fn, reads=(), writes=(), sig=True):
        self._deps(e, reads, writes)
        if not sig:
            pr, pw = self.pending.setdefault(e, ([], []))
            for b in reads:
                if b not in pr:
                    pr.append(b)
            for b in writes:
                if b not in pw:
                    pw.append(b)
            fn(self.eng[e])
            self.n_inst += 1
            return None
        if e in self.pending:
            pr, pw = self.pending.pop(e)
            reads = list(reads) + [b for b in pr if b not in reads]
            writes = list(writes) + [b for b in pw if b not in writes]
        c = self.cur.get(e)
        if c is None or c[1] >= SEM_LIMIT:
            c = [self.new_sem("s" + e), 0]
            self.cur[e] = c
        ins = fn(self.eng[e])
        c[1] += 1
        ins.then_inc(c[0], 1)
        tok = (c[0], c[1], e)
        self._commit(tok, reads, writes)
        self.n_inst += 1
        return tok

    def _dma_slot(self, q):
        R = 8
        ring = self.dma_rr.setdefault(q, {"sems": [], "i": 0})
        i = ring["i"]
        ring["i"] += 1
        slot = i % R
        if len(ring["sems"]) <= slot:
            ring["sems"].append([self.new_sem("d" + q), 0, None])
        s = ring["sems"][slot]
        self._wait(q, s[2])
        if s[1] >= SEM_LIMIT:
            s[0] = self.new_sem("d" + q)
            s[1] = 0
            s[2] = None
        return s

    def dma(self, q, out, in_, reads=(), writes=(), **kw):
        s = self._dma_slot(q)
        self._deps(q, reads, writes)
        ins = self.eng[q].dma_start(out=out, in_=in_, **kw)
        s[1] += 16
        ins.then_inc(s[0], 16)
        tok = (s[0], s[1], "dma")
        s[2] = tok
        self._commit(tok, reads, writes)
        self.n_inst += 1
        return tok

    def gather(self, out, in_, idx_ap, reads=(), writes=(), **kw):
        q = "pool"
        s = self._dma_slot(q)
        self._deps(q, reads, writes)
        ins = self.eng[q].indirect_dma_start(out=out, out_offset=None, in_=in_,
                                             in_offset=bass.IndirectOffsetOnAxis(ap=idx_ap, axis=0), **kw)
        s[1] += 16
        ins.then_inc(s[0], 16)
        tok = (s[0], s[1], "dma")
        s[2] = tok
        self._commit(tok, reads, writes)
        self.n_inst += 1
        return tok

    def finish(self, e="sp"):
        for q, ring in self.dma_rr.items():
            for s in ring["sems"]:
                self._wait(e, s[2])

    def copy(self, e, out_b, out_ap, in_b, in_ap):
        if e == "act":
            return self.op("act", lambda g: g.copy(out=out_ap, in_=in_ap), reads=[in_b], writes=[out_b])
        return self.op(e, lambda g: g.tensor_copy(out=out_ap, in_=in_ap), reads=[in_b], writes=[out_b])


def bcast_free(ap, shape):
    return ap.to_broadcast(list(shape))


def load_xT(k, xT_dram, xb, stage):
    for kt in range(KT):
        s = stage[kt % len(stage)]
        k.dma("sp", s[:], xT_dram[kt * 128:(kt + 1) * 128, :], reads=[xT_dram], writes=[s])
        k.copy("act" if kt % 2 == 0 else "dve", xb[kt], xb[kt][:], s, s[:])


def stream_linear(k, xb, w_dram, N, wst, wb, ps_banks, epilogue, NCH=512, KQ=4, cast_engs=("act", "dve"),
                  state=None):
    nkt = len(xb)
    NKQ = nkt // KQ
    st = state if state is not None else {"cnt": 0, "oc": 0}
    for c in range(N // NCH):
        n0 = c * NCH
        b = c % 2
        for q in range(NKQ):
            s = wst[st["cnt"] % len(wst)]
            src = w_dram[q * KQ * 128:(q + 1) * KQ * 128, n0:n0 + NCH].rearrange("(j p) c -> p j c", p=128)
            k.dma("sp", s[:], src, reads=[w_dram], writes=[s])
            k.copy(cast_engs[st["cnt"] % len(cast_engs)], wb[b][q], wb[b][q][:], s, s[:])
            st["cnt"] += 1
        for m in range(MT):
            p = ps_banks[st["oc"] % len(ps_banks)]
            for kt in range(nkt):
                q, j = divmod(kt, KQ)
                k.op("pe", lambda g: g.matmul(p[:], lhsT=xb[kt][:, m * 128:(m + 1) * 128], rhs=wb[b][q][:, j, :],
                                              start=(kt == 0), stop=(kt == nkt - 1)),
                     reads=[xb[kt], wb[b][q]], writes=[p], sig=(kt == nkt - 1))
            epilogue(c, m, p)
            st["oc"] += 1
    return st


def rope_tile(k, src_b, src3, cc, ss, m, groups, rot_w, rb, tmp_t, tmp_u):
    H4 = 4
    w = rot_w
    ccb = cc[:, m, 0:w].unsqueeze(1).to_broadcast([128, H4, w])
    k.op("dve", lambda g: g.tensor_tensor(out=tmp_t[:, :, 0:w], in0=src3[:, :, 0:w], in1=ccb, op=ALU.mult),
         reads=[src_b, cc], writes=[tmp_t])
    first = True
    for (lo, half) in groups:
        hi = lo + half
        s_lo = ss[:, m, lo:lo + half].unsqueeze(1).to_broadcast([128, H4, half])
        s_hi = ss[:, m, hi:hi + half].unsqueeze(1).to_broadcast([128, H4, half])
        k.op("dve", lambda g: g.tensor_tensor(out=tmp_u[:, :, lo:lo + half], in0=src3[:, :, hi:hi + half], in1=s_lo,
                                              op=ALU.mult), reads=[src_b, ss], writes=[tmp_u])
        k.op("dve", lambda g: g.tensor_tensor(out=tmp_u[:, :, hi:hi + half], in0=src3[:, :, lo:lo + half], in1=s_hi,
                                              op=ALU.mult), reads=[src_b, ss], writes=[tmp_u])
    k.op("dve", lambda g: g.tensor_tensor(out=rb[:, :, 0:w], in0=tmp_t[:, :, 0:w], in1=tmp_u[:, :, 0:w], op=ALU.add),
         reads=[tmp_t, tmp_u], writes=[rb])
    if w < 128:
        k.op("act", lambda g: g.copy(out=rb[:, :, w:128], in_=src3[:, :, w:128]), reads=[src_b], writes=[rb])


def phase_inproj(k, pre, xT_dram, w_dram, N, chunk_types, tabs, gains, ident_b, qkT_dram, v_dram, res, bg=None):
    xb = res["xb"]
    wst = res["wst"]
    wb = res["wb"]
    ps = res["ps_mm"]
    psT = res["ps_tr"]
    rbs = [k.sbuf(f"{pre}rb{i}", [128, 4, 128], BF16) for i in range(2)]
    tmp_t = k.sbuf(f"{pre}tt", [128, 4, 128], F32)
    tmp_u = k.sbuf(f"{pre}tu", [128, 4, 128], F32)
    qn = k.sbuf(f"{pre}qn", [128, 4, 128], F32)
    junk = k.sbuf(f"{pre}junk", [128, 128], F32)
    ssq = k.sbuf(f"{pre}ssq", [128, 4], F32)
    rstd = k.sbuf(f"{pre}rstd", [128, 4], F32)
    fmst = [k.sbuf(f"{pre}fm{i}", [128, 4, T], BF16) for i in range(2)]
    vst = [k.sbuf(f"{pre}vst{i}", [128, 512], BF16) for i in range(2)]
    cnt = {"e": 0}

    def epilogue(c, m, p):
        ty = chunk_types[c]
        i = cnt["e"]
        cnt["e"] += 1
        if bg is not None:
            bg()
            bg()
        if ty[0] == "v":
            vs = vst[i % 2]
            k.copy("act", vs, vs[:], p, p[:])
            k.dma("sp", v_dram[m * 128:(m + 1) * 128, ty[1]:ty[1] + 512], vs[:], reads=[vs], writes=[v_dram])
            return
        rb = rbs[i % 2]
        p3 = p[:].rearrange("p (h d) -> p h d", h=4)
        if ty[0] == "rope":
            rope_tile(k, p, p3, tabs["ccp"], tabs["ssp"], m, [(0, 16)], 32, rb, tmp_t, tmp_u)
        else:
            gb = gains[ty[2]]
            for j in range(4):
                k.op("act", lambda g: g.activation(out=junk[:], in_=p[:, j * 128:(j + 1) * 128], func=AF.Square,
                                                   accum_out=ssq[:, j:j + 1]), reads=[p], writes=[junk, ssq])
            k.op("dve", lambda g: g.tensor_scalar(out=rstd[:], in0=ssq[:], scalar1=1.0 / 128, scalar2=1e-6,
                                                  op0=ALU.mult, op1=ALU.add), reads=[ssq], writes=[rstd])
            k.op("act", lambda g: g.activation(out=rstd[:], in_=rstd[:], func=AF.Sqrt), reads=[rstd], writes=[rstd])
            k.op("dve", lambda g: g.reciprocal(out=rstd[:], in_=rstd[:]), reads=[rstd], writes=[rstd])
            k.op("dve", lambda g: g.tensor_tensor(out=qn[:], in0=p3, in1=rstd[:].unsqueeze(2).to_broadcast([128, 4, 128]),
                                                  op=ALU.mult), reads=[p, rstd], writes=[qn])
            k.op("dve", lambda g: g.tensor_tensor(out=qn[:], in0=qn[:], in1=gb[:].unsqueeze(1).to_broadcast([128, 4, 128]),
                                                  op=ALU.mult), reads=[qn, gb], writes=[qn])
            rope_tile(k, qn, qn[:], tabs["cca"], tabs["ssa"], m, [(0, 32), (64, 32)], 128, rb, tmp_t, tmp_u)
        pt = psT[i % len(psT)]
        for j in range(4):
            k.op("pe", lambda g: g.transpose(out=pt[:, j, 0:128], in_=rb[:, j, :], identity=ident_b[:]),
                 reads=[rb, ident_b], writes=[pt])
        fm = fmst[c % 2]
        k.copy("act" if i % 2 else "dve", fm, fm[:, :, m * 128:(m + 1) * 128], pt, pt[:, :, 0:128])
        if m == MT - 1:
            fm0 = ty[1]
            k.dma("sp", qkT_dram[fm0:fm0 + 4].rearrange("j p t -> p j t"), fm[:], reads=[fm], writes=[qkT_dram])

    stream_linear(k, xb, w_dram, N, wst, wb, ps, epilogue, state=res["lin_state"],
                  cast_engs=("act", "dve"))


def rope_tables():
    pos = np.arange(SEQ, dtype=np.float32)
    half = 16
    inv = (np.float32(500000.0) ** (-np.arange(half, dtype=np.float32) / half)).astype(np.float32)
    ang = pos[:, None] * inv[None, :]
    c, s = np.cos(ang).astype(np.float32), np.sin(ang).astype(np.float32)
    ccp = np.ones((SEQ, 128), np.float32)
    ssp = np.zeros((SEQ, 128), np.float32)
    ccp[:, 0:16] = c
    ccp[:, 16:32] = c
    ssp[:, 0:16] = -s
    ssp[:, 16:32] = s
    rows = (np.arange(SEQ) // 64).astype(np.float32)
    cols = (np.arange(SEQ) % 64).astype(np.float32)
    inv2 = (np.float32(10000.0) ** (-np.arange(32, dtype=np.float32) / 32)).astype(np.float32)
    ar = rows[:, None] * inv2[None, :]
    ac = cols[:, None] * inv2[None, :]
    cr, sr, cc_, sc = (np.cos(ar).astype(np.float32), np.sin(ar).astype(np.float32),
                       np.cos(ac).astype(np.float32), np.sin(ac).astype(np.float32))
    cca = np.concatenate([cr, cr, cc_, cc_], 1)
    ssa = np.concatenate([-sr, sr, -sc, sc], 1)
    return ccp, ssp, cca, ssa


def alloc_linear_res(k, pre):
    res = {}
    res["xb"] = [k.sbuf(f"{pre}xb{i}", [128, T], BF16) for i in range(KT)]
    res["xst"] = [k.sbuf(f"{pre}xst{i}", [128, T], F32) for i in range(2)]
    res["wst"] = [k.sbuf(f"{pre}wst{i}", [128, 4, 512], F32) for i in range(2)]
    res["wb"] = [[k.sbuf(f"{pre}wb{b}_{q}", [128, 4, 512], BF16) for q in range(KT // 4)] for b in range(2)]
    res["ps_mm"] = [k.psum(f"{pre}psmm{i}", [128, 512], F32) for i in range(4)]
    res["ps_tr"] = [k.psum(f"{pre}pstr{i}", [128, 4, 256], BF16) for i in range(2)]
    res["lin_state"] = {"cnt": 0, "oc": 0}
    return res


def load_tabs(k, pre, names, drams):
    tabs = {}
    for nm in names:
        t = k.sbuf(f"{pre}tab_{nm}", [128, MT, 128], F32)
        k.dma("sp", t[:], drams[nm][:, :].rearrange("(m p) d -> p m d", p=128), reads=[drams[nm]], writes=[t])
        tabs[nm] = t
    return tabs


AB_TYPES = ([("rope", c * 4) for c in range(4)] + [("rope", 16)] + [("v", 0)]
            + [("rope", 20 + c * 4) for c in range(6)] + [("rope", 44 + c * 4) for c in range(6)]
            + [("v", 512 + c * 512) for c in range(6)])
AB_NFM, AB_NV = 68, 3584
CD_TYPES = ([("rope", c * 4) for c in range(4)] + [("rope", 16 + c * 4) for c in range(4)]
            + [("v", c * 512) for c in range(4)]
            + [("axial", 32 + c * 4, "qg") for c in range(4)] + [("axial", 48, "kg")] + [("v", 2048)])
CD_NFM, CD_NV = 52, 2560


def build_inproj(layer, types=None, x_bf16=False, conv=False):
    nc = bass.Bass("TRN2", target_bir_lowering=False)
    if types is None:
        types = AB_TYPES if layer == 0 else CD_TYPES
    N = 512 * len(types)
    nfm, nv = (AB_NFM, AB_NV) if layer == 0 else (CD_NFM, CD_NV)
    with ExitStack() as st:
        k = K(nc, st)
        xT = k.dram("xT", [D, T], BF16 if x_bf16 else F32, "ExternalInput")
        w = k.dram("w", [D, N], F32, "ExternalInput")
        identb = k.dram("identb", [128, 128], BF16, "ExternalInput")
        tabd = {nm: k.dram(nm, [T, 128], F32, "ExternalInput") for nm in (("ccp", "ssp") if layer == 0 else ("ccp", "ssp", "cca", "ssa"))}
        qkT = k.dram("qkT", [nfm, 128, T], BF16, "ExternalOutput")
        vtm = k.dram("vtm", [T, nv], BF16, "ExternalOutput")
        res = alloc_linear_res(k, "a")
        ident_b = k.sbuf("identb_s", [128, 128], BF16)
        k.dma("sp", ident_b[:], identb[:, :], reads=[identb], writes=[ident_b])
        tabs = load_tabs(k, "a", list(tabd.keys()), tabd)
        gains = {}
        if layer == 1:
            for nm in ("qg", "kg"):
                gd = k.dram(nm, [1, 128], F32, "ExternalInput")
                gs = k.sbuf("g_" + nm, [128, 128], F32)
                k.dma("sp", gs[:], gd[0:1, :].to_broadcast([128, 128]), reads=[gd], writes=[gs])
                gains[nm] = gs
        if x_bf16:
            for kt in range(KT):
                k.dma("sp", res["xb"][kt][:], xT[kt * 128:(kt + 1) * 128, :], reads=[xT], writes=[res["xb"][kt]])
        else:
            load_xT(k, xT, res["xb"], res["xst"])
        bg = None
        if conv:
            RW = 16384 // NCORES
            pairs = []
            for nm in ("u0", "v0", "u1", "v1"):
                src = k.dram(nm + "p", [RW, D], F32, "ExternalInput")
                dst = k.dram(nm + "b", [RW, D], BF16, "ExternalOutput")
                pairs.append((src, dst))
            bg = cast_rows_bg(k, "cv", pairs, RW, ncol=4)
        phase_inproj(k, "a", xT, w, N, types, tabs, gains, ident_b, qkT, vtm, res, bg=bg)
        while bg is not None and bg():
            pass
        k.finish("sp")
    return nc


HALO = 1024
TE = T + 2 * HALO
NTE = TE // 128
B_DIL = (1, 4, 16)
B_DT = (1, 2, 8)
MASK_OFF = {"A": 0, 0: 3, 1: 6, 2: 11}
N_MASKS = 28


def band_masks():
    m = np.zeros((128, N_MASKS, 128), np.float32)
    kl = np.arange(128)[:, None]
    ql = np.arange(128)[None, :]
    for j, dt in enumerate((-1, 0, 1)):
        diff = (ql - kl) - dt * 128
        m[:, MASK_OFF["A"] + j, :] = (np.abs(diff) <= 128)
    for g in range(3):
        d, r = B_DIL[g], B_DT[g]
        for j, dt in enumerate(range(-r, r + 1)):
            diff = (ql - kl) - dt * 128
            m[:, MASK_OFF[g] + j, :] = (np.abs(diff) <= 64 * d) & (diff % d == 0)
    return m.astype(NP_BF16)


def attn_chunks(qT_ap, qT_b, kT_b, v_b, ktiles, mask0, acc, first, last):
    out = []
    n = len(ktiles)
    i0 = 0
    while i0 < n:
        nt = min(4, n - i0)
        out.append({"q": qT_ap, "qb": qT_b, "kT": kT_b, "v": v_b, "tiles": ktiles[i0:i0 + nt], "mask0": mask0 + i0, "acc": acc,
                    "start": first and i0 == 0, "stop": last and i0 + nt == n, "post": None})
        i0 += nt
    return out


def run_chunks(k, chunks, mask_b, ps_s, pts, cnt, scale):
    def S(ch):
        ps = ps_s[cnt["s"] % len(ps_s)]
        pt = pts[cnt["s"] % len(pts)]
        cnt["s"] += 1
        ch["pt"] = pt
        nt = len(ch["tiles"])
        for j, lt in enumerate(ch["tiles"]):
            k.op("pe", lambda g: g.matmul(ps[:, j, :], lhsT=ch["kT"][:, lt * 128:(lt + 1) * 128], rhs=ch["q"],
                                          start=True, stop=True), reads=[ch["kT"], ch["qb"]], writes=[ps], sig=(j == nt - 1))
        k.op("act", lambda g: g.activation(out=pt[:, 0:nt, :], in_=ps[:, 0:nt, :], func=AF.Exp, scale=scale),
             reads=[ps], writes=[pt])
        k.op("dve", lambda g: g.tensor_tensor(out=pt[:, 0:nt, :], in0=pt[:, 0:nt, :],
                                              in1=mask_b[:, ch["mask0"]:ch["mask0"] + nt, :], op=ALU.mult),
             reads=[pt, mask_b], writes=[pt])

    def P(ch):
        pt = ch["pt"]
        nt = len(ch["tiles"])
        for j, lt in enumerate(ch["tiles"]):
            k.op("pe", lambda g: g.matmul(ch["acc"][:, 0:129], lhsT=pt[:, j, :], rhs=ch["v"][:, lt, :],
                                          start=(ch["start"] and j == 0), stop=(ch["stop"] and j == nt - 1)),
                 reads=[pt, ch["v"]], writes=[ch["acc"]], sig=(j == nt - 1))
        if ch["post"] is not None:
            ch["post"]()

    if not chunks:
        return
    S(chunks[0])
    for i, ch in enumerate(chunks):
        if i + 1 < len(chunks):
            S(chunks[i + 1])
        P(ch)


def phase_attn_ab(k, pre, qT_d, kTe_d, vaug_d, sink_d, mask_d, identb_d, oT_d, bg=None):
    scale = 1.0 / math.sqrt(128.0)
    mask_b = k.sbuf(f"{pre}mask", [128, N_MASKS, 128], BF16)
    k.dma("sp", mask_b[:], mask_d[:, :, :], reads=[mask_d], writes=[mask_b])
    ident_b = k.sbuf(f"{pre}ident", [128, 128], BF16)
    k.dma("sp", ident_b[:], identb_d[:, :], reads=[identb_d], writes=[ident_b])
    esink = k.sbuf(f"{pre}esink", [128, 16], F32)
    k.dma("sp", esink[:], sink_d[0:1, :].to_broadcast([128, 16]), reads=[sink_d], writes=[esink])
    k.op("act", lambda g: g.activation(out=esink[:], in_=esink[:], func=AF.Exp), reads=[esink], writes=[esink])
    NKT = 46
    kbuf = [k.sbuf(f"{pre}kb{i}", [128, NKT * 128], BF16) for i in range(2)]
    vbuf = [k.sbuf(f"{pre}vb{i}", [128, NKT, 129], BF16) for i in range(2)]
    qbuf = [k.sbuf(f"{pre}qb{i}", [128, 4, T], BF16) for i in range(2)]
    ps_s = [k.psum(f"{pre}pss{i}", [128, 4, 128], F32) for i in range(3)]
    pts = [k.sbuf(f"{pre}pt{i}", [128, 4, 128], BF16) for i in range(4)]
    accs = [k.psum(f"{pre}acc{i}", [128, 512], F32) for i in range(2)]
    ps_t = [k.psum(f"{pre}pst{i}", [128, 1024], BF16) for i in range(2)]
    den = k.sbuf(f"{pre}den", [128, 1], F32)
    ob = [k.sbuf(f"{pre}ob{i}", [128, 128], BF16) for i in range(2)]
    oTs = [k.sbuf(f"{pre}oTs{i}", [128, T], BF16) for i in range(2)]
    cnt = {"s": 0, "a": 0, "o": 0}

    def finalize(acc, sink_col, ohead, m):
        if sink_col is not None:
            k.op("dve", lambda g: g.tensor_tensor(out=den[:], in0=acc[:, 128:129], in1=esink[:, sink_col:sink_col + 1],
                                                  op=ALU.add), reads=[acc, esink], writes=[den])
        else:
            k.op("dve", lambda g: g.tensor_copy(out=den[:], in_=acc[:, 128:129]), reads=[acc], writes=[den])
        k.op("dve", lambda g: g.reciprocal(out=den[:], in_=den[:]), reads=[den], writes=[den])
        o = ob[cnt["o"] % 2]
        k.op("dve", lambda g: g.tensor_scalar(out=o[:], in0=acc[:, 0:128], scalar1=den[:, 0:1], scalar2=None,
                                              op0=ALU.mult), reads=[acc, den], writes=[o])
        pt_ = ps_t[cnt["o"] % 2]
        k.op("pe", lambda g: g.transpose(out=pt_[:, 0:128], in_=o[:], identity=ident_b[:]),
             reads=[o, ident_b], writes=[pt_])
        oT = oTs[ohead % 2]
        k.copy("act", oT, oT[:, m * 128:(m + 1) * 128], pt_, pt_[:, 0:128])
        cnt["o"] += 1
        if m == MT - 1:
            k.dma("sp", oT_d[ohead], oT[:], reads=[oT], writes=[oT_d])

    job = 0
    for kvh in range(4):
        kb_, vb_, qb_ = kbuf[job % 2], vbuf[job % 2], qbuf[job % 2]
        job += 1
        k.dma("sp", kb_[:, 0:10 * 128], kTe_d[kvh, :, 7 * 128:17 * 128], reads=[kTe_d], writes=[kb_])
        k.dma("sp", vb_[:, 0:10, :], vaug_d[7 * 128:17 * 128, kvh, :].rearrange("(j p) c -> p j c", p=128),
              reads=[vaug_d], writes=[vb_])
        k.dma("sp", qb_[:], qT_d[kvh * 4:kvh * 4 + 4].rearrange("j p t -> p j t"), reads=[qT_d], writes=[qb_])
        chunks = []
        for g4 in range(4):
            h = kvh * 4 + g4
            for m in range(MT):
                acc = accs[cnt["a"] % 2]
                cnt["a"] += 1
                cs_ = attn_chunks(qb_[:, g4, m * 128:(m + 1) * 128], qb_, kb_, vb_, [m, m + 1, m + 2], MASK_OFF["A"], acc,
                                  True, True)
                cs_[-1]["post"] = (lambda acc=acc, h=h, m=m: finalize(acc, h, h, m))
                chunks += cs_
        run_chunks(k, chunks, mask_b, ps_s, pts, cnt, scale)
    base = [0, 10, 22]
    lo_ext = [7, 6, 0]
    nload = [10, 12, 24]
    for h in range(8):
        kb_, vb_, qb_ = kbuf[job % 2], vbuf[job % 2], qbuf[job % 2]
        job += 1
        for g in range(3):
            kblk = 4 + g * 8 + h
            k.dma("sp", kb_[:, base[g] * 128:(base[g] + nload[g]) * 128],
                  kTe_d[kblk, :, lo_ext[g] * 128:(lo_ext[g] + nload[g]) * 128], reads=[kTe_d], writes=[kb_])
            k.dma("sp", vb_[:, base[g]:base[g] + nload[g], :],
                  vaug_d[lo_ext[g] * 128:(lo_ext[g] + nload[g]) * 128, kblk, :].rearrange("(j p) c -> p j c", p=128),
                  reads=[vaug_d], writes=[vb_])
            k.dma("sp", qb_[:, g, :], qT_d[20 + g * 8 + h], reads=[qT_d], writes=[qb_])
        chunks = []
        for m in range(MT):
            acc = accs[cnt["a"] % 2]
            cnt["a"] += 1
            for g in range(3):
                r = B_DT[g]
                tiles = [base[g] + (8 + m + dt) - lo_ext[g] for dt in range(-r, r + 1)]
                chunks += attn_chunks(qb_[:, g, m * 128:(m + 1) * 128], qb_, kb_, vb_, tiles, MASK_OFF[g], acc, g == 0, g == 2)
            chunks[-1]["post"] = (lambda acc=acc, h=h, m=m: finalize(acc, None, 16 + h, m))
        run_chunks(k, chunks, mask_b, ps_s, pts, cnt, scale)


def build_attn_ab():
    nc = bass.Bass("TRN2", target_bir_lowering=False)
    with ExitStack() as st:
        k = K(nc, st)
        qT = k.dram("qT", [AB_NFM, 128, T], BF16, "ExternalInput")
        kTe = k.dram("kTe", [28, 128, TE], BF16, "ExternalInput")
        vaug = k.dram("vaug", [TE, 28, 129], BF16, "ExternalInput")
        sink = k.dram("sink", [1, 16], F32, "ExternalInput")
        masks = k.dram("masks", [128, N_MASKS, 128], BF16, "ExternalInput")
        identb = k.dram("identb", [128, 128], BF16, "ExternalInput")
        oT = k.dram("oT", [24, 128, T], BF16, "ExternalOutput")
        phase_attn_ab(k, "b", qT, kTe, vaug, sink, masks, identb, oT)
        k.finish("sp")
    return nc


def host_ext_kv(qkT_all, vtm_all, kblocks, vheads_cols):
    kfull = np.concatenate([q[kblocks] for q in qkT_all], axis=2)
    vfull = np.concatenate(vtm_all, axis=0)
    nk = len(kblocks)
    kpad = np.zeros((nk, 128, SEQ + 2 * HALO), dtype=kfull.dtype)
    kpad[:, :, HALO:HALO + SEQ] = kfull
    nvh = vfull.shape[1] // 128
    vpad = np.zeros((SEQ + 2 * HALO, nvh, 129), dtype=vfull.dtype)
    vpad[HALO:HALO + SEQ, :, 0:128] = vfull.reshape(SEQ, nvh, 128)
    vpad[HALO:HALO + SEQ, :, 128] = 1.0
    outs = []
    for c in range(NCORES):
        outs.append((np.ascontiguousarray(kpad[:, :, c * T:c * T + TE]), np.ascontiguousarray(vpad[c * T:c * T + TE])))
    return outs


ALPHA = float((2 * 2) ** 0.25)


def barrier(k):
    toks = []
    for e, c in k.cur.items():
        if c[1] > 0:
            toks.append((c[0], c[1], e))
    for q, ring in k.dma_rr.items():
        for s in ring["sems"]:
            if s[2] is not None:
                toks.append(s[2])
    for e in ("pe", "dve", "act", "pool", "sp"):
        for tk in toks:
            if tk[2] == e:
                continue
            k._wait(e, tk)


def phase_outproj(k, pre, oT_d, nkt, w_d, x_d, z_d, res):
    xb = res["xb"][:nkt]
    for kt in range(nkt):
        k.dma("sp", xb[kt][:], oT_d[kt], reads=[oT_d], writes=[xb[kt]])
    xin = [k.sbuf(f"{pre}xin{i}", [128, 512], F32) for i in range(3)]
    zt = [k.sbuf(f"{pre}zt{i}", [128, 512], F32) for i in range(3)]
    cnt = {"e": 0}

    def epilogue(c, m, p):
        i = cnt["e"]
        cnt["e"] += 1
        xi, zo = xin[i % 3], zt[i % 3]
        k.dma("sp", xi[:], x_d[m * 128:(m + 1) * 128, c * 512:(c + 1) * 512], reads=[x_d], writes=[xi])
        k.op("dve", lambda g: g.scalar_tensor_tensor(out=zo[:], in0=xi[:], scalar=ALPHA, in1=p[:], op0=ALU.mult,
                                                     op1=ALU.add), reads=[xi, p], writes=[zo])
        k.dma("sp", z_d[m * 128:(m + 1) * 128, c * 512:(c + 1) * 512], zo[:], reads=[zo], writes=[z_d])

    stream_linear(k, xb, w_d, D, res["wst"], res["wb"], res["ps_mm"], epilogue, state=res["lin_state"])


def phase_ln(k, pre, z_d, g_d, b_d, out_d, outT_d, identb_d, ps_tr, mt=MT):
    gt = k.sbuf(f"{pre}g", [128, D], F32)
    bt = k.sbuf(f"{pre}b", [128, D], F32)
    k.dma("sp", gt[:], g_d[0:1, :].to_broadcast([128, D]), reads=[g_d], writes=[gt])
    k.dma("sp", bt[:], b_d[0:1, :].to_broadcast([128, D]), reads=[b_d], writes=[bt])
    ident_b = k.sbuf(f"{pre}ident", [128, 128], BF16)
    k.dma("sp", ident_b[:], identb_d[:, :], reads=[identb_d], writes=[ident_b])
    zs = [k.sbuf(f"{pre}z{i}", [128, D], F32) for i in range(2)]
    os_ = [k.sbuf(f"{pre}o{i}", [128, D], F32) for i in range(2)]
    ob16 = k.sbuf(f"{pre}o16", [128, D], BF16)
    stg = [k.sbuf(f"{pre}stg{i}", [128, KT, 128], BF16) for i in range(2)]
    stats = k.sbuf(f"{pre}stats", [128, 8, 6], F32)
    mv = k.sbuf(f"{pre}mv", [128, 2], F32)
    rstd = k.sbuf(f"{pre}rstd", [128, 1], F32)
    for m in range(mt):
        z, o = zs[m % 2], os_[m % 2]
        k.dma("sp", z[:], z_d[m * 128:(m + 1) * 128, :], reads=[z_d], writes=[z])
        for c in range(8):
            k.op("dve", lambda g: g.bn_stats(out=stats[:, c, :], in_=z[:, c * 512:(c + 1) * 512]), reads=[z], writes=[stats])
        k.op("dve", lambda g: g.bn_aggr(out=mv[:], in_=stats[:].rearrange("p a b -> p (a b)")), reads=[stats], writes=[mv])
        k.op("dve", lambda g: g.tensor_scalar(out=rstd[:], in0=mv[:, 1:2], scalar1=1e-5, scalar2=None, op0=ALU.add),
             reads=[mv], writes=[rstd])
        k.op("act", lambda g: g.activation(out=rstd[:], in_=rstd[:], func=AF.Sqrt), reads=[rstd], writes=[rstd])
        k.op("dve", lambda g: g.reciprocal(out=rstd[:], in_=rstd[:]), reads=[rstd], writes=[rstd])
        k.op("dve", lambda g: g.tensor_scalar(out=o[:], in0=z[:], scalar1=mv[:, 0:1], scalar2=rstd[:, 0:1],
                                              op0=ALU.subtract, op1=ALU.mult), reads=[z, mv, rstd], writes=[o])
        k.op("pool", lambda g: g.tensor_tensor(out=o[:], in0=o[:], in1=gt[:], op=ALU.mult), reads=[o, gt], writes=[o])
        k.op("pool", lambda g: g.tensor_tensor(out=o[:], in0=o[:], in1=bt[:], op=ALU.add), reads=[o, bt], writes=[o])
        k.dma("sp", out_d[m * 128:(m + 1) * 128, :], o[:], reads=[o], writes=[out_d])
        if outT_d is not None:
            k.op("act", lambda g: g.copy(out=ob16[:], in_=o[:]), reads=[o], writes=[ob16])
            sg = stg[m % 2]
            for q in range(4):
                pt = ps_tr[q % len(ps_tr)]
                for j in range(8):
                    kt = q * 8 + j
                    k.op("pe", lambda g: g.transpose(out=pt[:, j, :], in_=ob16[:, kt * 128:(kt + 1) * 128],
                                                     identity=ident_b[:]), reads=[ob16, ident_b], writes=[pt])
                k.copy("act" if q % 2 else "dve", sg, sg[:, q * 8:(q + 1) * 8, :], pt, pt[:])
            k.dma("sp", outT_d[:, m * 128:(m + 1) * 128].rearrange("(kt p) t -> p kt t", p=128), sg[:],
                  reads=[sg], writes=[outT_d])


def build_outproj_ln(nkt, with_T):
    nc = bass.Bass("TRN2", target_bir_lowering=False)
    with ExitStack() as st:
        k = K(nc, st)
        oT = k.dram("oT", [nkt, 128, T], BF16, "ExternalInput")
        w = k.dram("w", [nkt * 128, D], F32, "ExternalInput")
        x = k.dram("x", [T, D], F32, "ExternalInput")
        g = k.dram("g", [1, D], F32, "ExternalInput")
        b = k.dram("b", [1, D], F32, "ExternalInput")
        identb = k.dram("identb", [128, 128], BF16, "ExternalInput")
        z = k.dram("z", [T, D], F32, "Internal")
        x1 = k.dram("x1", [T, D], F32, "ExternalOutput")
        x1T = k.dram("x1T", [D, T], BF16, "ExternalOutput") if with_T else None
        with ExitStack() as ph:
            k.stack = ph
            res = alloc_linear_res(k, "c")
            phase_outproj(k, "c", oT, nkt, w, x, z, res)
            barrier(k)
        k.stack = st
        with ExitStack() as ph:
            k.stack = ph
            ps_tr = [k.psum(f"lnpt{i}", [128, 8, 128], BF16) for i in range(2)]
            phase_ln(k, "l", z, g, b, x1, x1T, identb, ps_tr)
            barrier(k)
        k.stack = st
        k.finish("sp")
    return nc


NEG = -1.0e30


def phase_peer_scores(k, pre, x1T_d, wq_d, keysT_d, sc_d, xb):
    for kt in range(KT):
        k.dma("sp", xb[kt][:], x1T_d[kt * 128:(kt + 1) * 128, :], reads=[x1T_d], writes=[xb[kt]])
    keysT = k.sbuf(f"{pre}keysT", [128, 2048], F32)
    k.dma("sp", keysT[:], keysT_d[:, :], reads=[keysT_d], writes=[keysT])
    wst = [k.sbuf(f"{pre}wst{i}", [128, KT, 128], F32) for i in range(2)]
    wqb = [k.sbuf(f"{pre}wqb{i}", [128, KT, 128], BF16) for i in range(2)]
    qg = [k.sbuf(f"{pre}qg{i}", [128, T], F32) for i in range(2)]
    scst = [k.sbuf(f"{pre}scst{i}", [128, MT, 128], F32) for i in range(2)]
    ps = [k.psum(f"{pre}ps{i}", [128, 512], F32) for i in range(4)]
    ps2 = [k.psum(f"{pre}ps2{i}", [128, 512], F32) for i in range(2)]
    for g in range(16):
        ws, wb_, q_, sc_ = wst[g % 2], wqb[g % 2], qg[g % 2], scst[g % 2]
        k.dma("sp", ws[:], wq_d[:, g * 128:(g + 1) * 128].rearrange("(kt p) c -> p kt c", p=128), reads=[wq_d], writes=[ws])
        k.copy(("act", "dve")[g % 2], wb_, wb_[:], ws, ws[:])
        for tc in range(T // 512):
            p = ps[(g * 2 + tc) % 4]
            for kt in range(KT):
                k.op("pe", lambda e: e.matmul(p[:], lhsT=wb_[:, kt, :], rhs=xb[kt][:, tc * 512:(tc + 1) * 512],
                                              start=(kt == 0), stop=(kt == KT - 1)), reads=[wb_, xb[kt]], writes=[p])
            k.copy(("dve", "act")[tc % 2], q_, q_[:, tc * 512:(tc + 1) * 512], p, p[:])
        for m in range(MT):
            p2 = ps2[m % 2]
            k.op("pe", lambda e: e.matmul(p2[:, 0:128], lhsT=q_[:, m * 128:(m + 1) * 128], rhs=keysT[:, g * 128:(g + 1) * 128],
                                          start=True, stop=True), reads=[q_, keysT], writes=[p2])
            k.copy(("act", "dve")[m % 2], sc_, sc_[:, m, :], p2, p2[:, 0:128])
        k.dma("sp", sc_d[:, g * 128:(g + 1) * 128].rearrange("(m p) n -> p m n", p=128), sc_[:], reads=[sc_], writes=[sc_d])


def cast_rows_bg(k, pre, srcs_dsts, nrows, nbuf=2, ncol=1):
    W_ = D // ncol
    st = [k.sbuf(f"{pre}st{i}", [128, W_], F32) for i in range(nbuf)]
    cb = [k.sbuf(f"{pre}cb{i}", [128, W_], BF16) for i in range(nbuf)]
    jobs = [(src, dst, i, j) for (src, dst) in srcs_dsts for i in range(nrows // 128) for j in range(ncol)]
    state = {"n": 0}

    def step():
        n = state["n"]
        if n >= len(jobs):
            return False
        state["n"] += 1
        src, dst, i, j = jobs[n]
        s_, c_ = st[n % nbuf], cb[n % nbuf]
        k.dma("pool", s_[:], src[i * 128:(i + 1) * 128, j * W_:(j + 1) * W_], reads=[src], writes=[s_])
        k.copy("pool", c_, c_[:], s_, s_[:])
        k.dma("pool", dst[i * 128:(i + 1) * 128, j * W_:(j + 1) * W_], c_[:], reads=[c_], writes=[dst])
        return True
    return step


def phase_cast_rows(k, pre, srcs_dsts, nrows=16384):
    st = [k.sbuf(f"{pre}st{i}", [128, D], F32) for i in range(3)]
    cb = [k.sbuf(f"{pre}cb{i}", [128, D], BF16) for i in range(3)]
    jobs = [(src, dst, i) for (src, dst) in srcs_dsts for i in range(nrows // 128)]
    engs = ("act", "dve", "pool")

    def load(n):
        src, dst, i = jobs[n]
        k.dma("sp", st[n % 3][:], src[i * 128:(i + 1) * 128, :], reads=[src], writes=[st[n % 3]])

    for n in range(min(2, len(jobs))):
        load(n)
    for n in range(len(jobs)):
        if n + 2 < len(jobs):
            load(n + 2)
        src, dst, i = jobs[n]
        k.copy(engs[n % 3], cb[n % 3], cb[n % 3][:], st[n % 3], st[n % 3][:])
        k.dma("act", dst[i * 128:(i + 1) * 128, :], cb[n % 3][:], reads=[cb[n % 3]], writes=[dst])


def phase_peer_main(k, pre, x1_d, x1T_d, sc_d, u_d, v_d, pconst_d, identb_d, z_d, NG=4, mt=MT):
    ident_b = k.sbuf(f"{pre}ident", [128, 128], BF16)
    k.dma("sp", ident_b[:], identb_d[:, :], reads=[identb_d], writes=[ident_b])
    pc = k.sbuf(f"{pre}pc", [128, 48], F32)
    k.dma("sp", pc[:], pconst_d[:, :], reads=[pconst_d], writes=[pc])
    iota16, lo16, hi16 = pc[:, 0:16], pc[:, 16:32], pc[:, 32:48]
    sc = k.sbuf(f"{pre}sc", [128, 16, 128], F32)
    scr = k.sbuf(f"{pre}scr", [128, 16, 128], F32)
    sv = k.sbuf(f"{pre}sv", [128, 16, 16], F32)
    si = k.sbuf(f"{pre}si", [128, 16, 16], U32)
    sif = k.sbuf(f"{pre}sif", [128, 16, 16], F32)
    cand = k.sbuf(f"{pre}cand", [128, 8, 256], F32)
    cscr = k.sbuf(f"{pre}cscr", [128, 8, 256], F32)
    ts = k.sbuf(f"{pre}ts", [128, 8, 16], F32)
    tp = k.sbuf(f"{pre}tp", [128, 8, 16], U32)
    tpf = k.sbuf(f"{pre}tpf", [128, 8, 16], F32)
    w4a = k.sbuf(f"{pre}w4a", [128, 8, 16, 16], F32)
    w4b = k.sbuf(f"{pre}w4b", [128, 8, 16, 16], F32)
    r3a = k.sbuf(f"{pre}r3a", [128, 8, 16], F32)
    r3b = k.sbuf(f"{pre}r3b", [128, 8, 16], F32)
    r3c = k.sbuf(f"{pre}r3c", [128, 8, 16], F32)
    eidx_f = k.sbuf(f"{pre}eidxf", [128, 128], F32)
    eidx = k.sbuf(f"{pre}eidx", [128, 128], U32)
    gate = k.sbuf(f"{pre}gate", [128, 8, 16], F32)
    zsum = k.sbuf(f"{pre}zsum", [128, 8], F32)
    hids = [k.sbuf(f"{pre}hid{i}", [128, 128], F32) for i in range(2)]
    g1 = k.sbuf(f"{pre}g1", [128, 128], F32)
    wgts = [k.sbuf(f"{pre}wgt{i}", [128, 128], F32) for i in range(2)]
    xts = [k.sbuf(f"{pre}xt{i}", [128, D], F32) for i in range(2)]
    junk = k.sbuf(f"{pre}junk", [128, D], BF16)
    ug = [k.sbuf(f"{pre}ug{i}", [128, D], BF16) for i in range(NG)]
    vg = [k.sbuf(f"{pre}vg{i}", [128, D // 2], BF16) for i in range(NG)]
    dgs = [k.sbuf(f"{pre}dg{i}", [128, 128], BF16) for i in range(8)]
    zo = [k.sbuf(f"{pre}zo{i}", [128, 512], F32) for i in range(2)]
    xps = k.psum(f"{pre}xps", [128, D], BF16)
    acc = [k.psum(f"{pre}acc{i}", [128, 512], F32) for i in range(4)]

    def dve(fn, reads, writes):
        return k.op("dve", fn, reads=reads, writes=writes)

    eidx_t = [k.sbuf(f"{pre}eidx_t{i}", [128, 128], U32) for i in range(mt)]
    gate_t = [k.sbuf(f"{pre}gate_t{i}", [128, 128], F32) for i in range(mt)]
    evidx_t = [[k.sbuf(f"{pre}evidx_t{i}_{h}", [128, 128], U32) for h in range(2)] for i in range(mt)]

    def route(m):
        k.dma("sp", sc[:], sc_d[m * 128:(m + 1) * 128, :].rearrange("p (g n) -> p g n", g=16), reads=[sc_d], writes=[sc])
        for g in range(16):
            yield ('dve', lambda e: e.max(out=sv[:, g, 0:8], in_=sc[:, g, :]), [sc], [sv])
            yield ('dve', lambda e: e.max_index(out=si[:, g, 0:8], in_max=sv[:, g, 0:8], in_values=sc[:, g, :]), [sc, sv], [si])
            yield ('dve', lambda e: e.match_replace(out=scr[:, g, :], in_to_replace=sv[:, g, 0:8], in_values=sc[:, g, :],
                                          imm_value=NEG), [sc, sv], [scr])
            yield ('dve', lambda e: e.max(out=sv[:, g, 8:16], in_=scr[:, g, :]), [scr], [sv])
            yield ('dve', lambda e: e.max_index(out=si[:, g, 8:16], in_max=sv[:, g, 8:16], in_values=scr[:, g, :]), [scr, sv], [si])
        yield ('dve', lambda e: e.tensor_copy(out=sif[:], in_=si[:]), [si], [sif])
        sv4 = sv[:].rearrange("p (h c) r -> p h c r", c=2)
        sif4 = sif[:].rearrange("p (h c) r -> p h c r", c=2)
        c4 = cand[:].rearrange("p h (a b) -> p h a b", a=16)
        yield ('dve', lambda e: e.tensor_tensor(out=c4, in0=sv4[:, :, 0, :].unsqueeze(3).to_broadcast([128, 8, 16, 16]),
                                      in1=sv4[:, :, 1, :].unsqueeze(2).to_broadcast([128, 8, 16, 16]), op=ALU.add),
            [sv], [cand])
        for h in range(8):
            yield ('dve', lambda e: e.max(out=ts[:, h, 0:8], in_=cand[:, h, :]), [cand], [ts])
            yield ('dve', lambda e: e.max_index(out=tp[:, h, 0:8], in_max=ts[:, h, 0:8], in_values=cand[:, h, :]), [cand, ts], [tp])
            yield ('dve', lambda e: e.match_replace(out=cscr[:, h, :], in_to_replace=ts[:, h, 0:8], in_values=cand[:, h, :],
                                          imm_value=NEG), [cand, ts], [cscr])
            yield ('dve', lambda e: e.max(out=ts[:, h, 8:16], in_=cscr[:, h, :]), [cscr], [ts])
            yield ('dve', lambda e: e.max_index(out=tp[:, h, 8:16], in_max=ts[:, h, 8:16], in_values=cscr[:, h, :]), [cscr, ts], [tp])
        yield ('dve', lambda e: e.tensor_copy(out=tpf[:], in_=tp[:]), [tp], [tpf])
        tpf4 = tpf[:].unsqueeze(3).to_broadcast([128, 8, 16, 16])
        lo4 = lo16.unsqueeze(1).unsqueeze(1).to_broadcast([128, 8, 16, 16])
        hi4 = hi16.unsqueeze(1).unsqueeze(1).to_broadcast([128, 8, 16, 16])
        io4 = iota16.unsqueeze(1).unsqueeze(1).to_broadcast([128, 8, 16, 16])
        yield ('dve', lambda e: e.tensor_tensor(out=w4a[:], in0=tpf4, in1=lo4, op=ALU.is_ge), [tpf, pc], [w4a])
        yield ('dve', lambda e: e.tensor_tensor(out=w4b[:], in0=tpf4, in1=hi4, op=ALU.is_ge), [tpf, pc], [w4b])
        yield ('dve', lambda e: e.tensor_tensor(out=w4a[:], in0=w4a[:], in1=w4b[:], op=ALU.subtract), [w4a, w4b], [w4a])
        yield ('dve', lambda e: e.tensor_tensor(out=w4b[:], in0=w4a[:], in1=sif4[:, :, 0, :].unsqueeze(2).to_broadcast([128, 8, 16, 16]),
                                      op=ALU.mult), [w4a, sif], [w4b])
        yield ('dve', lambda e: e.tensor_reduce(out=r3a[:], in_=w4b[:], axis=AX.X, op=ALU.add), [w4b], [r3a])
        yield ('dve', lambda e: e.tensor_tensor(out=w4b[:], in0=w4a[:], in1=lo4, op=ALU.mult), [w4a, pc], [w4b])
        yield ('dve', lambda e: e.tensor_reduce(out=r3b[:], in_=w4b[:], axis=AX.X, op=ALU.add), [w4b], [r3b])
        yield ('dve', lambda e: e.tensor_tensor(out=r3b[:], in0=tpf[:], in1=r3b[:], op=ALU.subtract), [tpf, r3b], [r3b])
        yield ('dve', lambda e: e.tensor_tensor(out=w4a[:], in0=r3b[:].unsqueeze(3).to_broadcast([128, 8, 16, 16]), in1=io4,
                                      op=ALU.is_equal), [r3b, pc], [w4a])
        yield ('dve', lambda e: e.tensor_tensor(out=w4b[:], in0=w4a[:], in1=sif4[:, :, 1, :].unsqueeze(2).to_broadcast([128, 8, 16, 16]),
                                      op=ALU.mult), [w4a, sif], [w4b])
        yield ('dve', lambda e: e.tensor_reduce(out=r3c[:], in_=w4b[:], axis=AX.X, op=ALU.add), [w4b], [r3c])
        ef3 = eidx_f[:].rearrange("p (h r) -> p h r", h=8)
        yield ('dve', lambda e: e.scalar_tensor_tensor(out=ef3, in0=r3a[:], scalar=128.0, in1=r3c[:], op0=ALU.mult, op1=ALU.add),
            [r3a, r3c], [eidx_f])
        yield ('dve', lambda e: e.tensor_copy(out=eidx[:], in_=eidx_f[:]), [eidx_f], [eidx])
        yield ('dve', lambda e: e.tensor_tensor(out=gate[:], in0=ts[:], in1=ts[:, :, 0:1].to_broadcast([128, 8, 16]), op=ALU.subtract),
            [ts], [gate])
        yield ('act', lambda e: e.activation(out=gate[:], in_=gate[:], func=AF.Exp), [gate], [gate])
        yield ('dve', lambda e: e.tensor_reduce(out=zsum[:], in_=gate[:], axis=AX.X, op=ALU.add), [gate], [zsum])
        yield ('dve', lambda e: e.reciprocal(out=zsum[:], in_=zsum[:]), [zsum], [zsum])
        yield ('dve', lambda e: e.tensor_tensor(out=gate[:], in0=gate[:], in1=zsum[:].unsqueeze(2).to_broadcast([128, 8, 16]), op=ALU.mult),
            [gate, zsum], [gate])
        yield ('dve', lambda e: e.tensor_copy(out=eidx_t[m][:], in_=eidx[:]), [eidx], [eidx_t[m]])
        yield ('dve', lambda e: e.tensor_scalar(out=eidx_f[:], in0=eidx_f[:], scalar1=2.0, scalar2=None, op0=ALU.mult), [eidx_f], [eidx_f])
        yield ('dve', lambda e: e.tensor_copy(out=evidx_t[m][0][:], in_=eidx_f[:]), [eidx_f], [evidx_t[m][0]])
        yield ('dve', lambda e: e.tensor_scalar(out=eidx_f[:], in0=eidx_f[:], scalar1=1.0, scalar2=None, op0=ALU.add), [eidx_f], [eidx_f])
        yield ('dve', lambda e: e.tensor_copy(out=evidx_t[m][1][:], in_=eidx_f[:]), [eidx_f], [evidx_t[m][1]])
        yield ('dve', lambda e: e.tensor_copy(out=gate_t[m][:], in_=gate[:].rearrange("p h r -> p (h r)")), [gate], [gate_t[m]])


    def advance(gen, n=1):
        for _ in range(n):
            it = next(gen, None)
            if it is None:
                return False
            k.op(it[0], it[1], reads=it[2], writes=it[3])
        return True

    xTt = [k.sbuf(f"{pre}xTt{i}", [128, KT, 128], BF16) for i in range(2)]

    def load_x(m):
        xt = xts[m % 2]
        k.dma("sp", xt[:], x1_d[m * 128:(m + 1) * 128, :], reads=[x1_d], writes=[xt])
        xT_ = xTt[m % 2]
        k.dma("sp", xT_[:], x1T_d[:, m * 128:(m + 1) * 128].rearrange("(kt p) t -> p kt t", p=128), reads=[x1T_d], writes=[xT_])
        for kt in range(KT):
            k.op("pe", lambda e: e.transpose(out=xps[:, kt * 128:(kt + 1) * 128], in_=xT_[:, kt, :], identity=ident_b[:]),
                 reads=[xT_, ident_b], writes=[xps], sig=(kt == KT - 1))

    def u_slot(m, s):
        ub = ug[s % NG]
        hid = hids[m % 2]
        k.gather(ub[:], u_d[:, :], eidx_t[m][:, s:s + 1], reads=[eidx_t[m], u_d], writes=[ub])
        dve(lambda e: e.scalar_tensor_tensor(out=junk[:], in0=ub[:], scalar=1.0, in1=xps[:], op0=ALU.mult, op1=ALU.mult,
                                             accum_out=hid[:, s:s + 1]), [ub, xps], [junk, hid])

    def gelu_wgt(m):
        hid, wgt = hids[m % 2], wgts[m % 2]
        dve(lambda e: e.tensor_tensor(out=g1[:], in0=hid[:], in1=hid[:], op=ALU.mult), [hid], [g1])
        dve(lambda e: e.tensor_scalar(out=g1[:], in0=g1[:], scalar1=0.044715 * 1.5957691216057308,
                                      scalar2=1.5957691216057308, op0=ALU.mult, op1=ALU.add), [g1], [g1])
        dve(lambda e: e.tensor_tensor(out=g1[:], in0=g1[:], in1=hid[:], op=ALU.mult), [g1, hid], [g1])
        k.op("act", lambda e: e.activation(out=g1[:], in_=g1[:], func=AF.Sigmoid), reads=[g1], writes=[g1])
        dve(lambda e: e.tensor_tensor(out=g1[:], in0=g1[:], in1=hid[:], op=ALU.mult), [g1, hid], [g1])
        dve(lambda e: e.tensor_tensor(out=wgt[:], in0=g1[:], in1=gate_t[m][:], op=ALU.mult), [g1, gate_t[m]], [wgt])

    def run_all(gen):
        while advance(gen):
            pass

    run_all(route(0))
    load_x(0)
    r1 = route(1) if mt > 1 else None
    for s in range(128):
        u_slot(0, s)
        if r1 is not None:
            advance(r1, 2)
    if r1 is not None:
        run_all(r1)
    gelu_wgt(0)
    HD2 = D // 2
    for m in range(mt):
        xt, wgt = xts[m % 2], wgts[m % 2]
        has_next = m + 1 < mt
        r2 = route(m + 2) if m + 2 < mt else None
        if has_next:
            load_x(m + 1)
        for half in range(2):
            for s in range(128):
                vb_ = vg[s % NG]
                k.gather(vb_[:], v_d[:, :].rearrange("e (h c) -> (e h) c", h=2), evidx_t[m][half][:, s:s + 1], reads=[evidx_t[m][half], v_d], writes=[vb_])
                d_ = dgs[s % 8]
                dve(lambda e: e.tensor_scalar(out=d_[:], in0=ident_b[:], scalar1=wgt[:, s:s + 1], scalar2=None,
                                              op0=ALU.mult), [ident_b, wgt], [d_])
                for c in range(4):
                    k.op("pe", lambda e: e.matmul(acc[c][:], lhsT=d_[:], rhs=vb_[:, c * 512:(c + 1) * 512],
                                                  start=(s == 0), stop=(s == 127)), reads=[d_, vb_], writes=[acc[c]],
                         sig=(c == 3))
                if has_next and s % 2 == 0:
                    u_slot(m + 1, half * 64 + s // 2)
                if r2 is not None:
                    advance(r2, 1)
            for c in range(4):
                z_ = zo[c % 2]
                col = half * HD2 + c * 512
                dve(lambda e: e.scalar_tensor_tensor(out=z_[:], in0=xt[:, col:col + 512], scalar=ALPHA, in1=acc[c][:],
                                                     op0=ALU.mult, op1=ALU.add), [xt, acc[c]], [z_])
                k.dma("sp", z_d[m * 128:(m + 1) * 128, col:col + 512], z_[:], reads=[z_], writes=[z_d])
        if r2 is not None:
            run_all(r2)
        if has_next:
            gelu_wgt(m + 1)


def peer_consts():
    pc = np.zeros((128, 48), np.float32)
    pc[:, 0:16] = np.arange(16)
    pc[:, 16:32] = 16 * np.arange(16)
    pc[:, 32:48] = 16 * np.arange(16) + 16
    return pc


def build_peer(final, mt=MT):
    nc = bass.Bass("TRN2", target_bir_lowering=False)
    with ExitStack() as st:
        k = K(nc, st)
        x1 = k.dram("x1", [T, D], F32, "ExternalInput")
        x1T = k.dram("x1T", [D, T], BF16, "ExternalInput")
        wq = k.dram("wq", [D, 2048], F32, "ExternalInput")
        keysT = k.dram("keysT", [128, 2048], F32, "ExternalInput")
        ub16 = k.dram("u", [16384, D], BF16, "ExternalInput")
        vb16 = k.dram("v", [16384, D], BF16, "ExternalInput")
        g = k.dram("g", [1, D], F32, "ExternalInput")
        b = k.dram("b", [1, D], F32, "ExternalInput")
        pconst = k.dram("pconst", [128, 48], F32, "ExternalInput")
        identb = k.dram("identb", [128, 128], BF16, "ExternalInput")
        sc = k.dram("sc", [T, 2048], F32, "Internal")
        z = k.dram("z", [T, D], F32, "Internal")
        x2 = k.dram("x2", [T, D], F32, "ExternalOutput")
        x2T = None if final else k.dram("x2T", [D, T], BF16, "ExternalOutput")
        with ExitStack() as ph:
            k.stack = ph
            xb = [k.sbuf(f"pxb{i}", [128, T], BF16) for i in range(KT)]
            phase_peer_scores(k, "ps", x1T, wq, keysT, sc, xb)
            barrier(k)
        with ExitStack() as ph:
            k.stack = ph
            phase_peer_main(k, "pm", x1, x1T, sc, ub16, vb16, pconst, identb, z, mt=mt)
            barrier(k)
        with ExitStack() as ph:
            k.stack = ph
            ps_tr = [k.psum(f"lnpt{i}", [128, 8, 128], BF16) for i in range(2)]
            phase_ln(k, "pl", z, g, b, x2, x2T, identb, ps_tr, mt=mt)
            barrier(k)
        k.stack = st
        k.finish("sp")
    return nc


LAMBDA_INIT = 0.8 - 0.6 * math.exp(-0.3 * 1)
NKT_FULL = SEQ // 128


def dense_unit(k, kT, qT, q0, vaug, W, accs, ps_s, pts, cnt, scale, LA=2):
    slots = {}

    def score(kt):
        ps = ps_s[cnt["s"] % len(ps_s)]
        pt = pts[cnt["s"] % len(pts)]
        cnt["s"] += 1
        slots[kt] = pt
        k.op("pe", lambda g: g.matmul(ps[:], lhsT=kT[:, kt * 128:(kt + 1) * 128], rhs=qT[:, q0:q0 + 512],
                                      start=True, stop=True), reads=[kT, qT], writes=[ps])
        k.op("act", lambda g: g.activation(out=pt[:], in_=ps[:], func=AF.Exp, scale=scale), reads=[ps], writes=[pt])

    for kt in range(min(LA, NKT_FULL)):
        score(kt)
    for kt in range(NKT_FULL):
        if kt + LA < NKT_FULL:
            score(kt + LA)
        pt = slots.pop(kt)
        for j in range(4):
            k.op("pe", lambda g: g.matmul(accs[j][:, 0:W], lhsT=pt[:, j * 128:(j + 1) * 128], rhs=vaug[:, kt, 0:W],
                                          start=(kt == 0), stop=(kt == NKT_FULL - 1)), reads=[pt, vaug], writes=[accs[j]])


def phase_attn_cd(k, pre, qc_d, kc_d, vc_d, qd_d, kd_d, vd_d, lam_d, subg_d, identb_d, oT_d, nchunks=SEQ // 512):
    scale = 1.0 / math.sqrt(128.0)
    ident_b = k.sbuf(f"{pre}ident", [128, 128], BF16)
    k.dma("sp", ident_b[:], identb_d[:, :], reads=[identb_d], writes=[ident_b])
    lam_t = k.sbuf(f"{pre}lamt", [128, 4, 128], F32)
    k.dma("sp", lam_t[:], lam_d[:, :].unsqueeze(0).to_broadcast([128, 4, 128]), reads=[lam_d], writes=[lam_t])
    lprod = k.sbuf(f"{pre}lprod", [128, 2, 128], F32)
    lsum = k.sbuf(f"{pre}lsum", [128, 2], F32)
    neglam = k.sbuf(f"{pre}neglam", [128, 1], F32)
    lam4 = lam_t[:].rearrange("p (a b) d -> p a b d", b=2)
    k.op("dve", lambda g: g.tensor_tensor(out=lprod[:], in0=lam4[:, :, 0, :], in1=lam4[:, :, 1, :], op=ALU.mult),
         reads=[lam_t], writes=[lprod])
    k.op("dve", lambda g: g.tensor_reduce(out=lsum[:], in_=lprod[:], axis=AX.X, op=ALU.add), reads=[lprod], writes=[lsum])
    k.op("act", lambda g: g.activation(out=lsum[:], in_=lsum[:], func=AF.Exp), reads=[lsum], writes=[lsum])
    k.op("dve", lambda g: g.scalar_tensor_tensor(out=neglam[:], in0=lsum[:, 1:2], scalar=-LAMBDA_INIT, in1=lsum[:, 0:1],
                                                 op0=ALU.add, op1=ALU.subtract), reads=[lsum], writes=[neglam])
    subg = k.sbuf(f"{pre}subg", [128, 256], F32)
    k.dma("sp", subg[:], subg_d[0:1, :].to_broadcast([128, 256]), reads=[subg_d], writes=[subg])
    k.op("dve", lambda g: g.tensor_scalar(out=subg[:], in0=subg[:], scalar1=1.0 - LAMBDA_INIT, scalar2=None, op0=ALU.mult),
         reads=[subg], writes=[subg])

    kTs = [k.sbuf(f"{pre}kT{i}", [128, SEQ], BF16) for i in range(2)]
    qTs = [k.sbuf(f"{pre}qT{i}", [128, SEQ], BF16) for i in range(2)]
    vcs = k.sbuf(f"{pre}vc", [128, NKT_FULL, 257], BF16)
    vds = k.sbuf(f"{pre}vd", [128, NKT_FULL, 129], BF16)
    ps_s = [k.psum(f"{pre}pss{i}", [128, 512], F32) for i in range(3)]
    pts = [k.sbuf(f"{pre}pt{i}", [128, 512], BF16) for i in range(4)]
    accs = [k.psum(f"{pre}acc{i}", [128, 512], F32) for i in range(4)]
    ps_t = [k.psum(f"{pre}pst{i}", [128, 8, 128], BF16) for i in range(1)]
    o1s = k.sbuf(f"{pre}o1s", [128, nchunks * 4, 256], F32)
    den = k.sbuf(f"{pre}den", [128, 1], F32)
    o2 = k.sbuf(f"{pre}o2", [128, 256], F32)
    junk = k.sbuf(f"{pre}junk", [128, 256], F32)
    ssq = k.sbuf(f"{pre}ssq", [128, 1], F32)
    ob = [k.sbuf(f"{pre}ob{i}", [128, 256], BF16) for i in range(2)]
    oTst = [k.sbuf(f"{pre}oTst{i}", [128, 2, 512], BF16) for i in range(2)]
    cnt = {"s": 0, "o": 0, "t": 0}

    k.dma("sp", vcs[:], vc_d[:, :].rearrange("(j p) c -> p j c", p=128), reads=[vc_d], writes=[vcs])
    k.dma("sp", vds[:], vd_d[:, :].rearrange("(j p) c -> p j c", p=128), reads=[vd_d], writes=[vds])

    def recip_den(acc, W):
        k.op("dve", lambda g: g.reciprocal(out=den[:], in_=acc[:, W - 1:W]), reads=[acc], writes=[den])

    def emit_T(obuf, nblk, blk0, q0, j, last):
        pt_ = ps_t[cnt["t"] % len(ps_t)]
        cnt["t"] += 1
        for b_ in range(nblk):
            k.op("pe", lambda g: g.transpose(out=pt_[:, b_, :], in_=obuf[:, b_ * 128:(b_ + 1) * 128], identity=ident_b[:]),
                 reads=[obuf, ident_b], writes=[pt_])
        stg = oTst[(q0 // 512) % 2]
        k.copy("dve", stg, stg[:, 0:nblk, j * 128:(j + 1) * 128], pt_, pt_[:, 0:nblk, :])
        if last:
            k.dma("sp", oT_d[blk0:blk0 + nblk, :, q0:q0 + 512].rearrange("b p t -> p b t"), stg[:, 0:nblk, :],
                  reads=[stg], writes=[oT_d])

    for mp in range(2):
        kT, qT = kTs[mp], qTs[mp]
        k.dma("sp", kT[:], kc_d[mp], reads=[kc_d], writes=[kT])
        k.dma("sp", qT[:], qc_d[mp], reads=[qc_d], writes=[qT])
        for ch in range(nchunks):
            q0 = ch * 512
            dense_unit(k, kT, qT, q0, vcs, 257, accs, ps_s, pts, cnt, scale)
            for j in range(4):
                acc = accs[j]
                recip_den(acc, 257)
                if mp == 0:
                    k.op("dve", lambda g: g.tensor_scalar(out=o1s[:, ch * 4 + j, :], in0=acc[:, 0:256], scalar1=den[:, 0:1],
                                                          scalar2=None, op0=ALU.mult), reads=[acc, den], writes=[o1s])
                else:
                    k.op("dve", lambda g: g.tensor_scalar(out=o2[:], in0=acc[:, 0:256], scalar1=den[:, 0:1], scalar2=None,
                                                          op0=ALU.mult), reads=[acc, den], writes=[o2])
                    k.op("dve", lambda g: g.scalar_tensor_tensor(out=o2[:], in0=o2[:], scalar=neglam[:, 0:1],
                                                                 in1=o1s[:, ch * 4 + j, :], op0=ALU.mult, op1=ALU.add),
                         reads=[o2, neglam, o1s], writes=[o2])
                    k.op("act", lambda g: g.activation(out=junk[:], in_=o2[:], func=AF.Square, accum_out=ssq[:]),
                         reads=[o2], writes=[junk, ssq])
                    k.op("dve", lambda g: g.tensor_scalar(out=ssq[:], in0=ssq[:], scalar1=1.0 / 256, scalar2=1e-6,
                                                          op0=ALU.mult, op1=ALU.add), reads=[ssq], writes=[ssq])
                    k.op("act", lambda g: g.activation(out=ssq[:], in_=ssq[:], func=AF.Sqrt), reads=[ssq], writes=[ssq])
                    k.op("dve", lambda g: g.reciprocal(out=ssq[:], in_=ssq[:]), reads=[ssq], writes=[ssq])
                    o_ = ob[cnt["o"] % 2]
                    cnt["o"] += 1
                    k.op("dve", lambda g: g.scalar_tensor_tensor(out=o_[:], in0=o2[:], scalar=ssq[:, 0:1], in1=subg[:],
                                                                 op0=ALU.mult, op1=ALU.mult), reads=[o2, ssq, subg], writes=[o_])
                    emit_T(o_, 2, 0, q0, j, j == 3)
    kT = kTs[0]
    k.dma("sp", kT[:], kd_d[:, :], reads=[kd_d], writes=[kT])
    for hq in range(2):
        qT = qTs[hq]
        k.dma("sp", qT[:], qd_d[hq], reads=[qd_d], writes=[qT])
        for ch in range(nchunks):
            q0 = ch * 512
            dense_unit(k, kT, qT, q0, vds, 129, accs, ps_s, pts, cnt, scale)
            for j in range(4):
                acc = accs[j]
                recip_den(acc, 129)
                o_ = ob[cnt["o"] % 2]
                cnt["o"] += 1
                k.op("dve", lambda g: g.tensor_scalar(out=o_[:, 0:128], in0=acc[:, 0:128], scalar1=den[:, 0:1], scalar2=None,
                                                      op0=ALU.mult), reads=[acc, den], writes=[o_])
                emit_T(o_, 1, 2 + hq, q0, j, j == 3)


def build_attn_cd(nchunks=SEQ // 512):
    nc = bass.Bass("TRN2", target_bir_lowering=False)
    with ExitStack() as st:
        k = K(nc, st)
        qc = k.dram("qc", [2, 128, SEQ], BF16, "ExternalInput")
        kc = k.dram("kc", [2, 128, SEQ], BF16, "ExternalInput")
        vc = k.dram("vc", [SEQ, 257], BF16, "ExternalInput")
        qd = k.dram("qd", [2, 128, SEQ], BF16, "ExternalInput")
        kd = k.dram("kd", [128, SEQ], BF16, "ExternalInput")
        vd = k.dram("vd", [SEQ, 129], BF16, "ExternalInput")
        lam = k.dram("lam", [4, 128], F32, "ExternalInput")
        subg = k.dram("subg", [1, 256], F32, "ExternalInput")
        identb = k.dram("identb", [128, 128], BF16, "ExternalInput")
        oT = k.dram("oT", [4, 128, SEQ], BF16, "ExternalOutput")
        phase_attn_cd(k, "e", qc, kc, vc, qd, kd, vd, lam, subg, identb, oT, nchunks=nchunks)
        k.finish("sp")
    return nc


def _phase(k, outer):
    class _P:
        def __enter__(self_):
            self_.ph = ExitStack()
            self_.ph.__enter__()
            k.stack = self_.ph
            return self_.ph

        def __exit__(self_, *a):
            barrier(k)
            k.stack = outer
            return self_.ph.__exit__(*a)
    return _P()


def _peer_phases(k, st, tag, x1, x1T, wq, keysT, ub16, vb16, pconst, identb, sc, z, g, b, x2, x2T):
    with _phase(k, st):
        xb = [k.sbuf(f"{tag}pxb{i}", [128, T], BF16) for i in range(KT)]
        phase_peer_scores(k, tag + "ps", x1T, wq, keysT, sc, xb)
    with _phase(k, st):
        phase_peer_main(k, tag + "pm", x1, x1T, sc, ub16, vb16, pconst, identb, z)
    with _phase(k, st):
        ps_tr = [k.psum(f"{tag}lnpt{i}", [128, 8, 128], BF16) for i in range(2)]
        phase_ln(k, tag + "pl", z, g, b, x2, x2T, identb, ps_tr)


def build_mid():
    nc = bass.Bass("TRN2", target_bir_lowering=False)
    with ExitStack() as st:
        k = K(nc, st)
        EI, EO = "ExternalInput", "ExternalOutput"
        qT = k.dram("qT", [AB_NFM, 128, T], BF16, EI)
        kTe = k.dram("kTe", [28, 128, TE], BF16, EI)
        vaug = k.dram("vaug", [TE, 28, 129], BF16, EI)
        sink = k.dram("sink", [1, 16], F32, EI)
        masks = k.dram("masks", [128, N_MASKS, 128], BF16, EI)
        identb = k.dram("identb", [128, 128], BF16, EI)
        wo = k.dram("wo", [24 * 128, D], F32, EI)
        x = k.dram("x", [T, D], F32, EI)
        gm, bm = k.dram("gm", [1, D], F32, EI), k.dram("bm", [1, D], F32, EI)
        wq = k.dram("wq", [D, 2048], F32, EI)
        keysT = k.dram("keysT", [128, 2048], F32, EI)
        u16 = k.dram("u", [16384, D], BF16, EI)
        v16 = k.dram("v", [16384, D], BF16, EI)
        gf, bf = k.dram("gf", [1, D], F32, EI), k.dram("bf", [1, D], F32, EI)
        pconst = k.dram("pconst", [128, 48], F32, EI)
        w1 = k.dram("w1", [D, 9216], F32, EI)
        tabd = {nm: k.dram(nm, [T, 128], F32, EI) for nm in ("ccp", "ssp", "cca", "ssa")}
        qgd, kgd = k.dram("qg", [1, 128], F32, EI), k.dram("kg", [1, 128], F32, EI)
        oT = k.dram("oT", [24, 128, T], BF16)
        z = k.dram("z", [T, D], F32)
        x1 = k.dram("x1", [T, D], F32)
        x1T = k.dram("x1T", [D, T], BF16)
        sc = k.dram("sc", [T, 2048], F32)
        z2 = k.dram("z2", [T, D], F32)
        x2 = k.dram("x2", [T, D], F32, EO)
        x2T = k.dram("x2T", [D, T], BF16)
        qkT1 = k.dram("qkT1", [CD_NFM, 128, T], BF16, EO)
        vtm1 = k.dram("vtm1", [T, CD_NV], BF16, EO)
        with _phase(k, st):
            phase_attn_ab(k, "b", qT, kTe, vaug, sink, masks, identb, oT)
        with _phase(k, st):
            res = alloc_linear_res(k, "c")
            phase_outproj(k, "c", oT, 24, wo, x, z, res)
        with _phase(k, st):
            ps_tr = [k.psum(f"lnpt{i}", [128, 8, 128], BF16) for i in range(2)]
            phase_ln(k, "l", z, gm, bm, x1, x1T, identb, ps_tr)
        _peer_phases(k, st, "p", x1, x1T, wq, keysT, u16, v16, pconst, identb, sc, z2, gf, bf, x2, x2T)
        with _phase(k, st):
            res = alloc_linear_res(k, "a")
            ident_b = k.sbuf("identb_s", [128, 128], BF16)
            k.dma("sp", ident_b[:], identb[:, :], reads=[identb], writes=[ident_b])
            tabs = load_tabs(k, "a", list(tabd.keys()), tabd)
            gains = {}
            for nm, gd in (("qg", qgd), ("kg", kgd)):
                gs = k.sbuf("g_" + nm, [128, 128], F32)
                k.dma("sp", gs[:], gd[0:1, :].to_broadcast([128, 128]), reads=[gd], writes=[gs])
                gains[nm] = gs
            for kt in range(KT):
                k.dma("sp", res["xb"][kt][:], x2T[kt * 128:(kt + 1) * 128, :], reads=[x2T], writes=[res["xb"][kt]])
            phase_inproj(k, "a", x2T, w1, 9216, CD_TYPES, tabs, gains, ident_b, qkT1, vtm1, res)
        k.finish("sp")
    return nc


def build_tail():
    nc = bass.Bass("TRN2", target_bir_lowering=False)
    with ExitStack() as st:
        k = K(nc, st)
        EI, EO = "ExternalInput", "ExternalOutput"
        oT = k.dram("oT", [32, 128, T], BF16, EI)
        wo = k.dram("wo", [D, D], F32, EI)
        x = k.dram("x", [T, D], F32, EI)
        gm, bm = k.dram("gm", [1, D], F32, EI), k.dram("bm", [1, D], F32, EI)
        identb = k.dram("identb", [128, 128], BF16, EI)
        wq = k.dram("wq", [D, 2048], F32, EI)
        keysT = k.dram("keysT", [128, 2048], F32, EI)
        u16 = k.dram("u", [16384, D], BF16, EI)
        v16 = k.dram("v", [16384, D], BF16, EI)
        gf, bf = k.dram("gf", [1, D], F32, EI), k.dram("bf", [1, D], F32, EI)
        pconst = k.dram("pconst", [128, 48], F32, EI)
        z = k.dram("z", [T, D], F32)
        x1 = k.dram("x1", [T, D], F32)
        x1T = k.dram("x1T", [D, T], BF16)
        sc = k.dram("sc", [T, 2048], F32)
        z2 = k.dram("z2", [T, D], F32)
        out = k.dram("out", [T, D], F32, EO)
        with _phase(k, st):
            res = alloc_linear_res(k, "c")
            phase_outproj(k, "c", oT, 32, wo, x, z, res)
        with _phase(k, st):
            ps_tr = [k.psum(f"lnpt{i}", [128, 8, 128], BF16) for i in range(2)]
            phase_ln(k, "l", z, gm, bm, x1, x1T, identb, ps_tr)
        _peer_phases(k, st, "p", x1, x1T, wq, keysT, u16, v16, pconst, identb, sc, z2, gf, bf, out, None)
        k.finish("sp")
    return nc


def _run(nc, in_maps):
    res = run_bass_kernel_spmd(nc, in_maps, core_ids=list(range(NCORES)))
    return res.results


def _aug_ones(v):
    out = np.zeros(v.shape[:-1] + (v.shape[-1] + 1,), dtype=v.dtype)
    out[..., :-1] = v
    out[..., -1] = 1.0
    return out


def kernel(x, w_in_ab, sink_a, w_out_ab, w_in_cd, lam_q1, lam_k1, lam_q2, lam_k2, subln_g, q_norm_g, k_norm_g,
           w_out_cd, ln_mix_g, ln_mix_b, peer_wq, peer_keys, peer_u, peer_v, ln_ffn_g, ln_ffn_b):
    f32 = lambda a: np.ascontiguousarray(np.asarray(a, dtype=np.float32))
    x = f32(x)[0]
    ccp, ssp, cca, ssa = rope_tables()
    identb = np.eye(128, dtype=np.float32).astype(NP_BF16)
    masks = band_masks()
    pconst = peer_consts()
    cs = [slice(c * T, (c + 1) * T) for c in range(NCORES)]

    def keysT_of(l):
        return np.ascontiguousarray(f32(peer_keys)[l].reshape(16, 128, 128).transpose(2, 0, 1).reshape(128, 2048))

    w0 = f32(w_in_ab)[0]
    RW = 16384 // NCORES
    pu, pv = f32(peer_u), f32(peer_v)
    r = _run(build_inproj(0, conv=True), [{"xT": np.ascontiguousarray(x[cs[c]].T), "w": w0, "identb": identb,
                                           "ccp": ccp[cs[c]], "ssp": ssp[cs[c]],
                                           "u0p": pu[0, c * RW:(c + 1) * RW], "v0p": pv[0, c * RW:(c + 1) * RW],
                                           "u1p": pu[1, c * RW:(c + 1) * RW], "v1p": pv[1, c * RW:(c + 1) * RW]}
                                          for c in range(NCORES)])
    qkT_all = [r[c]["qkT"] for c in range(NCORES)]
    tabs16 = {nm: np.concatenate([r[c][nm + "b"] for c in range(NCORES)], axis=0) for nm in ("u0", "v0", "u1", "v1")}
    ext = host_ext_kv(qkT_all, [r[c]["vtm"] for c in range(NCORES)], [16, 17, 18, 19] + list(range(44, 68)), None)
    sink = f32(sink_a)[0].reshape(1, 16)
    lmg, lmb, lfg, lfb = f32(ln_mix_g), f32(ln_mix_b), f32(ln_ffn_g), f32(ln_ffn_b)
    pwq = f32(peer_wq)
    qg, kg = f32(q_norm_g)[0:1], f32(k_norm_g)[0:1]
    wo0, w1 = f32(w_out_ab)[0], f32(w_in_cd)[0]
    kT0 = keysT_of(0)
    r = _run(build_mid(), [{"qT": qkT_all[c], "kTe": ext[c][0], "vaug": ext[c][1], "sink": sink, "masks": masks, "identb": identb,
                            "wo": wo0, "x": x[cs[c]], "gm": lmg[0:1], "bm": lmb[0:1], "wq": pwq[0], "keysT": kT0,
                            "u": tabs16["u0"], "v": tabs16["v0"], "gf": lfg[0:1], "bf": lfb[0:1], "pconst": pconst, "w1": w1,
                            "ccp": ccp[cs[c]], "ssp": ssp[cs[c]], "cca": cca[cs[c]], "ssa": ssa[cs[c]], "qg": qg, "kg": kg}
                           for c in range(NCORES)])
    del ext, qkT_all
    x2 = [r[c]["x2"] for c in range(NCORES)]
    qk = np.concatenate([r[c]["qkT1"] for c in range(NCORES)], axis=2)
    vt = np.concatenate([r[c]["vtm1"] for c in range(NCORES)], axis=0)
    lam = np.concatenate([f32(lam_q1)[0:1], f32(lam_k1)[0:1], f32(lam_q2)[0:1], f32(lam_k2)[0:1]], 0)
    subg = f32(subln_g)[0:1]
    maps = []
    for c in range(NCORES):
        maps.append({"qc": np.ascontiguousarray(qk[2 * c:2 * c + 2]), "kc": np.ascontiguousarray(qk[16 + 2 * c:18 + 2 * c]),
                     "vc": _aug_ones(vt[:, 256 * c:256 * c + 256]), "qd": np.ascontiguousarray(qk[32 + 2 * c:34 + 2 * c]),
                     "kd": np.ascontiguousarray(qk[48 + c // 2]),
                     "vd": _aug_ones(vt[:, 2048 + 128 * (c // 2):2048 + 128 * (c // 2) + 128]),
                     "lam": lam, "subg": subg, "identb": identb})
    r = _run(build_attn_cd(), maps)
    del qk, vt, maps
    ofull = np.zeros((32, 128, SEQ), dtype=NP_BF16)
    for c in range(NCORES):
        o = r[c]["oT"]
        ofull[2 * c] = o[0]
        ofull[2 * c + 1] = o[1]
        ofull[16 + 2 * c] = o[2]
        ofull[16 + 2 * c + 1] = o[3]
    wo1 = f32(w_out_cd)[0]
    kT1 = keysT_of(1)
    r = _run(build_tail(), [{"oT": np.ascontiguousarray(ofull[:, :, cs[c]]), "wo": wo1, "x": x2[c], "gm": lmg[1:2], "bm": lmb[1:2],
                             "identb": identb, "wq": pwq[1], "keysT": kT1, "u": tabs16["u1"], "v": tabs16["v1"],
                             "gf": lfg[1:2], "bf": lfb[1:2], "pconst": pconst} for c in range(NCORES)])
    out = np.concatenate([r[c]["out"] for c in range(NCORES)], axis=0)
    return out[None].astype(np.float32)
```
